# Optimizing a Trainium2 kernel written in Bass

```python
import math
import jax, jax.numpy as jnp
from jax import lax
import numpy as np

D_MODEL = 1024
BATCH = 8
SEQ = 4096
DEPTH = 1

GRID_W = 64
CTX_LEN = 256
D_HYENA = 512
HYENA_ORDER = 2
FILTER_EMB = 33
FILTER_WIDTH = 64
FILTER_TARGET = 1e-2
FAST_DECAY_PCT = 0.3
SLOW_DECAY_PCT = 1.5
N_HEADS = 4
HEAD_DIM = 64
D_ATTN = N_HEADS * 2 * HEAD_DIM
ROPE_BASE = 10000.0
Q_BLOCK = 128
D_FF = ((8 * D_MODEL + 3 * 256 - 1) // (3 * 256)) * 256
EPS = 1e-6
OFF_Q = (HYENA_ORDER + 1) * D_HYENA
OFF_K = OFF_Q + D_ATTN
OFF_V = OFF_K + D_ATTN
OFF_G = OFF_V + D_ATTN
N_COLS = OFF_G + 2 * D_MODEL

kernel_name = 'hybrid_hyena_diffattn_dit_block'


def rmsnorm(x, g):
    xf = x.astype(jnp.float32)
    y = xf * lax.rsqrt(jnp.mean(xf * xf, axis=-1, keepdims=True) + EPS)
    return (y * g.astype(jnp.float32)).astype(x.dtype)


def modulate(h, shift, scale):
    return h * (1.0 + scale) + shift


def short_conv(u, w, b):
    L = u.shape[1]
    up = jnp.pad(u, ((0, 0), (1, 1), (0, 0)))
    return up[:, :L] * w[0] + up[:, 1:L + 1] * w[1] + up[:, 2:] * w[2] + b


def hyena_filters(L, fw1, fb1, fw2, fb2, fw3, fb3, ffreq):
    f32 = jnp.float32
    bands = (FILTER_EMB - 1) // 2
    t = jnp.linspace(0.0, 1.0, L, dtype=f32)[:, None]
    w = (2.0 * math.pi / L) * jnp.arange(L, dtype=f32)[:, None]
    fr = jnp.linspace(1e-4, bands - 1, bands, dtype=f32)[None, :]
    z = jnp.concatenate([t, jnp.cos(fr * w), -jnp.sin(fr * w)], axis=-1)
    freq = ffreq.astype(f32)
    hdn = jnp.sin(freq * (z @ fw1.astype(f32) + fb1.astype(f32)))
    hdn = jnp.sin(freq * (hdn @ fw2.astype(f32) + fb2.astype(f32)))
    h = (hdn @ fw3.astype(f32) + fb3.astype(f32)).reshape(L, HYENA_ORDER, 2, D_HYENA)
    deltas = jnp.abs(jnp.linspace(math.log(FILTER_TARGET) / SLOW_DECAY_PCT,
                                  math.log(FILTER_TARGET) / FAST_DECAY_PCT, D_HYENA, dtype=f32))
    h = h * jnp.exp(-t * deltas)[:, None, None, :]
    h_fwd, h_bwd = h[:, :, 0], h[:, :, 1]
    circ = jnp.concatenate([h_fwd, jnp.zeros((1, HYENA_ORDER, D_HYENA), f32), h_bwd[:0:-1]], axis=0)
    return jnp.fft.rfft(circ, axis=0)


def fftconv(u, hk, dbias):
    L = u.shape[1]
    uf = u.astype(jnp.float32)
    y = jnp.fft.irfft(jnp.fft.rfft(uf, n=2 * L, axis=1) * hk[None], n=2 * L, axis=1)[:, :L]
    return (y + uf * dbias.astype(jnp.float32)).astype(u.dtype)


def hyena_branch(u, conv_w, conv_b, fw1, fb1, fw2, fb2, fw3, fb3, ffreq, dbias):
    L = u.shape[1]
    u = short_conv(u, conv_w, conv_b)
    parts = jnp.split(u, HYENA_ORDER + 1, axis=-1)
    hk = hyena_filters(L, fw1, fb1, fw2, fb2, fw3, fb3, ffreq)
    z = parts[0]
    for n in range(HYENA_ORDER):
        z = parts[n + 1] * fftconv(z, hk[:, n], dbias[n])
    return z


def _rotate(x, pos):
    half = x.shape[-1] // 2
    inv = ROPE_BASE ** (-jnp.arange(half, dtype=jnp.float32) / half)
    ang = pos.astype(jnp.float32)[:, None] * inv[None, :]
    cos = jnp.cos(ang)[None, :, None, None, :].astype(x.dtype)
    sin = jnp.sin(ang)[None, :, None, None, :].astype(x.dtype)
    x1, x2 = x[..., :half], x[..., half:]
    return jnp.concatenate([x1 * cos - x2 * sin, x1 * sin + x2 * cos], axis=-1)


def rope_2d(x, row, col):
    h = x.shape[-1] // 2
    return jnp.concatenate([_rotate(x[..., :h], row), _rotate(x[..., h:], col)], axis=-1)


def diff_attention(q, k, v, lam):
    B, H, _, Lq, d = q.shape
    nb = Lq // Q_BLOCK
    qb = jnp.moveaxis(q.reshape(B, H, 2, nb, Q_BLOCK, d), 3, 0)
    scale = HEAD_DIM ** -0.5

    def block(qi):
        s = jnp.einsum('bhiqd,bhikd->bhiqk', qi, k).astype(jnp.float32) * scale
        p = jax.nn.softmax(s, axis=-1)
        a = (p[:, :, 0] - lam * p[:, :, 1]).astype(v.dtype)
        return jnp.einsum('bhqk,bhkv->bhqv', a, v)

    o = lax.map(block, qb)
    return jnp.moveaxis(o, 0, 2).reshape(B, H, Lq, 2 * d)


def heads_out(o, subln_g, lam_init):
    B, H, L, dv = o.shape
    o = rmsnorm(o, subln_g) * (1.0 - lam_init)
    return o.transpose(0, 2, 1, 3).reshape(B, L, H * dv)


def swiglu(h, w_gate, w_up, w_down):
    return (jax.nn.silu(h @ w_gate) * (h @ w_up)) @ w_down


def setup_inputs(seed: int = 0) -> dict:
    key = jax.random.key(seed)
    ks = jax.random.split(key, 40)
    f32 = jnp.float32

    def nrm(i, shape, s):
        return jax.random.normal(ks[i], shape, f32) * s

    return {
        'x': nrm(0, (BATCH, SEQ, D_MODEL), 1.0),
        'c': nrm(1, (BATCH, D_MODEL), 1.0),
        'ctx': nrm(2, (BATCH, CTX_LEN, D_MODEL), 1.0),
        'c_ctx': nrm(3, (D_MODEL,), 1.0),
        'w_ada': nrm(4, (DEPTH, D_MODEL, 6 * D_MODEL), 0.5 * D_MODEL ** -0.5),
        'b_ada': nrm(5, (DEPTH, 6 * D_MODEL), 0.01),
        'g_mix_pre': 1.0 + nrm(6, (DEPTH, D_MODEL), 0.05),
        'g_mix_post': 1.0 + nrm(7, (DEPTH, D_MODEL), 0.05),
        'g_ffn_pre': 1.0 + nrm(8, (DEPTH, D_MODEL), 0.05),
        'g_ffn_post': 1.0 + nrm(9, (DEPTH, D_MODEL), 0.05),
        'w_in': nrm(10, (DEPTH, D_MODEL, N_COLS), D_MODEL ** -0.5),
        'hy_conv_w': nrm(11, (DEPTH, 3, (HYENA_ORDER + 1) * D_HYENA), 0.5),
        'hy_conv_b': nrm(12, (DEPTH, (HYENA_ORDER + 1) * D_HYENA), 0.01),
        'hy_f_w1': nrm(13, (DEPTH, FILTER_EMB, FILTER_WIDTH), FILTER_EMB ** -0.5),
        'hy_f_b1': nrm(14, (DEPTH, FILTER_WIDTH), 0.1),
        'hy_f_w2': nrm(15, (DEPTH, FILTER_WIDTH, FILTER_WIDTH), FILTER_WIDTH ** -0.5),
        'hy_f_b2': nrm(16, (DEPTH, FILTER_WIDTH), 0.1),
        'hy_f_w3': nrm(17, (DEPTH, FILTER_WIDTH, HYENA_ORDER * 2 * D_HYENA), 0.03 * FILTER_WIDTH ** -0.5),
        'hy_f_b3': nrm(18, (DEPTH, HYENA_ORDER * 2 * D_HYENA), 0.003),
        'hy_f_freq': 1.0 + nrm(19, (DEPTH, FILTER_WIDTH), 0.05),
        'hy_bias': nrm(20, (DEPTH, HYENA_ORDER, D_HYENA), 1.0),
        'lambda_q1': nrm(21, (DEPTH, HEAD_DIM), 0.1),
        'lambda_k1': nrm(22, (DEPTH, HEAD_DIM), 0.1),
        'lambda_q2': nrm(23, (DEPTH, HEAD_DIM), 0.1),
        'lambda_k2': nrm(24, (DEPTH, HEAD_DIM), 0.1),
        'att_subln_g': 1.0 + nrm(25, (DEPTH, 2 * HEAD_DIM), 0.05),
        'w_hy_up': nrm(26, (DEPTH, D_HYENA, D_MODEL), D_HYENA ** -0.5),
        'w_att_up': nrm(27, (DEPTH, D_ATTN, D_MODEL), D_ATTN ** -0.5),
        'w_out': nrm(28, (DEPTH, D_MODEL, D_MODEL), D_MODEL ** -0.5),
        'w_ffn_gate': nrm(29, (DEPTH, D_MODEL, D_FF), D_MODEL ** -0.5),
        'w_ffn_up': nrm(30, (DEPTH, D_MODEL, D_FF), D_MODEL ** -0.5),
        'w_ffn_down': nrm(31, (DEPTH, D_FF, D_MODEL), D_FF ** -0.5),
    }


def reference(x, c, ctx, c_ctx, w_ada, b_ada, g_mix_pre, g_mix_post, g_ffn_pre, g_ffn_post,
              w_in, hy_conv_w, hy_conv_b, hy_f_w1, hy_f_b1, hy_f_w2, hy_f_b2, hy_f_w3, hy_f_b3,
              hy_f_freq, hy_bias, lambda_q1, lambda_k1, lambda_q2, lambda_k2, att_subln_g,
              w_hy_up, w_att_up, w_out, w_ffn_gate, w_ffn_up, w_ffn_down):
    B, L, D = x.shape
    C = ctx.shape[1]
    rows = L // GRID_W
    row = jnp.repeat(jnp.arange(rows, dtype=jnp.int32), GRID_W)
    col = jnp.tile(jnp.arange(GRID_W, dtype=jnp.int32), rows)
    f32 = jnp.float32

    for layer in range(DEPTH):
        last = layer == DEPTH - 1
        hy_p = (hy_conv_w[layer], hy_conv_b[layer], hy_f_w1[layer], hy_f_b1[layer], hy_f_w2[layer],
                hy_f_b2[layer], hy_f_w3[layer], hy_f_b3[layer], hy_f_freq[layer], hy_bias[layer])
        ada = jax.nn.silu(c) @ w_ada[layer] + b_ada[layer]
        sh1, sc1, g1, sh2, sc2, g2 = jnp.split(ada[:, None, :], 6, axis=-1)
        ada_c = jax.nn.silu(c_ctx) @ w_ada[layer] + b_ada[layer]
        csh1, csc1, cg1, csh2, csc2, cg2 = jnp.split(ada_c, 6, axis=-1)

        lam_init = 0.8 - 0.6 * math.exp(-0.3 * layer)
        lam = (jnp.exp(jnp.sum(lambda_q1[layer].astype(f32) * lambda_k1[layer].astype(f32)))
               - jnp.exp(jnp.sum(lambda_q2[layer].astype(f32) * lambda_k2[layer].astype(f32))) + lam_init)

        hc = modulate(rmsnorm(ctx, g_mix_pre[layer]), csh1, csc1)
        if last:
            kv_c = hc @ w_in[layer][:, OFF_K:OFF_G]
        else:
            proj_c = hc @ w_in[layer]
            kv_c = proj_c[..., OFF_K:OFF_G]
        k_c = kv_c[..., :D_ATTN].reshape(B, C, N_HEADS, 2, HEAD_DIM)
        v_c = kv_c[..., D_ATTN:].reshape(B, C, N_HEADS, 2 * HEAD_DIM)

        h = modulate(rmsnorm(x, g_mix_pre[layer]), sh1, sc1)
        proj = h @ w_in[layer]
        y_hy = hyena_branch(proj[..., :OFF_Q], *hy_p) @ w_hy_up[layer]
        q = rope_2d(proj[..., OFF_Q:OFF_K].reshape(B, L, N_HEADS, 2, HEAD_DIM), row, col)
        k = rope_2d(proj[..., OFF_K:OFF_V].reshape(B, L, N_HEADS, 2, HEAD_DIM), row, col)
        v = proj[..., OFF_V:OFF_G].reshape(B, L, N_HEADS, 2 * HEAD_DIM)
        k_all = jnp.concatenate([k, k_c], axis=1).transpose(0, 2, 3, 1, 4)
        v_all = jnp.concatenate([v, v_c], axis=1).transpose(0, 2, 1, 3)
        o = diff_attention(q.transpose(0, 2, 3, 1, 4), k_all, v_all, lam)
        y_att = heads_out(o, att_subln_g[layer], lam_init) @ w_att_up[layer]
        g_hy, g_att = jnp.split(jax.nn.sigmoid(proj[..., OFF_G:]), 2, axis=-1)
        mixed = (g_hy * y_hy + g_att * y_att) @ w_out[layer]
        x = x + g1 * rmsnorm(mixed, g_mix_post[layer])
        hf = modulate(rmsnorm(x, g_ffn_pre[layer]), sh2, sc2)
        f = swiglu(hf, w_ffn_gate[layer], w_ffn_up[layer], w_ffn_down[layer])
        x = x + g2 * rmsnorm(f, g_ffn_post[layer])

        if not last:
            qc = proj_c[..., OFF_Q:OFF_K].reshape(B, C, N_HEADS, 2, HEAD_DIM).transpose(0, 2, 3, 1, 4)
            oc = diff_attention(qc, k_c.transpose(0, 2, 3, 1, 4), v_c.transpose(0, 2, 1, 3), lam)
            yc_att = heads_out(oc, att_subln_g[layer], lam_init) @ w_att_up[layer]
            yc_hy = hyena_branch(proj_c[..., :OFF_Q], *hy_p) @ w_hy_up[layer]
            gc_hy, gc_att = jnp.split(jax.nn.sigmoid(proj_c[..., OFF_G:]), 2, axis=-1)
            mixed_c = (gc_hy * yc_hy + gc_att * yc_att) @ w_out[layer]
            ctx = ctx + cg1 * rmsnorm(mixed_c, g_mix_post[layer])
            hfc = modulate(rmsnorm(ctx, g_ffn_pre[layer]), csh2, csc2)
            fc = swiglu(hfc, w_ffn_gate[layer], w_ffn_up[layer], w_ffn_down[layer])
            ctx = ctx + cg2 * rmsnorm(fc, g_ffn_post[layer])
    return x
```

```python
import math
from contextlib import ExitStack
import numpy as np
import ml_dtypes
import concourse.bass as bass
import concourse.mybir as mybir
from concourse.bass_utils import run_bass_kernel_spmd

F32 = mybir.dt.float32
BF = mybir.dt.bfloat16
AF = mybir.ActivationFunctionType
ALU = mybir.AluOpType
AX = mybir.AxisListType

L = 4096
D = 1024
CT = 256
LK = L + CT
DH = 512
DFF = 2816
NFF = DFF // 128
OFF_Q, OFF_K, OFF_V, OFF_G = 1536, 2048, 2560, 3072
EPS = 1e-6
LAM_INIT = 0.8 - 0.6 * math.exp(0.0)
PI = math.pi


class Sem:
    def __init__(self, h):
        self.h = h
        self.count = 0


class T:
    __slots__ = ("name", "w", "r")

    def __init__(self, name=""):
        self.name = name
        self.w = None
        self.r = []


class Eng:
    def __init__(self, name, sem):
        self.name = name
        self.sem = sem
        self.ops = []
        self.seen = {}


class KB:
    def __init__(self, nc, nsem_dma=14):
        self.nc = nc
        self.engs = {}
        for n in ("pe", "act", "dve", "pool", "sp"):
            self.engs[n] = Eng(n, Sem(nc.alloc_semaphore("s_" + n)))
        self.dsems = {q: [Sem(nc.alloc_semaphore(f"d_{q}{i}")) for i in range(nsem_dma)] for q in ("sp", "pool")}
        self.drr = {"sp": 0, "pool": 0}

    def _waits(self, eng, reads, writes, extra=()):
        deps = {}

        def add(d):
            if d is None:
                return
            s, v = d
            if deps.get(s, 0) < v:
                deps[s] = v

        for t in reads:
            add(t.w)
        for t in writes:
            add(t.w)
            for d in t.r:
                add(d)
        for d in extra:
            add(d)
        out = []
        for s, v in deps.items():
            if s is eng.sem and eng.name == "pe":
                continue
            if eng.seen.get(s, 0) >= v:
                continue
            eng.seen[s] = v
            out.append((s, v))
        return out

    def _mark(self, tok, reads, writes):
        for t in reads:
            t.r = [d for d in t.r if d[0] is not tok[0]]
            t.r.append(tok)
        for t in writes:
            t.w = tok
            t.r = []

    def op(self, engname, fn, reads=(), writes=()):
        eng = self.engs[engname]
        waits = self._waits(eng, reads, writes)
        eng.sem.count += 1
        tok = (eng.sem, eng.sem.count)
        eng.ops.append((waits, fn, (eng.sem, 1)))
        self._mark(tok, reads, writes)
        return tok

    def dma(self, q, out_ap, in_ap, reads=(), writes=(), **kw):
        eng = self.engs[q]
        sems = self.dsems[q]
        s = sems[self.drr[q] % len(sems)]
        self.drr[q] += 1
        waits = self._waits(eng, reads, writes, extra=[(s, s.count)] if s.count else [])
        s.count += 16
        tok = (s, s.count)
        eng.ops.append((waits, lambda e: e.dma_start(out=out_ap, in_=in_ap, **kw), (s, 16)))
        self._mark(tok, reads, writes)
        return tok

    def collective(self, kind, ins, outs, reads=(), writes=()):
        eng = self.engs["pool"]
        if not hasattr(self, "ccsem"):
            self.ccsem = Sem(self.nc.alloc_semaphore("s_cc"))
        s = self.ccsem
        waits = self._waits(eng, reads, writes)
        s.count += 1
        tok = (s, s.count)
        eng.ops.append((waits, lambda e: e.collective_compute(kind, ALU.bypass, replica_groups=[list(range(8))], ins=ins, outs=outs), (s, 1)))
        self._mark(tok, reads, writes)
        return tok

    def barrier(self, include_cc=False):
        allsems = [e.sem for e in self.engs.values()] + [s for q in self.dsems.values() for s in q]
        if include_cc and hasattr(self, "ccsem"):
            allsems.append(self.ccsem)
        for eng in self.engs.values():
            waits = []
            for s in allsems:
                if s is eng.sem or s.count == 0:
                    continue
                if eng.seen.get(s, 0) >= s.count:
                    continue
                eng.seen[s] = s.count
                waits.append((s, s.count))
            if waits:
                eng.ops.append((waits, None, None))

    def finish(self):
        nc = self.nc
        self.barrier(include_cc=True)
        with nc.Block() as block:
            def emit(e, en):
                for waits, fn, inc in en.ops:
                    for (ws, wv) in waits:
                        e.wait_ge(ws.h, wv)
                    if fn is not None:
                        ins = fn(e)
                        ins.then_inc(inc[0].h, inc[1])

            @block.tensor
            def _(e):
                emit(e, self.engs["pe"])

            @block.scalar
            def _(e):
                emit(e, self.engs["act"])

            @block.vector
            def _(e):
                emit(e, self.engs["dve"])

            @block.gpsimd
            def _(e):
                emit(e, self.engs["pool"])

            @block.sync
            def _(e):
                emit(e, self.engs["sp"])


def _bf(a):
    return np.ascontiguousarray(a.astype(np.float32)).astype(ml_dtypes.bfloat16)


_CONST = None


def host_consts():
    global _CONST
    if _CONST is not None:
        return _CONST
    c = {}
    c["ident_bf"] = _bf(np.eye(128))
    c["ident_f"] = np.eye(128, dtype=np.float32)
    t = np.arange(L)
    row = (t // 64).astype(np.float32)
    col = (t % 64).astype(np.float32)
    inv = (10000.0 ** (-np.arange(16, dtype=np.float32) / 16)).astype(np.float32)
    cos64 = np.zeros((64, L), np.float32)
    sin64 = np.zeros((64, L), np.float32)
    for half, pos in ((0, row), (1, col)):
        ang = pos[None, :] * inv[:, None]
        base = half * 32
        cos64[base:base + 16] = np.cos(ang)
        cos64[base + 16:base + 32] = np.cos(ang)
        sin64[base:base + 16] = -np.sin(ang)
        sin64[base + 16:base + 32] = np.sin(ang)
    c["rope_cos"] = np.concatenate([cos64, cos64], 0)
    c["rope_sin"] = np.concatenate([sin64, sin64], 0)
    f32 = np.float32
    bands = 16
    tt = np.linspace(0.0, 1.0, L, dtype=f32)[:, None]
    w = (f32(2.0 * math.pi / L) * np.arange(L, dtype=f32))[:, None]
    fr = np.linspace(1e-4, bands - 1, bands, dtype=f32)[None, :]
    z = np.concatenate([tt, np.cos(fr * w), -np.sin(fr * w)], axis=-1).astype(f32)
    c["zT"] = np.ascontiguousarray(z.T)
    deltas = np.abs(np.linspace(math.log(1e-2) / 1.5, math.log(1e-2) / 0.3, DH, dtype=f32))
    nd = (-deltas).reshape(4, 128).T
    c["negdelta"] = np.ascontiguousarray(nd.astype(f32))
    offs = (np.arange(8) * 512 / (L - 1)).astype(f32)
    c["ndoff"] = np.ascontiguousarray((nd[:, :, None] * offs[None, None, :]).astype(f32))
    c["tv0"] = np.ascontiguousarray(np.broadcast_to((np.arange(512) / (L - 1)).astype(f32)[None, :], (128, 512)))
    n1 = np.arange(32)[:, None]
    k1 = np.arange(33)[None, :]
    a = 2 * np.pi * n1 * k1 / 64.0
    c["d1f"] = _bf(np.concatenate([np.cos(a), -np.sin(a)], 1))
    c["d1fp"] = np.ascontiguousarray(np.concatenate([c["d1f"], np.zeros((96, 66), ml_dtypes.bfloat16)], 0))
    c["d1b"] = _bf(np.concatenate([np.cos(a), np.sin(a)], 1))
    n2 = np.arange(128)[:, None, None]
    k1g = np.arange(33)[None, :, None]
    k2 = np.arange(128)[None, None, :]
    ang = 2 * np.pi * n2 * (k1g + 64 * k2) / 8192.0
    gr, gi = np.cos(ang), -np.sin(ang)
    c["gtab"] = _bf(np.stack([gr, gi, -gi], 2))
    k2e = np.arange(128)[:, None]
    n2e = np.arange(128)[None, :]
    ae = 2 * np.pi * k2e * n2e / 128.0
    c["etab"] = _bf(np.stack([np.cos(ae).reshape(128, 4, 32), np.sin(ae).reshape(128, 4, 32)], 2))
    k1t = np.arange(33)[:, None, None]
    n2t = np.arange(128)[None, :, None]
    n1t = np.arange(32)[None, None, :]
    at = 2 * np.pi * k1t * (128 * n1t + n2t) / 8192.0
    wgt = np.full((33, 1, 1), 2.0)
    wgt[0] = 1.0
    wgt[32] = 1.0
    tr = wgt * np.cos(at) / 8192.0
    ti = wgt * np.sin(at) / 8192.0
    t0 = np.concatenate([tr, -ti], 0)
    t1 = np.concatenate([-ti, -tr], 0)
    c["ttab"] = _bf(np.stack([t0, t1], 1))
    _CONST = c
    return c


CONST_SHAPES = {
    "ident_bf": ([128, 128], BF), "ident_f": ([128, 128], F32),
    "rope_cos": ([128, L], F32), "rope_sin": ([128, L], F32),
    "zT": ([33, L], F32), "negdelta": ([128, 4], F32), "ndoff": ([128, 4, 8], F32), "tv0": ([128, 512], F32),
    "d1f": ([32, 66], BF), "d1fp": ([128, 66], BF), "d1b": ([32, 66], BF), "gtab": ([128, 33, 3, 128], BF),
    "etab": ([128, 4, 2, 32], BF), "ttab": ([66, 2, 128, 32], BF),
}

IN_SHAPES = {
    "x": [L, D], "ctx": [CT, D], "cc": [128, 8, 2], "w_ada": [D, 6 * D], "b_adaT": [128, 48], "gcols": [128, 4, 8],
    "w_in": [D, 5120], "w_qk_sw": [D, 1024], "cw": [128, 12, 3], "cb": [128, 12],
    "fw1": [33, 64], "fb1": [64, 1], "fw2": [64, 64], "fb2": [64, 1], "fw3": [64, 2048], "fb3": [1, 2048],
    "ffreq": [64, 1], "hyb": [128, 2, 4], "lamv": [1, 256], "subg": [1, 128],
    "w_hy_up": [DH, D], "w_att_up": [DH, D], "w_out": [D, D], "w_fg": [D, DFF], "w_fu": [D, DFF], "w_fd": [DFF, D],
}


def layout_inputs(inp, b):
    f = lambda a: np.ascontiguousarray(np.asarray(a, dtype=np.float32))
    m = {}
    m["x"] = f(inp["x"][b])
    m["ctx"] = f(inp["ctx"][b])
    cc = np.stack([np.asarray(inp["c"][b]), np.asarray(inp["c_ctx"])], -1)
    m["cc"] = f(cc.reshape(8, 128, 2).transpose(1, 0, 2))
    m["w_ada"] = f(inp["w_ada"][0])
    m["b_adaT"] = f(np.asarray(inp["b_ada"][0]).reshape(48, 128).T)
    g = np.stack([np.asarray(inp[k][0]) for k in ("g_mix_pre", "g_mix_post", "g_ffn_pre", "g_ffn_post")], 0)
    m["gcols"] = f(g.reshape(4, 8, 128).transpose(2, 0, 1))
    w_in = np.asarray(inp["w_in"][0])
    m["w_in"] = f(w_in)
    perm = np.arange(1024).reshape(16, 2, 2, 16)[:, :, ::-1, :].reshape(-1)
    m["w_qk_sw"] = f(w_in[:, OFF_Q:OFF_V][:, perm])
    m["cw"] = f(np.asarray(inp["hy_conv_w"][0]).reshape(3, 12, 128).transpose(2, 1, 0))
    m["cb"] = f(np.asarray(inp["hy_conv_b"][0]).reshape(12, 128).T)
    m["fw1"] = f(inp["hy_f_w1"][0])
    m["fb1"] = f(np.asarray(inp["hy_f_b1"][0]).reshape(64, 1))
    m["fw2"] = f(inp["hy_f_w2"][0])
    m["fb2"] = f(np.asarray(inp["hy_f_b2"][0]).reshape(64, 1))
    m["fw3"] = f(inp["hy_f_w3"][0])
    m["fb3"] = f(np.asarray(inp["hy_f_b3"][0]).reshape(1, 2048))
    m["ffreq"] = f(np.asarray(inp["hy_f_freq"][0]).reshape(64, 1))
    m["hyb"] = f(np.asarray(inp["hy_bias"][0]).reshape(2, 4, 128).transpose(2, 0, 1))
    m["lamv"] = f(np.concatenate([np.asarray(inp[k][0]) for k in ("lambda_q1", "lambda_q2", "lambda_k1", "lambda_k2")]).reshape(1, 256))
    m["subg"] = f(np.asarray(inp["att_subln_g"][0]).reshape(1, 128))
    m["w_hy_up"] = f(inp["w_hy_up"][0])
    m["w_att_up"] = f(inp["w_att_up"][0])
    m["w_out"] = f(inp["w_out"][0])
    m["w_fg"] = f(inp["w_ffn_gate"][0])
    m["w_fu"] = f(inp["w_ffn_up"][0])
    m["w_fd"] = f(inp["w_ffn_down"][0])
    m.update(host_consts())
    return m


def build(dbg=(), stop_after=None, skip=()):
    nc = bass.Bass("TRN2", target_bir_lowering=False)
    kb = KB(nc)
    din = {}
    for k, shp in IN_SHAPES.items():
        din[k] = nc.dram_tensor(k, list(shp), F32, kind="ExternalInput").ap()
    for k, (shp, dt) in CONST_SHAPES.items():
        din[k] = nc.dram_tensor(k, list(shp), dt, kind="ExternalInput").ap()
    out = nc.dram_tensor("out", [L, D], F32, kind="ExternalOutput").ap()
    hT_d = nc.dram_tensor("hT_d", [128, 8, L], BF, kind="Internal").ap()
    mT_d = nc.dram_tensor("mT_d", [128, 8, L], BF, kind="Internal").ap()
    zT_d = nc.dram_tensor("zT_d", [128, 4, L], BF, kind="Internal").ap()
    aT_d = nc.dram_tensor("aT_d", [128, 4, L], BF, kind="Internal").ap()
    sig_d = nc.dram_tensor("sig_d", [5, 128, L], BF, kind="Internal").ap()
    dbg_out = {}

    def dbg_tensor(name, shape, dt=F32):
        dbg_out[name] = nc.dram_tensor("dbg_" + name, list(shape), dt, kind="ExternalOutput").ap()
        return dbg_out[name]

    psall = nc.alloc_psum_tensor("psall", [128, 4096], F32)
    psall_b = psall.bitcast(BF)
    ps = [psall[:, i * 512:(i + 1) * 512] for i in range(8)]
    psb = [psall_b[:, i * 1024:(i + 1) * 1024] for i in range(8)]
    tps = [T(f"ps{i}") for i in range(8)]

    def mm(out_ap, lhsT, rhs, start, stop, reads, writes, tile_position=None):
        if tile_position is None:
            kb.op("pe", lambda e: e.matmul(out_ap, lhsT, rhs, start=start, stop=stop), reads=reads, writes=writes)
        else:
            kb.op("pe", lambda e: e.matmul(out_ap, lhsT, rhs, start=start, stop=stop, tile_position=tile_position), reads=reads, writes=writes)

    def tr(out_ap, in_ap, ident, reads, writes):
        kb.op("pe", lambda e: e.transpose(out_ap, in_ap, ident), reads=reads, writes=writes)

    def act(out_ap, in_ap, func, reads, writes, **kw):
        kb.op("act", lambda e: e.activation(out=out_ap, in_=in_ap, func=func, **kw), reads=reads, writes=writes)

    def ts(eng, out_ap, in0, s1, s2, op0, op1, reads, writes, **kw):
        if s2 is None:
            kb.op(eng, lambda e: e.tensor_scalar(out_ap, in0, s1, None, op0, **kw), reads=reads, writes=writes)
        else:
            kb.op(eng, lambda e: e.tensor_scalar(out_ap, in0, s1, s2, op0, op1, **kw), reads=reads, writes=writes)

    def tt(eng, out_ap, in0, in1, op, reads, writes):
        kb.op(eng, lambda e: e.tensor_tensor(out_ap, in0, in1, op), reads=reads, writes=writes)

    def stt(out_ap, in0, scalar, in1, op0, op1, reads, writes):
        kb.op("dve", lambda e: e.scalar_tensor_tensor(out_ap, in0, scalar, in1, op0, op1), reads=reads, writes=writes)

    def cp(eng, out_ap, in_ap, reads, writes):
        if eng == "act":
            kb.op("act", lambda e: e.copy(out_ap, in_ap), reads=reads, writes=writes)
        else:
            kb.op(eng, lambda e: e.tensor_copy(out_ap, in_ap), reads=reads, writes=writes)

    def recip(out_ap, in_ap, reads, writes):
        kb.op("dve", lambda e: e.reciprocal(out_ap, in_ap), reads=reads, writes=writes)

    def rsum(out_ap, in_ap, reads, writes):
        kb.op("dve", lambda e: e.reduce_sum(out_ap, in_ap, AX.X), reads=reads, writes=writes)

    def memset(eng, ap, val, writes):
        kb.op(eng, lambda e: e.memset(ap, val), writes=writes)

    _ids = [0]

    def next_id():
        _ids[0] += 1
        return _ids[0]

    P = ExitStack()

    def sbp(name, shape, dt):
        return P.enter_context(nc.sbuf_tensor("sp_" + name, list(shape), dt))

    ident_bf = sbp("ident_bf", [128, 128], BF)
    ident_f = sbp("ident_f", [128, 128], F32)
    ones_f = sbp("ones_f", [128, 128], F32)
    mhalf = sbp("mhalf", [128, 32], F32)
    cols = sbp("cols", [128, 8, 8], F32)
    g1row = sbp("g1row", [128, D], F32)
    g2row = sbp("g2row", [128, D], F32)
    neglam = sbp("neglam", [128, 1], F32)
    sgrow = sbp("sgrow", [128, 128], F32)
    hcT = sbp("hcT", [128, 8, CT], BF)
    t_const = T("const")
    t_cols = T("cols")
    t_rows = T("rows")
    t_hcT = T("hcT")
    kb.dma("sp", ident_bf[:], din["ident_bf"], writes=[t_const])
    kb.dma("sp", ident_f[:], din["ident_f"], writes=[t_const])
    memset("dve", ones_f[:], 1.0, [t_const])
    memset("dve", mhalf[:], -0.5, [t_const])
    A1 = lambda k: cols[:, 0, k:k + 1]
    B1 = lambda k: cols[:, 1, k:k + 1]
    A2 = lambda k: cols[:, 2, k:k + 1]
    B2 = lambda k: cols[:, 3, k:k + 1]
    A1c = lambda k: cols[:, 4, k:k + 1]
    B1c = lambda k: cols[:, 5, k:k + 1]

    def rstd_of(ssq_ap, out_ap, inv_n, tl):
        n = ssq_ap.shape[-1] if len(ssq_ap.shape) > 1 else 1
        ts("dve", out_ap, ssq_ap, inv_n, EPS, ALU.mult, ALU.add, reads=[tl], writes=[tl])
        tt("pool", out_ap, out_ap, mhalf[:, 0:n], ALU.pow, reads=[tl, t_const], writes=[tl])

    with ExitStack() as ph:
        def sb(name, shape, dt):
            return ph.enter_context(nc.sbuf_tensor(f"s{next_id()}_" + name, list(shape), dt))

        ccs = sb("ccs", [128, 16], F32)
        scs = sb("scs", [128, 16], F32)
        bada = sb("bada", [128, 48], F32)
        gc = sb("gc", [128, 4, 8], F32)
        adaT = sb("adaT", [128, 48, 2], F32)
        wa = [sb(f"wa{i}", [128, 6 * D], F32) for i in range(2)]
        twa = [T("wa0"), T("wa1")]
        t_s = T("p0small")
        kb.dma("sp", ccs[:], din["cc"].rearrange("p k c -> p (k c)"), writes=[t_s])
        kb.dma("sp", bada[:], din["b_adaT"], writes=[t_s])
        kb.dma("sp", gc[:], din["gcols"], writes=[t_s])
        act(scs[:], ccs[:], AF.Silu, reads=[t_s], writes=[t_s])
        def ada_chunk(k):
            kb.dma("sp", wa[k % 2][:], din["w_ada"][k * 128:(k + 1) * 128, :], writes=[twa[k % 2]])
            for f in range(48):
                mm(ps[0][:, 2 * f:2 * f + 2], wa[k % 2][:, f * 128:(f + 1) * 128], scs[:, 2 * k:2 * k + 2],
                   start=(k == 0 and f == 0), stop=(k == 7 and f == 47), reads=[twa[k % 2], t_s], writes=[tps[0]])
        lamb = sb("lamb", [128, 256], F32)
        lp = sb("lp", [128, 2, 64], F32)
        le = sb("le", [128, 2], F32)
        kb.dma("sp", lamb[:], bass.AP(din["lamv"].tensor, 0, [[0, 128], [1, 256]]), writes=[t_s])
        kb.dma("sp", sgrow[:], bass.AP(din["subg"].tensor, 0, [[0, 128], [1, 128]]), writes=[t_rows])
        tt("dve", lp[:].rearrange("p a b -> p (a b)"), lamb[:, 0:128], lamb[:, 128:256], ALU.mult, reads=[t_s], writes=[t_s])
        rsum(le[:], lp[:], reads=[t_s], writes=[t_s])
        act(le[:], le[:], AF.Exp, reads=[t_s], writes=[t_s])
        tt("dve", neglam[:], le[:, 1:2], le[:, 0:1], ALU.subtract, reads=[t_s], writes=[t_cols])
        ts("dve", neglam[:], neglam[:], -LAM_INIT, None, ALU.add, None, reads=[t_cols], writes=[t_cols])
        ts("dve", sgrow[:], sgrow[:], 1.0 - LAM_INIT, None, ALU.mult, None, reads=[t_rows], writes=[t_rows])
        xts = [sb(f"xt{i}", [128, D], F32) for i in range(4)]
        txt = [T() for _ in range(4)]
        junk = sb("junk", [128, D], BF)
        t_junk = T()
        xsa = sb("xsa", [128, 34, D], BF)
        txs = [T() for _ in range(34)]
        ssq = sb("ssq", [128, 34], F32)
        t_ssq = [T() for _ in range(34)]
        hst = [sb(f"hst{i}", [128, 8, 512], BF) for i in range(2)]
        thst = [T(), T()]
        t_hd = T("hT_d")
        for i in range(34):
            lat = i < 32
            src = din["x"][i * 128:(i + 1) * 128, :] if lat else din["ctx"][(i - 32) * 128:(i - 31) * 128, :]
            xt, tx = xts[i % 4], txt[i % 4]
            if i % 4 == 0 and i // 4 < 8:
                ada_chunk(i // 4)
            kb.dma("sp", xt[:], src, writes=[tx])
            act(junk[:], xt[:], AF.Square, reads=[tx], writes=[t_junk, t_ssq[i]], accum_out=ssq[:, i:i + 1])
            rstd_of(ssq[:, i:i + 1], ssq[:, i:i + 1], 1.0 / D, t_ssq[i])
            if i >= 2:
                j = i - 2
                act(xsa[:, j, :], xts[j % 4][:], AF.Copy, reads=[txt[j % 4], t_ssq[j]], writes=[txs[j]], scale=ssq[:, j:j + 1])
        for j in (32, 33):
            act(xsa[:, j, :], xts[j % 4][:], AF.Copy, reads=[txt[j % 4], t_ssq[j]], writes=[txs[j]], scale=ssq[:, j:j + 1])
        for c in range(2):
            tt("dve", adaT[:, :, c], ps[0][:, c:96:2], bada[:], ALU.add, reads=[tps[0], t_s], writes=[t_s])
        for (dst, sc_f, g_i, c) in ((0, 8, 0, 0), (2, 32, 2, 0), (4, 8, 0, 1)):
            stt(cols[:, dst, :], adaT[:, sc_f:sc_f + 8, c], 1.0, gc[:, g_i, :], ALU.add, ALU.mult, reads=[t_s], writes=[t_cols])
        for (dst, sh_f, c) in ((1, 0, 0), (3, 24, 0), (5, 0, 1)):
            cp("dve", cols[:, dst, :], adaT[:, sh_f:sh_f + 8, c], reads=[t_s], writes=[t_cols])
        tt("dve", cols[:, 6, :], adaT[:, 16:24, 0], gc[:, 1, :], ALU.mult, reads=[t_s], writes=[t_cols])
        tt("dve", cols[:, 7, :], adaT[:, 40:48, 0], gc[:, 3, :], ALU.mult, reads=[t_s], writes=[t_cols])
        diag = sb("diag", [128, 4, 128], F32)
        t_diag = T("diag")
        for gi, rowt in ((6, g1row), (7, g2row)):
            for half in range(2):
                for j in range(4):
                    ts("dve", diag[:, j, :], ident_f[:], cols[:, gi, half * 4 + j:half * 4 + j + 1], None, ALU.mult, None,
                       reads=[t_const, t_cols], writes=[t_diag])
                for j in range(4):
                    mm(ps[1][:, j * 128:(j + 1) * 128], ones_f[:], diag[:, j, :], start=True, stop=True,
                       reads=[t_diag, t_const], writes=[tps[1]])
                cp("dve", rowt[:, half * 512:(half + 1) * 512], ps[1][:], reads=[tps[1]], writes=[t_rows])
        if "p0" in dbg:
            d = dbg_tensor("cols", [128, 64])
            kb.dma("sp", d, cols[:].rearrange("p a b -> p (a b)"), reads=[t_cols])
            d = dbg_tensor("g1row", [128, D])
            kb.dma("sp", d, g1row[:], reads=[t_rows])
            d = dbg_tensor("neglam", [128, 1])
            kb.dma("sp", d, neglam[:], reads=[t_cols])

        for i in range(34):
            lat = i < 32
            bk = 2 + i % 4
            for k in range(8):
                tr(psb[bk][:, k * 128:(k + 1) * 128], xsa[:, i, k * 128:(k + 1) * 128], ident_bf[:],
                   reads=[txs[i], t_const], writes=[tps[bk]])
            for k in range(8):
                if lat:
                    h = hst[(i // 4) % 2]
                    ts("dve", h[:, k, (i % 4) * 128:(i % 4 + 1) * 128], psb[bk][:, k * 128:(k + 1) * 128], A1(k), B1(k),
                       ALU.mult, ALU.add, reads=[tps[bk], t_cols], writes=[thst[(i // 4) % 2]])
                else:
                    ts("dve", hcT[:, k, (i - 32) * 128:(i - 31) * 128], psb[bk][:, k * 128:(k + 1) * 128], A1c(k), B1c(k),
                       ALU.mult, ALU.add, reads=[tps[bk], t_cols], writes=[t_hcT])
            if lat and i % 4 == 3:
                blk = i // 4
                kb.dma("sp", hT_d[:, :, blk * 512:(blk + 1) * 512], hst[blk % 2][:], reads=[thst[blk % 2]], writes=[t_hd])
        if "p1" in dbg:
            d = dbg_tensor("hT", [128, 8, L], BF)
            kb.dma("sp", d, hT_d, reads=[t_hd])
            d = dbg_tensor("hcT", [128, 8, CT], BF)
            kb.dma("sp", d, hcT[:], reads=[t_hcT])
        kb.barrier()
    if stop_after == "p1":
        kb.finish()
        return nc, dbg_out


    hall_d = nc.dram_tensor("hall_d", [1024, 8448], BF, kind="Internal").ap()
    t_hall = [T(f"hall{i}") for i in range(8)]
    with ExitStack() as ph:
        def sb(name, shape, dt):
            return ph.enter_context(nc.sbuf_tensor(f"s{next_id()}_" + name, list(shape), dt))

        _bk = [0]

        def nb():
            _bk[0] = (_bk[0] + 1) % 8
            return _bk[0]

        t_hc = T("hfconst")
        gtab = sb("gtab", [128, 33, 3, 128], BF)
        d1f = sb("d1fp", [128, 66], BF)
        tv0 = sb("tv0", [128, 512], F32)
        negd = sb("negd", [128, 4], F32)
        ndoff = sb("ndoff", [128, 4, 8], F32)
        hybs = sb("hybs", [128, 2, 4], F32)
        for dst, nm in ((gtab, "gtab"), (d1f, "d1fp"), (tv0, "tv0"), (negd, "negdelta"), (ndoff, "ndoff"), (hybs, "hyb")):
            kb.dma("sp", dst[:], din[nm], writes=[t_hc])
        hdn2 = sb("hdn2", [65, L], BF)
        fw3a = sb("fw3a", [65, 2048], BF)
        t_h2 = T("hdn2")
        t_fw3 = T("fw3")
        kb.dma("pool", fw3a[0:64, :], din["fw3"], writes=[t_fw3])
        kb.dma("pool", fw3a[64:65, :], din["fb3"], writes=[t_fw3])
        memset("pool", hdn2[64:65, :], 1.0, [t_h2])
        zTs = sb("zTs", [33, L], F32)
        fw1s = sb("fw1s", [33, 64], F32)
        fw2s = sb("fw2s", [64, 64], F32)
        fcol = sb("fcol", [64, 5], F32)
        t_f = T("fmlp")
        kb.dma("sp", zTs[:], din["zT"], writes=[t_f])
        kb.dma("sp", fw1s[:], din["fw1"], writes=[t_f])
        kb.dma("sp", fw2s[:], din["fw2"], writes=[t_f])
        kb.dma("sp", fcol[:, 0:1], din["ffreq"], writes=[t_f])
        kb.dma("sp", fcol[:, 1:2], din["fb1"], writes=[t_f])
        kb.dma("sp", fcol[:, 2:3], din["fb2"], writes=[t_f])
        tt("dve", fcol[:, 3:4], fcol[:, 0:1], fcol[:, 1:2], ALU.mult, reads=[t_f], writes=[t_f])
        tt("dve", fcol[:, 4:5], fcol[:, 0:1], fcol[:, 2:3], ALU.mult, reads=[t_f], writes=[t_f])
        halfpi = sb("halfpi", [64, 1], F32)
        memset("dve", halfpi[:], PI / 2, [t_f])
        NB_ = 2
        arg = [sb(f"arg{i}", [64, 512], F32) for i in range(NB_)]
        s4 = [sb(f"s4{i}", [64, 512], F32) for i in range(NB_)]
        c4 = [sb(f"c4{i}", [64, 512], F32) for i in range(NB_)]
        hd1 = [sb(f"hd1{i}", [64, 512], F32) for i in range(NB_)]
        t_m = [T() for _ in range(NB_)]

        def sin_layer(i_, ps_ap, tpsb, bias_col, out_ap, t_out):
            a_, s_, c_, tm = arg[i_], s4[i_], c4[i_], t_m[i_]
            ts("dve", a_[:], ps_ap, fcol[:, 0:1], fcol[:, bias_col:bias_col + 1], ALU.mult, ALU.add, reads=[tpsb, t_f], writes=[tm])
            act(s_[:], a_[:], AF.Sin, reads=[tm], writes=[tm], scale=0.25)
            act(c_[:], a_[:], AF.Sin, reads=[tm, t_f], writes=[tm], scale=0.25, bias=halfpi[:])
            tt("pool", a_[:], s_[:], s_[:], ALU.mult, reads=[tm], writes=[tm])
            ts("dve", a_[:], a_[:], -2.0, 1.0, ALU.mult, ALU.add, reads=[tm], writes=[tm])
            tt("pool", s_[:], s_[:], c_[:], ALU.mult, reads=[tm], writes=[tm])
            stt(out_ap, s_[:], 4.0, a_[:], ALU.mult, ALU.mult, reads=[tm], writes=[tm, t_out])

        for blk in range(8):
            sl = slice(blk * 512, (blk + 1) * 512)
            i_ = blk % NB_
            b = nb()
            mm(ps[b][0:64, :], fw1s[:], zTs[:, sl], start=True, stop=True, reads=[t_f], writes=[tps[b]])
            sin_layer(i_, ps[b][0:64, :], tps[b], 3, hd1[i_][:], t_m[i_])
            b = nb()
            mm(ps[b][0:64, :], fw2s[:], hd1[i_][:], start=True, stop=True, reads=[t_f, t_m[i_]], writes=[tps[b]])
            sin_layer(i_, ps[b][0:64, :], tps[b], 4, hdn2[0:64, sl], t_h2)
        dec = [sb(f"dec{i}", [128, L], BF) for i in range(2)]
        t_dec = [T(), T()]
        Kf = [[sb(f"Kf{i}_{d_}", [128, L], BF) for d_ in range(2)] for i in range(2)]
        t_Kf = [[T(), T()], [T(), T()]]
        Ut = [sb(f"Ut{i}", [128, 32, 128], BF) for i in range(2)]
        tUt = [T(), T()]
        for i in range(2):
            memset("pool", Ut[i][32:64, :, :], 0.0, [tUt[i]])
            memset("pool", Ut[i][64:128, :, :], 0.0, [tUt[i]])
        A = sb("A", [128, 66, 128], BF)
        tA = [T() for _ in range(33)]
        H = [sb(f"H{i}", [128, 33, 2, 128], BF) for i in range(2)]
        tH = [[T() for _ in range(33)] for _ in range(2)]
        t_sdf = [T() for _ in range(4)]
        cnt = [0]

        def stage_k(r):
            cc, o = r // 2, r % 2
            pi_ = r % 2
            if o == 0:
                for b8 in range(8):
                    act(dec[cc % 2][:, b8 * 512:(b8 + 1) * 512], tv0[:], AF.Exp, reads=[t_hc], writes=[t_dec[cc % 2]],
                        scale=negd[:, cc:cc + 1], bias=ndoff[:, cc, b8:b8 + 1])
            for d_ in range(2):
                col0 = (o * 2 + d_) * 512 + cc * 128
                kf, tk = Kf[pi_][d_], t_Kf[pi_][d_]
                for blk in range(8):
                    sl = slice(blk * 512, (blk + 1) * 512)
                    b = nb()
                    mm(ps[b][:], fw3a[:, col0:col0 + 128], hdn2[:, sl], start=True, stop=True, reads=[t_fw3, t_h2], writes=[tps[b]])
                    tt("dve", kf[:, sl], ps[b][:], dec[cc % 2][:, sl], ALU.mult, reads=[tps[b], t_dec[cc % 2]], writes=[tk])
                if d_ == 0:
                    tt("dve", kf[:, 0:1], kf[:, 0:1], hybs[:, o, cc:cc + 1], ALU.add, reads=[tk, t_hc], writes=[tk])
                else:
                    memset("dve", kf[:, 0:1], 0.0, [tk])
                kb.dma("pool", sig_d[pi_ * 2 + d_], kf[:], reads=[tk], writes=[t_sdf[pi_ * 2 + d_]])

        def stage_f(r):
            pi_ = r % 2
            Hh, tHh = H[pi_], tH[pi_]
            for d_ in range(2):
                slot = pi_ * 2 + d_
                for g in range(4):
                    u, tu = Ut[g % 2], tUt[g % 2]
                    kb.dma("sp", u[0:32, :, :], sig_d[slot][g * 32:(g + 1) * 32, :].rearrange("c (a b) -> a c b", a=32), reads=[t_sdf[slot]], writes=[tu])
                    j = 0
                    while j < 32:
                        n = min(7, 32 - j)
                        b = nb()
                        for jj in range(n):
                            mm(ps[b][:, jj * 66:(jj + 1) * 66], u[:, j + jj, :], d1f[:], start=True, stop=True, reads=[tu, t_hc], writes=[tps[b]])
                        c0 = g * 32 + j
                        cp("act" if cnt[0] % 2 == 0 else "dve", A[:, :, c0:c0 + n], ps[b][:, 0:n * 66].rearrange("p (c k) -> p k c", k=66),
                           reads=[tps[b]], writes=tA)
                        cnt[0] += 1
                        j += n
                for k1 in range(33):
                    b = nb()
                    ar, ai = A[:, k1, :], A[:, 33 + k1, :]
                    mm(ps[b][:, 0:128], gtab[:, k1, 0, :], ar, start=True, stop=False, reads=[t_hc, tA[k1]], writes=[tps[b]])
                    mm(ps[b][:, 0:128], gtab[:, k1, 2, :], ai, start=False, stop=True, reads=[t_hc, tA[k1]], writes=[tps[b]])
                    mm(ps[b][:, 128:256], gtab[:, k1, 1, :], ar, start=True, stop=False, reads=[t_hc, tA[k1]], writes=[tps[b]])
                    mm(ps[b][:, 128:256], gtab[:, k1, 0, :], ai, start=False, stop=True, reads=[t_hc, tA[k1]], writes=[tps[b]])
                    if d_ == 0:
                        cp("act", Hh[:, k1, :, :], ps[b][:, 0:256].rearrange("p (r c) -> p r c", r=2), reads=[tps[b]], writes=[tHh[k1]])
                    else:
                        tt("dve", Hh[:, k1, 0, :], Hh[:, k1, 0, :], ps[b][:, 0:128], ALU.add, reads=[tps[b], tHh[k1]], writes=[tHh[k1]])
                        tt("dve", Hh[:, k1, 1, :], Hh[:, k1, 1, :], ps[b][:, 128:256], ALU.subtract, reads=[tps[b], tHh[k1]], writes=[tHh[k1]])
            kb.dma("sp", hall_d[r * 128:(r + 1) * 128, :], Hh[:].rearrange("p a b c -> p (a b c)"), reads=tHh, writes=[t_hall[r]])

        stage_k(0)
        for r in range(8):
            if r + 1 < 8:
                stage_k(r + 1)
            stage_f(r)
        kb.barrier()
    if stop_after == "hf":
        kb.finish()
        return nc, dbg_out

    t_hd_r = T("hT_d_r")
    w_in_v = din["w_in"].rearrange("(k p) n -> p k n", p=128)
    w_sw_v = din["w_qk_sw"].rearrange("(k p) n -> p k n", p=128)

    def load_w(dst, src_view, c0, tl, ncols=128):
        kb.dma("pool", dst, src_view[:, :, c0:c0 + ncols], writes=[tl])

    with ExitStack() as ph:
        def sb(name, shape, dt):
            return ph.enter_context(nc.sbuf_tensor(f"s{next_id()}_" + name, list(shape), dt))

        NH = 0 if "p3" in skip else 4
        rcos = sb("rcos", [128, L], F32)
        rsin = sb("rsin", [128, L], F32)
        t_ropes = [T(f"rope{i}") for i in range(8)]
        for i in range(8):
            kb.dma("sp", rcos[:, i * 512:(i + 1) * 512], din["rope_cos"][:, i * 512:(i + 1) * 512], writes=[t_ropes[i]])
            kb.dma("sp", rsin[:, i * 512:(i + 1) * 512], din["rope_sin"][:, i * 512:(i + 1) * 512], writes=[t_ropes[i]])
        QT = [sb(f"QT{i}", [128, L], BF) for i in range(2)]
        KT = [[sb(f"KT{i}_{m_}", [128, LK], BF) for m_ in range(2)] for i in range(2)]
        V = [sb(f"V{i}", [128, 34, 129], BF) for i in range(2)]
        t_Q, t_K, t_V = [T(), T()], [T(), T()], [T(), T()]
        for i in range(2):
            memset("pool", KT[i][0][64:128, :], 0.0, [t_K[i]])
            memset("pool", KT[i][1][0:64, :], 0.0, [t_K[i]])
            memset("pool", V[i][:, :, 128:129], 1.0, [t_V[i]])
        wts = [[sb(f"aw{i}_{j}", [128, 8, 128], BF) for j in range(5)] for i in range(2)]
        twts = [[T() for _ in range(5)] for _ in range(2)]
        hb = [sb(f"hb{i}", [128, 8, 512], BF) for i in range(2)]
        thb = [T(), T()]
        rt = [sb(f"rt{i}", [128, 512], F32) for i in range(4)]
        trt = [T() for _ in range(4)]
        E = [sb(f"E{i}", [128, 1024], BF) for i in range(4)]
        tE = [T() for _ in range(4)]
        attst = [sb(f"attst{i}", [128, 512], BF) for i in range(2)]
        t_attst = [T(), T()]
        fin = sb("fin", [128, 16], F32)
        accs = sb("accs", [128, 1161], F32)
        oa = sb("oa", [128, 4, 128], F32)
        ob = sb("ob", [128, 4, 128], F32)
        on = sb("on", [128, 4, 128], BF)
        t_fin, t_on, t_accs, t_oa, t_ob = T(), T(), T(), T(), T()
        t_ad = T("aT_d")
        hbcnt = [0]

        def prologue(h, banks):
            bi = h % 2
            w_, tw_ = wts[bi], twts[bi]
            brr = [0]

            def nbk():
                brr[0] += 1
                return banks[brr[0] % len(banks)]

            load_w(w_[0][:], w_in_v, OFF_Q + h * 128, tw_[0])
            load_w(w_[1][:], w_sw_v, h * 128, tw_[1])
            load_w(w_[2][:], w_in_v, OFF_K + h * 128, tw_[2])
            load_w(w_[3][:], w_sw_v, 512 + h * 128, tw_[3])
            load_w(w_[4][:], w_in_v, OFF_V + h * 128, tw_[4])
            b = nbk()
            for k in range(8):
                mm(ps[b][:, 0:CT], w_[2][:, k, :], hcT[:, k, :], start=(k == 0), stop=(k == 7), reads=[tw_[2], t_hcT], writes=[tps[b]])
            cp("dve", KT[bi][0][0:64, L:LK], ps[b][0:64, 0:CT], reads=[tps[b]], writes=[t_K[bi]])
            cp("dve", KT[bi][1][64:128, L:LK], ps[b][64:128, 0:CT], reads=[tps[b]], writes=[t_K[bi]])
            yield
            b = nbk()
            for s_ in range(2):
                for k in range(8):
                    mm(ps[b][:, s_ * 128:(s_ + 1) * 128], hcT[:, k, s_ * 128:(s_ + 1) * 128], w_[4][:, k, :], start=(k == 0), stop=(k == 7),
                       reads=[tw_[4], t_hcT], writes=[tps[b]])
            cp("dve", V[bi][:, 32:34, 0:128], ps[b][:, 0:256].rearrange("p (s v) -> p s v", s=2), reads=[tps[b]], writes=[t_V[bi]])
            yield
            for blk in range(8):
                hi = hbcnt[0] % 2
                hbcnt[0] += 1
                hbb, th = hb[hi], thb[hi]
                kb.dma("sp", hbb[:], hT_d[:, :, blk * 512:(blk + 1) * 512], reads=[t_hd], writes=[th])
                sl = slice(blk * 512, (blk + 1) * 512)
                for qk in range(2):
                    for gg in range(2):
                        g = 2 * qk + gg
                        b = nbk()
                        for k in range(8):
                            mm(ps[b][:], w_[g][:, k, :], hbb[:, k, :], start=(k == 0), stop=(k == 7), reads=[tw_[g], th], writes=[tps[b]])
                        tt("dve", rt[g][:], ps[b][:], (rcos if gg == 0 else rsin)[:, sl], ALU.mult, reads=[tps[b], t_ropes[blk]], writes=[trt[g]])
                        yield
                    r0, r1 = rt[2 * qk], rt[2 * qk + 1]
                    if qk == 0:
                        tt("pool", QT[bi][:, sl], r0[:], r1[:], ALU.add, reads=[trt[0], trt[1]], writes=[t_Q[bi]])
                    else:
                        tt("pool", KT[bi][0][0:64, sl], r0[0:64, :], r1[0:64, :], ALU.add, reads=[trt[2], trt[3]], writes=[t_K[bi]])
                        tt("pool", KT[bi][1][64:128, sl], r0[64:128, :], r1[64:128, :], ALU.add, reads=[trt[2], trt[3]], writes=[t_K[bi]])
                b = nbk()
                for s_ in range(4):
                    for k in range(8):
                        mm(ps[b][:, s_ * 128:(s_ + 1) * 128], hbb[:, k, s_ * 128:(s_ + 1) * 128], w_[4][:, k, :], start=(k == 0), stop=(k == 7),
                           reads=[tw_[4], th], writes=[tps[b]])
                cp("dve", V[bi][:, blk * 4:blk * 4 + 4, 0:128], ps[b][:].rearrange("p (s v) -> p s v", s=4), reads=[tps[b]], writes=[t_V[bi]])
                yield

        def bc_last(ap2d, n):
            return bass.AP(ap2d.tensor, ap2d.offset, [list(ap2d.ap[0]), list(ap2d.ap[1]), [0, n]])

        items = [(qb, kc) for qb in range(8) for kc in range(34)]

        def head_loop(h, gen):
            bi = h % 2
            Qh, Kh, Vh = QT[bi], KT[bi], V[bi]

            def emit_S(idx):
                qb, kc = items[idx]
                b0 = (idx % 2) * 2
                for m_ in range(2):
                    mm(ps[b0 + m_][:], Kh[m_][:, kc * 128:(kc + 1) * 128], Qh[:, qb * 512:(qb + 1) * 512], start=True, stop=True,
                       reads=[t_K[bi], t_Q[bi]], writes=[tps[b0 + m_]])
                ei = idx % 4
                act(E[ei][:], psall[:, b0 * 512:(b0 + 2) * 512], AF.Exp, reads=[tps[b0], tps[b0 + 1]], writes=[tE[ei]], scale=0.125)

            def emit_AV(idx):
                qb, kc = items[idx]
                ei = idx % 4
                for m_ in range(2):
                    for s_ in range(4):
                        slot = m_ * 4 + s_
                        bk, c0 = 4 + slot // 3, (slot % 3) * 129
                        mm(ps[bk][:, c0:c0 + 129], E[ei][:, m_ * 512 + s_ * 128:m_ * 512 + (s_ + 1) * 128], Vh[:, kc, :],
                           start=(kc == 0 and slot % 3 == 0), stop=(kc == 33 and (slot % 3 == 2 or slot == 7)),
                           reads=[tE[ei], t_V[bi]], writes=[tps[bk]])
                if kc == 33:
                    finalize(qb)
                    pend.append(qb)
                if kc == 10 and pend:
                    finalize_tr(pend.pop())

            def finalize(qb):
                qsl = slice(qb * 512, (qb + 1) * 512)
                cp("dve", accs[:, 0:387], ps[4][:, 0:387], reads=[tps[4]], writes=[t_accs])
                cp("dve", accs[:, 387:774], ps[5][:, 0:387], reads=[tps[5]], writes=[t_accs])
                cp("dve", accs[:, 774:1032], ps[6][:, 0:258], reads=[tps[6]], writes=[t_accs])
                av = accs[:, 0:1032].rearrange("p (s c) -> p s c", c=129)
                recip(fin[:, 0:8], av[:, :, 128], reads=[t_accs], writes=[t_fin])
                ts("dve", fin[:, 4:8], fin[:, 4:8], neglam[:], None, ALU.mult, None, reads=[t_fin, t_cols], writes=[t_fin])
                tt("dve", oa[:], av[:, 0:4, 0:128], bc_last(fin[:, 0:4], 128), ALU.mult, reads=[t_accs, t_fin], writes=[t_oa])
                tt("dve", ob[:], av[:, 4:8, 0:128], bc_last(fin[:, 4:8], 128), ALU.mult, reads=[t_accs, t_fin], writes=[t_ob])
                tt("pool", ob[:], ob[:], oa[:], ALU.add, reads=[t_oa, t_ob], writes=[t_ob])
                tt("pool", oa[:], ob[:], ob[:], ALU.mult, reads=[t_ob], writes=[t_oa])
                rsum(fin[:, 8:12], oa[:], reads=[t_oa], writes=[t_fin])
                rstd_of(fin[:, 8:12], fin[:, 12:16], 1.0 / 128, t_fin)
                tt("dve", ob[:], ob[:], bc_last(fin[:, 12:16], 128), ALU.mult, reads=[t_ob, t_fin], writes=[t_ob])
                sg_b = bass.AP(sgrow[:].tensor, sgrow[:].offset, [list(sgrow[:].ap[0]), [0, 4], [1, 128]])
                tt("dve", on[:], ob[:], sg_b, ALU.mult, reads=[t_ob, t_rows], writes=[t_on])

            def finalize_tr(qb):
                qsl = slice(qb * 512, (qb + 1) * 512)
                for s_ in range(4):
                    tr(psb[7][:, s_ * 128:(s_ + 1) * 128], on[:, s_, :], ident_bf[:], reads=[t_on, t_const], writes=[tps[7]])
                cp("dve", attst[qb % 2][:], psb[7][:, 0:512], reads=[tps[7]], writes=[t_attst[qb % 2]])
                kb.dma("sp", aT_d[:, h, qsl], attst[qb % 2][:], reads=[t_attst[qb % 2]], writes=[t_ad])

            pend = []
            emit_S(0)
            emit_S(1)
            for idx in range(len(items)):
                if idx + 2 < len(items):
                    emit_S(idx + 2)
                emit_AV(idx)
                if gen is not None and idx % 6 == 3 and items[idx][1] not in (32, 33, 0):
                    next(gen, None)
            while pend:
                finalize_tr(pend.pop())
            if gen is not None:
                for _ in gen:
                    pass

        if NH:
            for _ in prologue(0, [0, 1, 2, 3, 4, 5, 6, 7]):
                pass
        for h in range(NH):
            gen = prologue(h + 1, [7]) if h + 1 < NH else None
            head_loop(h, gen)
        if "p3" in dbg:
            d = dbg_tensor("attT", [128, 4, L], BF)
            kb.dma("sp", d, aT_d, reads=[t_ad])
        kb.barrier()
    if stop_after == "p3":
        kb.finish()
        return nc, dbg_out

    with ExitStack() as ph:
        def sb(name, shape, dt):
            return ph.enter_context(nc.sbuf_tensor(f"s{next_id()}_" + name, list(shape), dt))

        _bk = [0]

        def nb():
            _bk[0] = (_bk[0] + 1) % 8
            return _bk[0]

        t_hc = T("hyconst")
        gtab = sb("gtab", [128, 33, 3, 128], BF)
        ttab = sb("ttab", [66, 2, 128, 32], BF)
        etab = sb("etab", [128, 4, 2, 32], BF)
        d1f = sb("d1fp", [128, 66], BF)
        cwt = sb("cwt", [128, 12, 3], F32)
        cbt = sb("cbt", [128, 12], F32)
        for dst, nm in ((gtab, "gtab"), (ttab, "ttab"), (etab, "etab"), (d1f, "d1fp"), (cwt, "cw"), (cbt, "cb")):
            kb.dma("sp", dst[:], din[nm], writes=[t_hc])
        Z = [sb(f"Z{i}", [128, L], BF) for i in range(3)]
        tZ = [T(f"Z{i}") for i in range(3)]
        H = sb("H", [128, 33, 2, 128], BF)
        tH = [T() for _ in range(33)]
        t_sd = [T(f"sig{i}") for i in range(5)]
        t_zd = T("zT_d")
        Ut = [sb(f"Ut{i}", [128, 32, 128], BF) for i in range(2)]
        tUt = [T(), T()]
        for i in range(2):
            memset("pool", Ut[i][32:64, :, :], 0.0, [tUt[i]])
            memset("pool", Ut[i][64:128, :, :], 0.0, [tUt[i]])
        A = sb("A", [128, 66, 128], BF)
        tA = [T() for _ in range(33)]
        f1cnt = [0]

        def f1_pre(sig, t_sig, slot):
            kb.dma("pool", sig_d[slot], sig, reads=[t_sig], writes=[t_sd[slot]])
            for g in range(2):
                kb.dma("pool", Ut[g][0:32, :, :], sig_d[slot][g * 32:(g + 1) * 32, :].rearrange("c (a b) -> a c b", a=32),
                       reads=[t_sd[slot]], writes=[tUt[g]])

        def f1_part(sig, t_sig, slot, pre=False):
            if not pre:
                f1_pre(sig, t_sig, slot)
            for g in range(4):
                u, tu = Ut[g % 2], tUt[g % 2]
                if g >= 2:
                    kb.dma("pool", u[0:32, :, :], sig_d[slot][g * 32:(g + 1) * 32, :].rearrange("c (a b) -> a c b", a=32),
                           reads=[t_sd[slot]], writes=[tu])
                j = 0
                while j < 32:
                    n = min(7, 32 - j)
                    b = nb()
                    for jj in range(n):
                        mm(ps[b][:, jj * 66:(jj + 1) * 66], u[:, j + jj, :], d1f[:], start=True, stop=True,
                           reads=[tu, t_hc], writes=[tps[b]])
                    c0 = g * 32 + j
                    eng = "act" if f1cnt[0] % 2 == 0 else "dve"
                    f1cnt[0] += 1
                    cp(eng, A[:, :, c0:c0 + n], ps[b][:, 0:n * 66].rearrange("p (c k) -> p k c", k=66), reads=[tps[b]], writes=tA)
                    j += n

        for cc in range(4):
            with ExitStack() as pa:
                def sba(name, shape, dt):
                    return pa.enter_context(nc.sbuf_tensor(f"s{next_id()}_" + name, list(shape), dt))

                wts = [sba(f"hw{i}", [128, 8, 128], BF) for i in range(3)]
                twts = [T() for _ in range(3)]
                hb = [sba(f"hhb{i}", [128, 8, 512], BF) for i in range(2)]
                thb = [T(), T()]
                Ur = [sba(f"Ur{i}", [128, L + 2], BF) for i in range(3)]
                tUr = [T() for _ in range(3)]
                ctmp = sba("ctmp", [128, L], F32)
                t_ct = T()
                for s in range(3):
                    load_w(wts[s][:], w_in_v, s * 512 + cc * 128, twts[s])
                    memset("pool", Ur[s][:, 0:1], 0.0, [tUr[s]])
                    memset("pool", Ur[s][:, L + 1:L + 2], 0.0, [tUr[s]])
                def proj_pass(sigs, off):
                    for blk in range(8):
                        hbb, th = hb[(blk + off) % 2], thb[(blk + off) % 2]
                        kb.dma("sp", hbb[:], hT_d[:, :, blk * 512:(blk + 1) * 512], reads=[t_hd], writes=[th])
                        for s in sigs:
                            b = nb()
                            for k in range(8):
                                mm(ps[b][:], wts[s][:, k, :], hbb[:, k, :], start=(k == 0), stop=(k == 7), reads=[twts[s], th], writes=[tps[b]])
                            cp("act", Ur[s][:, 1 + blk * 512:1 + (blk + 1) * 512], ps[b][:], reads=[tps[b]], writes=[tUr[s]])

                def sconv(s):
                    slot = s * 4 + cc
                    act(ctmp[:], Ur[s][:, 1:L + 1], AF.Identity, reads=[tUr[s], t_hc], writes=[t_ct],
                        scale=cwt[:, slot, 1:2], bias=cbt[:, slot:slot + 1])
                    stt(ctmp[:], Ur[s][:, 0:L], cwt[:, slot, 0:1], ctmp[:], ALU.mult, ALU.add, reads=[tUr[s], t_hc, t_ct], writes=[t_ct])
                    stt(Z[s][:], Ur[s][:, 2:L + 2], cwt[:, slot, 2:3], ctmp[:], ALU.mult, ALU.add, reads=[tUr[s], t_hc, t_ct], writes=[tZ[s]])

                proj_pass([0, 1, 2], 0)
                sconv(0)
                f1_pre(Z[0][:], tZ[0], 4)
                sconv(1)
                sconv(2)
                f1_part(Z[0][:], tZ[0], 4, pre=True)
                if "p2a" in dbg and cc == 0:
                    for s in range(3):
                        d = dbg_tensor(f"Z{s}", [128, L], BF)
                        kb.dma("sp", d, Z[s][:], reads=[tZ[s]])
                kb.barrier()
            with ExitStack() as pb:
                def sbb(name, shape, dt):
                    return pb.enter_context(nc.sbuf_tensor(f"s{next_id()}_" + name, list(shape), dt))

                Y = sbb("Y", [128, 128, 66], BF)
                tY = [T() for _ in range(33)]
                Pqs = [sbb(f"Pq{i}", [66, 64, 128], BF) for i in range(2)]
                t_Pqs = [T(), T()]
                pw = [sbb(f"pw{i}", [128, 2, 128], F32) for i in range(4)]
                tpw = [T() for _ in range(4)]

                def f2_part(consumer):
                    for k1 in range(33):
                        b = nb()
                        ar, ai = A[:, k1, :], A[:, 33 + k1, :]
                        mm(ps[b][:, 0:128], gtab[:, k1, 0, :], ar, start=True, stop=False, reads=[t_hc, tA[k1]], writes=[tps[b]])
                        mm(ps[b][:, 0:128], gtab[:, k1, 2, :], ai, start=False, stop=True, reads=[t_hc, tA[k1]], writes=[tps[b]])
                        mm(ps[b][:, 128:256], gtab[:, k1, 1, :], ar, start=True, stop=False, reads=[t_hc, tA[k1]], writes=[tps[b]])
                        mm(ps[b][:, 128:256], gtab[:, k1, 0, :], ai, start=False, stop=True, reads=[t_hc, tA[k1]], writes=[tps[b]])
                        consumer(k1, b)

                def conv(o, sig, t_sig, xm, t_xm, zout, t_zout):
                    r_ = 2 * cc + o
                    kb.dma("sp", H[:].rearrange("p a b c -> p (a b c)"), hall_d[r_ * 128:(r_ + 1) * 128, :], reads=[t_hall[r_]], writes=tH)
                    if "p2h" in dbg and cc == 0:
                        d = dbg_tensor(f"H{o}", [128, 33 * 2 * 128], BF)
                        kb.dma("sp", d, H[:].rearrange("p a b c -> p (a b c)"), reads=tH)

                    def cons_d(k1, b):
                        i0 = (2 * k1) % 4
                        p1, p2 = pw[i0], pw[i0 + 1]
                        xv = ps[b][:, 0:256].rearrange("p (r c) -> p r c", r=2)
                        tt("dve", p1[:], xv, bass.AP(H[:, k1, 0, :].tensor, H[:, k1, 0, :].offset, [list(H[:, k1, 0, :].ap[0]), [0, 2], [1, 128]]),
                           ALU.mult, reads=[tps[b], tH[k1]], writes=[tpw[i0]])
                        tt("dve", p2[:], xv, bass.AP(H[:, k1, 1, :].tensor, H[:, k1, 1, :].offset, [list(H[:, k1, 1, :].ap[0]), [0, 2], [1, 128]]),
                           ALU.mult, reads=[tps[b], tH[k1]], writes=[tpw[i0 + 1]])
                        tt("dve", Y[:, :, k1], p1[:, 0, :], p2[:, 1, :], ALU.subtract, reads=[tpw[i0], tpw[i0 + 1]], writes=[tY[k1]])
                        tt("pool", Y[:, :, 33 + k1], p2[:, 0, :], p1[:, 1, :], ALU.add, reads=[tpw[i0], tpw[i0 + 1]], writes=[tY[k1]])

                    if o == 1:
                        f1_part(sig, t_sig, 4)
                    f2_part(cons_d)
                    cnt = 0
                    zv = zout.rearrange("c (a b) -> c b a", a=32)
                    xv_ = xm.rearrange("c (a b) -> c b a", a=32)
                    def i1_stage(q):
                        Pq, t_Pq = Pqs[q % 2], t_Pqs[q % 2]
                        for c0 in range(0, 128, 8):
                            b = nb()
                            for jj in range(8):
                                mm(ps[b][0:66, jj * 64:(jj + 1) * 64], Y[:, c0 + jj, :], etab[:, q, :, :].rearrange("p e n -> p (e n)"),
                                   start=True, stop=True, reads=tY + [t_hc], writes=[tps[b]])
                            eng = "act" if (c0 // 8) % 2 == 0 else "dve"
                            cp(eng, Pq[:, :, c0:c0 + 8], ps[b][0:66, :].rearrange("p (c k) -> p k c", k=64), reads=[tps[b]], writes=[t_Pq])

                    def i2_stage(q):
                        Pq, t_Pq = Pqs[q % 2], t_Pqs[q % 2]
                        for hh in range(2):
                            b = nb()
                            for j in range(16):
                                n2l = hh * 16 + j
                                n2 = q * 32 + n2l
                                for e_ in range(2):
                                    mm(ps[b][:, j * 32:(j + 1) * 32], Pq[:, e_ * 32 + n2l, :], ttab[:, e_, n2, :], start=(e_ == 0), stop=(e_ == 1),
                                       reads=[t_Pq, t_hc], writes=[tps[b]])
                            n20 = q * 32 + hh * 16
                            tt("dve", zv[:, n20:n20 + 16, :], ps[b][:].rearrange("p (b a) -> p b a", a=32), xv_[:, n20:n20 + 16, :], ALU.mult,
                               reads=[tps[b], t_xm], writes=[t_zout])

                    i1_stage(0)
                    for q in range(4):
                        if q + 1 < 4:
                            i1_stage(q + 1)
                        i2_stage(q)

                conv(0, Z[0][:], tZ[0], Z[1][:], tZ[1], Z[0][:], tZ[0])
                if "p2c" in dbg and cc == 0:
                    d = dbg_tensor("z1", [128, L], BF)
                    kb.dma("sp", d, Z[0][:], reads=[tZ[0]])
                conv(1, Z[0][:], tZ[0], Z[2][:], tZ[2], Z[1][:], tZ[1])
                kb.dma("sp", zT_d[:, cc, :], Z[1][:], reads=[tZ[1]], writes=[t_zd])
                kb.barrier()
            if stop_after == "p2c0":
                break
        if "p2" in dbg:
            d = dbg_tensor("zT", [128, 4, L], BF)
            kb.dma("sp", d, zT_d, reads=[t_zd])
        kb.barrier()
    if stop_after in ("p2", "p2c0"):
        kb.finish()
        return nc, dbg_out

    t_md = T("mT_d")
    with ExitStack() as ph:
        def sb(name, shape, dt):
            return ph.enter_context(nc.sbuf_tensor(f"s{next_id()}_" + name, list(shape), dt))

        _bk = [0]

        def nb():
            _bk[0] = (_bk[0] + 1) % 8
            return _bk[0]

        zTs = sb("zTs", [128, 4, L], BF)
        aTs = sb("aTs", [128, 4, L], BF)
        wgt = sb("wgt", [128, 8, 2048], BF)
        whu = sb("whu", [128, 4, D], BF)
        wau = sb("wau", [128, 4, D], BF)
        t_w4 = T("w4")
        t_za = T("za")
        kb.dma("sp", zTs[:], zT_d, writes=[t_za])
        kb.dma("sp", aTs[:], aT_d, writes=[t_za])
        for k in range(8):
            kb.dma("pool", wgt[:, k, :], din["w_in"][k * 128:(k + 1) * 128, OFF_G:OFF_G + 2048], writes=[t_w4])
        kb.dma("pool", whu[:], din["w_hy_up"].rearrange("(k p) n -> p k n", p=128), writes=[t_w4])
        kb.dma("pool", wau[:], din["w_att_up"].rearrange("(k p) n -> p k n", p=128), writes=[t_w4])
        hb = [sb(f"mhb{i}", [128, 8, 512], BF) for i in range(2)]
        thb = [T(), T()]
        mst = [sb(f"mst{i}", [128, 8, 512], BF) for i in range(2)]
        tmst = [T(), T()]
        sg = [sb(f"sg{i}", [128, 512], F32) for i in range(4)]
        tsg = [T() for _ in range(4)]
        mm_ = [sb(f"mm{i}", [128, 512], F32) for i in range(4)]
        tmm = [T() for _ in range(4)]
        for blk in range(8):
            sl = slice(blk * 512, (blk + 1) * 512)
            hbb, th = hb[blk % 2], thb[blk % 2]
            kb.dma("sp", hbb[:], hT_d[:, :, sl], reads=[t_hd], writes=[th])
            for j in range(8):
                bg1, bg2, by1, by2 = nb(), nb(), nb(), nb()
                for k in range(8):
                    mm(ps[bg1][:], wgt[:, k, j * 128:(j + 1) * 128], hbb[:, k, :], start=(k == 0), stop=(k == 7), reads=[t_w4, th], writes=[tps[bg1]])
                for k in range(8):
                    mm(ps[bg2][:], wgt[:, k, 1024 + j * 128:1024 + (j + 1) * 128], hbb[:, k, :], start=(k == 0), stop=(k == 7), reads=[t_w4, th], writes=[tps[bg2]])
                for k in range(4):
                    mm(ps[by1][:], whu[:, k, j * 128:(j + 1) * 128], zTs[:, k, sl], start=(k == 0), stop=(k == 3), reads=[t_w4, t_za], writes=[tps[by1]])
                for k in range(4):
                    mm(ps[by2][:], wau[:, k, j * 128:(j + 1) * 128], aTs[:, k, sl], start=(k == 0), stop=(k == 3), reads=[t_w4, t_za], writes=[tps[by2]])
                i0 = (j % 2) * 2
                act(sg[i0][:], ps[bg1][:], AF.Sigmoid, reads=[tps[bg1]], writes=[tsg[i0]])
                act(sg[i0 + 1][:], ps[bg2][:], AF.Sigmoid, reads=[tps[bg2]], writes=[tsg[i0 + 1]])
                tt("dve", mm_[i0][:], ps[by1][:], sg[i0][:], ALU.mult, reads=[tps[by1], tsg[i0]], writes=[tmm[i0]])
                tt("dve", mm_[i0 + 1][:], ps[by2][:], sg[i0 + 1][:], ALU.mult, reads=[tps[by2], tsg[i0 + 1]], writes=[tmm[i0 + 1]])
                tt("pool", mst[blk % 2][:, j, :], mm_[i0][:], mm_[i0 + 1][:], ALU.add, reads=[tmm[i0], tmm[i0 + 1]], writes=[tmst[blk % 2]])
            kb.dma("sp", mT_d[:, :, sl], mst[blk % 2][:], reads=[tmst[blk % 2]], writes=[t_md])
        if "p4" in dbg:
            d = dbg_tensor("mT", [128, 8, L], BF)
            kb.dma("sp", d, mT_d, reads=[t_md])
        kb.barrier()
    if stop_after == "p4":
        kb.finish()
        return nc, dbg_out

    with ExitStack() as ph:
        def sb(name, shape, dt):
            return ph.enter_context(nc.sbuf_tensor(f"s{next_id()}_" + name, list(shape), dt))

        _bk = [0]

        def nb():
            _bk[0] = (_bk[0] + 1) % 8
            return _bk[0]

        wout = sb("wout", [128, 8, D], BF)
        wfg = sb("wfg", [128, 8, DFF], BF)
        wfu = sb("wfu", [128, 8, DFF], BF)
        wfd = sb("wfd", [128, NFF, D], BF)
        t_wo, t_wg, t_wu, t_wd = T("wo"), T("wg"), T("wu"), T("wd")
        kb.dma("pool", wout[:], din["w_out"].rearrange("(k p) n -> p k n", p=128), writes=[t_wo])
        for k in range(8):
            kb.dma("pool", wfg[:, k, :], din["w_fg"][k * 128:(k + 1) * 128, :], writes=[t_wg])
        for k in range(8):
            kb.dma("pool", wfu[:, k, :], din["w_fu"][k * 128:(k + 1) * 128, :], writes=[t_wu])
        for j in range(0, NFF, 2):
            kb.dma("pool", wfd[:, j:j + 2, :], din["w_fd"][j * 128:(j + 2) * 128, :].rearrange("(k p) n -> p k n", p=128), writes=[t_wd])
        xts = [sb(f"fx{i}", [128, D], F32) for i in range(3)]
        txt = [T(), T(), T()]
        mts = [sb(f"fm{i}", [128, 8, 128], BF) for i in range(2)]
        tmt = [T(), T()]
        tmp = sb("ftmp", [128, D], F32)
        t_tmp = T()
        junk = sb("fjunk", [128, D], BF)
        t_junk = T()
        xs = sb("fxs", [128, D], BF)
        t_xs = T()
        hfT = [sb(f"hfT{i}", [128, 8, 128], BF) for i in range(2)]
        t_hf = [T(), T()]
        aT = sb("aT", [128, NFF, 128], BF)
        t_aT = T()
        sgt = [sb(f"sgt{i}", [128, 512], F32) for i in range(2)]
        tsgt = [T(), T()]
        atm = sb("atm", [128, DFF], BF)
        t_atm = T()
        fin = sb("ffin", [128, 16], F32)
        t_fin = [T(), T(), T()]
        t_out = T("out")

        def norm_resid(banks, xt, tx, grow, c0, tf):
            for n in range(2):
                act(junk[:, n * 512:(n + 1) * 512], ps[banks[n]][:], AF.Square, reads=[tps[banks[n]]], writes=[t_junk, tf],
                    accum_out=fin[:, c0 + n:c0 + n + 1])
            tt("dve", fin[:, c0 + 2:c0 + 3], fin[:, c0:c0 + 1], fin[:, c0 + 1:c0 + 2], ALU.add, reads=[tf], writes=[tf])
            rstd_of(fin[:, c0 + 2:c0 + 3], fin[:, c0 + 3:c0 + 4], 1.0 / D, tf)
            for n in range(2):
                hs = slice(n * 512, (n + 1) * 512)
                stt(tmp[:, hs], ps[banks[n]][:], fin[:, c0 + 3:c0 + 4], grow[:, hs], ALU.mult, ALU.mult,
                    reads=[tps[banks[n]], tf, t_rows], writes=[t_tmp])
            tt("pool", xt[:], xt[:], tmp[:], ALU.add, reads=[t_tmp, tx], writes=[tx])

        def s1a(i):
            tsl = slice(i * 128, (i + 1) * 128)
            xt, tx = xts[i % 3], txt[i % 3]
            mt, tm = mts[i % 2], tmt[i % 2]
            kb.dma("sp", xt[:], din["x"][tsl, :], writes=[tx])
            kb.dma("sp", mt[:], mT_d[:, :, tsl], reads=[t_md], writes=[tm])
            for n in range(2):
                for k in range(8):
                    mm(ps[n][:], mt[:, k, :], wout[:, k, n * 512:(n + 1) * 512], start=(k == 0), stop=(k == 7),
                       reads=[tm, t_wo], writes=[tps[n]])

        def s1b(i):
            xt, tx = xts[i % 3], txt[i % 3]
            norm_resid([0, 1], xt, tx, g1row, 0, t_fin[0])
            act(junk[:], xt[:], AF.Square, reads=[tx], writes=[t_junk, t_fin[1]], accum_out=fin[:, 4:5])
            rstd_of(fin[:, 4:5], fin[:, 5:6], 1.0 / D, t_fin[1])
            act(xs[:], xt[:], AF.Copy, reads=[tx, t_fin[1]], writes=[t_xs], scale=fin[:, 5:6])

        def s1c(i):
            for k in range(8):
                tr(psb[4][:, k * 128:(k + 1) * 128], xs[:, k * 128:(k + 1) * 128], ident_bf[:], reads=[t_xs, t_const], writes=[tps[4]])
            for k in range(8):
                ts("dve", hfT[i % 2][:, k, :], psb[4][:, k * 128:(k + 1) * 128], A2(k), B2(k), ALU.mult, ALU.add,
                   reads=[tps[4], t_cols], writes=[t_hf[i % 2]])

        def s2a(i):
            h_, th_ = hfT[i % 2], t_hf[i % 2]
            pairs = [(5, 6), (7, 4)]
            for g in range(6):
                c0 = g * 512
                w_ = min(512, DFF - c0)
                bg, bu = pairs[g % 2]
                for k in range(8):
                    mm(ps[bg][:, 0:w_], h_[:, k, :], wfg[:, k, c0:c0 + w_], start=(k == 0), stop=(k == 7), reads=[t_wg, th_], writes=[tps[bg]])
                for k in range(8):
                    mm(ps[bu][:, 0:w_], h_[:, k, :], wfu[:, k, c0:c0 + w_], start=(k == 0), stop=(k == 7), reads=[t_wu, th_], writes=[tps[bu]])
                act(sgt[g % 2][:, 0:w_], ps[bg][:, 0:w_], AF.Silu, reads=[tps[bg]], writes=[tsgt[g % 2]])
                tt("dve", atm[:, c0:c0 + w_], ps[bu][:, 0:w_], sgt[g % 2][:, 0:w_], ALU.mult, reads=[tps[bu], tsgt[g % 2]], writes=[t_atm])
            j = 0
            bi = 0
            while j < NFF:
                n = min(8, NFF - j)
                b = (7, 4, 5)[bi % 3]
                bi += 1
                for jj in range(n):
                    tr(psb[b][:, jj * 128:(jj + 1) * 128], atm[:, (j + jj) * 128:(j + jj + 1) * 128], ident_bf[:], reads=[t_atm, t_const], writes=[tps[b]])
                cp("dve" if bi % 2 else "act", aT[:, j:j + n, :], psb[b][:, 0:n * 128].rearrange("p (j t) -> p j t", t=128), reads=[tps[b]], writes=[t_aT])
                j += n

        def s2b(i):
            tsl = slice(i * 128, (i + 1) * 128)
            xt, tx = xts[i % 3], txt[i % 3]
            for n in range(2):
                for j in range(NFF):
                    mm(ps[2 + n][:], aT[:, j, :], wfd[:, j, n * 512:(n + 1) * 512], start=(j == 0), stop=(j == NFF - 1),
                       reads=[t_aT, t_wd], writes=[tps[2 + n]])
            norm_resid([2, 3], xt, tx, g2row, 8, t_fin[2])
            kb.dma("sp", out[tsl, :], xt[:], reads=[tx], writes=[t_out])

        s1a(0)
        s1b(0)
        s1c(0)
        s1a(1)
        s1b(1)
        for i in range(32):
            s2a(i)
            if i + 1 < 32:
                s1c(i + 1)
            if i + 2 < 32:
                s1a(i + 2)
                s1b(i + 2)
            s2b(i)
        kb.barrier()
    kb.finish()
    return nc, dbg_out


_NC = None


def kernel(**inputs):
    global _NC
    if _NC is None:
        _NC = build()[0]
    in_maps = [layout_inputs(inputs, b) for b in range(8)]
    res = run_bass_kernel_spmd(_NC, in_maps, core_ids=list(range(8)))
    return np.stack([np.asarray(r["out"], dtype=np.float32) for r in res.results], 0)
```

```python
import math
from contextlib import ExitStack
import numpy as np
import ml_dtypes
import concourse.bass as bass
import concourse.mybir as mybir
from concourse.bass_utils import run_bass_kernel_spmd

F32 = mybir.dt.float32
BF = mybir.dt.bfloat16
AF = mybir.ActivationFunctionType
ALU = mybir.AluOpType
AX = mybir.AxisListType

L = 4096
D = 1024
CT = 256
LK = L + CT
DH = 512
DFF = 2816
NFF = DFF // 128
OFF_Q, OFF_K, OFF_V, OFF_G = 1536, 2048, 2560, 3072
EPS = 1e-6
LAM_INIT = 0.8 - 0.6 * math.exp(0.0)
PI = math.pi


class Sem:
    def __init__(self, h):
        self.h = h
        self.count = 0


class T:
    __slots__ = ("name", "w", "r")

    def __init__(self, name=""):
        self.name = name
        self.w = None
        self.r = []


class Eng:
    def __init__(self, name, sem):
        self.name = name
        self.sem = sem
        self.ops = []
        self.seen = {}


class KB:
    def __init__(self, nc, nsem_dma=14):
        self.nc = nc
        self.engs = {}
        for n in ("pe", "act", "dve", "pool", "sp"):
            self.engs[n] = Eng(n, Sem(nc.alloc_semaphore("s_" + n)))
        self.dsems = {q: [Sem(nc.alloc_semaphore(f"d_{q}{i}")) for i in range(nsem_dma)] for q in ("sp", "pool")}
        self.drr = {"sp": 0, "pool": 0}

    def _waits(self, eng, reads, writes, extra=()):
        deps = {}

        def add(d):
            if d is None:
                return
            s, v = d
            if deps.get(s, 0) < v:
                deps[s] = v

        for t in reads:
            add(t.w)
        for t in writes:
            add(t.w)
            for d in t.r:
                add(d)
        for d in extra:
            add(d)
        out = []
        for s, v in deps.items():
            if s is eng.sem and eng.name == "pe":
                continue
            if eng.seen.get(s, 0) >= v:
                continue
            eng.seen[s] = v
            out.append((s, v))
        return out

    def _mark(self, tok, reads, writes):
        for t in reads:
            t.r = [d for d in t.r if d[0] is not tok[0]]
            t.r.append(tok)
        for t in writes:
            t.w = tok
            t.r = []

    def op(self, engname, fn, reads=(), writes=()):
        eng = self.engs[engname]
        waits = self._waits(eng, reads, writes)
        eng.sem.count += 1
        tok = (eng.sem, eng.sem.count)
        eng.ops.append((waits, fn, (eng.sem, 1)))
        self._mark(tok, reads, writes)
        return tok

    def dma(self, q, out_ap, in_ap, reads=(), writes=(), **kw):
        eng = self.engs[q]
        sems = self.dsems[q]
        s = sems[self.drr[q] % len(sems)]
        self.drr[q] += 1
        waits = self._waits(eng, reads, writes, extra=[(s, s.count)] if s.count else [])
        s.count += 16
        tok = (s, s.count)
        eng.ops.append((waits, lambda e: e.dma_start(out=out_ap, in_=in_ap, **kw), (s, 16)))
        self._mark(tok, reads, writes)
        return tok

    def collective(self, kind, ins, outs, reads=(), writes=()):
        eng = self.engs["pool"]
        if not hasattr(self, "ccsem"):
            self.ccsem = Sem(self.nc.alloc_semaphore("s_cc"))
        s = self.ccsem
        waits = self._waits(eng, reads, writes)
        s.count += 1
        tok = (s, s.count)
        eng.ops.append((waits, lambda e: e.collective_compute(kind, ALU.bypass, replica_groups=[list(range(8))], ins=ins, outs=outs), (s, 1)))
        self._mark(tok, reads, writes)
        return tok

    def barrier(self, include_cc=False):
        allsems = [e.sem for e in self.engs.values()] + [s for q in self.dsems.values() for s in q]
        if include_cc and hasattr(self, "ccsem"):
            allsems.append(self.ccsem)
        for eng in self.engs.values():
            waits = []
            for s in allsems:
                if s is eng.sem or s.count == 0:
                    continue
                if eng.seen.get(s, 0) >= s.count:
                    continue
                eng.seen[s] = s.count
                waits.append((s, s.count))
            if waits:
                eng.ops.append((waits, None, None))

    def finish(self):
        nc = self.nc
        self.barrier(include_cc=True)
        with nc.Block() as block:
            def emit(e, en):
                for waits, fn, inc in en.ops:
                    for (ws, wv) in waits:
                        e.wait_ge(ws.h, wv)
                    if fn is not None:
                        ins = fn(e)
                        ins.then_inc(inc[0].h, inc[1])

            @block.tensor
            def _(e):
                emit(e, self.engs["pe"])

            @block.scalar
            def _(e):
                emit(e, self.engs["act"])

            @block.vector
            def _(e):
                emit(e, self.engs["dve"])

            @block.gpsimd
            def _(e):
                emit(e, self.engs["pool"])

            @block.sync
            def _(e):
                emit(e, self.engs["sp"])


def _bf(a):
    return np.ascontiguousarray(a.astype(np.float32)).astype(ml_dtypes.bfloat16)


_CONST = None


def host_consts():
    global _CONST
    if _CONST is not None:
        return _CONST
    c = {}
    c["ident_bf"] = _bf(np.eye(128))
    c["ident_f"] = np.eye(128, dtype=np.float32)
    t = np.arange(L)
    row = (t // 64).astype(np.float32)
    col = (t % 64).astype(np.float32)
    inv = (10000.0 ** (-np.arange(16, dtype=np.float32) / 16)).astype(np.float32)
    cos64 = np.zeros((64, L), np.float32)
    sin64 = np.zeros((64, L), np.float32)
    for half, pos in ((0, row), (1, col)):
        ang = pos[None, :] * inv[:, None]
        base = half * 32
        cos64[base:base + 16] = np.cos(ang)
        cos64[base + 16:base + 32] = np.cos(ang)
        sin64[base:base + 16] = -np.sin(ang)
        sin64[base + 16:base + 32] = np.sin(ang)
    c["rope_cos"] = np.concatenate([cos64, cos64], 0)
    c["rope_sin"] = np.concatenate([sin64, sin64], 0)
    f32 = np.float32
    bands = 16
    tt = np.linspace(0.0, 1.0, L, dtype=f32)[:, None]
    w = (f32(2.0 * math.pi / L) * np.arange(L, dtype=f32))[:, None]
    fr = np.linspace(1e-4, bands - 1, bands, dtype=f32)[None, :]
    z = np.concatenate([tt, np.cos(fr * w), -np.sin(fr * w)], axis=-1).astype(f32)
    c["zT"] = np.ascontiguousarray(z.T)
    deltas = np.abs(np.linspace(math.log(1e-2) / 1.5, math.log(1e-2) / 0.3, DH, dtype=f32))
    nd = (-deltas).reshape(4, 128).T
    c["negdelta"] = np.ascontiguousarray(nd.astype(f32))
    offs = (np.arange(8) * 512 / (L - 1)).astype(f32)
    c["ndoff"] = np.ascontiguousarray((nd[:, :, None] * offs[None, None, :]).astype(f32))
    c["tv0"] = np.ascontiguousarray(np.broadcast_to((np.arange(512) / (L - 1)).astype(f32)[None, :], (128, 512)))
    n1 = np.arange(32)[:, None]
    k1 = np.arange(33)[None, :]
    a = 2 * np.pi * n1 * k1 / 64.0
    c["d1f"] = _bf(np.concatenate([np.cos(a), -np.sin(a)], 1))
    c["d1fp"] = np.ascontiguousarray(np.concatenate([c["d1f"], np.zeros((96, 66), ml_dtypes.bfloat16)], 0))
    c["d1b"] = _bf(np.concatenate([np.cos(a), np.sin(a)], 1))
    n2 = np.arange(128)[:, None, None]
    k1g = np.arange(33)[None, :, None]
    k2 = np.arange(128)[None, None, :]
    ang = 2 * np.pi * n2 * (k1g + 64 * k2) / 8192.0
    gr, gi = np.cos(ang), -np.sin(ang)
    c["gtab"] = _bf(np.stack([gr, gi, -gi], 2))
    k2e = np.arange(128)[:, None]
    n2e = np.arange(128)[None, :]
    ae = 2 * np.pi * k2e * n2e / 128.0
    c["etab"] = _bf(np.stack([np.cos(ae).reshape(128, 4, 32), np.sin(ae).reshape(128, 4, 32)], 2))
    k1t = np.arange(33)[:, None, None]
    n2t = np.arange(128)[None, :, None]
    n1t = np.arange(32)[None, None, :]
    at = 2 * np.pi * k1t * (128 * n1t + n2t) / 8192.0
    wgt = np.full((33, 1, 1), 2.0)
    wgt[0] = 1.0
    wgt[32] = 1.0
    tr = wgt * np.cos(at) / 8192.0
    ti = wgt * np.sin(at) / 8192.0
    t0 = np.concatenate([tr, -ti], 0)
    t1 = np.concatenate([-ti, -tr], 0)
    c["ttab"] = _bf(np.stack([t0, t1], 1))
    _CONST = c
    return c


CONST_SHAPES = {
    "ident_bf": ([128, 128], BF), "ident_f": ([128, 128], F32),
    "rope_cos": ([128, L], F32), "rope_sin": ([128, L], F32),
    "zT": ([33, L], F32), "negdelta": ([128, 4], F32), "ndoff": ([128, 4, 8], F32), "tv0": ([128, 512], F32),
    "d1f": ([32, 66], BF), "d1fp": ([128, 66], BF), "d1b": ([32, 66], BF), "gtab": ([128, 33, 3, 128], BF),
    "etab": ([128, 4, 2, 32], BF), "ttab": ([66, 2, 128, 32], BF),
}

IN_SHAPES = {
    "x": [L, D], "ctx": [CT, D], "cc": [128, 8, 2], "w_ada": [D, 6 * D], "b_adaT": [128, 48], "gcols": [128, 4, 8],
    "w_in": [D, 5120], "w_qk_sw": [D, 1024], "cw": [128, 12, 3], "cb": [128, 12],
    "fw1": [33, 64], "fb1": [64, 1], "fw2": [64, 64], "fb2": [64, 1], "fw3": [64, 2048], "fb3": [1, 2048],
    "ffreq": [64, 1], "hyb": [128, 2, 4], "lamv": [1, 256], "subg": [1, 128],
    "w_hy_up": [DH, D], "w_att_up": [DH, D], "w_out": [D, D], "w_fg": [D, DFF], "w_fu": [D, DFF], "w_fd": [DFF, D],
}


def layout_inputs(inp, b):
    f = lambda a: np.ascontiguousarray(np.asarray(a, dtype=np.float32))
    m = {}
    m["x"] = f(inp["x"][b])
    m["ctx"] = f(inp["ctx"][b])
    cc = np.stack([np.asarray(inp["c"][b]), np.asarray(inp["c_ctx"])], -1)
    m["cc"] = f(cc.reshape(8, 128, 2).transpose(1, 0, 2))
    m["w_ada"] = f(inp["w_ada"][0])
    m["b_adaT"] = f(np.asarray(inp["b_ada"][0]).reshape(48, 128).T)
    g = np.stack([np.asarray(inp[k][0]) for k in ("g_mix_pre", "g_mix_post", "g_ffn_pre", "g_ffn_post")], 0)
    m["gcols"] = f(g.reshape(4, 8, 128).transpose(2, 0, 1))
    w_in = np.asarray(inp["w_in"][0])
    m["w_in"] = f(w_in)
    perm = np.arange(1024).reshape(16, 2, 2, 16)[:, :, ::-1, :].reshape(-1)
    m["w_qk_sw"] = f(w_in[:, OFF_Q:OFF_V][:, perm])
    m["cw"] = f(np.asarray(inp["hy_conv_w"][0]).reshape(3, 12, 128).transpose(2, 1, 0))
    m["cb"] = f(np.asarray(inp["hy_conv_b"][0]).reshape(12, 128).T)
    m["fw1"] = f(inp["hy_f_w1"][0])
    m["fb1"] = f(np.asarray(inp["hy_f_b1"][0]).reshape(64, 1))
    m["fw2"] = f(inp["hy_f_w2"][0])
    m["fb2"] = f(np.asarray(inp["hy_f_b2"][0]).reshape(64, 1))
    m["fw3"] = f(inp["hy_f_w3"][0])
    m["fb3"] = f(np.asarray(inp["hy_f_b3"][0]).reshape(1, 2048))
    m["ffreq"] = f(np.asarray(inp["hy_f_freq"][0]).reshape(64, 1))
    m["hyb"] = f(np.asarray(inp["hy_bias"][0]).reshape(2, 4, 128).transpose(2, 0, 1))
    m["lamv"] = f(np.concatenate([np.asarray(inp[k][0]) for k in ("lambda_q1", "lambda_q2", "lambda_k1", "lambda_k2")]).reshape(1, 256))
    m["subg"] = f(np.asarray(inp["att_subln_g"][0]).reshape(1, 128))
    m["w_hy_up"] = f(inp["w_hy_up"][0])
    m["w_att_up"] = f(inp["w_att_up"][0])
    m["w_out"] = f(inp["w_out"][0])
    m["w_fg"] = f(inp["w_ffn_gate"][0])
    m["w_fu"] = f(inp["w_ffn_up"][0])
    m["w_fd"] = f(inp["w_ffn_down"][0])
    m.update(host_consts())
    return m


def build(dbg=(), stop_after=None, skip=()):
    nc = bass.Bass("TRN2", target_bir_lowering=False)
    kb = KB(nc)
    din = {}
    for k, shp in IN_SHAPES.items():
        din[k] = nc.dram_tensor(k, list(shp), F32, kind="ExternalInput").ap()
    for k, (shp, dt) in CONST_SHAPES.items():
        din[k] = nc.dram_tensor(k, list(shp), dt, kind="ExternalInput").ap()
    out = nc.dram_tensor("out", [L, D], F32, kind="ExternalOutput").ap()
    hT_d = nc.dram_tensor("hT_d", [128, 8, L], BF, kind="Internal").ap()
    mT_d = nc.dram_tensor("mT_d", [128, 8, L], BF, kind="Internal").ap()
    zT_d = nc.dram_tensor("zT_d", [128, 4, L], BF, kind="Internal").ap()
    aT_d = nc.dram_tensor("aT_d", [128, 4, L], BF, kind="Internal").ap()
    sig_d = nc.dram_tensor("sig_d", [5, 128, L], BF, kind="Internal").ap()
    dbg_out = {}

    def dbg_tensor(name, shape, dt=F32):
        dbg_out[name] = nc.dram_tensor("dbg_" + name, list(shape), dt, kind="ExternalOutput").ap()
        return dbg_out[name]

    psall = nc.alloc_psum_tensor("psall", [128, 4096], F32)
    psall_b = psall.bitcast(BF)
    ps = [psall[:, i * 512:(i + 1) * 512] for i in range(8)]
    psb = [psall_b[:, i * 1024:(i + 1) * 1024] for i in range(8)]
    tps = [T(f"ps{i}") for i in range(8)]

    def mm(out_ap, lhsT, rhs, start, stop, reads, writes, tile_position=None):
        if tile_position is None:
            kb.op("pe", lambda e: e.matmul(out_ap, lhsT, rhs, start=start, stop=stop), reads=reads, writes=writes)
        else:
            kb.op("pe", lambda e: e.matmul(out_ap, lhsT, rhs, start=start, stop=stop, tile_position=tile_position), reads=reads, writes=writes)

    def tr(out_ap, in_ap, ident, reads, writes):
        kb.op("pe", lambda e: e.transpose(out_ap, in_ap, ident), reads=reads, writes=writes)

    def act(out_ap, in_ap, func, reads, writes, **kw):
        kb.op("act", lambda e: e.activation(out=out_ap, in_=in_ap, func=func, **kw), reads=reads, writes=writes)

    def ts(eng, out_ap, in0, s1, s2, op0, op1, reads, writes, **kw):
        if s2 is None:
            kb.op(eng, lambda e: e.tensor_scalar(out_ap, in0, s1, None, op0, **kw), reads=reads, writes=writes)
        else:
            kb.op(eng, lambda e: e.tensor_scalar(out_ap, in0, s1, s2, op0, op1, **kw), reads=reads, writes=writes)

    def tt(eng, out_ap, in0, in1, op, reads, writes):
        kb.op(eng, lambda e: e.tensor_tensor(out_ap, in0, in1, op), reads=reads, writes=writes)

    def stt(out_ap, in0, scalar, in1, op0, op1, reads, writes):
        kb.op("dve", lambda e: e.scalar_tensor_tensor(out_ap, in0, scalar, in1, op0, op1), reads=reads, writes=writes)

    def cp(eng, out_ap, in_ap, reads, writes):
        if eng == "act":
            kb.op("act", lambda e: e.copy(out_ap, in_ap), reads=reads, writes=writes)
        else:
            kb.op(eng, lambda e: e.tensor_copy(out_ap, in_ap), reads=reads, writes=writes)

    def recip(out_ap, in_ap, reads, writes):
        kb.op("dve", lambda e: e.reciprocal(out_ap, in_ap), reads=reads, writes=writes)

    def rsum(out_ap, in_ap, reads, writes):
        kb.op("dve", lambda e: e.reduce_sum(out_ap, in_ap, AX.X), reads=reads, writes=writes)

    def memset(eng, ap, val, writes):
        kb.op(eng, lambda e: e.memset(ap, val), writes=writes)

    _ids = [0]

    def next_id():
        _ids[0] += 1
        return _ids[0]

    P = ExitStack()

    def sbp(name, shape, dt):
        return P.enter_context(nc.sbuf_tensor("sp_" + name, list(shape), dt))

    ident_bf = sbp("ident_bf", [128, 128], BF)
    ident_f = sbp("ident_f", [128, 128], F32)
    ones_f = sbp("ones_f", [128, 128], F32)
    mhalf = sbp("mhalf", [128, 32], F32)
    cols = sbp("cols", [128, 8, 8], F32)
    g1row = sbp("g1row", [128, D], F32)
    g2row = sbp("g2row", [128, D], F32)
    neglam = sbp("neglam", [128, 1], F32)
    sgrow = sbp("sgrow", [128, 128], F32)
    hcT = sbp("hcT", [128, 8, CT], BF)
    t_const = T("const")
    t_cols = T("cols")
    t_rows = T("rows")
    t_hcT = T("hcT")
    kb.dma("sp", ident_bf[:], din["ident_bf"], writes=[t_const])
    kb.dma("sp", ident_f[:], din["ident_f"], writes=[t_const])
    memset("dve", ones_f[:], 1.0, [t_const])
    memset("dve", mhalf[:], -0.5, [t_const])
    A1 = lambda k: cols[:, 0, k:k + 1]
    B1 = lambda k: cols[:, 1, k:k + 1]
    A2 = lambda k: cols[:, 2, k:k + 1]
    B2 = lambda k: cols[:, 3, k:k + 1]
    A1c = lambda k: cols[:, 4, k:k + 1]
    B1c = lambda k: cols[:, 5, k:k + 1]

    def rstd_of(ssq_ap, out_ap, inv_n, tl):
        n = ssq_ap.shape[-1] if len(ssq_ap.shape) > 1 else 1
        ts("dve", out_ap, ssq_ap, inv_n, EPS, ALU.mult, ALU.add, reads=[tl], writes=[tl])
        tt("pool", out_ap, out_ap, mhalf[:, 0:n], ALU.pow, reads=[tl, t_const], writes=[tl])

    with ExitStack() as ph:
        def sb(name, shape, dt):
            return ph.enter_context(nc.sbuf_tensor(f"s{next_id()}_" + name, list(shape), dt))

        ccs = sb("ccs", [128, 16], F32)
        scs = sb("scs", [128, 16], F32)
        bada = sb("bada", [128, 48], F32)
        gc = sb("gc", [128, 4, 8], F32)
        adaT = sb("adaT", [128, 48, 2], F32)
        wa = [sb(f"wa{i}", [128, 6 * D], F32) for i in range(2)]
        twa = [T("wa0"), T("wa1")]
        t_s = T("p0small")
        kb.dma("sp", ccs[:], din["cc"].rearrange("p k c -> p (k c)"), writes=[t_s])
        kb.dma("sp", bada[:], din["b_adaT"], writes=[t_s])
        kb.dma("sp", gc[:], din["gcols"], writes=[t_s])
        act(scs[:], ccs[:], AF.Silu, reads=[t_s], writes=[t_s])
        def ada_chunk(k):
            kb.dma("sp", wa[k % 2][:], din["w_ada"][k * 128:(k + 1) * 128, :], writes=[twa[k % 2]])
            for f in range(48):
                mm(ps[0][:, 2 * f:2 * f + 2], wa[k % 2][:, f * 128:(f + 1) * 128], scs[:, 2 * k:2 * k + 2],
                   start=(k == 0 and f == 0), stop=(k == 7 and f == 47), reads=[twa[k % 2], t_s], writes=[tps[0]])
        lamb = sb("lamb", [128, 256], F32)
        lp = sb("lp", [128, 2, 64], F32)
        le = sb("le", [128, 2], F32)
        kb.dma("sp", lamb[:], bass.AP(din["lamv"].tensor, 0, [[0, 128], [1, 256]]), writes=[t_s])
        kb.dma("sp", sgrow[:], bass.AP(din["subg"].tensor, 0, [[0, 128], [1, 128]]), writes=[t_rows])
        tt("dve", lp[:].rearrange("p a b -> p (a b)"), lamb[:, 0:128], lamb[:, 128:256], ALU.mult, reads=[t_s], writes=[t_s])
        rsum(le[:], lp[:], reads=[t_s], writes=[t_s])
        act(le[:], le[:], AF.Exp, reads=[t_s], writes=[t_s])
        tt("dve", neglam[:], le[:, 1:2], le[:, 0:1], ALU.subtract, reads=[t_s], writes=[t_cols])
        ts("dve", neglam[:], neglam[:], -LAM_INIT, None, ALU.add, None, reads=[t_cols], writes=[t_cols])
        ts("dve", sgrow[:], sgrow[:], 1.0 - LAM_INIT, None, ALU.mult, None, reads=[t_rows], writes=[t_rows])
        xts = [sb(f"xt{i}", [128, D], F32) for i in range(4)]
        txt = [T() for _ in range(4)]
        junk = sb("junk", [128, D], BF)
        t_junk = T()
        xsa = sb("xsa", [128, 34, D], BF)
        txs = [T() for _ in range(34)]
        ssq = sb("ssq", [128, 34], F32)
        t_ssq = [T() for _ in range(34)]
        hst = [sb(f"hst{i}", [128, 8, 512], BF) for i in range(2)]
        thst = [T(), T()]
        t_hd = T("hT_d")
        for i in range(34):
            lat = i < 32
            src = din["x"][i * 128:(i + 1) * 128, :] if lat else din["ctx"][(i - 32) * 128:(i - 31) * 128, :]
            xt, tx = xts[i % 4], txt[i % 4]
            if i % 4 == 0 and i // 4 < 8:
                ada_chunk(i // 4)
            kb.dma("sp", xt[:], src, writes=[tx])
            act(junk[:], xt[:], AF.Square, reads=[tx], writes=[t_junk, t_ssq[i]], accum_out=ssq[:, i:i + 1])
            rstd_of(ssq[:, i:i + 1], ssq[:, i:i + 1], 1.0 / D, t_ssq[i])
            if i >= 2:
                j = i - 2
                act(xsa[:, j, :], xts[j % 4][:], AF.Copy, reads=[txt[j % 4], t_ssq[j]], writes=[txs[j]], scale=ssq[:, j:j + 1])
        for j in (32, 33):
            act(xsa[:, j, :], xts[j % 4][:], AF.Copy, reads=[txt[j % 4], t_ssq[j]], writes=[txs[j]], scale=ssq[:, j:j + 1])
        for c in range(2):
            tt("dve", adaT[:, :, c], ps[0][:, c:96:2], bada[:], ALU.add, reads=[tps[0], t_s], writes=[t_s])
        for (dst, sc_f, g_i, c) in ((0, 8, 0, 0), (2, 32, 2, 0), (4, 8, 0, 1)):
            stt(cols[:, dst, :], adaT[:, sc_f:sc_f + 8, c], 1.0, gc[:, g_i, :], ALU.add, ALU.mult, reads=[t_s], writes=[t_cols])
        for (dst, sh_f, c) in ((1, 0, 0), (3, 24, 0), (5, 0, 1)):
            cp("dve", cols[:, dst, :], adaT[:, sh_f:sh_f + 8, c], reads=[t_s], writes=[t_cols])
        tt("dve", cols[:, 6, :], adaT[:, 16:24, 0], gc[:, 1, :], ALU.mult, reads=[t_s], writes=[t_cols])
        tt("dve", cols[:, 7, :], adaT[:, 40:48, 0], gc[:, 3, :], ALU.mult, reads=[t_s], writes=[t_cols])
        diag = sb("diag", [128, 4, 128], F32)
        t_diag = T("diag")
        for gi, rowt in ((6, g1row), (7, g2row)):
            for half in range(2):
                for j in range(4):
                    ts("dve", diag[:, j, :], ident_f[:], cols[:, gi, half * 4 + j:half * 4 + j + 1], None, ALU.mult, None,
                       reads=[t_const, t_cols], writes=[t_diag])
                for j in range(4):
                    mm(ps[1][:, j * 128:(j + 1) * 128], ones_f[:], diag[:, j, :], start=True, stop=True,
                       reads=[t_diag, t_const], writes=[tps[1]])
                cp("dve", rowt[:, half * 512:(half + 1) * 512], ps[1][:], reads=[tps[1]], writes=[t_rows])
        if "p0" in dbg:
            d = dbg_tensor("cols", [128, 64])
            kb.dma("sp", d, cols[:].rearrange("p a b -> p (a b)"), reads=[t_cols])
            d = dbg_tensor("g1row", [128, D])
            kb.dma("sp", d, g1row[:], reads=[t_rows])
            d = dbg_tensor("neglam", [128, 1])
            kb.dma("sp", d, neglam[:], reads=[t_cols])

        for i in range(34):
            lat = i < 32
            bk = 2 + i % 4
            for k in range(8):
                tr(psb[bk][:, k * 128:(k + 1) * 128], xsa[:, i, k * 128:(k + 1) * 128], ident_bf[:],
                   reads=[txs[i], t_const], writes=[tps[bk]])
            for k in range(8):
                if lat:
                    h = hst[(i // 4) % 2]
                    ts("dve", h[:, k, (i % 4) * 128:(i % 4 + 1) * 128], psb[bk][:, k * 128:(k + 1) * 128], A1(k), B1(k),
                       ALU.mult, ALU.add, reads=[tps[bk], t_cols], writes=[thst[(i // 4) % 2]])
                else:
                    ts("dve", hcT[:, k, (i - 32) * 128:(i - 31) * 128], psb[bk][:, k * 128:(k + 1) * 128], A1c(k), B1c(k),
                       ALU.mult, ALU.add, reads=[tps[bk], t_cols], writes=[t_hcT])
            if lat and i % 4 == 3:
                blk = i // 4
                kb.dma("sp", hT_d[:, :, blk * 512:(blk + 1) * 512], hst[blk % 2][:], reads=[thst[blk % 2]], writes=[t_hd])
        if "p1" in dbg:
            d = dbg_tensor("hT", [128, 8, L], BF)
            kb.dma("sp", d, hT_d, reads=[t_hd])
            d = dbg_tensor("hcT", [128, 8, CT], BF)
            kb.dma("sp", d, hcT[:], reads=[t_hcT])
        kb.barrier()
    if stop_after == "p1":
        kb.finish()
        return nc, dbg_out


    hall_d = nc.dram_tensor("hall_d", [1024, 8448], BF, kind="Internal").ap()
    t_hall = [T(f"hall{i}") for i in range(8)]
    with ExitStack() as ph:
        def sb(name, shape, dt):
            return ph.enter_context(nc.sbuf_tensor(f"s{next_id()}_" + name, list(shape), dt))

        _bk = [0]

        def nb():
            _bk[0] = (_bk[0] + 1) % 8
            return _bk[0]

        t_hc = T("hfconst")
        gtab = sb("gtab", [128, 33, 3, 128], BF)
        d1f = sb("d1fp", [128, 66], BF)
        tv0 = sb("tv0", [128, 512], F32)
        negd = sb("negd", [128, 4], F32)
        ndoff = sb("ndoff", [128, 4, 8], F32)
        hybs = sb("hybs", [128, 2, 4], F32)
        for dst, nm in ((gtab, "gtab"), (d1f, "d1fp"), (tv0, "tv0"), (negd, "negdelta"), (ndoff, "ndoff"), (hybs, "hyb")):
            kb.dma("sp", dst[:], din[nm], writes=[t_hc])
        hdn2 = sb("hdn2", [65, L], BF)
        fw3a = sb("fw3a", [65, 2048], BF)
        t_h2 = T("hdn2")
        t_fw3 = T("fw3")
        kb.dma("pool", fw3a[0:64, :], din["fw3"], writes=[t_fw3])
        kb.dma("pool", fw3a[64:65, :], din["fb3"], writes=[t_fw3])
        memset("pool", hdn2[64:65, :], 1.0, [t_h2])
        zTs = sb("zTs", [33, L], F32)
        fw1s = sb("fw1s", [33, 64], F32)
        fw2s = sb("fw2s", [64, 64], F32)
        fcol = sb("fcol", [64, 5], F32)
        t_f = T("fmlp")
        kb.dma("sp", zTs[:], din["zT"], writes=[t_f])
        kb.dma("sp", fw1s[:], din["fw1"], writes=[t_f])
        kb.dma("sp", fw2s[:], din["fw2"], writes=[t_f])
        kb.dma("sp", fcol[:, 0:1], din["ffreq"], writes=[t_f])
        kb.dma("sp", fcol[:, 1:2], din["fb1"], writes=[t_f])
        kb.dma("sp", fcol[:, 2:3], din["fb2"], writes=[t_f])
        tt("dve", fcol[:, 3:4], fcol[:, 0:1], fcol[:, 1:2], ALU.mult, reads=[t_f], writes=[t_f])
        tt("dve", fcol[:, 4:5], fcol[:, 0:1], fcol[:, 2:3], ALU.mult, reads=[t_f], writes=[t_f])
        with ExitStack() as phm:
            def sbm(name, shape, dt):
                return phm.enter_context(nc.sbuf_tensor(f"s{next_id()}_" + name, list(shape), dt))

            halfpi = sbm("halfpi", [64, 1], F32)
            memset("dve", halfpi[:], PI / 2, [t_f])
            arg = [sbm(f"arg{i}", [64, 512], F32) for i in range(8)]
            s4 = [sbm(f"s4{i}", [64, 512], F32) for i in range(8)]
            c4 = [sbm(f"c4{i}", [64, 512], F32) for i in range(8)]
            hd1 = [sbm(f"hd1{i}", [64, 512], F32) for i in range(8)]
            t_m = [T() for _ in range(8)]

            def sin_layer_bf(ps_of, bias_col, out_of, t_out_of):
                for i_ in range(8):
                    ts("dve", arg[i_][:], ps[ps_of(i_)][0:64, :], fcol[:, 0:1], fcol[:, bias_col:bias_col + 1], ALU.mult, ALU.add,
                       reads=[tps[ps_of(i_)], t_f], writes=[t_m[i_]])
                for i_ in range(8):
                    act(s4[i_][:], arg[i_][:], AF.Sin, reads=[t_m[i_]], writes=[t_m[i_]], scale=0.25)
                    act(c4[i_][:], arg[i_][:], AF.Sin, reads=[t_m[i_], t_f], writes=[t_m[i_]], scale=0.25, bias=halfpi[:])
                for i_ in range(8):
                    tt("pool", arg[i_][:], s4[i_][:], s4[i_][:], ALU.mult, reads=[t_m[i_]], writes=[t_m[i_]])
                    tt("pool", s4[i_][:], s4[i_][:], c4[i_][:], ALU.mult, reads=[t_m[i_]], writes=[t_m[i_]])
                for i_ in range(8):
                    ts("dve", arg[i_][:], arg[i_][:], -2.0, 1.0, ALU.mult, ALU.add, reads=[t_m[i_]], writes=[t_m[i_]])
                    stt(out_of(i_), s4[i_][:], 4.0, arg[i_][:], ALU.mult, ALU.mult, reads=[t_m[i_]], writes=[t_m[i_], t_out_of(i_)])

            for blk in range(8):
                mm(ps[blk][0:64, :], fw1s[:], zTs[:, blk * 512:(blk + 1) * 512], start=True, stop=True, reads=[t_f], writes=[tps[blk]])
            sin_layer_bf(lambda i_: i_, 3, lambda i_: hd1[i_][:], lambda i_: t_m[i_])
            for blk in range(8):
                mm(ps[blk][0:64, :], fw2s[:], hd1[blk][:], start=True, stop=True, reads=[t_f, t_m[blk]], writes=[tps[blk]])
            sin_layer_bf(lambda i_: i_, 4, lambda i_: hdn2[0:64, i_ * 512:(i_ + 1) * 512], lambda i_: t_h2)
            kb.barrier()
        dec = [sb(f"dec{i}", [128, L], BF) for i in range(2)]
        t_dec = [T(), T()]
        Kf = [[sb(f"Kf{i}_{d_}", [128, L], BF) for d_ in range(2)] for i in range(2)]
        t_Kf = [[T(), T()], [T(), T()]]
        Ut = [sb(f"Ut{i}", [128, 32, 128], BF) for i in range(2)]
        tUt = [T(), T()]
        for i in range(2):
            memset("pool", Ut[i][32:64, :, :], 0.0, [tUt[i]])
            memset("pool", Ut[i][64:128, :, :], 0.0, [tUt[i]])
        A = sb("A", [128, 66, 128], BF)
        tA = [T() for _ in range(33)]
        H = [sb(f"H{i}", [128, 33, 2, 128], BF) for i in range(2)]
        tH = [[T() for _ in range(33)] for _ in range(2)]
        t_sdf = [T() for _ in range(4)]
        cnt = [0]

        def stage_k(r):
            cc, o = r // 2, r % 2
            pi_ = r % 2
            if o == 0:
                for b8 in range(8):
                    act(dec[cc % 2][:, b8 * 512:(b8 + 1) * 512], tv0[:], AF.Exp, reads=[t_hc], writes=[t_dec[cc % 2]],
                        scale=negd[:, cc:cc + 1], bias=ndoff[:, cc, b8:b8 + 1])
            for d_ in range(2):
                col0 = (o * 2 + d_) * 512 + cc * 128
                kf, tk = Kf[pi_][d_], t_Kf[pi_][d_]
                for blk in range(8):
                    sl = slice(blk * 512, (blk + 1) * 512)
                    b = nb()
                    mm(ps[b][:], fw3a[:, col0:col0 + 128], hdn2[:, sl], start=True, stop=True, reads=[t_fw3, t_h2], writes=[tps[b]])
                    tt("dve", kf[:, sl], ps[b][:], dec[cc % 2][:, sl], ALU.mult, reads=[tps[b], t_dec[cc % 2]], writes=[tk])
                if d_ == 0:
                    tt("dve", kf[:, 0:1], kf[:, 0:1], hybs[:, o, cc:cc + 1], ALU.add, reads=[tk, t_hc], writes=[tk])
                else:
                    memset("dve", kf[:, 0:1], 0.0, [tk])
                kb.dma("pool", sig_d[pi_ * 2 + d_], kf[:], reads=[tk], writes=[t_sdf[pi_ * 2 + d_]])

        def stage_f(r):
            pi_ = r % 2
            Hh, tHh = H[pi_], tH[pi_]
            for d_ in range(2):
                slot = pi_ * 2 + d_
                for g in range(4):
                    u, tu = Ut[g % 2], tUt[g % 2]
                    kb.dma("sp", u[0:32, :, :], sig_d[slot][g * 32:(g + 1) * 32, :].rearrange("c (a b) -> a c b", a=32), reads=[t_sdf[slot]], writes=[tu])
                    j = 0
                    while j < 32:
                        n = min(7, 32 - j)
                        b = nb()
                        for jj in range(n):
                            mm(ps[b][:, jj * 66:(jj + 1) * 66], u[:, j + jj, :], d1f[:], start=True, stop=True, reads=[tu, t_hc], writes=[tps[b]])
                        c0 = g * 32 + j
                        cp("act" if cnt[0] % 2 == 0 else "dve", A[:, :, c0:c0 + n], ps[b][:, 0:n * 66].rearrange("p (c k) -> p k c", k=66),
                           reads=[tps[b]], writes=tA)
                        cnt[0] += 1
                        j += n
                for k1 in range(33):
                    b = nb()
                    ar, ai = A[:, k1, :], A[:, 33 + k1, :]
                    mm(ps[b][:, 0:128], gtab[:, k1, 0, :], ar, start=True, stop=False, reads=[t_hc, tA[k1]], writes=[tps[b]])
                    mm(ps[b][:, 0:128], gtab[:, k1, 2, :], ai, start=False, stop=True, reads=[t_hc, tA[k1]], writes=[tps[b]])
                    mm(ps[b][:, 128:256], gtab[:, k1, 1, :], ar, start=True, stop=False, reads=[t_hc, tA[k1]], writes=[tps[b]])
                    mm(ps[b][:, 128:256], gtab[:, k1, 0, :], ai, start=False, stop=True, reads=[t_hc, tA[k1]], writes=[tps[b]])
                    if d_ == 0:
                        cp("act", Hh[:, k1, :, :], ps[b][:, 0:256].rearrange("p (r c) -> p r c", r=2), reads=[tps[b]], writes=[tHh[k1]])
                    else:
                        tt("dve", Hh[:, k1, 0, :], Hh[:, k1, 0, :], ps[b][:, 0:128], ALU.add, reads=[tps[b], tHh[k1]], writes=[tHh[k1]])
                        tt("dve", Hh[:, k1, 1, :], Hh[:, k1, 1, :], ps[b][:, 128:256], ALU.subtract, reads=[tps[b], tHh[k1]], writes=[tHh[k1]])
            kb.dma("sp", hall_d[r * 128:(r + 1) * 128, :], Hh[:].rearrange("p a b c -> p (a b c)"), reads=tHh, writes=[t_hall[r]])

        stage_k(0)
        for r in range(8):
            if r + 1 < 8:
                stage_k(r + 1)
            stage_f(r)
        kb.barrier()
    if stop_after == "hf":
        kb.finish()
        return nc, dbg_out

    t_hd_r = T("hT_d_r")
    w_in_v = din["w_in"].rearrange("(k p) n -> p k n", p=128)
    w_sw_v = din["w_qk_sw"].rearrange("(k p) n -> p k n", p=128)

    def load_w(dst, src_view, c0, tl, ncols=128):
        kb.dma("pool", dst, src_view[:, :, c0:c0 + ncols], writes=[tl])

    with ExitStack() as ph:
        def sb(name, shape, dt):
            return ph.enter_context(nc.sbuf_tensor(f"s{next_id()}_" + name, list(shape), dt))

        NH = 0 if "p3" in skip else 4
        rcos = sb("rcos", [128, L], F32)
        rsin = sb("rsin", [128, L], F32)
        t_ropes = [T(f"rope{i}") for i in range(8)]
        for i in range(8):
            kb.dma("sp", rcos[:, i * 512:(i + 1) * 512], din["rope_cos"][:, i * 512:(i + 1) * 512], writes=[t_ropes[i]])
            kb.dma("sp", rsin[:, i * 512:(i + 1) * 512], din["rope_sin"][:, i * 512:(i + 1) * 512], writes=[t_ropes[i]])
        QT = [sb(f"QT{i}", [128, L], BF) for i in range(2)]
        KT = [[sb(f"KT{i}_{m_}", [128, LK], BF) for m_ in range(2)] for i in range(2)]
        V = [sb(f"V{i}", [128, 34, 129], BF) for i in range(2)]
        t_Q, t_K, t_V = [T(), T()], [T(), T()], [T(), T()]
        for i in range(2):
            memset("pool", KT[i][0][64:128, :], 0.0, [t_K[i]])
            memset("pool", KT[i][1][0:64, :], 0.0, [t_K[i]])
            memset("pool", V[i][:, :, 128:129], 1.0, [t_V[i]])
        wts = [[sb(f"aw{i}_{j}", [128, 8, 128], BF) for j in range(5)] for i in range(2)]
        twts = [[T() for _ in range(5)] for _ in range(2)]
        hb = [sb(f"hb{i}", [128, 8, 512], BF) for i in range(2)]
        thb = [T(), T()]
        rt = [sb(f"rt{i}", [128, 512], F32) for i in range(4)]
        trt = [T() for _ in range(4)]
        E = [sb(f"E{i}", [128, 1024], BF) for i in range(4)]
        tE = [T() for _ in range(4)]
        attst = [sb(f"attst{i}", [128, 512], BF) for i in range(2)]
        t_attst = [T(), T()]
        fin = sb("fin", [128, 16], F32)
        accs = sb("accs", [128, 1161], F32)
        oa = sb("oa", [128, 4, 128], F32)
        ob = sb("ob", [128, 4, 128], F32)
        on = sb("on", [128, 4, 128], BF)
        t_fin, t_on, t_accs, t_oa, t_ob = T(), T(), T(), T(), T()
        t_ad = T("aT_d")
        hbcnt = [0]

        def prologue(h, banks):
            bi = h % 2
            w_, tw_ = wts[bi], twts[bi]
            brr = [0]

            def nbk():
                brr[0] += 1
                return banks[brr[0] % len(banks)]

            load_w(w_[0][:], w_in_v, OFF_Q + h * 128, tw_[0])
            load_w(w_[1][:], w_sw_v, h * 128, tw_[1])
            load_w(w_[2][:], w_in_v, OFF_K + h * 128, tw_[2])
            load_w(w_[3][:], w_sw_v, 512 + h * 128, tw_[3])
            load_w(w_[4][:], w_in_v, OFF_V + h * 128, tw_[4])
            b = nbk()
            for k in range(8):
                mm(ps[b][:, 0:CT], w_[2][:, k, :], hcT[:, k, :], start=(k == 0), stop=(k == 7), reads=[tw_[2], t_hcT], writes=[tps[b]])
            cp("dve", KT[bi][0][0:64, L:LK], ps[b][0:64, 0:CT], reads=[tps[b]], writes=[t_K[bi]])
            cp("dve", KT[bi][1][64:128, L:LK], ps[b][64:128, 0:CT], reads=[tps[b]], writes=[t_K[bi]])
            yield
            b = nbk()
            for s_ in range(2):
                for k in range(8):
                    mm(ps[b][:, s_ * 128:(s_ + 1) * 128], hcT[:, k, s_ * 128:(s_ + 1) * 128], w_[4][:, k, :], start=(k == 0), stop=(k == 7),
                       reads=[tw_[4], t_hcT], writes=[tps[b]])
            cp("dve", V[bi][:, 32:34, 0:128], ps[b][:, 0:256].rearrange("p (s v) -> p s v", s=2), reads=[tps[b]], writes=[t_V[bi]])
            yield
            for blk in range(8):
                hi = hbcnt[0] % 2
                hbcnt[0] += 1
                hbb, th = hb[hi], thb[hi]
                kb.dma("sp", hbb[:], hT_d[:, :, blk * 512:(blk + 1) * 512], reads=[t_hd], writes=[th])
                sl = slice(blk * 512, (blk + 1) * 512)
                for qk in range(2):
                    for gg in range(2):
                        g = 2 * qk + gg
                        b = nbk()
                        for k in range(8):
                            mm(ps[b][:], w_[g][:, k, :], hbb[:, k, :], start=(k == 0), stop=(k == 7), reads=[tw_[g], th], writes=[tps[b]])
                        tt("dve", rt[g][:], ps[b][:], (rcos if gg == 0 else rsin)[:, sl], ALU.mult, reads=[tps[b], t_ropes[blk]], writes=[trt[g]])
                        yield
                    r0, r1 = rt[2 * qk], rt[2 * qk + 1]
                    if qk == 0:
                        tt("pool", QT[bi][:, sl], r0[:], r1[:], ALU.add, reads=[trt[0], trt[1]], writes=[t_Q[bi]])
                    else:
                        tt("pool", KT[bi][0][0:64, sl], r0[0:64, :], r1[0:64, :], ALU.add, reads=[trt[2], trt[3]], writes=[t_K[bi]])
                        tt("pool", KT[bi][1][64:128, sl], r0[64:128, :], r1[64:128, :], ALU.add, reads=[trt[2], trt[3]], writes=[t_K[bi]])
                b = nbk()
                for s_ in range(4):
                    for k in range(8):
                        mm(ps[b][:, s_ * 128:(s_ + 1) * 128], hbb[:, k, s_ * 128:(s_ + 1) * 128], w_[4][:, k, :], start=(k == 0), stop=(k == 7),
                           reads=[tw_[4], th], writes=[tps[b]])
                cp("dve", V[bi][:, blk * 4:blk * 4 + 4, 0:128], ps[b][:].rearrange("p (s v) -> p s v", s=4), reads=[tps[b]], writes=[t_V[bi]])
                yield

        def bc_last(ap2d, n):
            return bass.AP(ap2d.tensor, ap2d.offset, [list(ap2d.ap[0]), list(ap2d.ap[1]), [0, n]])

        items = [(qb, kc) for qb in range(8) for kc in range(34)]

        def head_loop(h, gen):
            bi = h % 2
            Qh, Kh, Vh = QT[bi], KT[bi], V[bi]

            def emit_S(idx):
                qb, kc = items[idx]
                b0 = (idx % 2) * 2
                for m_ in range(2):
                    mm(ps[b0 + m_][:], Kh[m_][:, kc * 128:(kc + 1) * 128], Qh[:, qb * 512:(qb + 1) * 512], start=True, stop=True,
                       reads=[t_K[bi], t_Q[bi]], writes=[tps[b0 + m_]])
                ei = idx % 4
                act(E[ei][:], psall[:, b0 * 512:(b0 + 2) * 512], AF.Exp, reads=[tps[b0], tps[b0 + 1]], writes=[tE[ei]], scale=0.125)

            def emit_AV(idx):
                qb, kc = items[idx]
                ei = idx % 4
                for m_ in range(2):
                    for s_ in range(4):
                        slot = m_ * 4 + s_
                        bk, c0 = 4 + slot // 3, (slot % 3) * 129
                        mm(ps[bk][:, c0:c0 + 129], E[ei][:, m_ * 512 + s_ * 128:m_ * 512 + (s_ + 1) * 128], Vh[:, kc, :],
                           start=(kc == 0 and slot % 3 == 0), stop=(kc == 33 and (slot % 3 == 2 or slot == 7)),
                           reads=[tE[ei], t_V[bi]], writes=[tps[bk]])
                if kc == 33:
                    finalize(qb)
                    pend.append(qb)
                if kc == 10 and pend:
                    finalize_tr(pend.pop())

            def finalize(qb):
                qsl = slice(qb * 512, (qb + 1) * 512)
                cp("dve", accs[:, 0:387], ps[4][:, 0:387], reads=[tps[4]], writes=[t_accs])
                cp("dve", accs[:, 387:774], ps[5][:, 0:387], reads=[tps[5]], writes=[t_accs])
                cp("dve", accs[:, 774:1032], ps[6][:, 0:258], reads=[tps[6]], writes=[t_accs])
                av = accs[:, 0:1032].rearrange("p (s c) -> p s c", c=129)
                recip(fin[:, 0:8], av[:, :, 128], reads=[t_accs], writes=[t_fin])
                ts("dve", fin[:, 4:8], fin[:, 4:8], neglam[:], None, ALU.mult, None, reads=[t_fin, t_cols], writes=[t_fin])
                tt("dve", oa[:], av[:, 0:4, 0:128], bc_last(fin[:, 0:4], 128), ALU.mult, reads=[t_accs, t_fin], writes=[t_oa])
                tt("dve", ob[:], av[:, 4:8, 0:128], bc_last(fin[:, 4:8], 128), ALU.mult, reads=[t_accs, t_fin], writes=[t_ob])
                tt("pool", ob[:], ob[:], oa[:], ALU.add, reads=[t_oa, t_ob], writes=[t_ob])
                tt("pool", oa[:], ob[:], ob[:], ALU.mult, reads=[t_ob], writes=[t_oa])
                rsum(fin[:, 8:12], oa[:], reads=[t_oa], writes=[t_fin])
                rstd_of(fin[:, 8:12], fin[:, 12:16], 1.0 / 128, t_fin)
                tt("dve", ob[:], ob[:], bc_last(fin[:, 12:16], 128), ALU.mult, reads=[t_ob, t_fin], writes=[t_ob])
                sg_b = bass.AP(sgrow[:].tensor, sgrow[:].offset, [list(sgrow[:].ap[0]), [0, 4], [1, 128]])
                tt("dve", on[:], ob[:], sg_b, ALU.mult, reads=[t_ob, t_rows], writes=[t_on])

            def finalize_tr(qb):
                qsl = slice(qb * 512, (qb + 1) * 512)
                for s_ in range(4):
                    tr(psb[7][:, s_ * 128:(s_ + 1) * 128], on[:, s_, :], ident_bf[:], reads=[t_on, t_const], writes=[tps[7]])
                cp("dve", attst[qb % 2][:], psb[7][:, 0:512], reads=[tps[7]], writes=[t_attst[qb % 2]])
                kb.dma("sp", aT_d[:, h, qsl], attst[qb % 2][:], reads=[t_attst[qb % 2]], writes=[t_ad])

            pend = []
            emit_S(0)
            emit_S(1)
            for idx in range(len(items)):
                if idx + 2 < len(items):
                    emit_S(idx + 2)
                emit_AV(idx)
                if gen is not None and idx % 6 == 3 and items[idx][1] not in (32, 33, 0):
                    next(gen, None)
            while pend:
                finalize_tr(pend.pop())
            if gen is not None:
                for _ in gen:
                    pass

        if NH:
            for _ in prologue(0, [0, 1, 2, 3, 4, 5, 6, 7]):
                pass
        for h in range(NH):
            gen = prologue(h + 1, [7]) if h + 1 < NH else None
            head_loop(h, gen)
        if "p3" in dbg:
            d = dbg_tensor("attT", [128, 4, L], BF)
            kb.dma("sp", d, aT_d, reads=[t_ad])
        kb.barrier()
    if stop_after == "p3":
        kb.finish()
        return nc, dbg_out

    with ExitStack() as ph:
        def sb(name, shape, dt):
            return ph.enter_context(nc.sbuf_tensor(f"s{next_id()}_" + name, list(shape), dt))

        _bk = [0]

        def nb():
            _bk[0] = (_bk[0] + 1) % 8
            return _bk[0]

        t_hc = T("hyconst")
        gtab = sb("gtab", [128, 33, 3, 128], BF)
        ttab = sb("ttab", [66, 2, 128, 32], BF)
        etab = sb("etab", [128, 4, 2, 32], BF)
        d1f = sb("d1fp", [128, 66], BF)
        cwt = sb("cwt", [128, 12, 3], F32)
        cbt = sb("cbt", [128, 12], F32)
        for dst, nm in ((gtab, "gtab"), (ttab, "ttab"), (etab, "etab"), (d1f, "d1fp"), (cwt, "cw"), (cbt, "cb")):
            kb.dma("sp", dst[:], din[nm], writes=[t_hc])
        Z = [sb(f"Z{i}", [128, L], BF) for i in range(3)]
        tZ = [T(f"Z{i}") for i in range(3)]
        H = sb("H", [128, 33, 2, 128], BF)
        tH = [T() for _ in range(33)]
        t_sd = [T(f"sig{i}") for i in range(5)]
        t_zd = T("zT_d")
        Ut = [sb(f"Ut{i}", [128, 32, 128], BF) for i in range(2)]
        tUt = [T(), T()]
        for i in range(2):
            memset("pool", Ut[i][32:64, :, :], 0.0, [tUt[i]])
            memset("pool", Ut[i][64:128, :, :], 0.0, [tUt[i]])
        A = sb("A", [128, 66, 128], BF)
        tA = [T() for _ in range(33)]
        f1cnt = [0]

        def f1_pre(sig, t_sig, slot):
            kb.dma("pool", sig_d[slot], sig, reads=[t_sig], writes=[t_sd[slot]])
            for g in range(2):
                kb.dma("pool", Ut[g][0:32, :, :], sig_d[slot][g * 32:(g + 1) * 32, :].rearrange("c (a b) -> a c b", a=32),
                       reads=[t_sd[slot]], writes=[tUt[g]])

        def f1_part(sig, t_sig, slot, pre=False):
            if not pre:
                f1_pre(sig, t_sig, slot)
            for g in range(4):
                u, tu = Ut[g % 2], tUt[g % 2]
                if g >= 2:
                    kb.dma("pool", u[0:32, :, :], sig_d[slot][g * 32:(g + 1) * 32, :].rearrange("c (a b) -> a c b", a=32),
                           reads=[t_sd[slot]], writes=[tu])
                j = 0
                while j < 32:
                    n = min(7, 32 - j)
                    b = nb()
                    for jj in range(n):
                        mm(ps[b][:, jj * 66:(jj + 1) * 66], u[:, j + jj, :], d1f[:], start=True, stop=True,
                           reads=[tu, t_hc], writes=[tps[b]])
                    c0 = g * 32 + j
                    eng = "act" if f1cnt[0] % 2 == 0 else "dve"
                    f1cnt[0] += 1
                    cp(eng, A[:, :, c0:c0 + n], ps[b][:, 0:n * 66].rearrange("p (c k) -> p k c", k=66), reads=[tps[b]], writes=tA)
                    j += n

        for cc in range(4):
            with ExitStack() as pa:
                def sba(name, shape, dt):
                    return pa.enter_context(nc.sbuf_tensor(f"s{next_id()}_" + name, list(shape), dt))

                wts = [sba(f"hw{i}", [128, 8, 128], BF) for i in range(3)]
                twts = [T() for _ in range(3)]
                hb = [sba(f"hhb{i}", [128, 8, 512], BF) for i in range(3)]
                thb = [T(), T(), T()]
                Ur = [sba(f"Ur{i}", [128, L + 2], BF) for i in range(3)]
                tUr = [T() for _ in range(3)]
                ctmp = sba("ctmp", [128, L], F32)
                t_ct = T()
                for s in range(3):
                    load_w(wts[s][:], w_in_v, s * 512 + cc * 128, twts[s])
                    memset("pool", Ur[s][:, 0:1], 0.0, [tUr[s]])
                    memset("pool", Ur[s][:, L + 1:L + 2], 0.0, [tUr[s]])
                def proj_pass(sigs, off):
                    for pb_ in range(2):
                        kb.dma("sp", hb[pb_][:], hT_d[:, :, pb_ * 512:(pb_ + 1) * 512], reads=[t_hd], writes=[thb[pb_]])
                    for blk in range(8):
                        hbb, th = hb[blk % 3], thb[blk % 3]
                        if blk + 2 < 8:
                            kb.dma("sp", hb[(blk + 2) % 3][:], hT_d[:, :, (blk + 2) * 512:(blk + 3) * 512], reads=[t_hd], writes=[thb[(blk + 2) % 3]])
                        for s in sigs:
                            b = nb()
                            for k in range(8):
                                mm(ps[b][:], wts[s][:, k, :], hbb[:, k, :], start=(k == 0), stop=(k == 7), reads=[twts[s], th], writes=[tps[b]])
                            cp("act", Ur[s][:, 1 + blk * 512:1 + (blk + 1) * 512], ps[b][:], reads=[tps[b]], writes=[tUr[s]])

                def sconv(s):
                    slot = s * 4 + cc
                    act(ctmp[:], Ur[s][:, 1:L + 1], AF.Identity, reads=[tUr[s], t_hc], writes=[t_ct],
                        scale=cwt[:, slot, 1:2], bias=cbt[:, slot:slot + 1])
                    stt(ctmp[:], Ur[s][:, 0:L], cwt[:, slot, 0:1], ctmp[:], ALU.mult, ALU.add, reads=[tUr[s], t_hc, t_ct], writes=[t_ct])
                    stt(Z[s][:], Ur[s][:, 2:L + 2], cwt[:, slot, 2:3], ctmp[:], ALU.mult, ALU.add, reads=[tUr[s], t_hc, t_ct], writes=[tZ[s]])

                proj_pass([0, 1, 2], 0)
                sconv(0)
                f1_pre(Z[0][:], tZ[0], 4)
                sconv(1)
                sconv(2)
                f1_part(Z[0][:], tZ[0], 4, pre=True)
                if "p2a" in dbg and cc == 0:
                    for s in range(3):
                        d = dbg_tensor(f"Z{s}", [128, L], BF)
                        kb.dma("sp", d, Z[s][:], reads=[tZ[s]])
                kb.barrier()
            with ExitStack() as pb:
                def sbb(name, shape, dt):
                    return pb.enter_context(nc.sbuf_tensor(f"s{next_id()}_" + name, list(shape), dt))

                Y = sbb("Y", [128, 128, 66], BF)
                tY = [T() for _ in range(33)]
                Pqs = [sbb(f"Pq{i}", [66, 64, 128], BF) for i in range(2)]
                t_Pqs = [T(), T()]
                pw = [sbb(f"pw{i}", [128, 2, 128], F32) for i in range(4)]
                tpw = [T() for _ in range(4)]

                def f2_part(consumer):
                    for k1 in range(33):
                        b = nb()
                        ar, ai = A[:, k1, :], A[:, 33 + k1, :]
                        mm(ps[b][:, 0:128], gtab[:, k1, 0, :], ar, start=True, stop=False, reads=[t_hc, tA[k1]], writes=[tps[b]])
                        mm(ps[b][:, 0:128], gtab[:, k1, 2, :], ai, start=False, stop=True, reads=[t_hc, tA[k1]], writes=[tps[b]])
                        mm(ps[b][:, 128:256], gtab[:, k1, 1, :], ar, start=True, stop=False, reads=[t_hc, tA[k1]], writes=[tps[b]])
                        mm(ps[b][:, 128:256], gtab[:, k1, 0, :], ai, start=False, stop=True, reads=[t_hc, tA[k1]], writes=[tps[b]])
                        consumer(k1, b)

                def conv(o, sig, t_sig, xm, t_xm, zout, t_zout):
                    r_ = 2 * cc + o
                    kb.dma("sp", H[:].rearrange("p a b c -> p (a b c)"), hall_d[r_ * 128:(r_ + 1) * 128, :], reads=[t_hall[r_]], writes=tH)
                    if "p2h" in dbg and cc == 0:
                        d = dbg_tensor(f"H{o}", [128, 33 * 2 * 128], BF)
                        kb.dma("sp", d, H[:].rearrange("p a b c -> p (a b c)"), reads=tH)

                    def cons_d(k1, b):
                        i0 = (2 * k1) % 4
                        p1, p2 = pw[i0], pw[i0 + 1]
                        xv = ps[b][:, 0:256].rearrange("p (r c) -> p r c", r=2)
                        tt("dve", p1[:], xv, bass.AP(H[:, k1, 0, :].tensor, H[:, k1, 0, :].offset, [list(H[:, k1, 0, :].ap[0]), [0, 2], [1, 128]]),
                           ALU.mult, reads=[tps[b], tH[k1]], writes=[tpw[i0]])
                        tt("dve", p2[:], xv, bass.AP(H[:, k1, 1, :].tensor, H[:, k1, 1, :].offset, [list(H[:, k1, 1, :].ap[0]), [0, 2], [1, 128]]),
                           ALU.mult, reads=[tps[b], tH[k1]], writes=[tpw[i0 + 1]])
                        tt("dve", Y[:, :, k1], p1[:, 0, :], p2[:, 1, :], ALU.subtract, reads=[tpw[i0], tpw[i0 + 1]], writes=[tY[k1]])
                        tt("pool", Y[:, :, 33 + k1], p2[:, 0, :], p1[:, 1, :], ALU.add, reads=[tpw[i0], tpw[i0 + 1]], writes=[tY[k1]])

                    if o == 1:
                        f1_part(sig, t_sig, 4)
                    f2_part(cons_d)
                    cnt = 0
                    zv = zout.rearrange("c (a b) -> c b a", a=32)
                    xv_ = xm.rearrange("c (a b) -> c b a", a=32)
                    def i1_stage(q):
                        Pq, t_Pq = Pqs[q % 2], t_Pqs[q % 2]
                        for c0 in range(0, 128, 8):
                            b = nb()
                            for jj in range(8):
                                mm(ps[b][0:66, jj * 64:(jj + 1) * 64], Y[:, c0 + jj, :], etab[:, q, :, :].rearrange("p e n -> p (e n)"),
                                   start=True, stop=True, reads=tY + [t_hc], writes=[tps[b]])
                            eng = "act" if (c0 // 8) % 2 == 0 else "dve"
                            cp(eng, Pq[:, :, c0:c0 + 8], ps[b][0:66, :].rearrange("p (c k) -> p k c", k=64), reads=[tps[b]], writes=[t_Pq])

                    def i2_stage(q):
                        Pq, t_Pq = Pqs[q % 2], t_Pqs[q % 2]
                        for hh in range(2):
                            b = nb()
                            for j in range(16):
                                n2l = hh * 16 + j
                                n2 = q * 32 + n2l
                                for e_ in range(2):
                                    mm(ps[b][:, j * 32:(j + 1) * 32], Pq[:, e_ * 32 + n2l, :], ttab[:, e_, n2, :], start=(e_ == 0), stop=(e_ == 1),
                                       reads=[t_Pq, t_hc], writes=[tps[b]])
                            n20 = q * 32 + hh * 16
                            tt("dve", zv[:, n20:n20 + 16, :], ps[b][:].rearrange("p (b a) -> p b a", a=32), xv_[:, n20:n20 + 16, :], ALU.mult,
                               reads=[tps[b], t_xm], writes=[t_zout])

                    i1_stage(0)
                    for q in range(4):
                        if q + 1 < 4:
                            i1_stage(q + 1)
                        i2_stage(q)

                conv(0, Z[0][:], tZ[0], Z[1][:], tZ[1], Z[0][:], tZ[0])
                if "p2c" in dbg and cc == 0:
                    d = dbg_tensor("z1", [128, L], BF)
                    kb.dma("sp", d, Z[0][:], reads=[tZ[0]])
                conv(1, Z[0][:], tZ[0], Z[2][:], tZ[2], Z[1][:], tZ[1])
                kb.dma("sp", zT_d[:, cc, :], Z[1][:], reads=[tZ[1]], writes=[t_zd])
                kb.barrier()
            if stop_after == "p2c0":
                break
        if "p2" in dbg:
            d = dbg_tensor("zT", [128, 4, L], BF)
            kb.dma("sp", d, zT_d, reads=[t_zd])
        kb.barrier()
    if stop_after in ("p2", "p2c0"):
        kb.finish()
        return nc, dbg_out

    t_md = T("mT_d")
    with ExitStack() as ph:
        def sb(name, shape, dt):
            return ph.enter_context(nc.sbuf_tensor(f"s{next_id()}_" + name, list(shape), dt))

        _bk = [0]

        def nb():
            _bk[0] = (_bk[0] + 1) % 8
            return _bk[0]

        zTs = sb("zTs", [128, 4, L], BF)
        aTs = sb("aTs", [128, 4, L], BF)
        wgt = sb("wgt", [128, 8, 2048], BF)
        whu = sb("whu", [128, 4, D], BF)
        wau = sb("wau", [128, 4, D], BF)
        t_w4 = T("w4")
        t_za = T("za")
        kb.dma("sp", zTs[:], zT_d, writes=[t_za])
        kb.dma("sp", aTs[:], aT_d, writes=[t_za])
        for k in range(8):
            kb.dma("pool", wgt[:, k, :], din["w_in"][k * 128:(k + 1) * 128, OFF_G:OFF_G + 2048], writes=[t_w4])
        kb.dma("pool", whu[:], din["w_hy_up"].rearrange("(k p) n -> p k n", p=128), writes=[t_w4])
        kb.dma("pool", wau[:], din["w_att_up"].rearrange("(k p) n -> p k n", p=128), writes=[t_w4])
        hb = [sb(f"mhb{i}", [128, 8, 512], BF) for i in range(2)]
        thb = [T(), T()]
        mst = [sb(f"mst{i}", [128, 8, 512], BF) for i in range(2)]
        tmst = [T(), T()]
        sg = [sb(f"sg{i}", [128, 512], F32) for i in range(4)]
        tsg = [T() for _ in range(4)]
        mm_ = [sb(f"mm{i}", [128, 512], F32) for i in range(4)]
        tmm = [T() for _ in range(4)]
        kb.dma("sp", hb[0][:], hT_d[:, :, 0:512], reads=[t_hd], writes=[thb[0]])
        for blk in range(8):
            sl = slice(blk * 512, (blk + 1) * 512)
            hbb, th = hb[blk % 2], thb[blk % 2]
            if blk + 1 < 8:
                kb.dma("sp", hb[(blk + 1) % 2][:], hT_d[:, :, (blk + 1) * 512:(blk + 2) * 512], reads=[t_hd], writes=[thb[(blk + 1) % 2]])
            for j in range(8):
                bg1, bg2, by1, by2 = nb(), nb(), nb(), nb()
                for k in range(8):
                    mm(ps[bg1][:], wgt[:, k, j * 128:(j + 1) * 128], hbb[:, k, :], start=(k == 0), stop=(k == 7), reads=[t_w4, th], writes=[tps[bg1]])
                for k in range(8):
                    mm(ps[bg2][:], wgt[:, k, 1024 + j * 128:1024 + (j + 1) * 128], hbb[:, k, :], start=(k == 0), stop=(k == 7), reads=[t_w4, th], writes=[tps[bg2]])
                for k in range(4):
                    mm(ps[by1][:], whu[:, k, j * 128:(j + 1) * 128], zTs[:, k, sl], start=(k == 0), stop=(k == 3), reads=[t_w4, t_za], writes=[tps[by1]])
                for k in range(4):
                    mm(ps[by2][:], wau[:, k, j * 128:(j + 1) * 128], aTs[:, k, sl], start=(k == 0), stop=(k == 3), reads=[t_w4, t_za], writes=[tps[by2]])
                i0 = (j % 2) * 2
                act(sg[i0][:], ps[bg1][:], AF.Sigmoid, reads=[tps[bg1]], writes=[tsg[i0]])
                act(sg[i0 + 1][:], ps[bg2][:], AF.Sigmoid, reads=[tps[bg2]], writes=[tsg[i0 + 1]])
                tt("dve", mm_[i0][:], ps[by1][:], sg[i0][:], ALU.mult, reads=[tps[by1], tsg[i0]], writes=[tmm[i0]])
                tt("dve", mm_[i0 + 1][:], ps[by2][:], sg[i0 + 1][:], ALU.mult, reads=[tps[by2], tsg[i0 + 1]], writes=[tmm[i0 + 1]])
                tt("pool", mst[blk % 2][:, j, :], mm_[i0][:], mm_[i0 + 1][:], ALU.add, reads=[tmm[i0], tmm[i0 + 1]], writes=[tmst[blk % 2]])
            kb.dma("sp", mT_d[:, :, sl], mst[blk % 2][:], reads=[tmst[blk % 2]], writes=[t_md])
        if "p4" in dbg:
            d = dbg_tensor("mT", [128, 8, L], BF)
            kb.dma("sp", d, mT_d, reads=[t_md])
        kb.barrier()
    if stop_after == "p4":
        kb.finish()
        return nc, dbg_out

    with ExitStack() as ph:
        def sb(name, shape, dt):
            return ph.enter_context(nc.sbuf_tensor(f"s{next_id()}_" + name, list(shape), dt))

        _bk = [0]

        def nb():
            _bk[0] = (_bk[0] + 1) % 8
            return _bk[0]

        wout = sb("wout", [128, 8, D], BF)
        wfg = sb("wfg", [128, 8, DFF], BF)
        wfu = sb("wfu", [128, 8, DFF], BF)
        wfd = sb("wfd", [128, NFF, D], BF)
        t_wo, t_wg, t_wu, t_wd = T("wo"), T("wg"), T("wu"), T("wd")
        kb.dma("pool", wout[:], din["w_out"].rearrange("(k p) n -> p k n", p=128), writes=[t_wo])
        for k in range(8):
            kb.dma("pool", wfg[:, k, :], din["w_fg"][k * 128:(k + 1) * 128, :], writes=[t_wg])
        for k in range(8):
            kb.dma("pool", wfu[:, k, :], din["w_fu"][k * 128:(k + 1) * 128, :], writes=[t_wu])
        for j in range(0, NFF, 2):
            kb.dma("pool", wfd[:, j:j + 2, :], din["w_fd"][j * 128:(j + 2) * 128, :].rearrange("(k p) n -> p k n", p=128), writes=[t_wd])
        xts = [sb(f"fx{i}", [128, D], F32) for i in range(3)]
        txt = [T(), T(), T()]
        mts = [sb(f"fm{i}", [128, 8, 128], BF) for i in range(2)]
        tmt = [T(), T()]
        tmp = sb("ftmp", [128, D], F32)
        t_tmp = T()
        junk = sb("fjunk", [128, D], BF)
        t_junk = T()
        xs = sb("fxs", [128, D], BF)
        t_xs = T()
        hfT = [sb(f"hfT{i}", [128, 8, 128], BF) for i in range(2)]
        t_hf = [T(), T()]
        aT = sb("aT", [128, NFF, 128], BF)
        t_aT = T()
        sgt = [sb(f"sgt{i}", [128, 512], F32) for i in range(2)]
        tsgt = [T(), T()]
        atm = sb("atm", [128, DFF], BF)
        t_atm = T()
        fin = sb("ffin", [128, 16], F32)
        t_fin = [T(), T(), T()]
        t_out = T("out")

        def norm_resid(banks, xt, tx, grow, c0, tf):
            for n in range(2):
                act(junk[:, n * 512:(n + 1) * 512], ps[banks[n]][:], AF.Square, reads=[tps[banks[n]]], writes=[t_junk, tf],
                    accum_out=fin[:, c0 + n:c0 + n + 1])
            tt("dve", fin[:, c0 + 2:c0 + 3], fin[:, c0:c0 + 1], fin[:, c0 + 1:c0 + 2], ALU.add, reads=[tf], writes=[tf])
            rstd_of(fin[:, c0 + 2:c0 + 3], fin[:, c0 + 3:c0 + 4], 1.0 / D, tf)
            for n in range(2):
                hs = slice(n * 512, (n + 1) * 512)
                stt(tmp[:, hs], ps[banks[n]][:], fin[:, c0 + 3:c0 + 4], grow[:, hs], ALU.mult, ALU.mult,
                    reads=[tps[banks[n]], tf, t_rows], writes=[t_tmp])
            tt("pool", xt[:], xt[:], tmp[:], ALU.add, reads=[t_tmp, tx], writes=[tx])

        def s1a(i):
            tsl = slice(i * 128, (i + 1) * 128)
            xt, tx = xts[i % 3], txt[i % 3]
            mt, tm = mts[i % 2], tmt[i % 2]
            kb.dma("sp", xt[:], din["x"][tsl, :], writes=[tx])
            kb.dma("sp", mt[:], mT_d[:, :, tsl], reads=[t_md], writes=[tm])
            for n in range(2):
                for k in range(8):
                    mm(ps[n][:], mt[:, k, :], wout[:, k, n * 512:(n + 1) * 512], start=(k == 0), stop=(k == 7),
                       reads=[tm, t_wo], writes=[tps[n]])

        def s1b(i):
            xt, tx = xts[i % 3], txt[i % 3]
            norm_resid([0, 1], xt, tx, g1row, 0, t_fin[0])
            act(junk[:], xt[:], AF.Square, reads=[tx], writes=[t_junk, t_fin[1]], accum_out=fin[:, 4:5])
            rstd_of(fin[:, 4:5], fin[:, 5:6], 1.0 / D, t_fin[1])
            act(xs[:], xt[:], AF.Copy, reads=[tx, t_fin[1]], writes=[t_xs], scale=fin[:, 5:6])

        def s1c(i):
            for k in range(8):
                tr(psb[4][:, k * 128:(k + 1) * 128], xs[:, k * 128:(k + 1) * 128], ident_bf[:], reads=[t_xs, t_const], writes=[tps[4]])
            for k in range(8):
                ts("dve", hfT[i % 2][:, k, :], psb[4][:, k * 128:(k + 1) * 128], A2(k), B2(k), ALU.mult, ALU.add,
                   reads=[tps[4], t_cols], writes=[t_hf[i % 2]])

        def s2a(i):
            h_, th_ = hfT[i % 2], t_hf[i % 2]
            pairs = [(5, 6), (7, 4)]
            for g in range(6):
                c0 = g * 512
                w_ = min(512, DFF - c0)
                bg, bu = pairs[g % 2]
                for k in range(8):
                    mm(ps[bg][:, 0:w_], h_[:, k, :], wfg[:, k, c0:c0 + w_], start=(k == 0), stop=(k == 7), reads=[t_wg, th_], writes=[tps[bg]])
                for k in range(8):
                    mm(ps[bu][:, 0:w_], h_[:, k, :], wfu[:, k, c0:c0 + w_], start=(k == 0), stop=(k == 7), reads=[t_wu, th_], writes=[tps[bu]])
                act(sgt[g % 2][:, 0:w_], ps[bg][:, 0:w_], AF.Silu, reads=[tps[bg]], writes=[tsgt[g % 2]])
                tt("dve", atm[:, c0:c0 + w_], ps[bu][:, 0:w_], sgt[g % 2][:, 0:w_], ALU.mult, reads=[tps[bu], tsgt[g % 2]], writes=[t_atm])
            j = 0
            bi = 0
            while j < NFF:
                n = min(8, NFF - j)
                b = (7, 4, 5)[bi % 3]
                bi += 1
                for jj in range(n):
                    tr(psb[b][:, jj * 128:(jj + 1) * 128], atm[:, (j + jj) * 128:(j + jj + 1) * 128], ident_bf[:], reads=[t_atm, t_const], writes=[tps[b]])
                cp("dve" if bi % 2 else "act", aT[:, j:j + n, :], psb[b][:, 0:n * 128].rearrange("p (j t) -> p j t", t=128), reads=[tps[b]], writes=[t_aT])
                j += n

        def s2b(i):
            tsl = slice(i * 128, (i + 1) * 128)
            xt, tx = xts[i % 3], txt[i % 3]
            for n in range(2):
                for j in range(NFF):
                    mm(ps[2 + n][:], aT[:, j, :], wfd[:, j, n * 512:(n + 1) * 512], start=(j == 0), stop=(j == NFF - 1),
                       reads=[t_aT, t_wd], writes=[tps[2 + n]])
            norm_resid([2, 3], xt, tx, g2row, 8, t_fin[2])
            kb.dma("sp", out[tsl, :], xt[:], reads=[tx], writes=[t_out])

        s1a(0)
        s1b(0)
        s1c(0)
        s1a(1)
        s1b(1)
        for i in range(32):
            s2a(i)
            if i + 1 < 32:
                s1c(i + 1)
            if i + 2 < 32:
                s1a(i + 2)
                s1b(i + 2)
            s2b(i)
        kb.barrier()
    kb.finish()
    return nc, dbg_out


_NC = None


def kernel(**inputs):
    global _NC
    if _NC is None:
        _NC = build()[0]
    in_maps = [layout_inputs(inputs, b) for b in range(8)]
    res = run_bass_kernel_spmd(_NC, in_maps, core_ids=list(range(8)))
    return np.stack([np.asarray(r["out"], dtype=np.float32) for r in res.results], 0)
```

```python
import math
from contextlib import ExitStack
import numpy as np
import ml_dtypes
import concourse.bass as bass
import concourse.mybir as mybir
from concourse.bass_utils import run_bass_kernel_spmd

F32 = mybir.dt.float32
BF = mybir.dt.bfloat16
AF = mybir.ActivationFunctionType
ALU = mybir.AluOpType
AX = mybir.AxisListType

L = 4096
D = 1024
CT = 256
LK = L + CT
DH = 512
DFF = 2816
NFF = DFF // 128
OFF_Q, OFF_K, OFF_V, OFF_G = 1536, 2048, 2560, 3072
EPS = 1e-6
LAM_INIT = 0.8 - 0.6 * math.exp(0.0)
PI = math.pi


class Sem:
    def __init__(self, h):
        self.h = h
        self.count = 0


class T:
    __slots__ = ("name", "w", "r")

    def __init__(self, name=""):
        self.name = name
        self.w = None
        self.r = []


class Eng:
    def __init__(self, name, sem):
        self.name = name
        self.sem = sem
        self.ops = []
        self.seen = {}


class KB:
    def __init__(self, nc, nsem_dma=14):
        self.nc = nc
        self.engs = {}
        for n in ("pe", "act", "dve", "pool", "sp"):
            self.engs[n] = Eng(n, Sem(nc.alloc_semaphore("s_" + n)))
        self.dsems = {q: [Sem(nc.alloc_semaphore(f"d_{q}{i}")) for i in range(nsem_dma)] for q in ("sp", "pool")}
        self.drr = {"sp": 0, "pool": 0}

    def _waits(self, eng, reads, writes, extra=()):
        deps = {}

        def add(d):
            if d is None:
                return
            s, v = d
            if deps.get(s, 0) < v:
                deps[s] = v

        for t in reads:
            add(t.w)
        for t in writes:
            add(t.w)
            for d in t.r:
                add(d)
        for d in extra:
            add(d)
        out = []
        for s, v in deps.items():
            if s is eng.sem and eng.name == "pe":
                continue
            if eng.seen.get(s, 0) >= v:
                continue
            eng.seen[s] = v
            out.append((s, v))
        return out

    def _mark(self, tok, reads, writes):
        for t in reads:
            t.r = [d for d in t.r if d[0] is not tok[0]]
            t.r.append(tok)
        for t in writes:
            t.w = tok
            t.r = []

    def op(self, engname, fn, reads=(), writes=()):
        eng = self.engs[engname]
        waits = self._waits(eng, reads, writes)
        eng.sem.count += 1
        tok = (eng.sem, eng.sem.count)
        eng.ops.append((waits, fn, (eng.sem, 1)))
        self._mark(tok, reads, writes)
        return tok

    def dma(self, q, out_ap, in_ap, reads=(), writes=(), **kw):
        eng = self.engs[q]
        sems = self.dsems[q]
        s = sems[self.drr[q] % len(sems)]
        self.drr[q] += 1
        waits = self._waits(eng, reads, writes, extra=[(s, s.count)] if s.count else [])
        s.count += 16
        tok = (s, s.count)
        eng.ops.append((waits, lambda e: e.dma_start(out=out_ap, in_=in_ap, **kw), (s, 16)))
        self._mark(tok, reads, writes)
        return tok

    def collective(self, kind, ins, outs, reads=(), writes=()):
        eng = self.engs["pool"]
        if not hasattr(self, "ccsem"):
            self.ccsem = Sem(self.nc.alloc_semaphore("s_cc"))
        s = self.ccsem
        waits = self._waits(eng, reads, writes)
        s.count += 1
        tok = (s, s.count)
        eng.ops.append((waits, lambda e: e.collective_compute(kind, ALU.bypass, replica_groups=[list(range(8))], ins=ins, outs=outs), (s, 1)))
        self._mark(tok, reads, writes)
        return tok

    def barrier(self, include_cc=False):
        allsems = [e.sem for e in self.engs.values()] + [s for q in self.dsems.values() for s in q]
        if include_cc and hasattr(self, "ccsem"):
            allsems.append(self.ccsem)
        for eng in self.engs.values():
            waits = []
            for s in allsems:
                if s is eng.sem or s.count == 0:
                    continue
                if eng.seen.get(s, 0) >= s.count:
                    continue
                eng.seen[s] = s.count
                waits.append((s, s.count))
            if waits:
                eng.ops.append((waits, None, None))

    def finish(self):
        nc = self.nc
        self.barrier(include_cc=True)
        with nc.Block() as block:
            def emit(e, en):
                for waits, fn, inc in en.ops:
                    for (ws, wv) in waits:
                        e.wait_ge(ws.h, wv)
                    if fn is not None:
                        ins = fn(e)
                        ins.then_inc(inc[0].h, inc[1])

            @block.tensor
            def _(e):
                emit(e, self.engs["pe"])

            @block.scalar
            def _(e):
                emit(e, self.engs["act"])

            @block.vector
            def _(e):
                emit(e, self.engs["dve"])

            @block.gpsimd
            def _(e):
                emit(e, self.engs["pool"])

            @block.sync
            def _(e):
                emit(e, self.engs["sp"])


def _bf(a):
    return np.ascontiguousarray(a.astype(np.float32)).astype(ml_dtypes.bfloat16)


_CONST = None


def host_consts():
    global _CONST
    if _CONST is not None:
        return _CONST
    c = {}
    c["ident_bf"] = _bf(np.eye(128))
    c["ident_f"] = np.eye(128, dtype=np.float32)
    t = np.arange(L)
    row = (t // 64).astype(np.float32)
    col = (t % 64).astype(np.float32)
    inv = (10000.0 ** (-np.arange(16, dtype=np.float32) / 16)).astype(np.float32)
    cos64 = np.zeros((64, L), np.float32)
    sin64 = np.zeros((64, L), np.float32)
    for half, pos in ((0, row), (1, col)):
        ang = pos[None, :] * inv[:, None]
        base = half * 32
        cos64[base:base + 16] = np.cos(ang)
        cos64[base + 16:base + 32] = np.cos(ang)
        sin64[base:base + 16] = -np.sin(ang)
        sin64[base + 16:base + 32] = np.sin(ang)
    c["rope_cos"] = np.concatenate([cos64, cos64], 0)
    c["rope_sin"] = np.concatenate([sin64, sin64], 0)
    f32 = np.float32
    bands = 16
    tt = np.linspace(0.0, 1.0, L, dtype=f32)[:, None]
    w = (f32(2.0 * math.pi / L) * np.arange(L, dtype=f32))[:, None]
    fr = np.linspace(1e-4, bands - 1, bands, dtype=f32)[None, :]
    z = np.concatenate([tt, np.cos(fr * w), -np.sin(fr * w)], axis=-1).astype(f32)
    c["zT"] = np.ascontiguousarray(z.T)
    deltas = np.abs(np.linspace(math.log(1e-2) / 1.5, math.log(1e-2) / 0.3, DH, dtype=f32))
    nd = (-deltas).reshape(4, 128).T
    c["negdelta"] = np.ascontiguousarray(nd.astype(f32))
    offs = (np.arange(8) * 512 / (L - 1)).astype(f32)
    c["ndoff"] = np.ascontiguousarray((nd[:, :, None] * offs[None, None, :]).astype(f32))
    c["tv0"] = np.ascontiguousarray(np.broadcast_to((np.arange(512) / (L - 1)).astype(f32)[None, :], (128, 512)))
    n1 = np.arange(32)[:, None]
    k1 = np.arange(33)[None, :]
    a = 2 * np.pi * n1 * k1 / 64.0
    c["d1f"] = _bf(np.concatenate([np.cos(a), -np.sin(a)], 1))
    c["d1fp"] = np.ascontiguousarray(np.concatenate([c["d1f"], np.zeros((96, 66), ml_dtypes.bfloat16)], 0))
    c["d1b"] = _bf(np.concatenate([np.cos(a), np.sin(a)], 1))
    n2 = np.arange(128)[:, None, None]
    k1g = np.arange(33)[None, :, None]
    k2 = np.arange(128)[None, None, :]
    ang = 2 * np.pi * n2 * (k1g + 64 * k2) / 8192.0
    gr, gi = np.cos(ang), -np.sin(ang)
    c["gtab"] = _bf(np.stack([gr, gi, -gi], 2))
    k2e = np.arange(128)[:, None]
    n2e = np.arange(128)[None, :]
    ae = 2 * np.pi * k2e * n2e / 128.0
    c["etab"] = _bf(np.stack([np.cos(ae).reshape(128, 4, 32), np.sin(ae).reshape(128, 4, 32)], 2))
    k1t = np.arange(33)[:, None, None]
    n2t = np.arange(128)[None, :, None]
    n1t = np.arange(32)[None, None, :]
    at = 2 * np.pi * k1t * (128 * n1t + n2t) / 8192.0
    wgt = np.full((33, 1, 1), 2.0)
    wgt[0] = 1.0
    wgt[32] = 1.0
    tr = wgt * np.cos(at) / 8192.0
    ti = wgt * np.sin(at) / 8192.0
    t0 = np.concatenate([tr, -ti], 0)
    t1 = np.concatenate([-ti, -tr], 0)
    c["ttab"] = _bf(np.stack([t0, t1], 1))
    _CONST = c
    return c


CONST_SHAPES = {
    "ident_bf": ([128, 128], BF), "ident_f": ([128, 128], F32),
    "rope_cos": ([128, L], F32), "rope_sin": ([128, L], F32),
    "zT": ([33, L], F32), "negdelta": ([128, 4], F32), "ndoff": ([128, 4, 8], F32), "tv0": ([128, 512], F32),
    "d1f": ([32, 66], BF), "d1fp": ([128, 66], BF), "d1b": ([32, 66], BF), "gtab": ([128, 33, 3, 128], BF),
    "etab": ([128, 4, 2, 32], BF), "ttab": ([66, 2, 128, 32], BF),
}

IN_SHAPES = {
    "x": [L, D], "ctx": [CT, D], "cc": [128, 8, 2], "w_ada": [D, 6 * D], "b_adaT": [128, 48], "gcols": [128, 4, 8],
    "w_in": [D, 5120], "w_qk_sw": [D, 1024], "cw": [128, 12, 3], "cb": [128, 12],
    "fw1": [33, 64], "fb1": [64, 1], "fw2": [64, 64], "fb2": [64, 1], "fw3": [64, 2048], "fb3": [1, 2048],
    "ffreq": [64, 1], "hyb": [128, 2, 4], "lamv": [1, 256], "subg": [1, 128],
    "w_hy_up": [DH, D], "w_att_up": [DH, D], "w_out": [D, D], "w_fg": [D, DFF], "w_fu": [D, DFF], "w_fd": [DFF, D],
}


def layout_inputs(inp, b):
    f = lambda a: np.ascontiguousarray(np.asarray(a, dtype=np.float32))
    m = {}
    m["x"] = f(inp["x"][b])
    m["ctx"] = f(inp["ctx"][b])
    cc = np.stack([np.asarray(inp["c"][b]), np.asarray(inp["c_ctx"])], -1)
    m["cc"] = f(cc.reshape(8, 128, 2).transpose(1, 0, 2))
    m["w_ada"] = f(inp["w_ada"][0])
    m["b_adaT"] = f(np.asarray(inp["b_ada"][0]).reshape(48, 128).T)
    g = np.stack([np.asarray(inp[k][0]) for k in ("g_mix_pre", "g_mix_post", "g_ffn_pre", "g_ffn_post")], 0)
    m["gcols"] = f(g.reshape(4, 8, 128).transpose(2, 0, 1))
    w_in = np.asarray(inp["w_in"][0])
    m["w_in"] = f(w_in)
    perm = np.arange(1024).reshape(16, 2, 2, 16)[:, :, ::-1, :].reshape(-1)
    m["w_qk_sw"] = f(w_in[:, OFF_Q:OFF_V][:, perm])
    m["cw"] = f(np.asarray(inp["hy_conv_w"][0]).reshape(3, 12, 128).transpose(2, 1, 0))
    m["cb"] = f(np.asarray(inp["hy_conv_b"][0]).reshape(12, 128).T)
    m["fw1"] = f(inp["hy_f_w1"][0])
    m["fb1"] = f(np.asarray(inp["hy_f_b1"][0]).reshape(64, 1))
    m["fw2"] = f(inp["hy_f_w2"][0])
    m["fb2"] = f(np.asarray(inp["hy_f_b2"][0]).reshape(64, 1))
    m["fw3"] = f(inp["hy_f_w3"][0])
    m["fb3"] = f(np.asarray(inp["hy_f_b3"][0]).reshape(1, 2048))
    m["ffreq"] = f(np.asarray(inp["hy_f_freq"][0]).reshape(64, 1))
    m["hyb"] = f(np.asarray(inp["hy_bias"][0]).reshape(2, 4, 128).transpose(2, 0, 1))
    m["lamv"] = f(np.concatenate([np.asarray(inp[k][0]) for k in ("lambda_q1", "lambda_q2", "lambda_k1", "lambda_k2")]).reshape(1, 256))
    m["subg"] = f(np.asarray(inp["att_subln_g"][0]).reshape(1, 128))
    m["w_hy_up"] = f(inp["w_hy_up"][0])
    m["w_att_up"] = f(inp["w_att_up"][0])
    m["w_out"] = f(inp["w_out"][0])
    m["w_fg"] = f(inp["w_ffn_gate"][0])
    m["w_fu"] = f(inp["w_ffn_up"][0])
    m["w_fd"] = f(inp["w_ffn_down"][0])
    m.update(host_consts())
    return m


def build(dbg=(), stop_after=None, skip=()):
    nc = bass.Bass("TRN2", target_bir_lowering=False)
    kb = KB(nc)
    din = {}
    for k, shp in IN_SHAPES.items():
        din[k] = nc.dram_tensor(k, list(shp), F32, kind="ExternalInput").ap()
    for k, (shp, dt) in CONST_SHAPES.items():
        din[k] = nc.dram_tensor(k, list(shp), dt, kind="ExternalInput").ap()
    out = nc.dram_tensor("out", [L, D], F32, kind="ExternalOutput").ap()
    hT_d = nc.dram_tensor("hT_d", [128, 8, L], BF, kind="Internal").ap()
    mT_d = nc.dram_tensor("mT_d", [128, 8, L], BF, kind="Internal").ap()
    zT_d = nc.dram_tensor("zT_d", [128, 4, L], BF, kind="Internal").ap()
    aT_d = nc.dram_tensor("aT_d", [128, 4, L], BF, kind="Internal").ap()
    sig_d = nc.dram_tensor("sig_d", [5, 128, L], BF, kind="Internal").ap()
    dbg_out = {}

    def dbg_tensor(name, shape, dt=F32):
        dbg_out[name] = nc.dram_tensor("dbg_" + name, list(shape), dt, kind="ExternalOutput").ap()
        return dbg_out[name]

    psall = nc.alloc_psum_tensor("psall", [128, 4096], F32)
    psall_b = psall.bitcast(BF)
    ps = [psall[:, i * 512:(i + 1) * 512] for i in range(8)]
    psb = [psall_b[:, i * 1024:(i + 1) * 1024] for i in range(8)]
    tps = [T(f"ps{i}") for i in range(8)]

    def mm(out_ap, lhsT, rhs, start, stop, reads, writes, tile_position=None):
        if tile_position is None:
            kb.op("pe", lambda e: e.matmul(out_ap, lhsT, rhs, start=start, stop=stop), reads=reads, writes=writes)
        else:
            kb.op("pe", lambda e: e.matmul(out_ap, lhsT, rhs, start=start, stop=stop, tile_position=tile_position), reads=reads, writes=writes)

    def tr(out_ap, in_ap, ident, reads, writes):
        kb.op("pe", lambda e: e.transpose(out_ap, in_ap, ident), reads=reads, writes=writes)

    def act(out_ap, in_ap, func, reads, writes, **kw):
        kb.op("act", lambda e: e.activation(out=out_ap, in_=in_ap, func=func, **kw), reads=reads, writes=writes)

    def ts(eng, out_ap, in0, s1, s2, op0, op1, reads, writes, **kw):
        if s2 is None:
            kb.op(eng, lambda e: e.tensor_scalar(out_ap, in0, s1, None, op0, **kw), reads=reads, writes=writes)
        else:
            kb.op(eng, lambda e: e.tensor_scalar(out_ap, in0, s1, s2, op0, op1, **kw), reads=reads, writes=writes)

    def tt(eng, out_ap, in0, in1, op, reads, writes):
        kb.op(eng, lambda e: e.tensor_tensor(out_ap, in0, in1, op), reads=reads, writes=writes)

    def stt(out_ap, in0, scalar, in1, op0, op1, reads, writes):
        kb.op("dve", lambda e: e.scalar_tensor_tensor(out_ap, in0, scalar, in1, op0, op1), reads=reads, writes=writes)

    def cp(eng, out_ap, in_ap, reads, writes):
        if eng == "act":
            kb.op("act", lambda e: e.copy(out_ap, in_ap), reads=reads, writes=writes)
        else:
            kb.op(eng, lambda e: e.tensor_copy(out_ap, in_ap), reads=reads, writes=writes)

    def recip(out_ap, in_ap, reads, writes):
        kb.op("dve", lambda e: e.reciprocal(out_ap, in_ap), reads=reads, writes=writes)

    def rsum(out_ap, in_ap, reads, writes):
        kb.op("dve", lambda e: e.reduce_sum(out_ap, in_ap, AX.X), reads=reads, writes=writes)

    def memset(eng, ap, val, writes):
        kb.op(eng, lambda e: e.memset(ap, val), writes=writes)

    _ids = [0]

    def next_id():
        _ids[0] += 1
        return _ids[0]

    P = ExitStack()

    def sbp(name, shape, dt):
        return P.enter_context(nc.sbuf_tensor("sp_" + name, list(shape), dt))

    ident_bf = sbp("ident_bf", [128, 128], BF)
    ident_f = sbp("ident_f", [128, 128], F32)
    ones_f = sbp("ones_f", [128, 128], F32)
    mhalf = sbp("mhalf", [128, 32], F32)
    cols = sbp("cols", [128, 8, 8], F32)
    g1row = sbp("g1row", [128, D], F32)
    g2row = sbp("g2row", [128, D], F32)
    neglam = sbp("neglam", [128, 1], F32)
    sgrow = sbp("sgrow", [128, 128], F32)
    hcT = sbp("hcT", [128, 8, CT], BF)
    t_const = T("const")
    t_cols = T("cols")
    t_rows = T("rows")
    t_hcT = T("hcT")
    kb.dma("sp", ident_bf[:], din["ident_bf"], writes=[t_const])
    kb.dma("sp", ident_f[:], din["ident_f"], writes=[t_const])
    memset("dve", ones_f[:], 1.0, [t_const])
    memset("dve", mhalf[:], -0.5, [t_const])
    A1 = lambda k: cols[:, 0, k:k + 1]
    B1 = lambda k: cols[:, 1, k:k + 1]
    A2 = lambda k: cols[:, 2, k:k + 1]
    B2 = lambda k: cols[:, 3, k:k + 1]
    A1c = lambda k: cols[:, 4, k:k + 1]
    B1c = lambda k: cols[:, 5, k:k + 1]

    def rstd_of(ssq_ap, out_ap, inv_n, tl):
        n = ssq_ap.shape[-1] if len(ssq_ap.shape) > 1 else 1
        ts("dve", out_ap, ssq_ap, inv_n, EPS, ALU.mult, ALU.add, reads=[tl], writes=[tl])
        tt("pool", out_ap, out_ap, mhalf[:, 0:n], ALU.pow, reads=[tl, t_const], writes=[tl])

    with ExitStack() as ph:
        def sb(name, shape, dt):
            return ph.enter_context(nc.sbuf_tensor(f"s{next_id()}_" + name, list(shape), dt))

        ccs = sb("ccs", [128, 16], F32)
        scs = sb("scs", [128, 16], F32)
        bada = sb("bada", [128, 48], F32)
        gc = sb("gc", [128, 4, 8], F32)
        adaT = sb("adaT", [128, 48, 2], F32)
        wa = [sb(f"wa{i}", [128, 6 * D], F32) for i in range(2)]
        twa = [T("wa0"), T("wa1")]
        t_s = T("p0small")
        kb.dma("sp", ccs[:], din["cc"].rearrange("p k c -> p (k c)"), writes=[t_s])
        kb.dma("sp", bada[:], din["b_adaT"], writes=[t_s])
        kb.dma("sp", gc[:], din["gcols"], writes=[t_s])
        act(scs[:], ccs[:], AF.Silu, reads=[t_s], writes=[t_s])
        def ada_chunk(k):
            kb.dma("sp", wa[k % 2][:], din["w_ada"][k * 128:(k + 1) * 128, :], writes=[twa[k % 2]])
            for f in range(48):
                mm(ps[0][:, 2 * f:2 * f + 2], wa[k % 2][:, f * 128:(f + 1) * 128], scs[:, 2 * k:2 * k + 2],
                   start=(k == 0 and f == 0), stop=(k == 7 and f == 47), reads=[twa[k % 2], t_s], writes=[tps[0]])
        lamb = sb("lamb", [128, 256], F32)
        lp = sb("lp", [128, 2, 64], F32)
        le = sb("le", [128, 2], F32)
        kb.dma("sp", lamb[:], bass.AP(din["lamv"].tensor, 0, [[0, 128], [1, 256]]), writes=[t_s])
        kb.dma("sp", sgrow[:], bass.AP(din["subg"].tensor, 0, [[0, 128], [1, 128]]), writes=[t_rows])
        tt("dve", lp[:].rearrange("p a b -> p (a b)"), lamb[:, 0:128], lamb[:, 128:256], ALU.mult, reads=[t_s], writes=[t_s])
        rsum(le[:], lp[:], reads=[t_s], writes=[t_s])
        act(le[:], le[:], AF.Exp, reads=[t_s], writes=[t_s])
        tt("dve", neglam[:], le[:, 1:2], le[:, 0:1], ALU.subtract, reads=[t_s], writes=[t_cols])
        ts("dve", neglam[:], neglam[:], -LAM_INIT, None, ALU.add, None, reads=[t_cols], writes=[t_cols])
        ts("dve", sgrow[:], sgrow[:], 1.0 - LAM_INIT, None, ALU.mult, None, reads=[t_rows], writes=[t_rows])
        xts = [sb(f"xt{i}", [128, D], F32) for i in range(4)]
        txt = [T() for _ in range(4)]
        junk = sb("junk", [128, D], BF)
        t_junk = T()
        xsa = sb("xsa", [128, 34, D], BF)
        txs = [T() for _ in range(34)]
        ssq = sb("ssq", [128, 34], F32)
        t_ssq = [T() for _ in range(34)]
        hst = [sb(f"hst{i}", [128, 8, 512], BF) for i in range(2)]
        thst = [T(), T()]
        t_hd = T("hT_d")
        for i in range(34):
            lat = i < 32
            src = din["x"][i * 128:(i + 1) * 128, :] if lat else din["ctx"][(i - 32) * 128:(i - 31) * 128, :]
            xt, tx = xts[i % 4], txt[i % 4]
            if i % 4 == 0 and i // 4 < 8:
                ada_chunk(i // 4)
            kb.dma("sp", xt[:], src, writes=[tx])
            act(junk[:], xt[:], AF.Square, reads=[tx], writes=[t_junk, t_ssq[i]], accum_out=ssq[:, i:i + 1])
            rstd_of(ssq[:, i:i + 1], ssq[:, i:i + 1], 1.0 / D, t_ssq[i])
            if i >= 2:
                j = i - 2
                act(xsa[:, j, :], xts[j % 4][:], AF.Copy, reads=[txt[j % 4], t_ssq[j]], writes=[txs[j]], scale=ssq[:, j:j + 1])
        for j in (32, 33):
            act(xsa[:, j, :], xts[j % 4][:], AF.Copy, reads=[txt[j % 4], t_ssq[j]], writes=[txs[j]], scale=ssq[:, j:j + 1])
        for c in range(2):
            tt("dve", adaT[:, :, c], ps[0][:, c:96:2], bada[:], ALU.add, reads=[tps[0], t_s], writes=[t_s])
        for (dst, sc_f, g_i, c) in ((0, 8, 0, 0), (2, 32, 2, 0), (4, 8, 0, 1)):
            stt(cols[:, dst, :], adaT[:, sc_f:sc_f + 8, c], 1.0, gc[:, g_i, :], ALU.add, ALU.mult, reads=[t_s], writes=[t_cols])
        for (dst, sh_f, c) in ((1, 0, 0), (3, 24, 0), (5, 0, 1)):
            cp("dve", cols[:, dst, :], adaT[:, sh_f:sh_f + 8, c], reads=[t_s], writes=[t_cols])
        tt("dve", cols[:, 6, :], adaT[:, 16:24, 0], gc[:, 1, :], ALU.mult, reads=[t_s], writes=[t_cols])
        tt("dve", cols[:, 7, :], adaT[:, 40:48, 0], gc[:, 3, :], ALU.mult, reads=[t_s], writes=[t_cols])
        diag = sb("diag", [128, 4, 128], F32)
        t_diag = T("diag")
        for gi, rowt in ((6, g1row), (7, g2row)):
            for half in range(2):
                for j in range(4):
                    ts("dve", diag[:, j, :], ident_f[:], cols[:, gi, half * 4 + j:half * 4 + j + 1], None, ALU.mult, None,
                       reads=[t_const, t_cols], writes=[t_diag])
                for j in range(4):
                    mm(ps[1][:, j * 128:(j + 1) * 128], ones_f[:], diag[:, j, :], start=True, stop=True,
                       reads=[t_diag, t_const], writes=[tps[1]])
                cp("dve", rowt[:, half * 512:(half + 1) * 512], ps[1][:], reads=[tps[1]], writes=[t_rows])
        if "p0" in dbg:
            d = dbg_tensor("cols", [128, 64])
            kb.dma("sp", d, cols[:].rearrange("p a b -> p (a b)"), reads=[t_cols])
            d = dbg_tensor("g1row", [128, D])
            kb.dma("sp", d, g1row[:], reads=[t_rows])
            d = dbg_tensor("neglam", [128, 1])
            kb.dma("sp", d, neglam[:], reads=[t_cols])

        for i in range(34):
            lat = i < 32
            bk = 2 + i % 4
            for k in range(8):
                tr(psb[bk][:, k * 128:(k + 1) * 128], xsa[:, i, k * 128:(k + 1) * 128], ident_bf[:],
                   reads=[txs[i], t_const], writes=[tps[bk]])
            for k in range(8):
                if lat:
                    h = hst[(i // 4) % 2]
                    ts("dve", h[:, k, (i % 4) * 128:(i % 4 + 1) * 128], psb[bk][:, k * 128:(k + 1) * 128], A1(k), B1(k),
                       ALU.mult, ALU.add, reads=[tps[bk], t_cols], writes=[thst[(i // 4) % 2]])
                else:
                    ts("dve", hcT[:, k, (i - 32) * 128:(i - 31) * 128], psb[bk][:, k * 128:(k + 1) * 128], A1c(k), B1c(k),
                       ALU.mult, ALU.add, reads=[tps[bk], t_cols], writes=[t_hcT])
            if lat and i % 4 == 3:
                blk = i // 4
                kb.dma("sp", hT_d[:, :, blk * 512:(blk + 1) * 512], hst[blk % 2][:], reads=[thst[blk % 2]], writes=[t_hd])
        if "p1" in dbg:
            d = dbg_tensor("hT", [128, 8, L], BF)
            kb.dma("sp", d, hT_d, reads=[t_hd])
            d = dbg_tensor("hcT", [128, 8, CT], BF)
            kb.dma("sp", d, hcT[:], reads=[t_hcT])
        kb.barrier()
    if stop_after == "p1":
        kb.finish()
        return nc, dbg_out


    hall_d = nc.dram_tensor("hall_d", [1024, 8448], BF, kind="Internal").ap()
    t_hall = [T(f"hall{i}") for i in range(8)]
    with ExitStack() as ph:
        def sb(name, shape, dt):
            return ph.enter_context(nc.sbuf_tensor(f"s{next_id()}_" + name, list(shape), dt))

        _bk = [0]

        def nb():
            _bk[0] = (_bk[0] + 1) % 8
            return _bk[0]

        t_hc = T("hfconst")
        gtab = sb("gtab", [128, 33, 3, 128], BF)
        d1f = sb("d1fp", [128, 66], BF)
        tv0 = sb("tv0", [128, 512], F32)
        negd = sb("negd", [128, 4], F32)
        ndoff = sb("ndoff", [128, 4, 8], F32)
        hybs = sb("hybs", [128, 2, 4], F32)
        for dst, nm in ((gtab, "gtab"), (d1f, "d1fp"), (tv0, "tv0"), (negd, "negdelta"), (ndoff, "ndoff"), (hybs, "hyb")):
            kb.dma("sp", dst[:], din[nm], writes=[t_hc])
        hdn2 = sb("hdn2", [65, L], BF)
        fw3a = sb("fw3a", [65, 2048], BF)
        t_h2 = T("hdn2")
        t_fw3 = T("fw3")
        kb.dma("pool", fw3a[0:64, :], din["fw3"], writes=[t_fw3])
        kb.dma("pool", fw3a[64:65, :], din["fb3"], writes=[t_fw3])
        memset("pool", hdn2[64:65, :], 1.0, [t_h2])
        zTs = sb("zTs", [33, L], F32)
        fw1s = sb("fw1s", [33, 64], F32)
        fw2s = sb("fw2s", [64, 64], F32)
        fcol = sb("fcol", [64, 5], F32)
        t_f = T("fmlp")
        kb.dma("sp", zTs[:], din["zT"], writes=[t_f])
        kb.dma("sp", fw1s[:], din["fw1"], writes=[t_f])
        kb.dma("sp", fw2s[:], din["fw2"], writes=[t_f])
        kb.dma("sp", fcol[:, 0:1], din["ffreq"], writes=[t_f])
        kb.dma("sp", fcol[:, 1:2], din["fb1"], writes=[t_f])
        kb.dma("sp", fcol[:, 2:3], din["fb2"], writes=[t_f])
        tt("dve", fcol[:, 3:4], fcol[:, 0:1], fcol[:, 1:2], ALU.mult, reads=[t_f], writes=[t_f])
        tt("dve", fcol[:, 4:5], fcol[:, 0:1], fcol[:, 2:3], ALU.mult, reads=[t_f], writes=[t_f])
        with ExitStack() as phm:
            def sbm(name, shape, dt):
                return phm.enter_context(nc.sbuf_tensor(f"s{next_id()}_" + name, list(shape), dt))

            halfpi = sbm("halfpi", [64, 1], F32)
            memset("dve", halfpi[:], PI / 2, [t_f])
            arg = [sbm(f"arg{i}", [64, 512], F32) for i in range(8)]
            s4 = [sbm(f"s4{i}", [64, 512], F32) for i in range(8)]
            c4 = [sbm(f"c4{i}", [64, 512], F32) for i in range(8)]
            hd1 = [sbm(f"hd1{i}", [64, 512], F32) for i in range(8)]
            t_m = [T() for _ in range(8)]

            def sin_layer_bf(ps_of, bias_col, out_of, t_out_of):
                for i_ in range(8):
                    ts("dve", arg[i_][:], ps[ps_of(i_)][0:64, :], fcol[:, 0:1], fcol[:, bias_col:bias_col + 1], ALU.mult, ALU.add,
                       reads=[tps[ps_of(i_)], t_f], writes=[t_m[i_]])
                for i_ in range(8):
                    act(s4[i_][:], arg[i_][:], AF.Sin, reads=[t_m[i_]], writes=[t_m[i_]], scale=0.25)
                    act(c4[i_][:], arg[i_][:], AF.Sin, reads=[t_m[i_], t_f], writes=[t_m[i_]], scale=0.25, bias=halfpi[:])
                for i_ in range(8):
                    tt("pool", arg[i_][:], s4[i_][:], s4[i_][:], ALU.mult, reads=[t_m[i_]], writes=[t_m[i_]])
                    tt("pool", s4[i_][:], s4[i_][:], c4[i_][:], ALU.mult, reads=[t_m[i_]], writes=[t_m[i_]])
                for i_ in range(8):
                    ts("dve", arg[i_][:], arg[i_][:], -2.0, 1.0, ALU.mult, ALU.add, reads=[t_m[i_]], writes=[t_m[i_]])
                    stt(out_of(i_), s4[i_][:], 4.0, arg[i_][:], ALU.mult, ALU.mult, reads=[t_m[i_]], writes=[t_m[i_], t_out_of(i_)])

            for blk in range(8):
                mm(ps[blk][0:64, :], fw1s[:], zTs[:, blk * 512:(blk + 1) * 512], start=True, stop=True, reads=[t_f], writes=[tps[blk]])
            sin_layer_bf(lambda i_: i_, 3, lambda i_: hd1[i_][:], lambda i_: t_m[i_])
            for blk in range(8):
                mm(ps[blk][0:64, :], fw2s[:], hd1[blk][:], start=True, stop=True, reads=[t_f, t_m[blk]], writes=[tps[blk]])
            sin_layer_bf(lambda i_: i_, 4, lambda i_: hdn2[0:64, i_ * 512:(i_ + 1) * 512], lambda i_: t_h2)
            kb.barrier()
        dec = [sb(f"dec{i}", [128, L], BF) for i in range(2)]
        t_dec = [T(), T()]
        Kf = [[sb(f"Kf{i}_{d_}", [128, L], BF) for d_ in range(2)] for i in range(2)]
        t_Kf = [[T(), T()], [T(), T()]]
        Ut = [sb(f"Ut{i}", [128, 32, 128], BF) for i in range(2)]
        tUt = [T(), T()]
        for i in range(2):
            memset("pool", Ut[i][32:64, :, :], 0.0, [tUt[i]])
            memset("pool", Ut[i][64:128, :, :], 0.0, [tUt[i]])
        A = sb("A", [128, 66, 128], BF)
        tA = [T() for _ in range(33)]
        H = [sb(f"H{i}", [128, 33, 2, 128], BF) for i in range(2)]
        tH = [[T() for _ in range(33)] for _ in range(2)]
        t_sdf = [T() for _ in range(4)]
        cnt = [0]

        def stage_k(r):
            cc, o = r // 2, r % 2
            pi_ = r % 2
            if o == 0:
                for b8 in range(8):
                    act(dec[cc % 2][:, b8 * 512:(b8 + 1) * 512], tv0[:], AF.Exp, reads=[t_hc], writes=[t_dec[cc % 2]],
                        scale=negd[:, cc:cc + 1], bias=ndoff[:, cc, b8:b8 + 1])
            for d_ in range(2):
                col0 = (o * 2 + d_) * 512 + cc * 128
                kf, tk = Kf[pi_][d_], t_Kf[pi_][d_]
                for blk in range(8):
                    sl = slice(blk * 512, (blk + 1) * 512)
                    b = nb()
                    mm(ps[b][:], fw3a[:, col0:col0 + 128], hdn2[:, sl], start=True, stop=True, reads=[t_fw3, t_h2], writes=[tps[b]])
                    tt("dve", kf[:, sl], ps[b][:], dec[cc % 2][:, sl], ALU.mult, reads=[tps[b], t_dec[cc % 2]], writes=[tk])
                if d_ == 0:
                    tt("dve", kf[:, 0:1], kf[:, 0:1], hybs[:, o, cc:cc + 1], ALU.add, reads=[tk, t_hc], writes=[tk])
                else:
                    memset("dve", kf[:, 0:1], 0.0, [tk])
                kb.dma("pool", sig_d[pi_ * 2 + d_], kf[:], reads=[tk], writes=[t_sdf[pi_ * 2 + d_]])

        def stage_f(r):
            pi_ = r % 2
            Hh, tHh = H[pi_], tH[pi_]
            for d_ in range(2):
                slot = pi_ * 2 + d_
                for g in range(4):
                    u, tu = Ut[g % 2], tUt[g % 2]
                    kb.dma("sp", u[0:32, :, :], sig_d[slot][g * 32:(g + 1) * 32, :].rearrange("c (a b) -> a c b", a=32), reads=[t_sdf[slot]], writes=[tu])
                    j = 0
                    while j < 32:
                        n = min(7, 32 - j)
                        b = nb()
                        for jj in range(n):
                            mm(ps[b][:, jj * 66:(jj + 1) * 66], u[:, j + jj, :], d1f[:], start=True, stop=True, reads=[tu, t_hc], writes=[tps[b]])
                        c0 = g * 32 + j
                        cp("act" if cnt[0] % 2 == 0 else "dve", A[:, :, c0:c0 + n], ps[b][:, 0:n * 66].rearrange("p (c k) -> p k c", k=66),
                           reads=[tps[b]], writes=tA)
                        cnt[0] += 1
                        j += n
                for k1 in range(33):
                    b = nb()
                    ar, ai = A[:, k1, :], A[:, 33 + k1, :]
                    mm(ps[b][:, 0:128], gtab[:, k1, 0, :], ar, start=True, stop=False, reads=[t_hc, tA[k1]], writes=[tps[b]])
                    mm(ps[b][:, 0:128], gtab[:, k1, 2, :], ai, start=False, stop=True, reads=[t_hc, tA[k1]], writes=[tps[b]])
                    mm(ps[b][:, 128:256], gtab[:, k1, 1, :], ar, start=True, stop=False, reads=[t_hc, tA[k1]], writes=[tps[b]])
                    mm(ps[b][:, 128:256], gtab[:, k1, 0, :], ai, start=False, stop=True, reads=[t_hc, tA[k1]], writes=[tps[b]])
                    if d_ == 0:
                        cp("act", Hh[:, k1, :, :], ps[b][:, 0:256].rearrange("p (r c) -> p r c", r=2), reads=[tps[b]], writes=[tHh[k1]])
                    else:
                        tt("dve", Hh[:, k1, 0, :], Hh[:, k1, 0, :], ps[b][:, 0:128], ALU.add, reads=[tps[b], tHh[k1]], writes=[tHh[k1]])
                        tt("dve", Hh[:, k1, 1, :], Hh[:, k1, 1, :], ps[b][:, 128:256], ALU.subtract, reads=[tps[b], tHh[k1]], writes=[tHh[k1]])
            kb.dma("sp", hall_d[r * 128:(r + 1) * 128, :], Hh[:].rearrange("p a b c -> p (a b c)"), reads=tHh, writes=[t_hall[r]])

        stage_k(0)
        for r in range(8):
            if r + 1 < 8:
                stage_k(r + 1)
            stage_f(r)
        kb.barrier()
    if stop_after == "hf":
        kb.finish()
        return nc, dbg_out

    t_hd_r = T("hT_d_r")
    w_in_v = din["w_in"].rearrange("(k p) n -> p k n", p=128)
    w_sw_v = din["w_qk_sw"].rearrange("(k p) n -> p k n", p=128)

    def load_w(dst, src_view, c0, tl, ncols=128):
        kb.dma("pool", dst, src_view[:, :, c0:c0 + ncols], writes=[tl])

    with ExitStack() as ph:
        def sb(name, shape, dt):
            return ph.enter_context(nc.sbuf_tensor(f"s{next_id()}_" + name, list(shape), dt))

        NH = 0 if "p3" in skip else 4
        rcos = sb("rcos", [128, L], F32)
        rsin = sb("rsin", [128, L], F32)
        t_ropes = [T(f"rope{i}") for i in range(8)]
        for i in range(8):
            kb.dma("sp", rcos[:, i * 512:(i + 1) * 512], din["rope_cos"][:, i * 512:(i + 1) * 512], writes=[t_ropes[i]])
            kb.dma("sp", rsin[:, i * 512:(i + 1) * 512], din["rope_sin"][:, i * 512:(i + 1) * 512], writes=[t_ropes[i]])
        QT = [sb(f"QT{i}", [128, L], BF) for i in range(2)]
        KT = [[sb(f"KT{i}_{m_}", [128, LK], BF) for m_ in range(2)] for i in range(2)]
        V = [sb(f"V{i}", [128, 34, 129], BF) for i in range(2)]
        t_Q, t_K, t_V = [T(), T()], [T(), T()], [T(), T()]
        for i in range(2):
            memset("pool", KT[i][0][64:128, :], 0.0, [t_K[i]])
            memset("pool", KT[i][1][0:64, :], 0.0, [t_K[i]])
            memset("pool", V[i][:, :, 128:129], 1.0, [t_V[i]])
        wts = [[sb(f"aw{i}_{j}", [128, 8, 128], BF) for j in range(5)] for i in range(2)]
        twts = [[T() for _ in range(5)] for _ in range(2)]
        hb = [sb(f"hb{i}", [128, 8, 512], BF) for i in range(2)]
        thb = [T(), T()]
        rt = [sb(f"rt{i}", [128, 512], F32) for i in range(4)]
        trt = [T() for _ in range(4)]
        E = [sb(f"E{i}", [128, 1024], BF) for i in range(4)]
        tE = [T() for _ in range(4)]
        attst = [sb(f"attst{i}", [128, 512], BF) for i in range(2)]
        t_attst = [T(), T()]
        fin = sb("fin", [128, 16], F32)
        accs = sb("accs", [128, 1161], F32)
        oa = sb("oa", [128, 4, 128], F32)
        ob = sb("ob", [128, 4, 128], F32)
        on = sb("on", [128, 4, 128], BF)
        t_fin, t_on, t_accs, t_oa, t_ob = T(), T(), T(), T(), T()
        t_ad = T("aT_d")
        hbcnt = [0]

        def prologue(h, banks):
            bi = h % 2
            w_, tw_ = wts[bi], twts[bi]
            brr = [0]

            def nbk():
                brr[0] += 1
                return banks[brr[0] % len(banks)]

            load_w(w_[0][:], w_in_v, OFF_Q + h * 128, tw_[0])
            load_w(w_[1][:], w_sw_v, h * 128, tw_[1])
            load_w(w_[2][:], w_in_v, OFF_K + h * 128, tw_[2])
            load_w(w_[3][:], w_sw_v, 512 + h * 128, tw_[3])
            load_w(w_[4][:], w_in_v, OFF_V + h * 128, tw_[4])
            b = nbk()
            for k in range(8):
                mm(ps[b][:, 0:CT], w_[2][:, k, :], hcT[:, k, :], start=(k == 0), stop=(k == 7), reads=[tw_[2], t_hcT], writes=[tps[b]])
            cp("dve", KT[bi][0][0:64, L:LK], ps[b][0:64, 0:CT], reads=[tps[b]], writes=[t_K[bi]])
            cp("dve", KT[bi][1][64:128, L:LK], ps[b][64:128, 0:CT], reads=[tps[b]], writes=[t_K[bi]])
            yield
            b = nbk()
            for s_ in range(2):
                for k in range(8):
                    mm(ps[b][:, s_ * 128:(s_ + 1) * 128], hcT[:, k, s_ * 128:(s_ + 1) * 128], w_[4][:, k, :], start=(k == 0), stop=(k == 7),
                       reads=[tw_[4], t_hcT], writes=[tps[b]])
            cp("dve", V[bi][:, 32:34, 0:128], ps[b][:, 0:256].rearrange("p (s v) -> p s v", s=2), reads=[tps[b]], writes=[t_V[bi]])
            yield
            for blk in range(8):
                hi = hbcnt[0] % 2
                hbcnt[0] += 1
                hbb, th = hb[hi], thb[hi]
                kb.dma("sp", hbb[:], hT_d[:, :, blk * 512:(blk + 1) * 512], reads=[t_hd], writes=[th])
                sl = slice(blk * 512, (blk + 1) * 512)
                for qk in range(2):
                    for gg in range(2):
                        g = 2 * qk + gg
                        b = nbk()
                        for k in range(8):
                            mm(ps[b][:], w_[g][:, k, :], hbb[:, k, :], start=(k == 0), stop=(k == 7), reads=[tw_[g], th], writes=[tps[b]])
                        tt("dve", rt[g][:], ps[b][:], (rcos if gg == 0 else rsin)[:, sl], ALU.mult, reads=[tps[b], t_ropes[blk]], writes=[trt[g]])
                        yield
                    r0, r1 = rt[2 * qk], rt[2 * qk + 1]
                    if qk == 0:
                        tt("pool", QT[bi][:, sl], r0[:], r1[:], ALU.add, reads=[trt[0], trt[1]], writes=[t_Q[bi]])
                    else:
                        tt("pool", KT[bi][0][0:64, sl], r0[0:64, :], r1[0:64, :], ALU.add, reads=[trt[2], trt[3]], writes=[t_K[bi]])
                        tt("pool", KT[bi][1][64:128, sl], r0[64:128, :], r1[64:128, :], ALU.add, reads=[trt[2], trt[3]], writes=[t_K[bi]])
                b = nbk()
                for s_ in range(4):
                    for k in range(8):
                        mm(ps[b][:, s_ * 128:(s_ + 1) * 128], hbb[:, k, s_ * 128:(s_ + 1) * 128], w_[4][:, k, :], start=(k == 0), stop=(k == 7),
                           reads=[tw_[4], th], writes=[tps[b]])
                cp("dve", V[bi][:, blk * 4:blk * 4 + 4, 0:128], ps[b][:].rearrange("p (s v) -> p s v", s=4), reads=[tps[b]], writes=[t_V[bi]])
                yield

        def bc_last(ap2d, n):
            return bass.AP(ap2d.tensor, ap2d.offset, [list(ap2d.ap[0]), list(ap2d.ap[1]), [0, n]])

        items = [(qb, kc) for qb in range(8) for kc in range(34)]

        def head_loop(h, gen):
            bi = h % 2
            Qh, Kh, Vh = QT[bi], KT[bi], V[bi]

            def emit_S(idx):
                qb, kc = items[idx]
                b0 = (idx % 2) * 2
                for m_ in range(2):
                    mm(ps[b0 + m_][:], Kh[m_][:, kc * 128:(kc + 1) * 128], Qh[:, qb * 512:(qb + 1) * 512], start=True, stop=True,
                       reads=[t_K[bi], t_Q[bi]], writes=[tps[b0 + m_]])
                ei = idx % 4
                act(E[ei][:], psall[:, b0 * 512:(b0 + 2) * 512], AF.Exp, reads=[tps[b0], tps[b0 + 1]], writes=[tE[ei]], scale=0.125)

            def emit_AV(idx):
                qb, kc = items[idx]
                ei = idx % 4
                for m_ in range(2):
                    for s_ in range(4):
                        slot = m_ * 4 + s_
                        bk, c0 = 4 + slot // 3, (slot % 3) * 129
                        mm(ps[bk][:, c0:c0 + 129], E[ei][:, m_ * 512 + s_ * 128:m_ * 512 + (s_ + 1) * 128], Vh[:, kc, :],
                           start=(kc == 0 and slot % 3 == 0), stop=(kc == 33 and (slot % 3 == 2 or slot == 7)),
                           reads=[tE[ei], t_V[bi]], writes=[tps[bk]])
                if kc == 33:
                    finalize(qb)
                    pend.append(qb)
                if kc == 10 and pend:
                    finalize_tr(pend.pop())

            def finalize(qb):
                qsl = slice(qb * 512, (qb + 1) * 512)
                cp("dve", accs[:, 0:387], ps[4][:, 0:387], reads=[tps[4]], writes=[t_accs])
                cp("dve", accs[:, 387:774], ps[5][:, 0:387], reads=[tps[5]], writes=[t_accs])
                cp("dve", accs[:, 774:1032], ps[6][:, 0:258], reads=[tps[6]], writes=[t_accs])
                av = accs[:, 0:1032].rearrange("p (s c) -> p s c", c=129)
                recip(fin[:, 0:8], av[:, :, 128], reads=[t_accs], writes=[t_fin])
                ts("dve", fin[:, 4:8], fin[:, 4:8], neglam[:], None, ALU.mult, None, reads=[t_fin, t_cols], writes=[t_fin])
                tt("dve", oa[:], av[:, 0:4, 0:128], bc_last(fin[:, 0:4], 128), ALU.mult, reads=[t_accs, t_fin], writes=[t_oa])
                tt("dve", ob[:], av[:, 4:8, 0:128], bc_last(fin[:, 4:8], 128), ALU.mult, reads=[t_accs, t_fin], writes=[t_ob])
                tt("pool", ob[:], ob[:], oa[:], ALU.add, reads=[t_oa, t_ob], writes=[t_ob])
                tt("pool", oa[:], ob[:], ob[:], ALU.mult, reads=[t_ob], writes=[t_oa])
                rsum(fin[:, 8:12], oa[:], reads=[t_oa], writes=[t_fin])
                rstd_of(fin[:, 8:12], fin[:, 12:16], 1.0 / 128, t_fin)
                tt("dve", ob[:], ob[:], bc_last(fin[:, 12:16], 128), ALU.mult, reads=[t_ob, t_fin], writes=[t_ob])
                sg_b = bass.AP(sgrow[:].tensor, sgrow[:].offset, [list(sgrow[:].ap[0]), [0, 4], [1, 128]])
                tt("dve", on[:], ob[:], sg_b, ALU.mult, reads=[t_ob, t_rows], writes=[t_on])

            def finalize_tr(qb):
                qsl = slice(qb * 512, (qb + 1) * 512)
                for s_ in range(4):
                    tr(psb[7][:, s_ * 128:(s_ + 1) * 128], on[:, s_, :], ident_bf[:], reads=[t_on, t_const], writes=[tps[7]])
                cp("dve", attst[qb % 2][:], psb[7][:, 0:512], reads=[tps[7]], writes=[t_attst[qb % 2]])
                kb.dma("sp", aT_d[:, h, qsl], attst[qb % 2][:], reads=[t_attst[qb % 2]], writes=[t_ad])

            pend = []
            emit_S(0)
            emit_S(1)
            for idx in range(len(items)):
                if idx + 2 < len(items):
                    emit_S(idx + 2)
                emit_AV(idx)
                if gen is not None and idx % 6 == 3 and items[idx][1] not in (32, 33, 0):
                    next(gen, None)
            while pend:
                finalize_tr(pend.pop())
            if gen is not None:
                for _ in gen:
                    pass

        if NH:
            for _ in prologue(0, [0, 1, 2, 3, 4, 5, 6, 7]):
                pass
        for h in range(NH):
            gen = prologue(h + 1, [7]) if h + 1 < NH else None
            head_loop(h, gen)
        if "p3" in dbg:
            d = dbg_tensor("attT", [128, 4, L], BF)
            kb.dma("sp", d, aT_d, reads=[t_ad])
        kb.barrier()
    if stop_after == "p3":
        kb.finish()
        return nc, dbg_out

    with ExitStack() as ph:
        def sb(name, shape, dt):
            return ph.enter_context(nc.sbuf_tensor(f"s{next_id()}_" + name, list(shape), dt))

        _bk = [0]

        def nb():
            _bk[0] = (_bk[0] + 1) % 8
            return _bk[0]

        t_hc = T("hyconst")
        gtab = sb("gtab", [128, 33, 3, 128], BF)
        ttab = sb("ttab", [66, 2, 128, 32], BF)
        etab = sb("etab", [128, 4, 2, 32], BF)
        d1f = sb("d1fp", [128, 66], BF)
        cwt = sb("cwt", [128, 12, 3], F32)
        cbt = sb("cbt", [128, 12], F32)
        for dst, nm in ((gtab, "gtab"), (ttab, "ttab"), (etab, "etab"), (d1f, "d1fp"), (cwt, "cw"), (cbt, "cb")):
            kb.dma("sp", dst[:], din[nm], writes=[t_hc])
        Z = [sb(f"Z{i}", [128, L], BF) for i in range(3)]
        tZ = [T(f"Z{i}") for i in range(3)]
        H = sb("H", [128, 33, 2, 128], BF)
        tH = [T() for _ in range(33)]
        t_sd = [T(f"sig{i}") for i in range(5)]
        t_zd = T("zT_d")
        Ut = [sb(f"Ut{i}", [128, 32, 128], BF) for i in range(2)]
        tUt = [T(), T()]
        for i in range(2):
            memset("pool", Ut[i][32:64, :, :], 0.0, [tUt[i]])
            memset("pool", Ut[i][64:128, :, :], 0.0, [tUt[i]])
        A = sb("A", [128, 66, 128], BF)
        tA = [T() for _ in range(33)]
        f1cnt = [0]

        def f1_pre(sig, t_sig, slot):
            kb.dma("pool", sig_d[slot], sig, reads=[t_sig], writes=[t_sd[slot]])
            for g in range(2):
                kb.dma("pool", Ut[g][0:32, :, :], sig_d[slot][g * 32:(g + 1) * 32, :].rearrange("c (a b) -> a c b", a=32),
                       reads=[t_sd[slot]], writes=[tUt[g]])

        def f1_part(sig, t_sig, slot, pre=False):
            if not pre:
                f1_pre(sig, t_sig, slot)
            for g in range(4):
                u, tu = Ut[g % 2], tUt[g % 2]
                if g >= 2:
                    kb.dma("pool", u[0:32, :, :], sig_d[slot][g * 32:(g + 1) * 32, :].rearrange("c (a b) -> a c b", a=32),
                           reads=[t_sd[slot]], writes=[tu])
                j = 0
                while j < 32:
                    n = min(7, 32 - j)
                    b = nb()
                    for jj in range(n):
                        mm(ps[b][:, jj * 66:(jj + 1) * 66], u[:, j + jj, :], d1f[:], start=True, stop=True,
                           reads=[tu, t_hc], writes=[tps[b]])
                    c0 = g * 32 + j
                    eng = "act" if f1cnt[0] % 2 == 0 else "dve"
                    f1cnt[0] += 1
                    cp(eng, A[:, :, c0:c0 + n], ps[b][:, 0:n * 66].rearrange("p (c k) -> p k c", k=66), reads=[tps[b]], writes=tA)
                    j += n

        for cc in range(4):
            with ExitStack() as pa:
                def sba(name, shape, dt):
                    return pa.enter_context(nc.sbuf_tensor(f"s{next_id()}_" + name, list(shape), dt))

                wts = [sba(f"hw{i}", [128, 8, 128], BF) for i in range(3)]
                twts = [T() for _ in range(3)]
                hb = [sba(f"hhb{i}", [128, 8, 512], BF) for i in range(2)]
                thb = [T(), T()]
                Ur = [sba(f"Ur{i}", [128, L + 2], BF) for i in range(3)]
                tUr = [T() for _ in range(3)]
                ctmp = sba("ctmp", [128, L], F32)
                t_ct = T()
                for s in range(3):
                    load_w(wts[s][:], w_in_v, s * 512 + cc * 128, twts[s])
                    memset("pool", Ur[s][:, 0:1], 0.0, [tUr[s]])
                    memset("pool", Ur[s][:, L + 1:L + 2], 0.0, [tUr[s]])
                def proj_pass(sigs, off):
                    for blk in range(8):
                        hbb, th = hb[(blk + off) % 2], thb[(blk + off) % 2]
                        kb.dma("sp", hbb[:], hT_d[:, :, blk * 512:(blk + 1) * 512], reads=[t_hd], writes=[th])
                        for s in sigs:
                            b = nb()
                            for k in range(8):
                                mm(ps[b][:], wts[s][:, k, :], hbb[:, k, :], start=(k == 0), stop=(k == 7), reads=[twts[s], th], writes=[tps[b]])
                            cp("act", Ur[s][:, 1 + blk * 512:1 + (blk + 1) * 512], ps[b][:], reads=[tps[b]], writes=[tUr[s]])

                def sconv(s):
                    slot = s * 4 + cc
                    act(ctmp[:], Ur[s][:, 1:L + 1], AF.Identity, reads=[tUr[s], t_hc], writes=[t_ct],
                        scale=cwt[:, slot, 1:2], bias=cbt[:, slot:slot + 1])
                    stt(ctmp[:], Ur[s][:, 0:L], cwt[:, slot, 0:1], ctmp[:], ALU.mult, ALU.add, reads=[tUr[s], t_hc, t_ct], writes=[t_ct])
                    stt(Z[s][:], Ur[s][:, 2:L + 2], cwt[:, slot, 2:3], ctmp[:], ALU.mult, ALU.add, reads=[tUr[s], t_hc, t_ct], writes=[tZ[s]])

                proj_pass([0, 1, 2], 0)
                sconv(0)
                f1_pre(Z[0][:], tZ[0], 4)
                sconv(1)
                sconv(2)
                f1_part(Z[0][:], tZ[0], 4, pre=True)
                if "p2a" in dbg and cc == 0:
                    for s in range(3):
                        d = dbg_tensor(f"Z{s}", [128, L], BF)
                        kb.dma("sp", d, Z[s][:], reads=[tZ[s]])
                kb.barrier()
            with ExitStack() as pb:
                def sbb(name, shape, dt):
                    return pb.enter_context(nc.sbuf_tensor(f"s{next_id()}_" + name, list(shape), dt))

                Y = sbb("Y", [128, 128, 66], BF)
                tY = [T() for _ in range(33)]
                Pqs = [sbb(f"Pq{i}", [66, 64, 128], BF) for i in range(2)]
                t_Pqs = [T(), T()]
                pw = [sbb(f"pw{i}", [128, 2, 128], F32) for i in range(4)]
                tpw = [T() for _ in range(4)]

                def f2_part(consumer):
                    for k1 in range(33):
                        b = nb()
                        ar, ai = A[:, k1, :], A[:, 33 + k1, :]
                        mm(ps[b][:, 0:128], gtab[:, k1, 0, :], ar, start=True, stop=False, reads=[t_hc, tA[k1]], writes=[tps[b]])
                        mm(ps[b][:, 0:128], gtab[:, k1, 2, :], ai, start=False, stop=True, reads=[t_hc, tA[k1]], writes=[tps[b]])
                        mm(ps[b][:, 128:256], gtab[:, k1, 1, :], ar, start=True, stop=False, reads=[t_hc, tA[k1]], writes=[tps[b]])
                        mm(ps[b][:, 128:256], gtab[:, k1, 0, :], ai, start=False, stop=True, reads=[t_hc, tA[k1]], writes=[tps[b]])
                        consumer(k1, b)

                def conv(o, sig, t_sig, xm, t_xm, zout, t_zout):
                    r_ = 2 * cc + o
                    kb.dma("sp", H[:].rearrange("p a b c -> p (a b c)"), hall_d[r_ * 128:(r_ + 1) * 128, :], reads=[t_hall[r_]], writes=tH)
                    if "p2h" in dbg and cc == 0:
                        d = dbg_tensor(f"H{o}", [128, 33 * 2 * 128], BF)
                        kb.dma("sp", d, H[:].rearrange("p a b c -> p (a b c)"), reads=tH)

                    def cons_d(k1, b):
                        i0 = (2 * k1) % 4
                        p1, p2 = pw[i0], pw[i0 + 1]
                        xv = ps[b][:, 0:256].rearrange("p (r c) -> p r c", r=2)
                        tt("dve", p1[:], xv, bass.AP(H[:, k1, 0, :].tensor, H[:, k1, 0, :].offset, [list(H[:, k1, 0, :].ap[0]), [0, 2], [1, 128]]),
                           ALU.mult, reads=[tps[b], tH[k1]], writes=[tpw[i0]])
                        tt("dve", p2[:], xv, bass.AP(H[:, k1, 1, :].tensor, H[:, k1, 1, :].offset, [list(H[:, k1, 1, :].ap[0]), [0, 2], [1, 128]]),
                           ALU.mult, reads=[tps[b], tH[k1]], writes=[tpw[i0 + 1]])
                        tt("dve", Y[:, :, k1], p1[:, 0, :], p2[:, 1, :], ALU.subtract, reads=[tpw[i0], tpw[i0 + 1]], writes=[tY[k1]])
                        tt("pool", Y[:, :, 33 + k1], p2[:, 0, :], p1[:, 1, :], ALU.add, reads=[tpw[i0], tpw[i0 + 1]], writes=[tY[k1]])

                    if o == 1:
                        f1_part(sig, t_sig, 4)
                    f2_part(cons_d)
                    cnt = 0
                    zv = zout.rearrange("c (a b) -> c b a", a=32)
                    xv_ = xm.rearrange("c (a b) -> c b a", a=32)
                    def i1_stage(q):
                        Pq, t_Pq = Pqs[q % 2], t_Pqs[q % 2]
                        for c0 in range(0, 128, 8):
                            b = nb()
                            for jj in range(8):
                                mm(ps[b][0:66, jj * 64:(jj + 1) * 64], Y[:, c0 + jj, :], etab[:, q, :, :].rearrange("p e n -> p (e n)"),
                                   start=True, stop=True, reads=tY + [t_hc], writes=[tps[b]])
                            eng = "act" if (c0 // 8) % 2 == 0 else "dve"
                            cp(eng, Pq[:, :, c0:c0 + 8], ps[b][0:66, :].rearrange("p (c k) -> p k c", k=64), reads=[tps[b]], writes=[t_Pq])

                    def i2_stage(q):
                        Pq, t_Pq = Pqs[q % 2], t_Pqs[q % 2]
                        for hh in range(2):
                            b = nb()
                            for j in range(16):
                                n2l = hh * 16 + j
                                n2 = q * 32 + n2l
                                for e_ in range(2):
                                    mm(ps[b][:, j * 32:(j + 1) * 32], Pq[:, e_ * 32 + n2l, :], ttab[:, e_, n2, :], start=(e_ == 0), stop=(e_ == 1),
                                       reads=[t_Pq, t_hc], writes=[tps[b]])
                            n20 = q * 32 + hh * 16
                            tt("dve", zv[:, n20:n20 + 16, :], ps[b][:].rearrange("p (b a) -> p b a", a=32), xv_[:, n20:n20 + 16, :], ALU.mult,
                               reads=[tps[b], t_xm], writes=[t_zout])

                    i1_stage(0)
                    for q in range(4):
                        if q + 1 < 4:
                            i1_stage(q + 1)
                        i2_stage(q)

                conv(0, Z[0][:], tZ[0], Z[1][:], tZ[1], Z[0][:], tZ[0])
                if "p2c" in dbg and cc == 0:
                    d = dbg_tensor("z1", [128, L], BF)
                    kb.dma("sp", d, Z[0][:], reads=[tZ[0]])
                conv(1, Z[0][:], tZ[0], Z[2][:], tZ[2], Z[1][:], tZ[1])
                kb.dma("sp", zT_d[:, cc, :], Z[1][:], reads=[tZ[1]], writes=[t_zd])
                kb.barrier()
            if stop_after == "p2c0":
                break
        if "p2" in dbg:
            d = dbg_tensor("zT", [128, 4, L], BF)
            kb.dma("sp", d, zT_d, reads=[t_zd])
        kb.barrier()
    if stop_after in ("p2", "p2c0"):
        kb.finish()
        return nc, dbg_out

    t_md = T("mT_d")
    with ExitStack() as ph:
        def sb(name, shape, dt):
            return ph.enter_context(nc.sbuf_tensor(f"s{next_id()}_" + name, list(shape), dt))

        _bk = [0]

        def nb():
            _bk[0] = (_bk[0] + 1) % 8
            return _bk[0]

        zTs = sb("zTs", [128, 4, L], BF)
        aTs = sb("aTs", [128, 4, L], BF)
        wgt = sb("wgt", [128, 8, 2048], BF)
        whu = sb("whu", [128, 4, D], BF)
        wau = sb("wau", [128, 4, D], BF)
        t_w4 = T("w4")
        t_za = T("za")
        kb.dma("sp", zTs[:], zT_d, writes=[t_za])
        kb.dma("sp", aTs[:], aT_d, writes=[t_za])
        for k in range(8):
            kb.dma("pool", wgt[:, k, :], din["w_in"][k * 128:(k + 1) * 128, OFF_G:OFF_G + 2048], writes=[t_w4])
        kb.dma("pool", whu[:], din["w_hy_up"].rearrange("(k p) n -> p k n", p=128), writes=[t_w4])
        kb.dma("pool", wau[:], din["w_att_up"].rearrange("(k p) n -> p k n", p=128), writes=[t_w4])
        hb = [sb(f"mhb{i}", [128, 8, 512], BF) for i in range(2)]
        thb = [T(), T()]
        mst = [sb(f"mst{i}", [128, 8, 512], BF) for i in range(2)]
        tmst = [T(), T()]
        sg = [sb(f"sg{i}", [128, 512], F32) for i in range(4)]
        tsg = [T() for _ in range(4)]
        mm_ = [sb(f"mm{i}", [128, 512], F32) for i in range(4)]
        tmm = [T() for _ in range(4)]
        kb.dma("sp", hb[0][:], hT_d[:, :, 0:512], reads=[t_hd], writes=[thb[0]])
        for blk in range(8):
            sl = slice(blk * 512, (blk + 1) * 512)
            hbb, th = hb[blk % 2], thb[blk % 2]
            if blk + 1 < 8:
                kb.dma("sp", hb[(blk + 1) % 2][:], hT_d[:, :, (blk + 1) * 512:(blk + 2) * 512], reads=[t_hd], writes=[thb[(blk + 1) % 2]])
            for j in range(8):
                bg1, bg2, by1, by2 = nb(), nb(), nb(), nb()
                for k in range(8):
                    mm(ps[bg1][:], wgt[:, k, j * 128:(j + 1) * 128], hbb[:, k, :], start=(k == 0), stop=(k == 7), reads=[t_w4, th], writes=[tps[bg1]])
                for k in range(8):
                    mm(ps[bg2][:], wgt[:, k, 1024 + j * 128:1024 + (j + 1) * 128], hbb[:, k, :], start=(k == 0), stop=(k == 7), reads=[t_w4, th], writes=[tps[bg2]])
                for k in range(4):
                    mm(ps[by1][:], whu[:, k, j * 128:(j + 1) * 128], zTs[:, k, sl], start=(k == 0), stop=(k == 3), reads=[t_w4, t_za], writes=[tps[by1]])
                for k in range(4):
                    mm(ps[by2][:], wau[:, k, j * 128:(j + 1) * 128], aTs[:, k, sl], start=(k == 0), stop=(k == 3), reads=[t_w4, t_za], writes=[tps[by2]])
                i0 = (j % 2) * 2
                act(sg[i0][:], ps[bg1][:], AF.Sigmoid, reads=[tps[bg1]], writes=[tsg[i0]])
                act(sg[i0 + 1][:], ps[bg2][:], AF.Sigmoid, reads=[tps[bg2]], writes=[tsg[i0 + 1]])
                tt("dve", mm_[i0][:], ps[by1][:], sg[i0][:], ALU.mult, reads=[tps[by1], tsg[i0]], writes=[tmm[i0]])
                tt("dve", mm_[i0 + 1][:], ps[by2][:], sg[i0 + 1][:], ALU.mult, reads=[tps[by2], tsg[i0 + 1]], writes=[tmm[i0 + 1]])
                tt("pool", mst[blk % 2][:, j, :], mm_[i0][:], mm_[i0 + 1][:], ALU.add, reads=[tmm[i0], tmm[i0 + 1]], writes=[tmst[blk % 2]])
            kb.dma("sp", mT_d[:, :, sl], mst[blk % 2][:], reads=[tmst[blk % 2]], writes=[t_md])
        if "p4" in dbg:
            d = dbg_tensor("mT", [128, 8, L], BF)
            kb.dma("sp", d, mT_d, reads=[t_md])
        kb.barrier()
    if stop_after == "p4":
        kb.finish()
        return nc, dbg_out

    with ExitStack() as ph:
        def sb(name, shape, dt):
            return ph.enter_context(nc.sbuf_tensor(f"s{next_id()}_" + name, list(shape), dt))

        _bk = [0]

        def nb():
            _bk[0] = (_bk[0] + 1) % 8
            return _bk[0]

        wout = sb("wout", [128, 8, D], BF)
        wfg = sb("wfg", [128, 8, DFF], BF)
        wfu = sb("wfu", [128, 8, DFF], BF)
        wfd = sb("wfd", [128, NFF, D], BF)
        t_wo, t_wg, t_wu, t_wd = T("wo"), T("wg"), T("wu"), T("wd")
        kb.dma("pool", wout[:], din["w_out"].rearrange("(k p) n -> p k n", p=128), writes=[t_wo])
        for k in range(8):
            kb.dma("pool", wfg[:, k, :], din["w_fg"][k * 128:(k + 1) * 128, :], writes=[t_wg])
        for k in range(8):
            kb.dma("pool", wfu[:, k, :], din["w_fu"][k * 128:(k + 1) * 128, :], writes=[t_wu])
        for j in range(0, NFF, 2):
            kb.dma("pool", wfd[:, j:j + 2, :], din["w_fd"][j * 128:(j + 2) * 128, :].rearrange("(k p) n -> p k n", p=128), writes=[t_wd])
        xts = [sb(f"fx{i}", [128, D], F32) for i in range(3)]
        txt = [T(), T(), T()]
        mts = [sb(f"fm{i}", [128, 8, 128], BF) for i in range(2)]
        tmt = [T(), T()]
        tmp = sb("ftmp", [128, D], F32)
        t_tmp = T()
        junk = sb("fjunk", [128, D], BF)
        t_junk = T()
        xs = sb("fxs", [128, D], BF)
        t_xs = T()
        hfT = [sb(f"hfT{i}", [128, 8, 128], BF) for i in range(2)]
        t_hf = [T(), T()]
        aT = sb("aT", [128, NFF, 128], BF)
        t_aT = T()
        sgt = [sb(f"sgt{i}", [128, 512], F32) for i in range(2)]
        tsgt = [T(), T()]
        atm = sb("atm", [128, DFF], BF)
        t_atm = T()
        fin = sb("ffin", [128, 16], F32)
        t_fin = [T(), T(), T()]
        t_out = T("out")

        def norm_resid(banks, xt, tx, grow, c0, tf):
            for n in range(2):
                act(junk[:, n * 512:(n + 1) * 512], ps[banks[n]][:], AF.Square, reads=[tps[banks[n]]], writes=[t_junk, tf],
                    accum_out=fin[:, c0 + n:c0 + n + 1])
            tt("dve", fin[:, c0 + 2:c0 + 3], fin[:, c0:c0 + 1], fin[:, c0 + 1:c0 + 2], ALU.add, reads=[tf], writes=[tf])
            rstd_of(fin[:, c0 + 2:c0 + 3], fin[:, c0 + 3:c0 + 4], 1.0 / D, tf)
            for n in range(2):
                hs = slice(n * 512, (n + 1) * 512)
                stt(tmp[:, hs], ps[banks[n]][:], fin[:, c0 + 3:c0 + 4], grow[:, hs], ALU.mult, ALU.mult,
                    reads=[tps[banks[n]], tf, t_rows], writes=[t_tmp])
            tt("pool", xt[:], xt[:], tmp[:], ALU.add, reads=[t_tmp, tx], writes=[tx])

        def s1a(i):
            tsl = slice(i * 128, (i + 1) * 128)
            xt, tx = xts[i % 3], txt[i % 3]
            mt, tm = mts[i % 2], tmt[i % 2]
            kb.dma("sp", xt[:], din["x"][tsl, :], writes=[tx])
            kb.dma("sp", mt[:], mT_d[:, :, tsl], reads=[t_md], writes=[tm])
            for n in range(2):
                for k in range(8):
                    mm(ps[n][:], mt[:, k, :], wout[:, k, n * 512:(n + 1) * 512], start=(k == 0), stop=(k == 7),
                       reads=[tm, t_wo], writes=[tps[n]])

        def s1b(i):
            xt, tx = xts[i % 3], txt[i % 3]
            norm_resid([0, 1], xt, tx, g1row, 0, t_fin[0])
            act(junk[:], xt[:], AF.Square, reads=[tx], writes=[t_junk, t_fin[1]], accum_out=fin[:, 4:5])
            rstd_of(fin[:, 4:5], fin[:, 5:6], 1.0 / D, t_fin[1])
            act(xs[:], xt[:], AF.Copy, reads=[tx, t_fin[1]], writes=[t_xs], scale=fin[:, 5:6])

        def s1c(i):
            for k in range(8):
                tr(psb[4][:, k * 128:(k + 1) * 128], xs[:, k * 128:(k + 1) * 128], ident_bf[:], reads=[t_xs, t_const], writes=[tps[4]])
            for k in range(8):
                ts("dve", hfT[i % 2][:, k, :], psb[4][:, k * 128:(k + 1) * 128], A2(k), B2(k), ALU.mult, ALU.add,
                   reads=[tps[4], t_cols], writes=[t_hf[i % 2]])

        def s2a(i):
            h_, th_ = hfT[i % 2], t_hf[i % 2]
            pairs = [(5, 6), (7, 4)]
            for g in range(6):
                c0 = g * 512
                w_ = min(512, DFF - c0)
                bg, bu = pairs[g % 2]
                for k in range(8):
                    mm(ps[bg][:, 0:w_], h_[:, k, :], wfg[:, k, c0:c0 + w_], start=(k == 0), stop=(k == 7), reads=[t_wg, th_], writes=[tps[bg]])
                for k in range(8):
                    mm(ps[bu][:, 0:w_], h_[:, k, :], wfu[:, k, c0:c0 + w_], start=(k == 0), stop=(k == 7), reads=[t_wu, th_], writes=[tps[bu]])
                act(sgt[g % 2][:, 0:w_], ps[bg][:, 0:w_], AF.Silu, reads=[tps[bg]], writes=[tsgt[g % 2]])
                tt("dve", atm[:, c0:c0 + w_], ps[bu][:, 0:w_], sgt[g % 2][:, 0:w_], ALU.mult, reads=[tps[bu], tsgt[g % 2]], writes=[t_atm])
            j = 0
            bi = 0
            while j < NFF:
                n = min(8, NFF - j)
                b = (7, 4, 5)[bi % 3]
                bi += 1
                for jj in range(n):
                    tr(psb[b][:, jj * 128:(jj + 1) * 128], atm[:, (j + jj) * 128:(j + jj + 1) * 128], ident_bf[:], reads=[t_atm, t_const], writes=[tps[b]])
                cp("dve" if bi % 2 else "act", aT[:, j:j + n, :], psb[b][:, 0:n * 128].rearrange("p (j t) -> p j t", t=128), reads=[tps[b]], writes=[t_aT])
                j += n

        def s2b(i):
            tsl = slice(i * 128, (i + 1) * 128)
            xt, tx = xts[i % 3], txt[i % 3]
            for n in range(2):
                for j in range(NFF):
                    mm(ps[2 + n][:], aT[:, j, :], wfd[:, j, n * 512:(n + 1) * 512], start=(j == 0), stop=(j == NFF - 1),
                       reads=[t_aT, t_wd], writes=[tps[2 + n]])
            norm_resid([2, 3], xt, tx, g2row, 8, t_fin[2])
            kb.dma("sp", out[tsl, :], xt[:], reads=[tx], writes=[t_out])

        s1a(0)
        s1b(0)
        s1c(0)
        s1a(1)
        s1b(1)
        for i in range(32):
            s2a(i)
            if i + 1 < 32:
                s1c(i + 1)
            if i + 2 < 32:
                s1a(i + 2)
                s1b(i + 2)
            s2b(i)
        kb.barrier()
    kb.finish()
    return nc, dbg_out


_NC = None


def kernel(**inputs):
    global _NC
    if _NC is None:
        _NC = build()[0]
    in_maps = [layout_inputs(inputs, b) for b in range(8)]
    res = run_bass_kernel_spmd(_NC, in_maps, core_ids=list(range(8)))
    return np.stack([np.asarray(r["out"], dtype=np.float32) for r in res.results], 0)
```

```python
import math
from contextlib import ExitStack
import numpy as np
import ml_dtypes
import concourse.bass as bass
import concourse.mybir as mybir
from concourse.bass_utils import run_bass_kernel_spmd

F32 = mybir.dt.float32
BF = mybir.dt.bfloat16
AF = mybir.ActivationFunctionType
ALU = mybir.AluOpType
AX = mybir.AxisListType

L = 4096
D = 1024
CT = 256
LK = L + CT
DH = 512
DFF = 2816
NFF = DFF // 128
OFF_Q, OFF_K, OFF_V, OFF_G = 1536, 2048, 2560, 3072
EPS = 1e-6
LAM_INIT = 0.8 - 0.6 * math.exp(0.0)
PI = math.pi


class Sem:
    def __init__(self, h):
        self.h = h
        self.count = 0


class T:
    __slots__ = ("name", "w", "r")

    def __init__(self, name=""):
        self.name = name
        self.w = None
        self.r = []


class Eng:
    def __init__(self, name, sem):
        self.name = name
        self.sem = sem
        self.ops = []
        self.seen = {}


class KB:
    def __init__(self, nc, nsem_dma=14):
        self.nc = nc
        self.engs = {}
        for n in ("pe", "act", "dve", "pool", "sp"):
            self.engs[n] = Eng(n, Sem(nc.alloc_semaphore("s_" + n)))
        self.dsems = {q: [Sem(nc.alloc_semaphore(f"d_{q}{i}")) for i in range(nsem_dma)] for q in ("sp", "pool")}
        self.drr = {"sp": 0, "pool": 0}

    def _waits(self, eng, reads, writes, extra=()):
        deps = {}

        def add(d):
            if d is None:
                return
            s, v = d
            if deps.get(s, 0) < v:
                deps[s] = v

        for t in reads:
            add(t.w)
        for t in writes:
            add(t.w)
            for d in t.r:
                add(d)
        for d in extra:
            add(d)
        out = []
        for s, v in deps.items():
            if s is eng.sem and eng.name == "pe":
                continue
            if eng.seen.get(s, 0) >= v:
                continue
            eng.seen[s] = v
            out.append((s, v))
        return out

    def _mark(self, tok, reads, writes):
        for t in reads:
            t.r = [d for d in t.r if d[0] is not tok[0]]
            t.r.append(tok)
        for t in writes:
            t.w = tok
            t.r = []

    def op(self, engname, fn, reads=(), writes=()):
        eng = self.engs[engname]
        waits = self._waits(eng, reads, writes)
        eng.sem.count += 1
        tok = (eng.sem, eng.sem.count)
        eng.ops.append((waits, fn, (eng.sem, 1)))
        self._mark(tok, reads, writes)
        return tok

    def dma(self, q, out_ap, in_ap, reads=(), writes=(), **kw):
        eng = self.engs[q]
        sems = self.dsems[q]
        s = sems[self.drr[q] % len(sems)]
        self.drr[q] += 1
        waits = self._waits(eng, reads, writes, extra=[(s, s.count)] if s.count else [])
        s.count += 16
        tok = (s, s.count)
        eng.ops.append((waits, lambda e: e.dma_start(out=out_ap, in_=in_ap, **kw), (s, 16)))
        self._mark(tok, reads, writes)
        return tok

    def collective(self, kind, ins, outs, reads=(), writes=()):
        eng = self.engs["pool"]
        if not hasattr(self, "ccsem"):
            self.ccsem = Sem(self.nc.alloc_semaphore("s_cc"))
        s = self.ccsem
        waits = self._waits(eng, reads, writes)
        s.count += 1
        tok = (s, s.count)
        eng.ops.append((waits, lambda e: e.collective_compute(kind, ALU.bypass, replica_groups=[list(range(8))], ins=ins, outs=outs), (s, 1)))
        self._mark(tok, reads, writes)
        return tok

    def barrier(self, include_cc=False):
        allsems = [e.sem for e in self.engs.values()] + [s for q in self.dsems.values() for s in q]
        if include_cc and hasattr(self, "ccsem"):
            allsems.append(self.ccsem)
        for eng in self.engs.values():
            waits = []
            for s in allsems:
                if s is eng.sem or s.count == 0:
                    continue
                if eng.seen.get(s, 0) >= s.count:
                    continue
                eng.seen[s] = s.count
                waits.append((s, s.count))
            if waits:
                eng.ops.append((waits, None, None))

    def finish(self):
        nc = self.nc
        self.barrier(include_cc=True)
        with nc.Block() as block:
            def emit(e, en):
                for waits, fn, inc in en.ops:
                    for (ws, wv) in waits:
                        e.wait_ge(ws.h, wv)
                    if fn is not None:
                        ins = fn(e)
                        ins.then_inc(inc[0].h, inc[1])

            @block.tensor
            def _(e):
                emit(e, self.engs["pe"])

            @block.scalar
            def _(e):
                emit(e, self.engs["act"])

            @block.vector
            def _(e):
                emit(e, self.engs["dve"])

            @block.gpsimd
            def _(e):
                emit(e, self.engs["pool"])

            @block.sync
            def _(e):
                emit(e, self.engs["sp"])


def _bf(a):
    return np.ascontiguousarray(a.astype(np.float32)).astype(ml_dtypes.bfloat16)


_CONST = None


def host_consts():
    global _CONST
    if _CONST is not None:
        return _CONST
    c = {}
    c["ident_bf"] = _bf(np.eye(128))
    c["ident_f"] = np.eye(128, dtype=np.float32)
    t = np.arange(L)
    row = (t // 64).astype(np.float32)
    col = (t % 64).astype(np.float32)
    inv = (10000.0 ** (-np.arange(16, dtype=np.float32) / 16)).astype(np.float32)
    cos64 = np.zeros((64, L), np.float32)
    sin64 = np.zeros((64, L), np.float32)
    for half, pos in ((0, row), (1, col)):
        ang = pos[None, :] * inv[:, None]
        base = half * 32
        cos64[base:base + 16] = np.cos(ang)
        cos64[base + 16:base + 32] = np.cos(ang)
        sin64[base:base + 16] = -np.sin(ang)
        sin64[base + 16:base + 32] = np.sin(ang)
    c["rope_cos"] = np.concatenate([cos64, cos64], 0)
    c["rope_sin"] = np.concatenate([sin64, sin64], 0)
    f32 = np.float32
    bands = 16
    tt = np.linspace(0.0, 1.0, L, dtype=f32)[:, None]
    w = (f32(2.0 * math.pi / L) * np.arange(L, dtype=f32))[:, None]
    fr = np.linspace(1e-4, bands - 1, bands, dtype=f32)[None, :]
    z = np.concatenate([tt, np.cos(fr * w), -np.sin(fr * w)], axis=-1).astype(f32)
    c["zT"] = np.ascontiguousarray(z.T)
    deltas = np.abs(np.linspace(math.log(1e-2) / 1.5, math.log(1e-2) / 0.3, DH, dtype=f32))
    nd = (-deltas).reshape(4, 128).T
    c["negdelta"] = np.ascontiguousarray(nd.astype(f32))
    offs = (np.arange(8) * 512 / (L - 1)).astype(f32)
    c["ndoff"] = np.ascontiguousarray((nd[:, :, None] * offs[None, None, :]).astype(f32))
    c["tv0"] = np.ascontiguousarray(np.broadcast_to((np.arange(512) / (L - 1)).astype(f32)[None, :], (128, 512)))
    n1 = np.arange(32)[:, None]
    k1 = np.arange(33)[None, :]
    a = 2 * np.pi * n1 * k1 / 64.0
    c["d1f"] = _bf(np.concatenate([np.cos(a), -np.sin(a)], 1))
    c["d1fp"] = np.ascontiguousarray(np.concatenate([c["d1f"], np.zeros((96, 66), ml_dtypes.bfloat16)], 0))
    c["d1b"] = _bf(np.concatenate([np.cos(a), np.sin(a)], 1))
    n2 = np.arange(128)[:, None, None]
    k1g = np.arange(33)[None, :, None]
    k2 = np.arange(128)[None, None, :]
    ang = 2 * np.pi * n2 * (k1g + 64 * k2) / 8192.0
    gr, gi = np.cos(ang), -np.sin(ang)
    c["gtab"] = _bf(np.stack([gr, gi, -gi], 2))
    k2e = np.arange(128)[:, None]
    n2e = np.arange(128)[None, :]
    ae = 2 * np.pi * k2e * n2e / 128.0
    c["etab"] = _bf(np.stack([np.cos(ae).reshape(128, 4, 32), np.sin(ae).reshape(128, 4, 32)], 2))
    k1t = np.arange(33)[:, None, None]
    n2t = np.arange(128)[None, :, None]
    n1t = np.arange(32)[None, None, :]
    at = 2 * np.pi * k1t * (128 * n1t + n2t) / 8192.0
    wgt = np.full((33, 1, 1), 2.0)
    wgt[0] = 1.0
    wgt[32] = 1.0
    tr = wgt * np.cos(at) / 8192.0
    ti = wgt * np.sin(at) / 8192.0
    t0 = np.concatenate([tr, -ti], 0)
    t1 = np.concatenate([-ti, -tr], 0)
    c["ttab"] = _bf(np.stack([t0, t1], 1))
    _CONST = c
    return c


CONST_SHAPES = {
    "ident_bf": ([128, 128], BF), "ident_f": ([128, 128], F32),
    "rope_cos": ([128, L], F32), "rope_sin": ([128, L], F32),
    "zT": ([33, L], F32), "negdelta": ([128, 4], F32), "ndoff": ([128, 4, 8], F32), "tv0": ([128, 512], F32),
    "d1f": ([32, 66], BF), "d1fp": ([128, 66], BF), "d1b": ([32, 66], BF), "gtab": ([128, 33, 3, 128], BF),
    "etab": ([128, 4, 2, 32], BF), "ttab": ([66, 2, 128, 32], BF),
}

IN_SHAPES = {
    "x": [L, D], "ctx": [CT, D], "cc": [128, 8, 2], "w_ada": [D, 6 * D], "b_adaT": [128, 48], "gcols": [128, 4, 8],
    "w_in": [D, 5120], "w_qk_sw": [D, 1024], "cw": [128, 12, 3], "cb": [128, 12],
    "fw1": [33, 64], "fb1": [64, 1], "fw2": [64, 64], "fb2": [64, 1], "fw3": [64, 2048], "fb3": [1, 2048],
    "ffreq": [64, 1], "hyb": [128, 2, 4], "lamv": [1, 256], "subg": [1, 128],
    "w_hy_up": [DH, D], "w_att_up": [DH, D], "w_out": [D, D], "w_fg": [D, DFF], "w_fu": [D, DFF], "w_fd": [DFF, D],
}


def layout_inputs(inp, b):
    f = lambda a: np.ascontiguousarray(np.asarray(a, dtype=np.float32))
    m = {}
    m["x"] = f(inp["x"][b])
    m["ctx"] = f(inp["ctx"][b])
    cc = np.stack([np.asarray(inp["c"][b]), np.asarray(inp["c_ctx"])], -1)
    m["cc"] = f(cc.reshape(8, 128, 2).transpose(1, 0, 2))
    m["w_ada"] = f(inp["w_ada"][0])
    m["b_adaT"] = f(np.asarray(inp["b_ada"][0]).reshape(48, 128).T)
    g = np.stack([np.asarray(inp[k][0]) for k in ("g_mix_pre", "g_mix_post", "g_ffn_pre", "g_ffn_post")], 0)
    m["gcols"] = f(g.reshape(4, 8, 128).transpose(2, 0, 1))
    w_in = np.asarray(inp["w_in"][0])
    m["w_in"] = f(w_in)
    perm = np.arange(1024).reshape(16, 2, 2, 16)[:, :, ::-1, :].reshape(-1)
    m["w_qk_sw"] = f(w_in[:, OFF_Q:OFF_V][:, perm])
    m["cw"] = f(np.asarray(inp["hy_conv_w"][0]).reshape(3, 12, 128).transpose(2, 1, 0))
    m["cb"] = f(np.asarray(inp["hy_conv_b"][0]).reshape(12, 128).T)
    m["fw1"] = f(inp["hy_f_w1"][0])
    m["fb1"] = f(np.asarray(inp["hy_f_b1"][0]).reshape(64, 1))
    m["fw2"] = f(inp["hy_f_w2"][0])
    m["fb2"] = f(np.asarray(inp["hy_f_b2"][0]).reshape(64, 1))
    m["fw3"] = f(inp["hy_f_w3"][0])
    m["fb3"] = f(np.asarray(inp["hy_f_b3"][0]).reshape(1, 2048))
    m["ffreq"] = f(np.asarray(inp["hy_f_freq"][0]).reshape(64, 1))
    m["hyb"] = f(np.asarray(inp["hy_bias"][0]).reshape(2, 4, 128).transpose(2, 0, 1))
    m["lamv"] = f(np.concatenate([np.asarray(inp[k][0]) for k in ("lambda_q1", "lambda_q2", "lambda_k1", "lambda_k2")]).reshape(1, 256))
    m["subg"] = f(np.asarray(inp["att_subln_g"][0]).reshape(1, 128))
    m["w_hy_up"] = f(inp["w_hy_up"][0])
    m["w_att_up"] = f(inp["w_att_up"][0])
    m["w_out"] = f(inp["w_out"][0])
    m["w_fg"] = f(inp["w_ffn_gate"][0])
    m["w_fu"] = f(inp["w_ffn_up"][0])
    m["w_fd"] = f(inp["w_ffn_down"][0])
    m.update(host_consts())
    return m


def build(dbg=(), stop_after=None, skip=()):
    nc = bass.Bass("TRN2", target_bir_lowering=False)
    kb = KB(nc)
    din = {}
    for k, shp in IN_SHAPES.items():
        din[k] = nc.dram_tensor(k, list(shp), F32, kind="ExternalInput").ap()
    for k, (shp, dt) in CONST_SHAPES.items():
        din[k] = nc.dram_tensor(k, list(shp), dt, kind="ExternalInput").ap()
    out = nc.dram_tensor("out", [L, D], F32, kind="ExternalOutput").ap()
    hT_d = nc.dram_tensor("hT_d", [128, 8, L], BF, kind="Internal").ap()
    mT_d = nc.dram_tensor("mT_d", [128, 8, L], BF, kind="Internal").ap()
    zT_d = nc.dram_tensor("zT_d", [128, 4, L], BF, kind="Internal").ap()
    aT_d = nc.dram_tensor("aT_d", [128, 4, L], BF, kind="Internal").ap()
    sig_d = nc.dram_tensor("sig_d", [5, 128, L], BF, kind="Internal").ap()
    dbg_out = {}

    def dbg_tensor(name, shape, dt=F32):
        dbg_out[name] = nc.dram_tensor("dbg_" + name, list(shape), dt, kind="ExternalOutput").ap()
        return dbg_out[name]

    psall = nc.alloc_psum_tensor("psall", [128, 4096], F32)
    psall_b = psall.bitcast(BF)
    ps = [psall[:, i * 512:(i + 1) * 512] for i in range(8)]
    psb = [psall_b[:, i * 1024:(i + 1) * 1024] for i in range(8)]
    tps = [T(f"ps{i}") for i in range(8)]

    def mm(out_ap, lhsT, rhs, start, stop, reads, writes, tile_position=None):
        if tile_position is None:
            kb.op("pe", lambda e: e.matmul(out_ap, lhsT, rhs, start=start, stop=stop), reads=reads, writes=writes)
        else:
            kb.op("pe", lambda e: e.matmul(out_ap, lhsT, rhs, start=start, stop=stop, tile_position=tile_position), reads=reads, writes=writes)

    def tr(out_ap, in_ap, ident, reads, writes):
        kb.op("pe", lambda e: e.transpose(out_ap, in_ap, ident), reads=reads, writes=writes)

    def act(out_ap, in_ap, func, reads, writes, **kw):
        kb.op("act", lambda e: e.activation(out=out_ap, in_=in_ap, func=func, **kw), reads=reads, writes=writes)

    def ts(eng, out_ap, in0, s1, s2, op0, op1, reads, writes, **kw):
        if s2 is None:
            kb.op(eng, lambda e: e.tensor_scalar(out_ap, in0, s1, None, op0, **kw), reads=reads, writes=writes)
        else:
            kb.op(eng, lambda e: e.tensor_scalar(out_ap, in0, s1, s2, op0, op1, **kw), reads=reads, writes=writes)

    def tt(eng, out_ap, in0, in1, op, reads, writes):
        kb.op(eng, lambda e: e.tensor_tensor(out_ap, in0, in1, op), reads=reads, writes=writes)

    def stt(out_ap, in0, scalar, in1, op0, op1, reads, writes):
        kb.op("dve", lambda e: e.scalar_tensor_tensor(out_ap, in0, scalar, in1, op0, op1), reads=reads, writes=writes)

    def cp(eng, out_ap, in_ap, reads, writes):
        if eng == "act":
            kb.op("act", lambda e: e.copy(out_ap, in_ap), reads=reads, writes=writes)
        else:
            kb.op(eng, lambda e: e.tensor_copy(out_ap, in_ap), reads=reads, writes=writes)

    def recip(out_ap, in_ap, reads, writes):
        kb.op("dve", lambda e: e.reciprocal(out_ap, in_ap), reads=reads, writes=writes)

    def rsum(out_ap, in_ap, reads, writes):
        kb.op("dve", lambda e: e.reduce_sum(out_ap, in_ap, AX.X), reads=reads, writes=writes)

    def memset(eng, ap, val, writes):
        kb.op(eng, lambda e: e.memset(ap, val), writes=writes)

    _ids = [0]

    def next_id():
        _ids[0] += 1
        return _ids[0]

    P = ExitStack()

    def sbp(name, shape, dt):
        return P.enter_context(nc.sbuf_tensor("sp_" + name, list(shape), dt))

    ident_bf = sbp("ident_bf", [128, 128], BF)
    ident_f = sbp("ident_f", [128, 128], F32)
    ones_f = sbp("ones_f", [128, 128], F32)
    mhalf = sbp("mhalf", [128, 32], F32)
    cols = sbp("cols", [128, 8, 8], F32)
    g1row = sbp("g1row", [128, D], F32)
    g2row = sbp("g2row", [128, D], F32)
    neglam = sbp("neglam", [128, 1], F32)
    sgrow = sbp("sgrow", [128, 128], F32)
    hcT = sbp("hcT", [128, 8, CT], BF)
    t_const = T("const")
    t_cols = T("cols")
    t_rows = T("rows")
    t_hcT = T("hcT")
    kb.dma("sp", ident_bf[:], din["ident_bf"], writes=[t_const])
    kb.dma("sp", ident_f[:], din["ident_f"], writes=[t_const])
    memset("dve", ones_f[:], 1.0, [t_const])
    memset("dve", mhalf[:], -0.5, [t_const])
    A1 = lambda k: cols[:, 0, k:k + 1]
    B1 = lambda k: cols[:, 1, k:k + 1]
    A2 = lambda k: cols[:, 2, k:k + 1]
    B2 = lambda k: cols[:, 3, k:k + 1]
    A1c = lambda k: cols[:, 4, k:k + 1]
    B1c = lambda k: cols[:, 5, k:k + 1]

    def rstd_of(ssq_ap, out_ap, inv_n, tl):
        n = ssq_ap.shape[-1] if len(ssq_ap.shape) > 1 else 1
        ts("dve", out_ap, ssq_ap, inv_n, EPS, ALU.mult, ALU.add, reads=[tl], writes=[tl])
        tt("pool", out_ap, out_ap, mhalf[:, 0:n], ALU.pow, reads=[tl, t_const], writes=[tl])

    with ExitStack() as ph:
        def sb(name, shape, dt):
            return ph.enter_context(nc.sbuf_tensor(f"s{next_id()}_" + name, list(shape), dt))

        ccs = sb("ccs", [128, 16], F32)
        scs = sb("scs", [128, 16], F32)
        bada = sb("bada", [128, 48], F32)
        gc = sb("gc", [128, 4, 8], F32)
        adaT = sb("adaT", [128, 48, 2], F32)
        wa = [sb(f"wa{i}", [128, 6 * D], F32) for i in range(2)]
        twa = [T("wa0"), T("wa1")]
        t_s = T("p0small")
        kb.dma("sp", ccs[:], din["cc"].rearrange("p k c -> p (k c)"), writes=[t_s])
        kb.dma("sp", bada[:], din["b_adaT"], writes=[t_s])
        kb.dma("sp", gc[:], din["gcols"], writes=[t_s])
        act(scs[:], ccs[:], AF.Silu, reads=[t_s], writes=[t_s])
        def ada_chunk(k):
            kb.dma("sp", wa[k % 2][:], din["w_ada"][k * 128:(k + 1) * 128, :], writes=[twa[k % 2]])
            for f in range(48):
                mm(ps[0][:, 2 * f:2 * f + 2], wa[k % 2][:, f * 128:(f + 1) * 128], scs[:, 2 * k:2 * k + 2],
                   start=(k == 0 and f == 0), stop=(k == 7 and f == 47), reads=[twa[k % 2], t_s], writes=[tps[0]])
        lamb = sb("lamb", [128, 256], F32)
        lp = sb("lp", [128, 2, 64], F32)
        le = sb("le", [128, 2], F32)
        kb.dma("sp", lamb[:], bass.AP(din["lamv"].tensor, 0, [[0, 128], [1, 256]]), writes=[t_s])
        kb.dma("sp", sgrow[:], bass.AP(din["subg"].tensor, 0, [[0, 128], [1, 128]]), writes=[t_rows])
        tt("dve", lp[:].rearrange("p a b -> p (a b)"), lamb[:, 0:128], lamb[:, 128:256], ALU.mult, reads=[t_s], writes=[t_s])
        rsum(le[:], lp[:], reads=[t_s], writes=[t_s])
        act(le[:], le[:], AF.Exp, reads=[t_s], writes=[t_s])
        tt("dve", neglam[:], le[:, 1:2], le[:, 0:1], ALU.subtract, reads=[t_s], writes=[t_cols])
        ts("dve", neglam[:], neglam[:], -LAM_INIT, None, ALU.add, None, reads=[t_cols], writes=[t_cols])
        ts("dve", sgrow[:], sgrow[:], 1.0 - LAM_INIT, None, ALU.mult, None, reads=[t_rows], writes=[t_rows])
        xts = [sb(f"xt{i}", [128, D], F32) for i in range(4)]
        txt = [T() for _ in range(4)]
        junk = sb("junk", [128, D], BF)
        t_junk = T()
        xsa = sb("xsa", [128, 34, D], BF)
        txs = [T() for _ in range(34)]
        ssq = sb("ssq", [128, 34], F32)
        t_ssq = [T() for _ in range(34)]
        hst = [sb(f"hst{i}", [128, 8, 512], BF) for i in range(2)]
        thst = [T(), T()]
        t_hd = T("hT_d")
        for i in range(34):
            lat = i < 32
            src = din["x"][i * 128:(i + 1) * 128, :] if lat else din["ctx"][(i - 32) * 128:(i - 31) * 128, :]
            xt, tx = xts[i % 4], txt[i % 4]
            if i % 4 == 0 and i // 4 < 8:
                ada_chunk(i // 4)
            kb.dma("sp", xt[:], src, writes=[tx])
            act(junk[:], xt[:], AF.Square, reads=[tx], writes=[t_junk, t_ssq[i]], accum_out=ssq[:, i:i + 1])
            rstd_of(ssq[:, i:i + 1], ssq[:, i:i + 1], 1.0 / D, t_ssq[i])
            if i >= 2:
                j = i - 2
                act(xsa[:, j, :], xts[j % 4][:], AF.Copy, reads=[txt[j % 4], t_ssq[j]], writes=[txs[j]], scale=ssq[:, j:j + 1])
        for j in (32, 33):
            act(xsa[:, j, :], xts[j % 4][:], AF.Copy, reads=[txt[j % 4], t_ssq[j]], writes=[txs[j]], scale=ssq[:, j:j + 1])
        for c in range(2):
            tt("dve", adaT[:, :, c], ps[0][:, c:96:2], bada[:], ALU.add, reads=[tps[0], t_s], writes=[t_s])
        for (dst, sc_f, g_i, c) in ((0, 8, 0, 0), (2, 32, 2, 0), (4, 8, 0, 1)):
            stt(cols[:, dst, :], adaT[:, sc_f:sc_f + 8, c], 1.0, gc[:, g_i, :], ALU.add, ALU.mult, reads=[t_s], writes=[t_cols])
        for (dst, sh_f, c) in ((1, 0, 0), (3, 24, 0), (5, 0, 1)):
            cp("dve", cols[:, dst, :], adaT[:, sh_f:sh_f + 8, c], reads=[t_s], writes=[t_cols])
        tt("dve", cols[:, 6, :], adaT[:, 16:24, 0], gc[:, 1, :], ALU.mult, reads=[t_s], writes=[t_cols])
        tt("dve", cols[:, 7, :], adaT[:, 40:48, 0], gc[:, 3, :], ALU.mult, reads=[t_s], writes=[t_cols])
        diag = sb("diag", [128, 4, 128], F32)
        t_diag = T("diag")
        for gi, rowt in ((6, g1row), (7, g2row)):
            for half in range(2):
                for j in range(4):
                    ts("dve", diag[:, j, :], ident_f[:], cols[:, gi, half * 4 + j:half * 4 + j + 1], None, ALU.mult, None,
                       reads=[t_const, t_cols], writes=[t_diag])
                for j in range(4):
                    mm(ps[1][:, j * 128:(j + 1) * 128], ones_f[:], diag[:, j, :], start=True, stop=True,
                       reads=[t_diag, t_const], writes=[tps[1]])
                cp("dve", rowt[:, half * 512:(half + 1) * 512], ps[1][:], reads=[tps[1]], writes=[t_rows])
        if "p0" in dbg:
            d = dbg_tensor("cols", [128, 64])
            kb.dma("sp", d, cols[:].rearrange("p a b -> p (a b)"), reads=[t_cols])
            d = dbg_tensor("g1row", [128, D])
            kb.dma("sp", d, g1row[:], reads=[t_rows])
            d = dbg_tensor("neglam", [128, 1])
            kb.dma("sp", d, neglam[:], reads=[t_cols])

        for i in range(34):
            lat = i < 32
            bk = 2 + i % 4
            for k in range(8):
                tr(psb[bk][:, k * 128:(k + 1) * 128], xsa[:, i, k * 128:(k + 1) * 128], ident_bf[:],
                   reads=[txs[i], t_const], writes=[tps[bk]])
            for k in range(8):
                if lat:
                    h = hst[(i // 4) % 2]
                    ts("dve", h[:, k, (i % 4) * 128:(i % 4 + 1) * 128], psb[bk][:, k * 128:(k + 1) * 128], A1(k), B1(k),
                       ALU.mult, ALU.add, reads=[tps[bk], t_cols], writes=[thst[(i // 4) % 2]])
                else:
                    ts("dve", hcT[:, k, (i - 32) * 128:(i - 31) * 128], psb[bk][:, k * 128:(k + 1) * 128], A1c(k), B1c(k),
                       ALU.mult, ALU.add, reads=[tps[bk], t_cols], writes=[t_hcT])
            if lat and i % 4 == 3:
                blk = i // 4
                kb.dma("sp", hT_d[:, :, blk * 512:(blk + 1) * 512], hst[blk % 2][:], reads=[thst[blk % 2]], writes=[t_hd])
        if "p1" in dbg:
            d = dbg_tensor("hT", [128, 8, L], BF)
            kb.dma("sp", d, hT_d, reads=[t_hd])
            d = dbg_tensor("hcT", [128, 8, CT], BF)
            kb.dma("sp", d, hcT[:], reads=[t_hcT])
        kb.barrier()
    if stop_after == "p1":
        kb.finish()
        return nc, dbg_out


    hall_d = nc.dram_tensor("hall_d", [1024, 8448], BF, kind="Internal").ap()
    t_hall = [T(f"hall{i}") for i in range(8)]
    with ExitStack() as ph:
        def sb(name, shape, dt):
            return ph.enter_context(nc.sbuf_tensor(f"s{next_id()}_" + name, list(shape), dt))

        _bk = [0]

        def nb():
            _bk[0] = (_bk[0] + 1) % 8
            return _bk[0]

        t_hc = T("hfconst")
        gtab = sb("gtab", [128, 33, 3, 128], BF)
        d1f = sb("d1fp", [128, 66], BF)
        tv0 = sb("tv0", [128, 512], F32)
        negd = sb("negd", [128, 4], F32)
        ndoff = sb("ndoff", [128, 4, 8], F32)
        hybs = sb("hybs", [128, 2, 4], F32)
        for dst, nm in ((gtab, "gtab"), (d1f, "d1fp"), (tv0, "tv0"), (negd, "negdelta"), (ndoff, "ndoff"), (hybs, "hyb")):
            kb.dma("sp", dst[:], din[nm], writes=[t_hc])
        hdn2 = sb("hdn2", [65, L], BF)
        fw3a = sb("fw3a", [65, 2048], BF)
        t_h2 = T("hdn2")
        t_fw3 = T("fw3")
        kb.dma("pool", fw3a[0:64, :], din["fw3"], writes=[t_fw3])
        kb.dma("pool", fw3a[64:65, :], din["fb3"], writes=[t_fw3])
        memset("pool", hdn2[64:65, :], 1.0, [t_h2])
        zTs = sb("zTs", [33, L], F32)
        fw1s = sb("fw1s", [33, 64], F32)
        fw2s = sb("fw2s", [64, 64], F32)
        fcol = sb("fcol", [64, 5], F32)
        t_f = T("fmlp")
        kb.dma("sp", zTs[:], din["zT"], writes=[t_f])
        kb.dma("sp", fw1s[:], din["fw1"], writes=[t_f])
        kb.dma("sp", fw2s[:], din["fw2"], writes=[t_f])
        kb.dma("sp", fcol[:, 0:1], din["ffreq"], writes=[t_f])
        kb.dma("sp", fcol[:, 1:2], din["fb1"], writes=[t_f])
        kb.dma("sp", fcol[:, 2:3], din["fb2"], writes=[t_f])
        tt("dve", fcol[:, 3:4], fcol[:, 0:1], fcol[:, 1:2], ALU.mult, reads=[t_f], writes=[t_f])
        tt("dve", fcol[:, 4:5], fcol[:, 0:1], fcol[:, 2:3], ALU.mult, reads=[t_f], writes=[t_f])
        with ExitStack() as phm:
            def sbm(name, shape, dt):
                return phm.enter_context(nc.sbuf_tensor(f"s{next_id()}_" + name, list(shape), dt))

            halfpi = sbm("halfpi", [64, 1], F32)
            memset("dve", halfpi[:], PI / 2, [t_f])
            arg = [sbm(f"arg{i}", [64, 512], F32) for i in range(8)]
            s4 = [sbm(f"s4{i}", [64, 512], F32) for i in range(8)]
            c4 = [sbm(f"c4{i}", [64, 512], F32) for i in range(8)]
            hd1 = [sbm(f"hd1{i}", [64, 512], F32) for i in range(8)]
            t_m = [T() for _ in range(8)]

            def sin_layer_bf(ps_of, bias_col, out_of, t_out_of):
                for i_ in range(8):
                    ts("dve", arg[i_][:], ps[ps_of(i_)][0:64, :], fcol[:, 0:1], fcol[:, bias_col:bias_col + 1], ALU.mult, ALU.add,
                       reads=[tps[ps_of(i_)], t_f], writes=[t_m[i_]])
                for i_ in range(8):
                    act(s4[i_][:], arg[i_][:], AF.Sin, reads=[t_m[i_]], writes=[t_m[i_]], scale=0.25)
                    act(c4[i_][:], arg[i_][:], AF.Sin, reads=[t_m[i_], t_f], writes=[t_m[i_]], scale=0.25, bias=halfpi[:])
                for i_ in range(8):
                    tt("pool", arg[i_][:], s4[i_][:], s4[i_][:], ALU.mult, reads=[t_m[i_]], writes=[t_m[i_]])
                    tt("pool", s4[i_][:], s4[i_][:], c4[i_][:], ALU.mult, reads=[t_m[i_]], writes=[t_m[i_]])
                for i_ in range(8):
                    ts("dve", arg[i_][:], arg[i_][:], -2.0, 1.0, ALU.mult, ALU.add, reads=[t_m[i_]], writes=[t_m[i_]])
                    stt(out_of(i_), s4[i_][:], 4.0, arg[i_][:], ALU.mult, ALU.mult, reads=[t_m[i_]], writes=[t_m[i_], t_out_of(i_)])

            for blk in range(8):
                mm(ps[blk][0:64, :], fw1s[:], zTs[:, blk * 512:(blk + 1) * 512], start=True, stop=True, reads=[t_f], writes=[tps[blk]])
            sin_layer_bf(lambda i_: i_, 3, lambda i_: hd1[i_][:], lambda i_: t_m[i_])
            for blk in range(8):
                mm(ps[blk][0:64, :], fw2s[:], hd1[blk][:], start=True, stop=True, reads=[t_f, t_m[blk]], writes=[tps[blk]])
            sin_layer_bf(lambda i_: i_, 4, lambda i_: hdn2[0:64, i_ * 512:(i_ + 1) * 512], lambda i_: t_h2)
            kb.barrier()
        dec = [sb(f"dec{i}", [128, L], BF) for i in range(2)]
        t_dec = [T(), T()]
        Kf = [[sb(f"Kf{i}_{d_}", [128, L], BF) for d_ in range(2)] for i in range(2)]
        t_Kf = [[T(), T()], [T(), T()]]
        Ut = [sb(f"Ut{i}", [128, 32, 128], BF) for i in range(2)]
        tUt = [T(), T()]
        for i in range(2):
            memset("pool", Ut[i][32:64, :, :], 0.0, [tUt[i]])
            memset("pool", Ut[i][64:128, :, :], 0.0, [tUt[i]])
        A = sb("A", [128, 66, 128], BF)
        tA = [T() for _ in range(33)]
        H = [sb(f"H{i}", [128, 33, 2, 128], BF) for i in range(2)]
        tH = [[T() for _ in range(33)] for _ in range(2)]
        t_sdf = [T() for _ in range(4)]
        cnt = [0]

        def stage_k(r):
            cc, o = r // 2, r % 2
            pi_ = r % 2
            if o == 0:
                for b8 in range(8):
                    act(dec[cc % 2][:, b8 * 512:(b8 + 1) * 512], tv0[:], AF.Exp, reads=[t_hc], writes=[t_dec[cc % 2]],
                        scale=negd[:, cc:cc + 1], bias=ndoff[:, cc, b8:b8 + 1])
            for d_ in range(2):
                col0 = (o * 2 + d_) * 512 + cc * 128
                kf, tk = Kf[pi_][d_], t_Kf[pi_][d_]
                for blk in range(8):
                    sl = slice(blk * 512, (blk + 1) * 512)
                    b = nb()
                    mm(ps[b][:], fw3a[:, col0:col0 + 128], hdn2[:, sl], start=True, stop=True, reads=[t_fw3, t_h2], writes=[tps[b]])
                    tt("dve", kf[:, sl], ps[b][:], dec[cc % 2][:, sl], ALU.mult, reads=[tps[b], t_dec[cc % 2]], writes=[tk])
                if d_ == 0:
                    tt("dve", kf[:, 0:1], kf[:, 0:1], hybs[:, o, cc:cc + 1], ALU.add, reads=[tk, t_hc], writes=[tk])
                else:
                    memset("dve", kf[:, 0:1], 0.0, [tk])
                kb.dma("pool", sig_d[pi_ * 2 + d_], kf[:], reads=[tk], writes=[t_sdf[pi_ * 2 + d_]])

        def stage_f(r):
            pi_ = r % 2
            Hh, tHh = H[pi_], tH[pi_]
            for d_ in range(2):
                slot = pi_ * 2 + d_
                for g in range(4):
                    u, tu = Ut[g % 2], tUt[g % 2]
                    kb.dma("sp", u[0:32, :, :], sig_d[slot][g * 32:(g + 1) * 32, :].rearrange("c (a b) -> a c b", a=32), reads=[t_sdf[slot]], writes=[tu])
                    j = 0
                    while j < 32:
                        n = min(7, 32 - j)
                        b = nb()
                        for jj in range(n):
                            mm(ps[b][:, jj * 66:(jj + 1) * 66], u[:, j + jj, :], d1f[:], start=True, stop=True, reads=[tu, t_hc], writes=[tps[b]])
                        c0 = g * 32 + j
                        cp("act" if cnt[0] % 2 == 0 else "dve", A[:, :, c0:c0 + n], ps[b][:, 0:n * 66].rearrange("p (c k) -> p k c", k=66),
                           reads=[tps[b]], writes=tA)
                        cnt[0] += 1
                        j += n
                k1 = 0
                while k1 < 33:
                    n = 2 if k1 + 1 < 33 else 1
                    b = nb()
                    for u_ in range(n):
                        kk = k1 + u_
                        o_ = u_ * 256
                        ar, ai = A[:, kk, :], A[:, 33 + kk, :]
                        mm(ps[b][:, o_:o_ + 128], gtab[:, kk, 0, :], ar, start=True, stop=False, reads=[t_hc, tA[kk]], writes=[tps[b]])
                        mm(ps[b][:, o_:o_ + 128], gtab[:, kk, 2, :], ai, start=False, stop=True, reads=[t_hc, tA[kk]], writes=[tps[b]])
                        mm(ps[b][:, o_ + 128:o_ + 256], gtab[:, kk, 1, :], ar, start=True, stop=False, reads=[t_hc, tA[kk]], writes=[tps[b]])
                        mm(ps[b][:, o_ + 128:o_ + 256], gtab[:, kk, 0, :], ai, start=False, stop=True, reads=[t_hc, tA[kk]], writes=[tps[b]])
                    xv = ps[b][:, 0:n * 256].rearrange("p (k r c) -> p k r c", k=n, r=2)
                    wh = [tHh[k1 + u_] for u_ in range(n)]
                    if d_ == 0:
                        cp("act", Hh[:, k1:k1 + n, :, :], xv, reads=[tps[b]], writes=wh)
                    else:
                        tt("dve", Hh[:, k1:k1 + n, 0, :], Hh[:, k1:k1 + n, 0, :], xv[:, :, 0, :], ALU.add, reads=[tps[b]] + wh, writes=wh)
                        tt("dve", Hh[:, k1:k1 + n, 1, :], Hh[:, k1:k1 + n, 1, :], xv[:, :, 1, :], ALU.subtract, reads=[tps[b]] + wh, writes=wh)
                    k1 += n
            kb.dma("sp", hall_d[r * 128:(r + 1) * 128, :], Hh[:].rearrange("p a b c -> p (a b c)"), reads=tHh, writes=[t_hall[r]])

        stage_k(0)
        for r in range(8):
            if r + 1 < 8:
                stage_k(r + 1)
            stage_f(r)
        kb.barrier()
    if stop_after == "hf":
        kb.finish()
        return nc, dbg_out

    t_hd_r = T("hT_d_r")
    w_in_v = din["w_in"].rearrange("(k p) n -> p k n", p=128)
    w_sw_v = din["w_qk_sw"].rearrange("(k p) n -> p k n", p=128)

    def load_w(dst, src_view, c0, tl, ncols=128):
        kb.dma("pool", dst, src_view[:, :, c0:c0 + ncols], writes=[tl])

    with ExitStack() as ph:
        def sb(name, shape, dt):
            return ph.enter_context(nc.sbuf_tensor(f"s{next_id()}_" + name, list(shape), dt))

        NH = 0 if "p3" in skip else 4
        rcos = sb("rcos", [128, L], F32)
        rsin = sb("rsin", [128, L], F32)
        t_ropes = [T(f"rope{i}") for i in range(8)]
        for i in range(8):
            kb.dma("sp", rcos[:, i * 512:(i + 1) * 512], din["rope_cos"][:, i * 512:(i + 1) * 512], writes=[t_ropes[i]])
            kb.dma("sp", rsin[:, i * 512:(i + 1) * 512], din["rope_sin"][:, i * 512:(i + 1) * 512], writes=[t_ropes[i]])
        QT = [sb(f"QT{i}", [128, L], BF) for i in range(2)]
        KT = [[sb(f"KT{i}_{m_}", [128, LK], BF) for m_ in range(2)] for i in range(2)]
        V = [sb(f"V{i}", [128, 34, 129], BF) for i in range(2)]
        t_Q, t_K, t_V = [T(), T()], [T(), T()], [T(), T()]
        for i in range(2):
            memset("pool", KT[i][0][64:128, :], 0.0, [t_K[i]])
            memset("pool", KT[i][1][0:64, :], 0.0, [t_K[i]])
            memset("pool", V[i][:, :, 128:129], 1.0, [t_V[i]])
        wts = [[sb(f"aw{i}_{j}", [128, 8, 128], BF) for j in range(5)] for i in range(2)]
        twts = [[T() for _ in range(5)] for _ in range(2)]
        hb = [sb(f"hb{i}", [128, 8, 512], BF) for i in range(2)]
        thb = [T(), T()]
        rt = [sb(f"rt{i}", [128, 512], F32) for i in range(4)]
        trt = [T() for _ in range(4)]
        E = [sb(f"E{i}", [128, 1024], BF) for i in range(4)]
        tE = [T() for _ in range(4)]
        attst = [sb(f"attst{i}", [128, 512], BF) for i in range(2)]
        t_attst = [T(), T()]
        fin = sb("fin", [128, 16], F32)
        accs = sb("accs", [128, 1161], F32)
        oa = sb("oa", [128, 4, 128], F32)
        ob = sb("ob", [128, 4, 128], F32)
        on = sb("on", [128, 4, 128], BF)
        t_fin, t_on, t_accs, t_oa, t_ob = T(), T(), T(), T(), T()
        t_ad = T("aT_d")
        hbcnt = [0]

        def prologue(h, banks):
            bi = h % 2
            w_, tw_ = wts[bi], twts[bi]
            brr = [0]

            def nbk():
                brr[0] += 1
                return banks[brr[0] % len(banks)]

            load_w(w_[0][:], w_in_v, OFF_Q + h * 128, tw_[0])
            load_w(w_[1][:], w_sw_v, h * 128, tw_[1])
            load_w(w_[2][:], w_in_v, OFF_K + h * 128, tw_[2])
            load_w(w_[3][:], w_sw_v, 512 + h * 128, tw_[3])
            load_w(w_[4][:], w_in_v, OFF_V + h * 128, tw_[4])
            b = nbk()
            for k in range(8):
                mm(ps[b][:, 0:CT], w_[2][:, k, :], hcT[:, k, :], start=(k == 0), stop=(k == 7), reads=[tw_[2], t_hcT], writes=[tps[b]])
            cp("dve", KT[bi][0][0:64, L:LK], ps[b][0:64, 0:CT], reads=[tps[b]], writes=[t_K[bi]])
            cp("dve", KT[bi][1][64:128, L:LK], ps[b][64:128, 0:CT], reads=[tps[b]], writes=[t_K[bi]])
            yield
            b = nbk()
            for s_ in range(2):
                for k in range(8):
                    mm(ps[b][:, s_ * 128:(s_ + 1) * 128], hcT[:, k, s_ * 128:(s_ + 1) * 128], w_[4][:, k, :], start=(k == 0), stop=(k == 7),
                       reads=[tw_[4], t_hcT], writes=[tps[b]])
            cp("dve", V[bi][:, 32:34, 0:128], ps[b][:, 0:256].rearrange("p (s v) -> p s v", s=2), reads=[tps[b]], writes=[t_V[bi]])
            yield
            for blk in range(8):
                hi = hbcnt[0] % 2
                hbcnt[0] += 1
                hbb, th = hb[hi], thb[hi]
                kb.dma("sp", hbb[:], hT_d[:, :, blk * 512:(blk + 1) * 512], reads=[t_hd], writes=[th])
                sl = slice(blk * 512, (blk + 1) * 512)
                for qk in range(2):
                    for gg in range(2):
                        g = 2 * qk + gg
                        b = nbk()
                        for k in range(8):
                            mm(ps[b][:], w_[g][:, k, :], hbb[:, k, :], start=(k == 0), stop=(k == 7), reads=[tw_[g], th], writes=[tps[b]])
                        tt("dve", rt[g][:], ps[b][:], (rcos if gg == 0 else rsin)[:, sl], ALU.mult, reads=[tps[b], t_ropes[blk]], writes=[trt[g]])
                        yield
                    r0, r1 = rt[2 * qk], rt[2 * qk + 1]
                    if qk == 0:
                        tt("pool", QT[bi][:, sl], r0[:], r1[:], ALU.add, reads=[trt[0], trt[1]], writes=[t_Q[bi]])
                    else:
                        tt("pool", KT[bi][0][0:64, sl], r0[0:64, :], r1[0:64, :], ALU.add, reads=[trt[2], trt[3]], writes=[t_K[bi]])
                        tt("pool", KT[bi][1][64:128, sl], r0[64:128, :], r1[64:128, :], ALU.add, reads=[trt[2], trt[3]], writes=[t_K[bi]])
                b = nbk()
                for s_ in range(4):
                    for k in range(8):
                        mm(ps[b][:, s_ * 128:(s_ + 1) * 128], hbb[:, k, s_ * 128:(s_ + 1) * 128], w_[4][:, k, :], start=(k == 0), stop=(k == 7),
                           reads=[tw_[4], th], writes=[tps[b]])
                cp("dve", V[bi][:, blk * 4:blk * 4 + 4, 0:128], ps[b][:].rearrange("p (s v) -> p s v", s=4), reads=[tps[b]], writes=[t_V[bi]])
                yield

        def bc_last(ap2d, n):
            return bass.AP(ap2d.tensor, ap2d.offset, [list(ap2d.ap[0]), list(ap2d.ap[1]), [0, n]])

        items = [(qb, kc) for qb in range(8) for kc in range(34)]

        def head_loop(h, gen):
            bi = h % 2
            Qh, Kh, Vh = QT[bi], KT[bi], V[bi]

            def emit_S(idx):
                qb, kc = items[idx]
                b0 = (idx % 2) * 2
                for m_ in range(2):
                    mm(ps[b0 + m_][:], Kh[m_][:, kc * 128:(kc + 1) * 128], Qh[:, qb * 512:(qb + 1) * 512], start=True, stop=True,
                       reads=[t_K[bi], t_Q[bi]], writes=[tps[b0 + m_]])
                ei = idx % 4
                act(E[ei][:], psall[:, b0 * 512:(b0 + 2) * 512], AF.Exp, reads=[tps[b0], tps[b0 + 1]], writes=[tE[ei]], scale=0.125)

            def emit_AV(idx):
                qb, kc = items[idx]
                ei = idx % 4
                for m_ in range(2):
                    for s_ in range(4):
                        slot = m_ * 4 + s_
                        bk, c0 = 4 + slot // 3, (slot % 3) * 129
                        mm(ps[bk][:, c0:c0 + 129], E[ei][:, m_ * 512 + s_ * 128:m_ * 512 + (s_ + 1) * 128], Vh[:, kc, :],
                           start=(kc == 0 and slot % 3 == 0), stop=(kc == 33 and (slot % 3 == 2 or slot == 7)),
                           reads=[tE[ei], t_V[bi]], writes=[tps[bk]])
                if kc == 33:
                    finalize(qb)
                    pend.append(qb)
                if kc == 10 and pend:
                    finalize_tr(pend.pop())

            def finalize(qb):
                qsl = slice(qb * 512, (qb + 1) * 512)
                cp("dve", accs[:, 0:387], ps[4][:, 0:387], reads=[tps[4]], writes=[t_accs])
                cp("dve", accs[:, 387:774], ps[5][:, 0:387], reads=[tps[5]], writes=[t_accs])
                cp("dve", accs[:, 774:1032], ps[6][:, 0:258], reads=[tps[6]], writes=[t_accs])
                av = accs[:, 0:1032].rearrange("p (s c) -> p s c", c=129)
                recip(fin[:, 0:8], av[:, :, 128], reads=[t_accs], writes=[t_fin])
                ts("dve", fin[:, 4:8], fin[:, 4:8], neglam[:], None, ALU.mult, None, reads=[t_fin, t_cols], writes=[t_fin])
                tt("dve", oa[:], av[:, 0:4, 0:128], bc_last(fin[:, 0:4], 128), ALU.mult, reads=[t_accs, t_fin], writes=[t_oa])
                tt("dve", ob[:], av[:, 4:8, 0:128], bc_last(fin[:, 4:8], 128), ALU.mult, reads=[t_accs, t_fin], writes=[t_ob])
                tt("pool", ob[:], ob[:], oa[:], ALU.add, reads=[t_oa, t_ob], writes=[t_ob])
                tt("pool", oa[:], ob[:], ob[:], ALU.mult, reads=[t_ob], writes=[t_oa])
                rsum(fin[:, 8:12], oa[:], reads=[t_oa], writes=[t_fin])
                rstd_of(fin[:, 8:12], fin[:, 12:16], 1.0 / 128, t_fin)
                tt("dve", ob[:], ob[:], bc_last(fin[:, 12:16], 128), ALU.mult, reads=[t_ob, t_fin], writes=[t_ob])
                sg_b = bass.AP(sgrow[:].tensor, sgrow[:].offset, [list(sgrow[:].ap[0]), [0, 4], [1, 128]])
                tt("dve", on[:], ob[:], sg_b, ALU.mult, reads=[t_ob, t_rows], writes=[t_on])

            def finalize_tr(qb):
                qsl = slice(qb * 512, (qb + 1) * 512)
                for s_ in range(4):
                    tr(psb[7][:, s_ * 128:(s_ + 1) * 128], on[:, s_, :], ident_bf[:], reads=[t_on, t_const], writes=[tps[7]])
                cp("dve", attst[qb % 2][:], psb[7][:, 0:512], reads=[tps[7]], writes=[t_attst[qb % 2]])
                kb.dma("sp", aT_d[:, h, qsl], attst[qb % 2][:], reads=[t_attst[qb % 2]], writes=[t_ad])

            pend = []
            emit_S(0)
            emit_S(1)
            for idx in range(len(items)):
                if idx + 2 < len(items):
                    emit_S(idx + 2)
                emit_AV(idx)
                if gen is not None and idx % 6 == 3 and items[idx][1] not in (32, 33, 0):
                    next(gen, None)
            while pend:
                finalize_tr(pend.pop())
            if gen is not None:
                for _ in gen:
                    pass

        if NH:
            for _ in prologue(0, [0, 1, 2, 3, 4, 5, 6, 7]):
                pass
        for h in range(NH):
            gen = prologue(h + 1, [7]) if h + 1 < NH else None
            head_loop(h, gen)
        if "p3" in dbg:
            d = dbg_tensor("attT", [128, 4, L], BF)
            kb.dma("sp", d, aT_d, reads=[t_ad])
        kb.barrier()
    if stop_after == "p3":
        kb.finish()
        return nc, dbg_out

    with ExitStack() as ph:
        def sb(name, shape, dt):
            return ph.enter_context(nc.sbuf_tensor(f"s{next_id()}_" + name, list(shape), dt))

        _bk = [0]

        def nb():
            _bk[0] = (_bk[0] + 1) % 8
            return _bk[0]

        t_hc = T("hyconst")
        gtab = sb("gtab", [128, 33, 3, 128], BF)
        ttab = sb("ttab", [66, 2, 128, 32], BF)
        etab = sb("etab", [128, 4, 2, 32], BF)
        d1f = sb("d1fp", [128, 66], BF)
        cwt = sb("cwt", [128, 12, 3], F32)
        cbt = sb("cbt", [128, 12], F32)
        for dst, nm in ((gtab, "gtab"), (ttab, "ttab"), (etab, "etab"), (d1f, "d1fp"), (cwt, "cw"), (cbt, "cb")):
            kb.dma("sp", dst[:], din[nm], writes=[t_hc])
        Z = [sb(f"Z{i}", [128, L], BF) for i in range(3)]
        tZ = [T(f"Z{i}") for i in range(3)]
        H = sb("H", [128, 33, 2, 128], BF)
        tH = [T() for _ in range(33)]
        t_sd = [T(f"sig{i}") for i in range(5)]
        t_zd = T("zT_d")
        Ut = [sb(f"Ut{i}", [128, 32, 128], BF) for i in range(2)]
        tUt = [T(), T()]
        for i in range(2):
            memset("pool", Ut[i][32:64, :, :], 0.0, [tUt[i]])
            memset("pool", Ut[i][64:128, :, :], 0.0, [tUt[i]])
        A = sb("A", [128, 66, 128], BF)
        tA = [T() for _ in range(33)]
        f1cnt = [0]

        def f1_pre(sig, t_sig, slot):
            kb.dma("pool", sig_d[slot], sig, reads=[t_sig], writes=[t_sd[slot]])
            for g in range(2):
                kb.dma("pool", Ut[g][0:32, :, :], sig_d[slot][g * 32:(g + 1) * 32, :].rearrange("c (a b) -> a c b", a=32),
                       reads=[t_sd[slot]], writes=[tUt[g]])

        def f1_part(sig, t_sig, slot, pre=False):
            if not pre:
                f1_pre(sig, t_sig, slot)
            for g in range(4):
                u, tu = Ut[g % 2], tUt[g % 2]
                if g >= 2:
                    kb.dma("pool", u[0:32, :, :], sig_d[slot][g * 32:(g + 1) * 32, :].rearrange("c (a b) -> a c b", a=32),
                           reads=[t_sd[slot]], writes=[tu])
                j = 0
                while j < 32:
                    n = min(7, 32 - j)
                    b = nb()
                    for jj in range(n):
                        mm(ps[b][:, jj * 66:(jj + 1) * 66], u[:, j + jj, :], d1f[:], start=True, stop=True,
                           reads=[tu, t_hc], writes=[tps[b]])
                    c0 = g * 32 + j
                    eng = "act" if f1cnt[0] % 2 == 0 else "dve"
                    f1cnt[0] += 1
                    cp(eng, A[:, :, c0:c0 + n], ps[b][:, 0:n * 66].rearrange("p (c k) -> p k c", k=66), reads=[tps[b]], writes=tA)
                    j += n

        for cc in range(4):
            with ExitStack() as pa:
                def sba(name, shape, dt):
                    return pa.enter_context(nc.sbuf_tensor(f"s{next_id()}_" + name, list(shape), dt))

                wts = [sba(f"hw{i}", [128, 8, 128], BF) for i in range(3)]
                twts = [T() for _ in range(3)]
                hb = [sba(f"hhb{i}", [128, 8, 512], BF) for i in range(2)]
                thb = [T(), T()]
                Ur = [sba(f"Ur{i}", [128, L + 2], BF) for i in range(3)]
                tUr = [T() for _ in range(3)]
                ctmp = sba("ctmp", [128, L], F32)
                t_ct = T()
                for s in range(3):
                    load_w(wts[s][:], w_in_v, s * 512 + cc * 128, twts[s])
                    memset("pool", Ur[s][:, 0:1], 0.0, [tUr[s]])
                    memset("pool", Ur[s][:, L + 1:L + 2], 0.0, [tUr[s]])
                def proj_pass(sigs, off):
                    for blk in range(8):
                        hbb, th = hb[(blk + off) % 2], thb[(blk + off) % 2]
                        kb.dma("sp", hbb[:], hT_d[:, :, blk * 512:(blk + 1) * 512], reads=[t_hd], writes=[th])
                        for s in sigs:
                            b = nb()
                            for k in range(8):
                                mm(ps[b][:], wts[s][:, k, :], hbb[:, k, :], start=(k == 0), stop=(k == 7), reads=[twts[s], th], writes=[tps[b]])
                            cp("act", Ur[s][:, 1 + blk * 512:1 + (blk + 1) * 512], ps[b][:], reads=[tps[b]], writes=[tUr[s]])

                def sconv(s):
                    slot = s * 4 + cc
                    act(ctmp[:], Ur[s][:, 1:L + 1], AF.Identity, reads=[tUr[s], t_hc], writes=[t_ct],
                        scale=cwt[:, slot, 1:2], bias=cbt[:, slot:slot + 1])
                    stt(ctmp[:], Ur[s][:, 0:L], cwt[:, slot, 0:1], ctmp[:], ALU.mult, ALU.add, reads=[tUr[s], t_hc, t_ct], writes=[t_ct])
                    stt(Z[s][:], Ur[s][:, 2:L + 2], cwt[:, slot, 2:3], ctmp[:], ALU.mult, ALU.add, reads=[tUr[s], t_hc, t_ct], writes=[tZ[s]])

                proj_pass([0, 1, 2], 0)
                sconv(0)
                f1_pre(Z[0][:], tZ[0], 4)
                sconv(1)
                sconv(2)
                f1_part(Z[0][:], tZ[0], 4, pre=True)
                if "p2a" in dbg and cc == 0:
                    for s in range(3):
                        d = dbg_tensor(f"Z{s}", [128, L], BF)
                        kb.dma("sp", d, Z[s][:], reads=[tZ[s]])
                kb.barrier()
            with ExitStack() as pb:
                def sbb(name, shape, dt):
                    return pb.enter_context(nc.sbuf_tensor(f"s{next_id()}_" + name, list(shape), dt))

                Y = sbb("Y", [128, 128, 66], BF)
                tY = [T() for _ in range(33)]
                Pqs = [sbb(f"Pq{i}", [66, 64, 128], BF) for i in range(2)]
                t_Pqs = [T(), T()]
                pw = [sbb(f"pw{i}", [128, 2, 2, 128], F32) for i in range(4)]
                tpw = [T() for _ in range(4)]

                def f2_part(consumer):
                    k1 = 0
                    while k1 < 33:
                        n = 2 if k1 + 1 < 33 else 1
                        b = nb()
                        for u_ in range(n):
                            kk = k1 + u_
                            o_ = u_ * 256
                            ar, ai = A[:, kk, :], A[:, 33 + kk, :]
                            mm(ps[b][:, o_:o_ + 128], gtab[:, kk, 0, :], ar, start=True, stop=False, reads=[t_hc, tA[kk]], writes=[tps[b]])
                            mm(ps[b][:, o_:o_ + 128], gtab[:, kk, 2, :], ai, start=False, stop=True, reads=[t_hc, tA[kk]], writes=[tps[b]])
                            mm(ps[b][:, o_ + 128:o_ + 256], gtab[:, kk, 1, :], ar, start=True, stop=False, reads=[t_hc, tA[kk]], writes=[tps[b]])
                            mm(ps[b][:, o_ + 128:o_ + 256], gtab[:, kk, 0, :], ai, start=False, stop=True, reads=[t_hc, tA[kk]], writes=[tps[b]])
                        consumer(k1, n, b)
                        k1 += n

                def conv(o, sig, t_sig, xm, t_xm, zout, t_zout):
                    r_ = 2 * cc + o
                    kb.dma("sp", H[:].rearrange("p a b c -> p (a b c)"), hall_d[r_ * 128:(r_ + 1) * 128, :], reads=[t_hall[r_]], writes=tH)
                    if "p2h" in dbg and cc == 0:
                        d = dbg_tensor(f"H{o}", [128, 33 * 2 * 128], BF)
                        kb.dma("sp", d, H[:].rearrange("p a b c -> p (a b c)"), reads=tH)

                    def cons_d(k1, n, b):
                        i0 = ((k1 // 2) % 2) * 2
                        p1, p2 = pw[i0], pw[i0 + 1]
                        xv = ps[b][:, 0:n * 256].rearrange("p (k r c) -> p k r c", k=n, r=2)
                        hb_ = H[:, k1, 0, :]
                        pst = list(hb_.ap[0])
                        hr = bass.AP(hb_.tensor, hb_.offset, [pst, [256, n], [0, 2], [1, 128]])
                        hi = bass.AP(hb_.tensor, hb_.offset + 128, [pst, [256, n], [0, 2], [1, 128]])
                        rd = [tps[b]] + [tH[k1 + u_] for u_ in range(n)]
                        tt("dve", p1[:, 0:n, :, :], xv, hr, ALU.mult, reads=rd, writes=[tpw[i0]])
                        tt("dve", p2[:, 0:n, :, :], xv, hi, ALU.mult, reads=rd, writes=[tpw[i0 + 1]])

                        def ck(t_, r_):
                            v = t_[:, 0:n, r_, :]
                            return bass.AP(v.tensor, v.offset, [list(v.ap[0]), [1, 128], [256, n]])

                        wy = [tY[k1 + u_] for u_ in range(n)]
                        tt("dve", Y[:, :, k1:k1 + n], ck(p1, 0), ck(p2, 1), ALU.subtract, reads=[tpw[i0], tpw[i0 + 1]], writes=wy)
                        tt("pool", Y[:, :, 33 + k1:33 + k1 + n], ck(p2, 0), ck(p1, 1), ALU.add, reads=[tpw[i0], tpw[i0 + 1]], writes=wy)

                    if o == 1:
                        f1_part(sig, t_sig, 4)
                    f2_part(cons_d)
                    cnt = 0
                    zv = zout.rearrange("c (a b) -> c b a", a=32)
                    xv_ = xm.rearrange("c (a b) -> c b a", a=32)
                    def i1_stage(q):
                        Pq, t_Pq = Pqs[q % 2], t_Pqs[q % 2]
                        for c0 in range(0, 128, 8):
                            b = nb()
                            for jj in range(8):
                                mm(ps[b][0:66, jj * 64:(jj + 1) * 64], Y[:, c0 + jj, :], etab[:, q, :, :].rearrange("p e n -> p (e n)"),
                                   start=True, stop=True, reads=tY + [t_hc], writes=[tps[b]])
                            eng = "dve" if (c0 // 8) % 4 == 3 else "act"
                            cp(eng, Pq[:, :, c0:c0 + 8], ps[b][0:66, :].rearrange("p (c k) -> p k c", k=64), reads=[tps[b]], writes=[t_Pq])

                    def i2_stage(q):
                        Pq, t_Pq = Pqs[q % 2], t_Pqs[q % 2]
                        for hh in range(2):
                            b = nb()
                            for j in range(16):
                                n2l = hh * 16 + j
                                n2 = q * 32 + n2l
                                for e_ in range(2):
                                    mm(ps[b][:, j * 32:(j + 1) * 32], Pq[:, e_ * 32 + n2l, :], ttab[:, e_, n2, :], start=(e_ == 0), stop=(e_ == 1),
                                       reads=[t_Pq, t_hc], writes=[tps[b]])
                            n20 = q * 32 + hh * 16
                            tt("dve", zv[:, n20:n20 + 16, :], ps[b][:].rearrange("p (b a) -> p b a", a=32), xv_[:, n20:n20 + 16, :], ALU.mult,
                               reads=[tps[b], t_xm], writes=[t_zout])

                    i1_stage(0)
                    for q in range(4):
                        if q + 1 < 4:
                            i1_stage(q + 1)
                        i2_stage(q)

                conv(0, Z[0][:], tZ[0], Z[1][:], tZ[1], Z[0][:], tZ[0])
                if "p2c" in dbg and cc == 0:
                    d = dbg_tensor("z1", [128, L], BF)
                    kb.dma("sp", d, Z[0][:], reads=[tZ[0]])
                conv(1, Z[0][:], tZ[0], Z[2][:], tZ[2], Z[1][:], tZ[1])
                kb.dma("sp", zT_d[:, cc, :], Z[1][:], reads=[tZ[1]], writes=[t_zd])
                kb.barrier()
            if stop_after == "p2c0":
                break
        if "p2" in dbg:
            d = dbg_tensor("zT", [128, 4, L], BF)
            kb.dma("sp", d, zT_d, reads=[t_zd])
        kb.barrier()
    if stop_after in ("p2", "p2c0"):
        kb.finish()
        return nc, dbg_out

    t_md = T("mT_d")
    with ExitStack() as ph:
        def sb(name, shape, dt):
            return ph.enter_context(nc.sbuf_tensor(f"s{next_id()}_" + name, list(shape), dt))

        _bk = [0]

        def nb():
            _bk[0] = (_bk[0] + 1) % 8
            return _bk[0]

        zTs = sb("zTs", [128, 4, L], BF)
        aTs = sb("aTs", [128, 4, L], BF)
        wgt = sb("wgt", [128, 8, 2048], BF)
        whu = sb("whu", [128, 4, D], BF)
        wau = sb("wau", [128, 4, D], BF)
        t_w4 = T("w4")
        t_za = T("za")
        kb.dma("sp", zTs[:], zT_d, writes=[t_za])
        kb.dma("sp", aTs[:], aT_d, writes=[t_za])
        for k in range(8):
            kb.dma("pool", wgt[:, k, :], din["w_in"][k * 128:(k + 1) * 128, OFF_G:OFF_G + 2048], writes=[t_w4])
        kb.dma("pool", whu[:], din["w_hy_up"].rearrange("(k p) n -> p k n", p=128), writes=[t_w4])
        kb.dma("pool", wau[:], din["w_att_up"].rearrange("(k p) n -> p k n", p=128), writes=[t_w4])
        hb = [sb(f"mhb{i}", [128, 8, 512], BF) for i in range(2)]
        thb = [T(), T()]
        mst = [sb(f"mst{i}", [128, 8, 512], BF) for i in range(2)]
        tmst = [T(), T()]
        sg = [sb(f"sg{i}", [128, 512], F32) for i in range(4)]
        tsg = [T() for _ in range(4)]
        mm_ = [sb(f"mm{i}", [128, 512], F32) for i in range(4)]
        tmm = [T() for _ in range(4)]
        kb.dma("sp", hb[0][:], hT_d[:, :, 0:512], reads=[t_hd], writes=[thb[0]])
        for blk in range(8):
            sl = slice(blk * 512, (blk + 1) * 512)
            hbb, th = hb[blk % 2], thb[blk % 2]
            if blk + 1 < 8:
                kb.dma("sp", hb[(blk + 1) % 2][:], hT_d[:, :, (blk + 1) * 512:(blk + 2) * 512], reads=[t_hd], writes=[thb[(blk + 1) % 2]])
            for j in range(8):
                bg1, bg2, by1, by2 = nb(), nb(), nb(), nb()
                for k in range(8):
                    mm(ps[bg1][:], wgt[:, k, j * 128:(j + 1) * 128], hbb[:, k, :], start=(k == 0), stop=(k == 7), reads=[t_w4, th], writes=[tps[bg1]])
                for k in range(8):
                    mm(ps[bg2][:], wgt[:, k, 1024 + j * 128:1024 + (j + 1) * 128], hbb[:, k, :], start=(k == 0), stop=(k == 7), reads=[t_w4, th], writes=[tps[bg2]])
                for k in range(4):
                    mm(ps[by1][:], whu[:, k, j * 128:(j + 1) * 128], zTs[:, k, sl], start=(k == 0), stop=(k == 3), reads=[t_w4, t_za], writes=[tps[by1]])
                for k in range(4):
                    mm(ps[by2][:], wau[:, k, j * 128:(j + 1) * 128], aTs[:, k, sl], start=(k == 0), stop=(k == 3), reads=[t_w4, t_za], writes=[tps[by2]])
                i0 = (j % 2) * 2
                act(sg[i0][:], ps[bg1][:], AF.Sigmoid, reads=[tps[bg1]], writes=[tsg[i0]])
                act(sg[i0 + 1][:], ps[bg2][:], AF.Sigmoid, reads=[tps[bg2]], writes=[tsg[i0 + 1]])
                tt("dve", mm_[i0][:], ps[by1][:], sg[i0][:], ALU.mult, reads=[tps[by1], tsg[i0]], writes=[tmm[i0]])
                tt("dve", mm_[i0 + 1][:], ps[by2][:], sg[i0 + 1][:], ALU.mult, reads=[tps[by2], tsg[i0 + 1]], writes=[tmm[i0 + 1]])
                tt("pool", mst[blk % 2][:, j, :], mm_[i0][:], mm_[i0 + 1][:], ALU.add, reads=[tmm[i0], tmm[i0 + 1]], writes=[tmst[blk % 2]])
            kb.dma("sp", mT_d[:, :, sl], mst[blk % 2][:], reads=[tmst[blk % 2]], writes=[t_md])
        if "p4" in dbg:
            d = dbg_tensor("mT", [128, 8, L], BF)
            kb.dma("sp", d, mT_d, reads=[t_md])
        kb.barrier()
    if stop_after == "p4":
        kb.finish()
        return nc, dbg_out

    with ExitStack() as ph:
        def sb(name, shape, dt):
            return ph.enter_context(nc.sbuf_tensor(f"s{next_id()}_" + name, list(shape), dt))

        _bk = [0]

        def nb():
            _bk[0] = (_bk[0] + 1) % 8
            return _bk[0]

        wout = sb("wout", [128, 8, D], BF)
        wfg = sb("wfg", [128, 8, DFF], BF)
        wfu = sb("wfu", [128, 8, DFF], BF)
        wfd = sb("wfd", [128, NFF, D], BF)
        t_wo, t_wg, t_wu, t_wd = T("wo"), T("wg"), T("wu"), T("wd")
        kb.dma("pool", wout[:], din["w_out"].rearrange("(k p) n -> p k n", p=128), writes=[t_wo])
        for k in range(8):
            kb.dma("pool", wfg[:, k, :], din["w_fg"][k * 128:(k + 1) * 128, :], writes=[t_wg])
        for k in range(8):
            kb.dma("pool", wfu[:, k, :], din["w_fu"][k * 128:(k + 1) * 128, :], writes=[t_wu])
        for j in range(0, NFF, 2):
            kb.dma("pool", wfd[:, j:j + 2, :], din["w_fd"][j * 128:(j + 2) * 128, :].rearrange("(k p) n -> p k n", p=128), writes=[t_wd])
        xts = [sb(f"fx{i}", [128, D], F32) for i in range(3)]
        txt = [T(), T(), T()]
        mts = [sb(f"fm{i}", [128, 8, 128], BF) for i in range(2)]
        tmt = [T(), T()]
        tmp = sb("ftmp", [128, D], F32)
        t_tmp = T()
        junk = sb("fjunk", [128, D], BF)
        t_junk = T()
        xs = sb("fxs", [128, D], BF)
        t_xs = T()
        hfT = [sb(f"hfT{i}", [128, 8, 128], BF) for i in range(2)]
        t_hf = [T(), T()]
        aT = sb("aT", [128, NFF, 128], BF)
        t_aT = T()
        sgt = [sb(f"sgt{i}", [128, 512], F32) for i in range(2)]
        tsgt = [T(), T()]
        atm = sb("atm", [128, DFF], BF)
        t_atm = T()
        fin = sb("ffin", [128, 16], F32)
        t_fin = [T(), T(), T()]
        t_out = T("out")

        def norm_resid(banks, xt, tx, grow, c0, tf):
            for n in range(2):
                act(junk[:, n * 512:(n + 1) * 512], ps[banks[n]][:], AF.Square, reads=[tps[banks[n]]], writes=[t_junk, tf],
                    accum_out=fin[:, c0 + n:c0 + n + 1])
            tt("dve", fin[:, c0 + 2:c0 + 3], fin[:, c0:c0 + 1], fin[:, c0 + 1:c0 + 2], ALU.add, reads=[tf], writes=[tf])
            rstd_of(fin[:, c0 + 2:c0 + 3], fin[:, c0 + 3:c0 + 4], 1.0 / D, tf)
            for n in range(2):
                hs = slice(n * 512, (n + 1) * 512)
                stt(tmp[:, hs], ps[banks[n]][:], fin[:, c0 + 3:c0 + 4], grow[:, hs], ALU.mult, ALU.mult,
                    reads=[tps[banks[n]], tf, t_rows], writes=[t_tmp])
            tt("pool", xt[:], xt[:], tmp[:], ALU.add, reads=[t_tmp, tx], writes=[tx])

        def s1a(i):
            tsl = slice(i * 128, (i + 1) * 128)
            xt, tx = xts[i % 3], txt[i % 3]
            mt, tm = mts[i % 2], tmt[i % 2]
            kb.dma("sp", xt[:], din["x"][tsl, :], writes=[tx])
            kb.dma("sp", mt[:], mT_d[:, :, tsl], reads=[t_md], writes=[tm])
            for n in range(2):
                for k in range(8):
                    mm(ps[n][:], mt[:, k, :], wout[:, k, n * 512:(n + 1) * 512], start=(k == 0), stop=(k == 7),
                       reads=[tm, t_wo], writes=[tps[n]])

        def s1b(i):
            xt, tx = xts[i % 3], txt[i % 3]
            norm_resid([0, 1], xt, tx, g1row, 0, t_fin[0])
            act(junk[:], xt[:], AF.Square, reads=[tx], writes=[t_junk, t_fin[1]], accum_out=fin[:, 4:5])
            rstd_of(fin[:, 4:5], fin[:, 5:6], 1.0 / D, t_fin[1])
            act(xs[:], xt[:], AF.Copy, reads=[tx, t_fin[1]], writes=[t_xs], scale=fin[:, 5:6])

        def s1c(i):
            for k in range(8):
                tr(psb[4][:, k * 128:(k + 1) * 128], xs[:, k * 128:(k + 1) * 128], ident_bf[:], reads=[t_xs, t_const], writes=[tps[4]])
            for k in range(8):
                ts("dve", hfT[i % 2][:, k, :], psb[4][:, k * 128:(k + 1) * 128], A2(k), B2(k), ALU.mult, ALU.add,
                   reads=[tps[4], t_cols], writes=[t_hf[i % 2]])

        def s2a(i):
            h_, th_ = hfT[i % 2], t_hf[i % 2]
            pairs = [(5, 6), (7, 4)]
            for g in range(6):
                c0 = g * 512
                w_ = min(512, DFF - c0)
                bg, bu = pairs[g % 2]
                for k in range(8):
                    mm(ps[bg][:, 0:w_], h_[:, k, :], wfg[:, k, c0:c0 + w_], start=(k == 0), stop=(k == 7), reads=[t_wg, th_], writes=[tps[bg]])
                for k in range(8):
                    mm(ps[bu][:, 0:w_], h_[:, k, :], wfu[:, k, c0:c0 + w_], start=(k == 0), stop=(k == 7), reads=[t_wu, th_], writes=[tps[bu]])
                act(sgt[g % 2][:, 0:w_], ps[bg][:, 0:w_], AF.Silu, reads=[tps[bg]], writes=[tsgt[g % 2]])
                tt("dve", atm[:, c0:c0 + w_], ps[bu][:, 0:w_], sgt[g % 2][:, 0:w_], ALU.mult, reads=[tps[bu], tsgt[g % 2]], writes=[t_atm])
            j = 0
            bi = 0
            while j < NFF:
                n = min(8, NFF - j)
                b = (7, 4, 5)[bi % 3]
                bi += 1
                for jj in range(n):
                    tr(psb[b][:, jj * 128:(jj + 1) * 128], atm[:, (j + jj) * 128:(j + jj + 1) * 128], ident_bf[:], reads=[t_atm, t_const], writes=[tps[b]])
                cp("dve" if bi % 2 else "act", aT[:, j:j + n, :], psb[b][:, 0:n * 128].rearrange("p (j t) -> p j t", t=128), reads=[tps[b]], writes=[t_aT])
                j += n

        def s2b(i):
            tsl = slice(i * 128, (i + 1) * 128)
            xt, tx = xts[i % 3], txt[i % 3]
            for n in range(2):
                for j in range(NFF):
                    mm(ps[2 + n][:], aT[:, j, :], wfd[:, j, n * 512:(n + 1) * 512], start=(j == 0), stop=(j == NFF - 1),
                       reads=[t_aT, t_wd], writes=[tps[2 + n]])
            norm_resid([2, 3], xt, tx, g2row, 8, t_fin[2])
            kb.dma("sp", out[tsl, :], xt[:], reads=[tx], writes=[t_out])

        s1a(0)
        s1b(0)
        s1c(0)
        s1a(1)
        s1b(1)
        for i in range(32):
            s2a(i)
            if i + 1 < 32:
                s1c(i + 1)
            if i + 2 < 32:
                s1a(i + 2)
                s1b(i + 2)
            s2b(i)
        kb.barrier()
    kb.finish()
    return nc, dbg_out


_NC = None


def kernel(**inputs):
    global _NC
    if _NC is None:
        _NC = build()[0]
    in_maps = [layout_inputs(inputs, b) for b in range(8)]
    res = run_bass_kernel_spmd(_NC, in_maps, core_ids=list(range(8)))
    return np.stack([np.asarray(r["out"], dtype=np.float32) for r in res.results], 0)
```

```python
import math
from contextlib import ExitStack
import numpy as np
import ml_dtypes
import concourse.bass as bass
import concourse.mybir as mybir
from concourse.bass_utils import run_bass_kernel_spmd

F32 = mybir.dt.float32
BF = mybir.dt.bfloat16
AF = mybir.ActivationFunctionType
ALU = mybir.AluOpType
AX = mybir.AxisListType

L = 4096
D = 1024
CT = 256
LK = L + CT
DH = 512
DFF = 2816
NFF = DFF // 128
OFF_Q, OFF_K, OFF_V, OFF_G = 1536, 2048, 2560, 3072
EPS = 1e-6
LAM_INIT = 0.8 - 0.6 * math.exp(0.0)
PI = math.pi


class Sem:
    def __init__(self, h):
        self.h = h
        self.count = 0


class T:
    __slots__ = ("name", "w", "r")

    def __init__(self, name=""):
        self.name = name
        self.w = None
        self.r = []


class Eng:
    def __init__(self, name, sem):
        self.name = name
        self.sem = sem
        self.ops = []
        self.seen = {}


class KB:
    def __init__(self, nc, nsem_dma=14):
        self.nc = nc
        self.engs = {}
        for n in ("pe", "act", "dve", "pool", "sp"):
            self.engs[n] = Eng(n, Sem(nc.alloc_semaphore("s_" + n)))
        self.dsems = {q: [Sem(nc.alloc_semaphore(f"d_{q}{i}")) for i in range(nsem_dma)] for q in ("sp", "pool")}
        self.drr = {"sp": 0, "pool": 0}

    def _waits(self, eng, reads, writes, extra=()):
        deps = {}

        def add(d):
            if d is None:
                return
            s, v = d
            if deps.get(s, 0) < v:
                deps[s] = v

        for t in reads:
            add(t.w)
        for t in writes:
            add(t.w)
            for d in t.r:
                add(d)
        for d in extra:
            add(d)
        out = []
        for s, v in deps.items():
            if s is eng.sem and eng.name == "pe":
                continue
            if eng.seen.get(s, 0) >= v:
                continue
            eng.seen[s] = v
            out.append((s, v))
        return out

    def _mark(self, tok, reads, writes):
        for t in reads:
            t.r = [d for d in t.r if d[0] is not tok[0]]
            t.r.append(tok)
        for t in writes:
            t.w = tok
            t.r = []

    def op(self, engname, fn, reads=(), writes=()):
        eng = self.engs[engname]
        waits = self._waits(eng, reads, writes)
        eng.sem.count += 1
        tok = (eng.sem, eng.sem.count)
        eng.ops.append((waits, fn, (eng.sem, 1)))
        self._mark(tok, reads, writes)
        return tok

    def dma(self, q, out_ap, in_ap, reads=(), writes=(), **kw):
        eng = self.engs[q]
        sems = self.dsems[q]
        s = sems[self.drr[q] % len(sems)]
        self.drr[q] += 1
        waits = self._waits(eng, reads, writes, extra=[(s, s.count)] if s.count else [])
        s.count += 16
        tok = (s, s.count)
        eng.ops.append((waits, lambda e: e.dma_start(out=out_ap, in_=in_ap, **kw), (s, 16)))
        self._mark(tok, reads, writes)
        return tok

    def collective(self, kind, ins, outs, reads=(), writes=()):
        eng = self.engs["pool"]
        if not hasattr(self, "ccsem"):
            self.ccsem = Sem(self.nc.alloc_semaphore("s_cc"))
        s = self.ccsem
        waits = self._waits(eng, reads, writes)
        s.count += 1
        tok = (s, s.count)
        eng.ops.append((waits, lambda e: e.collective_compute(kind, ALU.bypass, replica_groups=[list(range(8))], ins=ins, outs=outs), (s, 1)))
        self._mark(tok, reads, writes)
        return tok

    def barrier(self, include_cc=False):
        allsems = [e.sem for e in self.engs.values()] + [s for q in self.dsems.values() for s in q]
        if include_cc and hasattr(self, "ccsem"):
            allsems.append(self.ccsem)
        for eng in self.engs.values():
            waits = []
            for s in allsems:
                if s is eng.sem or s.count == 0:
                    continue
                if eng.seen.get(s, 0) >= s.count:
                    continue
                eng.seen[s] = s.count
                waits.append((s, s.count))
            if waits:
                eng.ops.append((waits, None, None))

    def finish(self):
        nc = self.nc
        self.barrier(include_cc=True)
        with nc.Block() as block:
            def emit(e, en):
                for waits, fn, inc in en.ops:
                    for (ws, wv) in waits:
                        e.wait_ge(ws.h, wv)
                    if fn is not None:
                        ins = fn(e)
                        ins.then_inc(inc[0].h, inc[1])

            @block.tensor
            def _(e):
                emit(e, self.engs["pe"])

            @block.scalar
            def _(e):
                emit(e, self.engs["act"])

            @block.vector
            def _(e):
                emit(e, self.engs["dve"])

            @block.gpsimd
            def _(e):
                emit(e, self.engs["pool"])

            @block.sync
            def _(e):
                emit(e, self.engs["sp"])


def _bf(a):
    return np.ascontiguousarray(a.astype(np.float32)).astype(ml_dtypes.bfloat16)


_CONST = None


def host_consts():
    global _CONST
    if _CONST is not None:
        return _CONST
    c = {}
    c["ident_bf"] = _bf(np.eye(128))
    c["ident_f"] = np.eye(128, dtype=np.float32)
    t = np.arange(L)
    row = (t // 64).astype(np.float32)
    col = (t % 64).astype(np.float32)
    inv = (10000.0 ** (-np.arange(16, dtype=np.float32) / 16)).astype(np.float32)
    cos64 = np.zeros((64, L), np.float32)
    sin64 = np.zeros((64, L), np.float32)
    for half, pos in ((0, row), (1, col)):
        ang = pos[None, :] * inv[:, None]
        base = half * 32
        cos64[base:base + 16] = np.cos(ang)
        cos64[base + 16:base + 32] = np.cos(ang)
        sin64[base:base + 16] = -np.sin(ang)
        sin64[base + 16:base + 32] = np.sin(ang)
    c["rope_cos"] = np.concatenate([cos64, cos64], 0)
    c["rope_sin"] = np.concatenate([sin64, sin64], 0)
    f32 = np.float32
    bands = 16
    tt = np.linspace(0.0, 1.0, L, dtype=f32)[:, None]
    w = (f32(2.0 * math.pi / L) * np.arange(L, dtype=f32))[:, None]
    fr = np.linspace(1e-4, bands - 1, bands, dtype=f32)[None, :]
    z = np.concatenate([tt, np.cos(fr * w), -np.sin(fr * w)], axis=-1).astype(f32)
    c["zT"] = np.ascontiguousarray(z.T)
    deltas = np.abs(np.linspace(math.log(1e-2) / 1.5, math.log(1e-2) / 0.3, DH, dtype=f32))
    nd = (-deltas).reshape(4, 128).T
    c["negdelta"] = np.ascontiguousarray(nd.astype(f32))
    offs = (np.arange(8) * 512 / (L - 1)).astype(f32)
    c["ndoff"] = np.ascontiguousarray((nd[:, :, None] * offs[None, None, :]).astype(f32))
    c["tv0"] = np.ascontiguousarray(np.broadcast_to((np.arange(512) / (L - 1)).astype(f32)[None, :], (128, 512)))
    n1 = np.arange(32)[:, None]
    k1 = np.arange(33)[None, :]
    a = 2 * np.pi * n1 * k1 / 64.0
    c["d1f"] = _bf(np.concatenate([np.cos(a), -np.sin(a)], 1))
    c["d1fp"] = np.ascontiguousarray(np.concatenate([c["d1f"], np.zeros((96, 66), ml_dtypes.bfloat16)], 0))
    c["d1b"] = _bf(np.concatenate([np.cos(a), np.sin(a)], 1))
    n2 = np.arange(128)[:, None, None]
    k1g = np.arange(33)[None, :, None]
    k2 = np.arange(128)[None, None, :]
    ang = 2 * np.pi * n2 * (k1g + 64 * k2) / 8192.0
    gr, gi = np.cos(ang), -np.sin(ang)
    c["gtab"] = _bf(np.stack([gr, gi, -gi], 2))
    k2e = np.arange(128)[:, None]
    n2e = np.arange(128)[None, :]
    ae = 2 * np.pi * k2e * n2e / 128.0
    c["etab"] = _bf(np.stack([np.cos(ae).reshape(128, 4, 32), np.sin(ae).reshape(128, 4, 32)], 2))
    k1t = np.arange(33)[:, None, None]
    n2t = np.arange(128)[None, :, None]
    n1t = np.arange(32)[None, None, :]
    at = 2 * np.pi * k1t * (128 * n1t + n2t) / 8192.0
    wgt = np.full((33, 1, 1), 2.0)
    wgt[0] = 1.0
    wgt[32] = 1.0
    tr = wgt * np.cos(at) / 8192.0
    ti = wgt * np.sin(at) / 8192.0
    t0 = np.concatenate([tr, -ti], 0)
    t1 = np.concatenate([-ti, -tr], 0)
    c["ttab"] = _bf(np.stack([t0, t1], 1))
    _CONST = c
    return c


CONST_SHAPES = {
    "ident_bf": ([128, 128], BF), "ident_f": ([128, 128], F32),
    "rope_cos": ([128, L], F32), "rope_sin": ([128, L], F32),
    "zT": ([33, L], F32), "negdelta": ([128, 4], F32), "ndoff": ([128, 4, 8], F32), "tv0": ([128, 512], F32),
    "d1f": ([32, 66], BF), "d1fp": ([128, 66], BF), "d1b": ([32, 66], BF), "gtab": ([128, 33, 3, 128], BF),
    "etab": ([128, 4, 2, 32], BF), "ttab": ([66, 2, 128, 32], BF),
}

IN_SHAPES = {
    "x": [L, D], "ctx": [CT, D], "cc": [128, 8, 2], "w_ada": [D, 6 * D], "b_adaT": [128, 48], "gcols": [128, 4, 8],
    "w_in": [D, 5120], "w_qk_sw": [D, 1024], "cw": [128, 12, 3], "cb": [128, 12],
    "fw1": [33, 64], "fb1": [64, 1], "fw2": [64, 64], "fb2": [64, 1], "fw3": [64, 2048], "fb3": [1, 2048],
    "ffreq": [64, 1], "hyb": [128, 2, 4], "lamv": [1, 256], "subg": [1, 128],
    "w_hy_up": [DH, D], "w_att_up": [DH, D], "w_out": [D, D], "w_fg": [D, DFF], "w_fu": [D, DFF], "w_fd": [DFF, D],
}


def layout_inputs(inp, b):
    f = lambda a: np.ascontiguousarray(np.asarray(a, dtype=np.float32))
    m = {}
    m["x"] = f(inp["x"][b])
    m["ctx"] = f(inp["ctx"][b])
    cc = np.stack([np.asarray(inp["c"][b]), np.asarray(inp["c_ctx"])], -1)
    m["cc"] = f(cc.reshape(8, 128, 2).transpose(1, 0, 2))
    m["w_ada"] = f(inp["w_ada"][0])
    m["b_adaT"] = f(np.asarray(inp["b_ada"][0]).reshape(48, 128).T)
    g = np.stack([np.asarray(inp[k][0]) for k in ("g_mix_pre", "g_mix_post", "g_ffn_pre", "g_ffn_post")], 0)
    m["gcols"] = f(g.reshape(4, 8, 128).transpose(2, 0, 1))
    w_in = np.asarray(inp["w_in"][0])
    m["w_in"] = f(w_in)
    perm = np.arange(1024).reshape(16, 2, 2, 16)[:, :, ::-1, :].reshape(-1)
    m["w_qk_sw"] = f(w_in[:, OFF_Q:OFF_V][:, perm])
    m["cw"] = f(np.asarray(inp["hy_conv_w"][0]).reshape(3, 12, 128).transpose(2, 1, 0))
    m["cb"] = f(np.asarray(inp["hy_conv_b"][0]).reshape(12, 128).T)
    m["fw1"] = f(inp["hy_f_w1"][0])
    m["fb1"] = f(np.asarray(inp["hy_f_b1"][0]).reshape(64, 1))
    m["fw2"] = f(inp["hy_f_w2"][0])
    m["fb2"] = f(np.asarray(inp["hy_f_b2"][0]).reshape(64, 1))
    m["fw3"] = f(inp["hy_f_w3"][0])
    m["fb3"] = f(np.asarray(inp["hy_f_b3"][0]).reshape(1, 2048))
    m["ffreq"] = f(np.asarray(inp["hy_f_freq"][0]).reshape(64, 1))
    m["hyb"] = f(np.asarray(inp["hy_bias"][0]).reshape(2, 4, 128).transpose(2, 0, 1))
    m["lamv"] = f(np.concatenate([np.asarray(inp[k][0]) for k in ("lambda_q1", "lambda_q2", "lambda_k1", "lambda_k2")]).reshape(1, 256))
    m["subg"] = f(np.asarray(inp["att_subln_g"][0]).reshape(1, 128))
    m["w_hy_up"] = f(inp["w_hy_up"][0])
    m["w_att_up"] = f(inp["w_att_up"][0])
    m["w_out"] = f(inp["w_out"][0])
    m["w_fg"] = f(inp["w_ffn_gate"][0])
    m["w_fu"] = f(inp["w_ffn_up"][0])
    m["w_fd"] = f(inp["w_ffn_down"][0])
    m.update(host_consts())
    return m


def build(dbg=(), stop_after=None, skip=()):
    nc = bass.Bass("TRN2", target_bir_lowering=False)
    kb = KB(nc)
    din = {}
    for k, shp in IN_SHAPES.items():
        din[k] = nc.dram_tensor(k, list(shp), F32, kind="ExternalInput").ap()
    for k, (shp, dt) in CONST_SHAPES.items():
        din[k] = nc.dram_tensor(k, list(shp), dt, kind="ExternalInput").ap()
    out = nc.dram_tensor("out", [L, D], F32, kind="ExternalOutput").ap()
    hT_d = nc.dram_tensor("hT_d", [128, 8, L], BF, kind="Internal").ap()
    mT_d = nc.dram_tensor("mT_d", [128, 8, L], BF, kind="Internal").ap()
    zT_d = nc.dram_tensor("zT_d", [128, 4, L], BF, kind="Internal").ap()
    aT_d = nc.dram_tensor("aT_d", [128, 4, L], BF, kind="Internal").ap()
    sig_d = nc.dram_tensor("sig_d", [5, 128, L], BF, kind="Internal").ap()
    dbg_out = {}

    def dbg_tensor(name, shape, dt=F32):
        dbg_out[name] = nc.dram_tensor("dbg_" + name, list(shape), dt, kind="ExternalOutput").ap()
        return dbg_out[name]

    psall = nc.alloc_psum_tensor("psall", [128, 4096], F32)
    psall_b = psall.bitcast(BF)
    ps = [psall[:, i * 512:(i + 1) * 512] for i in range(8)]
    psb = [psall_b[:, i * 1024:(i + 1) * 1024] for i in range(8)]
    tps = [T(f"ps{i}") for i in range(8)]

    def mm(out_ap, lhsT, rhs, start, stop, reads, writes, tile_position=None):
        if tile_position is None:
            kb.op("pe", lambda e: e.matmul(out_ap, lhsT, rhs, start=start, stop=stop), reads=reads, writes=writes)
        else:
            kb.op("pe", lambda e: e.matmul(out_ap, lhsT, rhs, start=start, stop=stop, tile_position=tile_position), reads=reads, writes=writes)

    def tr(out_ap, in_ap, ident, reads, writes):
        kb.op("pe", lambda e: e.transpose(out_ap, in_ap, ident), reads=reads, writes=writes)

    def act(out_ap, in_ap, func, reads, writes, **kw):
        kb.op("act", lambda e: e.activation(out=out_ap, in_=in_ap, func=func, **kw), reads=reads, writes=writes)

    def ts(eng, out_ap, in0, s1, s2, op0, op1, reads, writes, **kw):
        if s2 is None:
            kb.op(eng, lambda e: e.tensor_scalar(out_ap, in0, s1, None, op0, **kw), reads=reads, writes=writes)
        else:
            kb.op(eng, lambda e: e.tensor_scalar(out_ap, in0, s1, s2, op0, op1, **kw), reads=reads, writes=writes)

    def tt(eng, out_ap, in0, in1, op, reads, writes):
        kb.op(eng, lambda e: e.tensor_tensor(out_ap, in0, in1, op), reads=reads, writes=writes)

    def stt(out_ap, in0, scalar, in1, op0, op1, reads, writes):
        kb.op("dve", lambda e: e.scalar_tensor_tensor(out_ap, in0, scalar, in1, op0, op1), reads=reads, writes=writes)

    def cp(eng, out_ap, in_ap, reads, writes):
        if eng == "act":
            kb.op("act", lambda e: e.copy(out_ap, in_ap), reads=reads, writes=writes)
        else:
            kb.op(eng, lambda e: e.tensor_copy(out_ap, in_ap), reads=reads, writes=writes)

    def recip(out_ap, in_ap, reads, writes):
        kb.op("dve", lambda e: e.reciprocal(out_ap, in_ap), reads=reads, writes=writes)

    def rsum(out_ap, in_ap, reads, writes):
        kb.op("dve", lambda e: e.reduce_sum(out_ap, in_ap, AX.X), reads=reads, writes=writes)

    def memset(eng, ap, val, writes):
        kb.op(eng, lambda e: e.memset(ap, val), writes=writes)

    _ids = [0]

    def next_id():
        _ids[0] += 1
        return _ids[0]

    P = ExitStack()

    def sbp(name, shape, dt):
        return P.enter_context(nc.sbuf_tensor("sp_" + name, list(shape), dt))

    ident_bf = sbp("ident_bf", [128, 128], BF)
    ident_f = sbp("ident_f", [128, 128], F32)
    ones_f = sbp("ones_f", [128, 128], F32)
    mhalf = sbp("mhalf", [128, 32], F32)
    cols = sbp("cols", [128, 8, 8], F32)
    g1row = sbp("g1row", [128, D], F32)
    g2row = sbp("g2row", [128, D], F32)
    neglam = sbp("neglam", [128, 1], F32)
    sgrow = sbp("sgrow", [128, 128], F32)
    hcT = sbp("hcT", [128, 8, CT], BF)
    t_const = T("const")
    t_cols = T("cols")
    t_rows = T("rows")
    t_hcT = T("hcT")
    kb.dma("sp", ident_bf[:], din["ident_bf"], writes=[t_const])
    kb.dma("sp", ident_f[:], din["ident_f"], writes=[t_const])
    memset("dve", ones_f[:], 1.0, [t_const])
    memset("dve", mhalf[:], -0.5, [t_const])
    A1 = lambda k: cols[:, 0, k:k + 1]
    B1 = lambda k: cols[:, 1, k:k + 1]
    A2 = lambda k: cols[:, 2, k:k + 1]
    B2 = lambda k: cols[:, 3, k:k + 1]
    A1c = lambda k: cols[:, 4, k:k + 1]
    B1c = lambda k: cols[:, 5, k:k + 1]

    def rstd_of(ssq_ap, out_ap, inv_n, tl):
        n = ssq_ap.shape[-1] if len(ssq_ap.shape) > 1 else 1
        ts("dve", out_ap, ssq_ap, inv_n, EPS, ALU.mult, ALU.add, reads=[tl], writes=[tl])
        tt("pool", out_ap, out_ap, mhalf[:, 0:n], ALU.pow, reads=[tl, t_const], writes=[tl])

    with ExitStack() as ph:
        def sb(name, shape, dt):
            return ph.enter_context(nc.sbuf_tensor(f"s{next_id()}_" + name, list(shape), dt))

        ccs = sb("ccs", [128, 16], F32)
        scs = sb("scs", [128, 16], F32)
        bada = sb("bada", [128, 48], F32)
        gc = sb("gc", [128, 4, 8], F32)
        adaT = sb("adaT", [128, 48, 2], F32)
        wa = [sb(f"wa{i}", [128, 6 * D], F32) for i in range(2)]
        twa = [T("wa0"), T("wa1")]
        t_s = T("p0small")
        kb.dma("sp", ccs[:], din["cc"].rearrange("p k c -> p (k c)"), writes=[t_s])
        kb.dma("sp", bada[:], din["b_adaT"], writes=[t_s])
        kb.dma("sp", gc[:], din["gcols"], writes=[t_s])
        act(scs[:], ccs[:], AF.Silu, reads=[t_s], writes=[t_s])
        def ada_chunk(k):
            kb.dma("sp", wa[k % 2][:], din["w_ada"][k * 128:(k + 1) * 128, :], writes=[twa[k % 2]])
            for f in range(48):
                mm(ps[0][:, 2 * f:2 * f + 2], wa[k % 2][:, f * 128:(f + 1) * 128], scs[:, 2 * k:2 * k + 2],
                   start=(k == 0 and f == 0), stop=(k == 7 and f == 47), reads=[twa[k % 2], t_s], writes=[tps[0]])
        lamb = sb("lamb", [128, 256], F32)
        lp = sb("lp", [128, 2, 64], F32)
        le = sb("le", [128, 2], F32)
        kb.dma("sp", lamb[:], bass.AP(din["lamv"].tensor, 0, [[0, 128], [1, 256]]), writes=[t_s])
        kb.dma("sp", sgrow[:], bass.AP(din["subg"].tensor, 0, [[0, 128], [1, 128]]), writes=[t_rows])
        tt("dve", lp[:].rearrange("p a b -> p (a b)"), lamb[:, 0:128], lamb[:, 128:256], ALU.mult, reads=[t_s], writes=[t_s])
        rsum(le[:], lp[:], reads=[t_s], writes=[t_s])
        act(le[:], le[:], AF.Exp, reads=[t_s], writes=[t_s])
        tt("dve", neglam[:], le[:, 1:2], le[:, 0:1], ALU.subtract, reads=[t_s], writes=[t_cols])
        ts("dve", neglam[:], neglam[:], -LAM_INIT, None, ALU.add, None, reads=[t_cols], writes=[t_cols])
        ts("dve", sgrow[:], sgrow[:], 1.0 - LAM_INIT, None, ALU.mult, None, reads=[t_rows], writes=[t_rows])
        xts = [sb(f"xt{i}", [128, D], F32) for i in range(4)]
        txt = [T() for _ in range(4)]
        junk = sb("junk", [128, D], BF)
        t_junk = T()
        xsa = sb("xsa", [128, 34, D], BF)
        txs = [T() for _ in range(34)]
        ssq = sb("ssq", [128, 34], F32)
        t_ssq = [T() for _ in range(34)]
        hst = [sb(f"hst{i}", [128, 8, 512], BF) for i in range(2)]
        thst = [T(), T()]
        t_hd = T("hT_d")
        for i in range(34):
            lat = i < 32
            src = din["x"][i * 128:(i + 1) * 128, :] if lat else din["ctx"][(i - 32) * 128:(i - 31) * 128, :]
            xt, tx = xts[i % 4], txt[i % 4]
            if i % 4 == 0 and i // 4 < 8:
                ada_chunk(i // 4)
            kb.dma("sp", xt[:], src, writes=[tx])
            act(junk[:], xt[:], AF.Square, reads=[tx], writes=[t_junk, t_ssq[i]], accum_out=ssq[:, i:i + 1])
            rstd_of(ssq[:, i:i + 1], ssq[:, i:i + 1], 1.0 / D, t_ssq[i])
            if i >= 2:
                j = i - 2
                act(xsa[:, j, :], xts[j % 4][:], AF.Copy, reads=[txt[j % 4], t_ssq[j]], writes=[txs[j]], scale=ssq[:, j:j + 1])
        for j in (32, 33):
            act(xsa[:, j, :], xts[j % 4][:], AF.Copy, reads=[txt[j % 4], t_ssq[j]], writes=[txs[j]], scale=ssq[:, j:j + 1])
        for c in range(2):
            tt("dve", adaT[:, :, c], ps[0][:, c:96:2], bada[:], ALU.add, reads=[tps[0], t_s], writes=[t_s])
        for (dst, sc_f, g_i, c) in ((0, 8, 0, 0), (2, 32, 2, 0), (4, 8, 0, 1)):
            stt(cols[:, dst, :], adaT[:, sc_f:sc_f + 8, c], 1.0, gc[:, g_i, :], ALU.add, ALU.mult, reads=[t_s], writes=[t_cols])
        for (dst, sh_f, c) in ((1, 0, 0), (3, 24, 0), (5, 0, 1)):
            cp("dve", cols[:, dst, :], adaT[:, sh_f:sh_f + 8, c], reads=[t_s], writes=[t_cols])
        tt("dve", cols[:, 6, :], adaT[:, 16:24, 0], gc[:, 1, :], ALU.mult, reads=[t_s], writes=[t_cols])
        tt("dve", cols[:, 7, :], adaT[:, 40:48, 0], gc[:, 3, :], ALU.mult, reads=[t_s], writes=[t_cols])
        diag = sb("diag", [128, 4, 128], F32)
        t_diag = T("diag")
        for gi, rowt in ((6, g1row), (7, g2row)):
            for half in range(2):
                for j in range(4):
                    ts("dve", diag[:, j, :], ident_f[:], cols[:, gi, half * 4 + j:half * 4 + j + 1], None, ALU.mult, None,
                       reads=[t_const, t_cols], writes=[t_diag])
                for j in range(4):
                    mm(ps[1][:, j * 128:(j + 1) * 128], ones_f[:], diag[:, j, :], start=True, stop=True,
                       reads=[t_diag, t_const], writes=[tps[1]])
                cp("dve", rowt[:, half * 512:(half + 1) * 512], ps[1][:], reads=[tps[1]], writes=[t_rows])
        if "p0" in dbg:
            d = dbg_tensor("cols", [128, 64])
            kb.dma("sp", d, cols[:].rearrange("p a b -> p (a b)"), reads=[t_cols])
            d = dbg_tensor("g1row", [128, D])
            kb.dma("sp", d, g1row[:], reads=[t_rows])
            d = dbg_tensor("neglam", [128, 1])
            kb.dma("sp", d, neglam[:], reads=[t_cols])

        for i in range(34):
            lat = i < 32
            bk = 2 + i % 4
            for k in range(8):
                tr(psb[bk][:, k * 128:(k + 1) * 128], xsa[:, i, k * 128:(k + 1) * 128], ident_bf[:],
                   reads=[txs[i], t_const], writes=[tps[bk]])
            for k in range(8):
                if lat:
                    h = hst[(i // 4) % 2]
                    ts("dve", h[:, k, (i % 4) * 128:(i % 4 + 1) * 128], psb[bk][:, k * 128:(k + 1) * 128], A1(k), B1(k),
                       ALU.mult, ALU.add, reads=[tps[bk], t_cols], writes=[thst[(i // 4) % 2]])
                else:
                    ts("dve", hcT[:, k, (i - 32) * 128:(i - 31) * 128], psb[bk][:, k * 128:(k + 1) * 128], A1c(k), B1c(k),
                       ALU.mult, ALU.add, reads=[tps[bk], t_cols], writes=[t_hcT])
            if lat and i % 4 == 3:
                blk = i // 4
                kb.dma("sp", hT_d[:, :, blk * 512:(blk + 1) * 512], hst[blk % 2][:], reads=[thst[blk % 2]], writes=[t_hd])
        if "p1" in dbg:
            d = dbg_tensor("hT", [128, 8, L], BF)
            kb.dma("sp", d, hT_d, reads=[t_hd])
            d = dbg_tensor("hcT", [128, 8, CT], BF)
            kb.dma("sp", d, hcT[:], reads=[t_hcT])
        kb.barrier()
    if stop_after == "p1":
        kb.finish()
        return nc, dbg_out


    hall_d = nc.dram_tensor("hall_d", [1024, 8448], BF, kind="Internal").ap()
    t_hall = [T(f"hall{i}") for i in range(8)]
    with ExitStack() as ph:
        def sb(name, shape, dt):
            return ph.enter_context(nc.sbuf_tensor(f"s{next_id()}_" + name, list(shape), dt))

        _bk = [0]

        def nb():
            _bk[0] = (_bk[0] + 1) % 8
            return _bk[0]

        t_hc = T("hfconst")
        gtab = sb("gtab", [128, 33, 3, 128], BF)
        d1f = sb("d1fp", [128, 66], BF)
        tv0 = sb("tv0", [128, 512], F32)
        negd = sb("negd", [128, 4], F32)
        ndoff = sb("ndoff", [128, 4, 8], F32)
        hybs = sb("hybs", [128, 2, 4], F32)
        for dst, nm in ((gtab, "gtab"), (d1f, "d1fp"), (tv0, "tv0"), (negd, "negdelta"), (ndoff, "ndoff"), (hybs, "hyb")):
            kb.dma("sp", dst[:], din[nm], writes=[t_hc])
        hdn2 = sb("hdn2", [65, L], BF)
        fw3a = sb("fw3a", [65, 2048], BF)
        t_h2 = T("hdn2")
        t_fw3 = T("fw3")
        kb.dma("pool", fw3a[0:64, :], din["fw3"], writes=[t_fw3])
        kb.dma("pool", fw3a[64:65, :], din["fb3"], writes=[t_fw3])
        memset("pool", hdn2[64:65, :], 1.0, [t_h2])
        zTs = sb("zTs", [33, L], F32)
        fw1s = sb("fw1s", [33, 64], F32)
        fw2s = sb("fw2s", [64, 64], F32)
        fcol = sb("fcol", [64, 5], F32)
        t_f = T("fmlp")
        kb.dma("sp", zTs[:], din["zT"], writes=[t_f])
        kb.dma("sp", fw1s[:], din["fw1"], writes=[t_f])
        kb.dma("sp", fw2s[:], din["fw2"], writes=[t_f])
        kb.dma("sp", fcol[:, 0:1], din["ffreq"], writes=[t_f])
        kb.dma("sp", fcol[:, 1:2], din["fb1"], writes=[t_f])
        kb.dma("sp", fcol[:, 2:3], din["fb2"], writes=[t_f])
        tt("dve", fcol[:, 3:4], fcol[:, 0:1], fcol[:, 1:2], ALU.mult, reads=[t_f], writes=[t_f])
        tt("dve", fcol[:, 4:5], fcol[:, 0:1], fcol[:, 2:3], ALU.mult, reads=[t_f], writes=[t_f])
        with ExitStack() as phm:
            def sbm(name, shape, dt):
                return phm.enter_context(nc.sbuf_tensor(f"s{next_id()}_" + name, list(shape), dt))

            halfpi = sbm("halfpi", [64, 1], F32)
            memset("dve", halfpi[:], PI / 2, [t_f])
            arg = [sbm(f"arg{i}", [64, 512], F32) for i in range(8)]
            s4 = [sbm(f"s4{i}", [64, 512], F32) for i in range(8)]
            c4 = [sbm(f"c4{i}", [64, 512], F32) for i in range(8)]
            hd1 = [sbm(f"hd1{i}", [64, 512], F32) for i in range(8)]
            t_m = [T() for _ in range(8)]

            def sin_layer_bf(ps_of, bias_col, out_of, t_out_of):
                for i_ in range(8):
                    ts("dve", arg[i_][:], ps[ps_of(i_)][0:64, :], fcol[:, 0:1], fcol[:, bias_col:bias_col + 1], ALU.mult, ALU.add,
                       reads=[tps[ps_of(i_)], t_f], writes=[t_m[i_]])
                for i_ in range(8):
                    act(s4[i_][:], arg[i_][:], AF.Sin, reads=[t_m[i_]], writes=[t_m[i_]], scale=0.25)
                    act(c4[i_][:], arg[i_][:], AF.Sin, reads=[t_m[i_], t_f], writes=[t_m[i_]], scale=0.25, bias=halfpi[:])
                for i_ in range(8):
                    tt("pool", arg[i_][:], s4[i_][:], s4[i_][:], ALU.mult, reads=[t_m[i_]], writes=[t_m[i_]])
                    tt("pool", s4[i_][:], s4[i_][:], c4[i_][:], ALU.mult, reads=[t_m[i_]], writes=[t_m[i_]])
                for i_ in range(8):
                    ts("dve", arg[i_][:], arg[i_][:], -2.0, 1.0, ALU.mult, ALU.add, reads=[t_m[i_]], writes=[t_m[i_]])
                    stt(out_of(i_), s4[i_][:], 4.0, arg[i_][:], ALU.mult, ALU.mult, reads=[t_m[i_]], writes=[t_m[i_], t_out_of(i_)])

            for blk in range(8):
                mm(ps[blk][0:64, :], fw1s[:], zTs[:, blk * 512:(blk + 1) * 512], start=True, stop=True, reads=[t_f], writes=[tps[blk]])
            sin_layer_bf(lambda i_: i_, 3, lambda i_: hd1[i_][:], lambda i_: t_m[i_])
            for blk in range(8):
                mm(ps[blk][0:64, :], fw2s[:], hd1[blk][:], start=True, stop=True, reads=[t_f, t_m[blk]], writes=[tps[blk]])
            sin_layer_bf(lambda i_: i_, 4, lambda i_: hdn2[0:64, i_ * 512:(i_ + 1) * 512], lambda i_: t_h2)
            kb.barrier()
        dec = [sb(f"dec{i}", [128, L], BF) for i in range(2)]
        t_dec = [T(), T()]
        Kf = [[sb(f"Kf{i}_{d_}", [128, L], BF) for d_ in range(2)] for i in range(2)]
        t_Kf = [[T(), T()], [T(), T()]]
        Ut = [sb(f"Ut{i}", [128, 32, 128], BF) for i in range(2)]
        tUt = [T(), T()]
        for i in range(2):
            memset("pool", Ut[i][32:64, :, :], 0.0, [tUt[i]])
            memset("pool", Ut[i][64:128, :, :], 0.0, [tUt[i]])
        A = sb("A", [128, 66, 128], BF)
        tA = [T() for _ in range(33)]
        H = [sb(f"H{i}", [128, 33, 2, 128], BF) for i in range(2)]
        tH = [[T() for _ in range(33)] for _ in range(2)]
        t_sdf = [T() for _ in range(4)]
        cnt = [0]

        def stage_k(r):
            cc, o = r // 2, r % 2
            pi_ = r % 2
            if o == 0:
                for b8 in range(8):
                    act(dec[cc % 2][:, b8 * 512:(b8 + 1) * 512], tv0[:], AF.Exp, reads=[t_hc], writes=[t_dec[cc % 2]],
                        scale=negd[:, cc:cc + 1], bias=ndoff[:, cc, b8:b8 + 1])
            for d_ in range(2):
                col0 = (o * 2 + d_) * 512 + cc * 128
                kf, tk = Kf[pi_][d_], t_Kf[pi_][d_]
                for blk in range(8):
                    sl = slice(blk * 512, (blk + 1) * 512)
                    b = nb()
                    mm(ps[b][:], fw3a[:, col0:col0 + 128], hdn2[:, sl], start=True, stop=True, reads=[t_fw3, t_h2], writes=[tps[b]])
                    tt("dve", kf[:, sl], ps[b][:], dec[cc % 2][:, sl], ALU.mult, reads=[tps[b], t_dec[cc % 2]], writes=[tk])
                if d_ == 0:
                    tt("dve", kf[:, 0:1], kf[:, 0:1], hybs[:, o, cc:cc + 1], ALU.add, reads=[tk, t_hc], writes=[tk])
                else:
                    memset("dve", kf[:, 0:1], 0.0, [tk])
                kb.dma("pool", sig_d[pi_ * 2 + d_], kf[:], reads=[tk], writes=[t_sdf[pi_ * 2 + d_]])

        def stage_f(r):
            pi_ = r % 2
            Hh, tHh = H[pi_], tH[pi_]
            for d_ in range(2):
                slot = pi_ * 2 + d_
                for g in range(4):
                    u, tu = Ut[g % 2], tUt[g % 2]
                    kb.dma("sp", u[0:32, :, :], sig_d[slot][g * 32:(g + 1) * 32, :].rearrange("c (a b) -> a c b", a=32), reads=[t_sdf[slot]], writes=[tu])
                    j = 0
                    while j < 32:
                        n = min(7, 32 - j)
                        b = nb()
                        for jj in range(n):
                            mm(ps[b][:, jj * 66:(jj + 1) * 66], u[:, j + jj, :], d1f[:], start=True, stop=True, reads=[tu, t_hc], writes=[tps[b]])
                        c0 = g * 32 + j
                        cp("act" if cnt[0] % 2 == 0 else "dve", A[:, :, c0:c0 + n], ps[b][:, 0:n * 66].rearrange("p (c k) -> p k c", k=66),
                           reads=[tps[b]], writes=tA)
                        cnt[0] += 1
                        j += n
                k1 = 0
                while k1 < 33:
                    n = 2 if k1 + 1 < 33 else 1
                    b = nb()
                    for u_ in range(n):
                        kk = k1 + u_
                        o_ = u_ * 256
                        ar, ai = A[:, kk, :], A[:, 33 + kk, :]
                        mm(ps[b][:, o_:o_ + 128], gtab[:, kk, 0, :], ar, start=True, stop=False, reads=[t_hc, tA[kk]], writes=[tps[b]])
                        mm(ps[b][:, o_:o_ + 128], gtab[:, kk, 2, :], ai, start=False, stop=True, reads=[t_hc, tA[kk]], writes=[tps[b]])
                        mm(ps[b][:, o_ + 128:o_ + 256], gtab[:, kk, 1, :], ar, start=True, stop=False, reads=[t_hc, tA[kk]], writes=[tps[b]])
                        mm(ps[b][:, o_ + 128:o_ + 256], gtab[:, kk, 0, :], ai, start=False, stop=True, reads=[t_hc, tA[kk]], writes=[tps[b]])
                    xv = ps[b][:, 0:n * 256].rearrange("p (k r c) -> p k r c", k=n, r=2)
                    wh = [tHh[k1 + u_] for u_ in range(n)]
                    if d_ == 0:
                        cp("act", Hh[:, k1:k1 + n, :, :], xv, reads=[tps[b]], writes=wh)
                    else:
                        tt("dve", Hh[:, k1:k1 + n, 0, :], Hh[:, k1:k1 + n, 0, :], xv[:, :, 0, :], ALU.add, reads=[tps[b]] + wh, writes=wh)
                        tt("dve", Hh[:, k1:k1 + n, 1, :], Hh[:, k1:k1 + n, 1, :], xv[:, :, 1, :], ALU.subtract, reads=[tps[b]] + wh, writes=wh)
                    k1 += n
            kb.dma("sp", hall_d[r * 128:(r + 1) * 128, :], Hh[:].rearrange("p a b c -> p (a b c)"), reads=tHh, writes=[t_hall[r]])

        stage_k(0)
        for r in range(8):
            if r + 1 < 8:
                stage_k(r + 1)
            stage_f(r)
        kb.barrier()
    if stop_after == "hf":
        kb.finish()
        return nc, dbg_out

    t_hd_r = T("hT_d_r")
    w_in_v = din["w_in"].rearrange("(k p) n -> p k n", p=128)
    w_sw_v = din["w_qk_sw"].rearrange("(k p) n -> p k n", p=128)

    def load_w(dst, src_view, c0, tl, ncols=128):
        kb.dma("pool", dst, src_view[:, :, c0:c0 + ncols], writes=[tl])

    with ExitStack() as ph:
        def sb(name, shape, dt):
            return ph.enter_context(nc.sbuf_tensor(f"s{next_id()}_" + name, list(shape), dt))

        NH = 0 if "p3" in skip else 4
        rcos = sb("rcos", [128, L], F32)
        rsin = sb("rsin", [128, L], F32)
        t_ropes = [T(f"rope{i}") for i in range(8)]
        for i in range(8):
            kb.dma("sp", rcos[:, i * 512:(i + 1) * 512], din["rope_cos"][:, i * 512:(i + 1) * 512], writes=[t_ropes[i]])
            kb.dma("sp", rsin[:, i * 512:(i + 1) * 512], din["rope_sin"][:, i * 512:(i + 1) * 512], writes=[t_ropes[i]])
        QT = [sb(f"QT{i}", [128, L], BF) for i in range(2)]
        KT = [[sb(f"KT{i}_{m_}", [128, LK], BF) for m_ in range(2)] for i in range(2)]
        V = [sb(f"V{i}", [128, 34, 129], BF) for i in range(2)]
        t_Q, t_K, t_V = [T(), T()], [T(), T()], [T(), T()]
        for i in range(2):
            memset("pool", KT[i][0][64:128, :], 0.0, [t_K[i]])
            memset("pool", KT[i][1][0:64, :], 0.0, [t_K[i]])
            memset("pool", V[i][:, :, 128:129], 1.0, [t_V[i]])
        wts = [[sb(f"aw{i}_{j}", [128, 8, 128], BF) for j in range(5)] for i in range(2)]
        twts = [[T() for _ in range(5)] for _ in range(2)]
        hb = [sb(f"hb{i}", [128, 8, 512], BF) for i in range(2)]
        thb = [T(), T()]
        rt = [sb(f"rt{i}", [128, 512], F32) for i in range(4)]
        trt = [T() for _ in range(4)]
        E = [sb(f"E{i}", [128, 1024], BF) for i in range(4)]
        tE = [T() for _ in range(4)]
        attst = [sb(f"attst{i}", [128, 512], BF) for i in range(2)]
        t_attst = [T(), T()]
        fin = sb("fin", [128, 16], F32)
        accs = sb("accs", [128, 1161], F32)
        oa = sb("oa", [128, 4, 128], F32)
        ob = sb("ob", [128, 4, 128], F32)
        on = sb("on", [128, 4, 128], BF)
        t_fin, t_on, t_accs, t_oa, t_ob = T(), T(), T(), T(), T()
        t_ad = T("aT_d")
        hbcnt = [0]

        def prologue(h, banks):
            bi = h % 2
            w_, tw_ = wts[bi], twts[bi]
            brr = [0]

            def nbk():
                brr[0] += 1
                return banks[brr[0] % len(banks)]

            load_w(w_[0][:], w_in_v, OFF_Q + h * 128, tw_[0])
            load_w(w_[1][:], w_sw_v, h * 128, tw_[1])
            load_w(w_[2][:], w_in_v, OFF_K + h * 128, tw_[2])
            load_w(w_[3][:], w_sw_v, 512 + h * 128, tw_[3])
            load_w(w_[4][:], w_in_v, OFF_V + h * 128, tw_[4])
            b = nbk()
            for k in range(8):
                mm(ps[b][:, 0:CT], w_[2][:, k, :], hcT[:, k, :], start=(k == 0), stop=(k == 7), reads=[tw_[2], t_hcT], writes=[tps[b]])
            cp("dve", KT[bi][0][0:64, L:LK], ps[b][0:64, 0:CT], reads=[tps[b]], writes=[t_K[bi]])
            cp("dve", KT[bi][1][64:128, L:LK], ps[b][64:128, 0:CT], reads=[tps[b]], writes=[t_K[bi]])
            yield
            b = nbk()
            for s_ in range(2):
                for k in range(8):
                    mm(ps[b][:, s_ * 128:(s_ + 1) * 128], hcT[:, k, s_ * 128:(s_ + 1) * 128], w_[4][:, k, :], start=(k == 0), stop=(k == 7),
                       reads=[tw_[4], t_hcT], writes=[tps[b]])
            cp("dve", V[bi][:, 32:34, 0:128], ps[b][:, 0:256].rearrange("p (s v) -> p s v", s=2), reads=[tps[b]], writes=[t_V[bi]])
            yield
            for blk in range(8):
                hi = hbcnt[0] % 2
                hbcnt[0] += 1
                hbb, th = hb[hi], thb[hi]
                kb.dma("sp", hbb[:], hT_d[:, :, blk * 512:(blk + 1) * 512], reads=[t_hd], writes=[th])
                sl = slice(blk * 512, (blk + 1) * 512)
                for qk in range(2):
                    for gg in range(2):
                        g = 2 * qk + gg
                        b = nbk()
                        for k in range(8):
                            mm(ps[b][:], w_[g][:, k, :], hbb[:, k, :], start=(k == 0), stop=(k == 7), reads=[tw_[g], th], writes=[tps[b]])
                        tt("dve", rt[g][:], ps[b][:], (rcos if gg == 0 else rsin)[:, sl], ALU.mult, reads=[tps[b], t_ropes[blk]], writes=[trt[g]])
                        yield
                    r0, r1 = rt[2 * qk], rt[2 * qk + 1]
                    if qk == 0:
                        tt("pool", QT[bi][:, sl], r0[:], r1[:], ALU.add, reads=[trt[0], trt[1]], writes=[t_Q[bi]])
                    else:
                        tt("pool", KT[bi][0][0:64, sl], r0[0:64, :], r1[0:64, :], ALU.add, reads=[trt[2], trt[3]], writes=[t_K[bi]])
                        tt("pool", KT[bi][1][64:128, sl], r0[64:128, :], r1[64:128, :], ALU.add, reads=[trt[2], trt[3]], writes=[t_K[bi]])
                b = nbk()
                for s_ in range(4):
                    for k in range(8):
                        mm(ps[b][:, s_ * 128:(s_ + 1) * 128], hbb[:, k, s_ * 128:(s_ + 1) * 128], w_[4][:, k, :], start=(k == 0), stop=(k == 7),
                           reads=[tw_[4], th], writes=[tps[b]])
                cp("dve", V[bi][:, blk * 4:blk * 4 + 4, 0:128], ps[b][:].rearrange("p (s v) -> p s v", s=4), reads=[tps[b]], writes=[t_V[bi]])
                yield

        def bc_last(ap2d, n):
            return bass.AP(ap2d.tensor, ap2d.offset, [list(ap2d.ap[0]), list(ap2d.ap[1]), [0, n]])

        items = [(qb, kc) for qb in range(8) for kc in range(34)]

        def head_loop(h, gen):
            bi = h % 2
            Qh, Kh, Vh = QT[bi], KT[bi], V[bi]

            def emit_S(idx):
                qb, kc = items[idx]
                b0 = (idx % 2) * 2
                for m_ in range(2):
                    pr = slice(64 * m_, 64 * m_ + 64)
                    mm(ps[b0 + m_][:], Kh[m_][pr, kc * 128:(kc + 1) * 128], Qh[pr, qb * 512:(qb + 1) * 512], start=True, stop=True,
                       reads=[t_K[bi], t_Q[bi]], writes=[tps[b0 + m_]])
                ei = idx % 4
                act(E[ei][:], psall[:, b0 * 512:(b0 + 2) * 512], AF.Exp, reads=[tps[b0], tps[b0 + 1]], writes=[tE[ei]], scale=0.125)

            def emit_AV(idx):
                qb, kc = items[idx]
                ei = idx % 4
                for m_ in range(2):
                    for s_ in range(4):
                        slot = m_ * 4 + s_
                        bk, c0 = 4 + slot // 3, (slot % 3) * 129
                        mm(ps[bk][:, c0:c0 + 129], E[ei][:, m_ * 512 + s_ * 128:m_ * 512 + (s_ + 1) * 128], Vh[:, kc, :],
                           start=(kc == 0 and slot % 3 == 0), stop=(kc == 33 and (slot % 3 == 2 or slot == 7)),
                           reads=[tE[ei], t_V[bi]], writes=[tps[bk]])
                if kc == 33:
                    finalize(qb)
                    pend.append(qb)
                if kc == 10 and pend:
                    finalize_tr(pend.pop())

            def finalize(qb):
                qsl = slice(qb * 512, (qb + 1) * 512)
                cp("dve", accs[:, 0:387], ps[4][:, 0:387], reads=[tps[4]], writes=[t_accs])
                cp("dve", accs[:, 387:774], ps[5][:, 0:387], reads=[tps[5]], writes=[t_accs])
                cp("dve", accs[:, 774:1032], ps[6][:, 0:258], reads=[tps[6]], writes=[t_accs])
                av = accs[:, 0:1032].rearrange("p (s c) -> p s c", c=129)
                recip(fin[:, 0:8], av[:, :, 128], reads=[t_accs], writes=[t_fin])
                ts("dve", fin[:, 4:8], fin[:, 4:8], neglam[:], None, ALU.mult, None, reads=[t_fin, t_cols], writes=[t_fin])
                tt("dve", oa[:], av[:, 0:4, 0:128], bc_last(fin[:, 0:4], 128), ALU.mult, reads=[t_accs, t_fin], writes=[t_oa])
                tt("dve", ob[:], av[:, 4:8, 0:128], bc_last(fin[:, 4:8], 128), ALU.mult, reads=[t_accs, t_fin], writes=[t_ob])
                tt("pool", ob[:], ob[:], oa[:], ALU.add, reads=[t_oa, t_ob], writes=[t_ob])
                tt("pool", oa[:], ob[:], ob[:], ALU.mult, reads=[t_ob], writes=[t_oa])
                rsum(fin[:, 8:12], oa[:], reads=[t_oa], writes=[t_fin])
                rstd_of(fin[:, 8:12], fin[:, 12:16], 1.0 / 128, t_fin)
                tt("dve", ob[:], ob[:], bc_last(fin[:, 12:16], 128), ALU.mult, reads=[t_ob, t_fin], writes=[t_ob])
                sg_b = bass.AP(sgrow[:].tensor, sgrow[:].offset, [list(sgrow[:].ap[0]), [0, 4], [1, 128]])
                tt("dve", on[:], ob[:], sg_b, ALU.mult, reads=[t_ob, t_rows], writes=[t_on])

            def finalize_tr(qb):
                qsl = slice(qb * 512, (qb + 1) * 512)
                for s_ in range(4):
                    tr(psb[7][:, s_ * 128:(s_ + 1) * 128], on[:, s_, :], ident_bf[:], reads=[t_on, t_const], writes=[tps[7]])
                cp("dve", attst[qb % 2][:], psb[7][:, 0:512], reads=[tps[7]], writes=[t_attst[qb % 2]])
                kb.dma("sp", aT_d[:, h, qsl], attst[qb % 2][:], reads=[t_attst[qb % 2]], writes=[t_ad])

            pend = []
            emit_S(0)
            emit_S(1)
            for idx in range(len(items)):
                if idx + 2 < len(items):
                    emit_S(idx + 2)
                emit_AV(idx)
                if gen is not None and idx % 6 == 3 and items[idx][1] not in (32, 33, 0):
                    next(gen, None)
            while pend:
                finalize_tr(pend.pop())
            if gen is not None:
                for _ in gen:
                    pass

        if NH:
            for _ in prologue(0, [0, 1, 2, 3, 4, 5, 6, 7]):
                pass
        for h in range(NH):
            gen = prologue(h + 1, [7]) if h + 1 < NH else None
            head_loop(h, gen)
        if "p3" in dbg:
            d = dbg_tensor("attT", [128, 4, L], BF)
            kb.dma("sp", d, aT_d, reads=[t_ad])
        kb.barrier()
    if stop_after == "p3":
        kb.finish()
        return nc, dbg_out

    with ExitStack() as ph:
        def sb(name, shape, dt):
            return ph.enter_context(nc.sbuf_tensor(f"s{next_id()}_" + name, list(shape), dt))

        _bk = [0]

        def nb():
            _bk[0] = (_bk[0] + 1) % 8
            return _bk[0]

        t_hc = T("hyconst")
        gtab = sb("gtab", [128, 33, 3, 128], BF)
        ttab = sb("ttab", [66, 2, 128, 32], BF)
        etab = sb("etab", [128, 4, 2, 32], BF)
        d1f = sb("d1fp", [128, 66], BF)
        cwt = sb("cwt", [128, 12, 3], F32)
        cbt = sb("cbt", [128, 12], F32)
        for dst, nm in ((gtab, "gtab"), (ttab, "ttab"), (etab, "etab"), (d1f, "d1fp"), (cwt, "cw"), (cbt, "cb")):
            kb.dma("sp", dst[:], din[nm], writes=[t_hc])
        Z = [sb(f"Z{i}", [128, L], BF) for i in range(3)]
        tZ = [T(f"Z{i}") for i in range(3)]
        H = sb("H", [128, 33, 2, 128], BF)
        tH = [T() for _ in range(33)]
        t_sd = [T(f"sig{i}") for i in range(5)]
        t_zd = T("zT_d")
        Ut = [sb(f"Ut{i}", [128, 32, 128], BF) for i in range(2)]
        tUt = [T(), T()]
        for i in range(2):
            memset("pool", Ut[i][32:64, :, :], 0.0, [tUt[i]])
            memset("pool", Ut[i][64:128, :, :], 0.0, [tUt[i]])
        A = sb("A", [128, 66, 128], BF)
        tA = [T() for _ in range(33)]
        f1cnt = [0]

        def f1_pre(sig, t_sig, slot):
            kb.dma("pool", sig_d[slot], sig, reads=[t_sig], writes=[t_sd[slot]])
            for g in range(2):
                kb.dma("pool", Ut[g][0:32, :, :], sig_d[slot][g * 32:(g + 1) * 32, :].rearrange("c (a b) -> a c b", a=32),
                       reads=[t_sd[slot]], writes=[tUt[g]])

        def f1_part(sig, t_sig, slot, pre=False, act_only=False):
            if not pre:
                f1_pre(sig, t_sig, slot)
            for g in range(4):
                u, tu = Ut[g % 2], tUt[g % 2]
                if g >= 2:
                    kb.dma("pool", u[0:32, :, :], sig_d[slot][g * 32:(g + 1) * 32, :].rearrange("c (a b) -> a c b", a=32),
                           reads=[t_sd[slot]], writes=[tu])
                j = 0
                while j < 32:
                    n = min(7, 32 - j)
                    b = nb()
                    for jj in range(n):
                        mm(ps[b][:, jj * 66:(jj + 1) * 66], u[:, j + jj, :], d1f[:], start=True, stop=True,
                           reads=[tu, t_hc], writes=[tps[b]])
                    c0 = g * 32 + j
                    eng = "act" if (act_only or f1cnt[0] % 2 == 0) else "dve"
                    f1cnt[0] += 1
                    cp(eng, A[:, :, c0:c0 + n], ps[b][:, 0:n * 66].rearrange("p (c k) -> p k c", k=66), reads=[tps[b]], writes=tA)
                    j += n

        for cc in range(4):
            with ExitStack() as pa:
                def sba(name, shape, dt):
                    return pa.enter_context(nc.sbuf_tensor(f"s{next_id()}_" + name, list(shape), dt))

                wts = [sba(f"hw{i}", [128, 8, 128], BF) for i in range(3)]
                twts = [T() for _ in range(3)]
                hb = [sba(f"hhb{i}", [128, 8, 512], BF) for i in range(2)]
                thb = [T(), T()]
                Ur = [sba(f"Ur{i}", [128, L + 2], BF) for i in range(3)]
                tUr = [T() for _ in range(3)]
                ctmp = sba("ctmp", [128, L], F32)
                t_ct = T()
                for s in range(3):
                    load_w(wts[s][:], w_in_v, s * 512 + cc * 128, twts[s])
                    memset("pool", Ur[s][:, 0:1], 0.0, [tUr[s]])
                    memset("pool", Ur[s][:, L + 1:L + 2], 0.0, [tUr[s]])
                def proj_pass(sigs, off):
                    for blk in range(8):
                        hbb, th = hb[(blk + off) % 2], thb[(blk + off) % 2]
                        kb.dma("sp", hbb[:], hT_d[:, :, blk * 512:(blk + 1) * 512], reads=[t_hd], writes=[th])
                        for s in sigs:
                            b = nb()
                            for k in range(8):
                                mm(ps[b][:], wts[s][:, k, :], hbb[:, k, :], start=(k == 0), stop=(k == 7), reads=[twts[s], th], writes=[tps[b]])
                            cp("act", Ur[s][:, 1 + blk * 512:1 + (blk + 1) * 512], ps[b][:], reads=[tps[b]], writes=[tUr[s]])

                def sconv(s):
                    slot = s * 4 + cc
                    act(ctmp[:], Ur[s][:, 1:L + 1], AF.Identity, reads=[tUr[s], t_hc], writes=[t_ct],
                        scale=cwt[:, slot, 1:2], bias=cbt[:, slot:slot + 1])
                    stt(ctmp[:], Ur[s][:, 0:L], cwt[:, slot, 0:1], ctmp[:], ALU.mult, ALU.add, reads=[tUr[s], t_hc, t_ct], writes=[t_ct])
                    stt(Z[s][:], Ur[s][:, 2:L + 2], cwt[:, slot, 2:3], ctmp[:], ALU.mult, ALU.add, reads=[tUr[s], t_hc, t_ct], writes=[tZ[s]])

                proj_pass([0, 1, 2], 0)
                sconv(0)
                f1_pre(Z[0][:], tZ[0], 4)
                sconv(1)
                f1_part(Z[0][:], tZ[0], 4, pre=True, act_only=True)
                sconv(2)
                if "p2a" in dbg and cc == 0:
                    for s in range(3):
                        d = dbg_tensor(f"Z{s}", [128, L], BF)
                        kb.dma("sp", d, Z[s][:], reads=[tZ[s]])
                kb.barrier()
            with ExitStack() as pb:
                def sbb(name, shape, dt):
                    return pb.enter_context(nc.sbuf_tensor(f"s{next_id()}_" + name, list(shape), dt))

                Y = sbb("Y", [128, 128, 66], BF)
                tY = [T() for _ in range(33)]
                Pqs = [sbb(f"Pq{i}", [66, 64, 128], BF) for i in range(2)]
                t_Pqs = [T(), T()]
                pw = [sbb(f"pw{i}", [128, 2, 2, 128], F32) for i in range(4)]
                tpw = [T() for _ in range(4)]

                def f2_part(consumer):
                    k1 = 0
                    while k1 < 33:
                        n = 2 if k1 + 1 < 33 else 1
                        b = nb()
                        for u_ in range(n):
                            kk = k1 + u_
                            o_ = u_ * 256
                            ar, ai = A[:, kk, :], A[:, 33 + kk, :]
                            mm(ps[b][:, o_:o_ + 128], gtab[:, kk, 0, :], ar, start=True, stop=False, reads=[t_hc, tA[kk]], writes=[tps[b]])
                            mm(ps[b][:, o_:o_ + 128], gtab[:, kk, 2, :], ai, start=False, stop=True, reads=[t_hc, tA[kk]], writes=[tps[b]])
                            mm(ps[b][:, o_ + 128:o_ + 256], gtab[:, kk, 1, :], ar, start=True, stop=False, reads=[t_hc, tA[kk]], writes=[tps[b]])
                            mm(ps[b][:, o_ + 128:o_ + 256], gtab[:, kk, 0, :], ai, start=False, stop=True, reads=[t_hc, tA[kk]], writes=[tps[b]])
                        consumer(k1, n, b)
                        k1 += n

                def conv(o, sig, t_sig, xm, t_xm, zout, t_zout):
                    r_ = 2 * cc + o
                    kb.dma("sp", H[:].rearrange("p a b c -> p (a b c)"), hall_d[r_ * 128:(r_ + 1) * 128, :], reads=[t_hall[r_]], writes=tH)
                    if "p2h" in dbg and cc == 0:
                        d = dbg_tensor(f"H{o}", [128, 33 * 2 * 128], BF)
                        kb.dma("sp", d, H[:].rearrange("p a b c -> p (a b c)"), reads=tH)

                    def cons_d(k1, n, b):
                        i0 = ((k1 // 2) % 2) * 2
                        p1, p2 = pw[i0], pw[i0 + 1]
                        xv = ps[b][:, 0:n * 256].rearrange("p (k r c) -> p k r c", k=n, r=2)
                        hb_ = H[:, k1, 0, :]
                        pst = list(hb_.ap[0])
                        hr = bass.AP(hb_.tensor, hb_.offset, [pst, [256, n], [0, 2], [1, 128]])
                        hi = bass.AP(hb_.tensor, hb_.offset + 128, [pst, [256, n], [0, 2], [1, 128]])
                        rd = [tps[b]] + [tH[k1 + u_] for u_ in range(n)]
                        tt("dve", p1[:, 0:n, :, :], xv, hr, ALU.mult, reads=rd, writes=[tpw[i0]])
                        tt("dve", p2[:, 0:n, :, :], xv, hi, ALU.mult, reads=rd, writes=[tpw[i0 + 1]])

                        def ck(t_, r_):
                            v = t_[:, 0:n, r_, :]
                            return bass.AP(v.tensor, v.offset, [list(v.ap[0]), [1, 128], [256, n]])

                        wy = [tY[k1 + u_] for u_ in range(n)]
                        tt("dve", Y[:, :, k1:k1 + n], ck(p1, 0), ck(p2, 1), ALU.subtract, reads=[tpw[i0], tpw[i0 + 1]], writes=wy)
                        tt("pool", Y[:, :, 33 + k1:33 + k1 + n], ck(p2, 0), ck(p1, 1), ALU.add, reads=[tpw[i0], tpw[i0 + 1]], writes=wy)

                    if o == 1:
                        f1_part(sig, t_sig, 4)
                    f2_part(cons_d)
                    cnt = 0
                    zv = zout.rearrange("c (a b) -> c b a", a=32)
                    xv_ = xm.rearrange("c (a b) -> c b a", a=32)
                    def i1_stage(q):
                        Pq, t_Pq = Pqs[q % 2], t_Pqs[q % 2]
                        for c0 in range(0, 128, 8):
                            b = nb()
                            for jj in range(8):
                                mm(ps[b][0:66, jj * 64:(jj + 1) * 64], Y[:, c0 + jj, :], etab[:, q, :, :].rearrange("p e n -> p (e n)"),
                                   start=True, stop=True, reads=tY + [t_hc], writes=[tps[b]])
                            eng = "dve" if (c0 // 8) % 4 == 3 else "act"
                            cp(eng, Pq[:, :, c0:c0 + 8], ps[b][0:66, :].rearrange("p (c k) -> p k c", k=64), reads=[tps[b]], writes=[t_Pq])

                    def i2_stage(q):
                        Pq, t_Pq = Pqs[q % 2], t_Pqs[q % 2]
                        for hh in range(2):
                            b = nb()
                            for j in range(16):
                                n2l = hh * 16 + j
                                n2 = q * 32 + n2l
                                for e_ in range(2):
                                    mm(ps[b][:, j * 32:(j + 1) * 32], Pq[:, e_ * 32 + n2l, :], ttab[:, e_, n2, :], start=(e_ == 0), stop=(e_ == 1),
                                       reads=[t_Pq, t_hc], writes=[tps[b]])
                            n20 = q * 32 + hh * 16
                            tt("dve", zv[:, n20:n20 + 16, :], ps[b][:].rearrange("p (b a) -> p b a", a=32), xv_[:, n20:n20 + 16, :], ALU.mult,
                               reads=[tps[b], t_xm], writes=[t_zout])

                    i1_stage(0)
                    for q in range(4):
                        if q + 1 < 4:
                            i1_stage(q + 1)
                        i2_stage(q)

                conv(0, Z[0][:], tZ[0], Z[1][:], tZ[1], Z[0][:], tZ[0])
                if "p2c" in dbg and cc == 0:
                    d = dbg_tensor("z1", [128, L], BF)
                    kb.dma("sp", d, Z[0][:], reads=[tZ[0]])
                conv(1, Z[0][:], tZ[0], Z[2][:], tZ[2], Z[1][:], tZ[1])
                kb.dma("sp", zT_d[:, cc, :], Z[1][:], reads=[tZ[1]], writes=[t_zd])
                kb.barrier()
            if stop_after == "p2c0":
                break
        if "p2" in dbg:
            d = dbg_tensor("zT", [128, 4, L], BF)
            kb.dma("sp", d, zT_d, reads=[t_zd])
        kb.barrier()
    if stop_after in ("p2", "p2c0"):
        kb.finish()
        return nc, dbg_out

    t_md = T("mT_d")
    with ExitStack() as ph:
        def sb(name, shape, dt):
            return ph.enter_context(nc.sbuf_tensor(f"s{next_id()}_" + name, list(shape), dt))

        _bk = [0]

        def nb():
            _bk[0] = (_bk[0] + 1) % 8
            return _bk[0]

        zTs = sb("zTs", [128, 4, L], BF)
        aTs = sb("aTs", [128, 4, L], BF)
        wgt = sb("wgt", [128, 8, 2048], BF)
        whu = sb("whu", [128, 4, D], BF)
        wau = sb("wau", [128, 4, D], BF)
        t_wg = [T(f"wg{j}") for j in range(16)]
        t_wh = [T(f"wh{j}") for j in range(8)]
        t_wa = [T(f"wa{j}") for j in range(8)]
        t_zb = [T(f"zb{i}") for i in range(8)]
        t_ab = [T(f"ab{i}") for i in range(8)]
        whv = din["w_hy_up"].rearrange("(k p) n -> p k n", p=128)
        wav = din["w_att_up"].rearrange("(k p) n -> p k n", p=128)
        for j in range(8):
            for half in range(2):
                jj = half * 8 + j
                load_w(wgt[:, :, jj * 128:(jj + 1) * 128], w_in_v, OFF_G + jj * 128, t_wg[jj])
            kb.dma("pool", whu[:, :, j * 128:(j + 1) * 128], whv[:, :, j * 128:(j + 1) * 128], writes=[t_wh[j]])
            kb.dma("pool", wau[:, :, j * 128:(j + 1) * 128], wav[:, :, j * 128:(j + 1) * 128], writes=[t_wa[j]])
        for i in range(8):
            kb.dma("sp", zTs[:, :, i * 512:(i + 1) * 512], zT_d[:, :, i * 512:(i + 1) * 512], writes=[t_zb[i]])
            kb.dma("sp", aTs[:, :, i * 512:(i + 1) * 512], aT_d[:, :, i * 512:(i + 1) * 512], writes=[t_ab[i]])
        hb = [sb(f"mhb{i}", [128, 8, 512], BF) for i in range(2)]
        thb = [T(), T()]
        mst = [sb(f"mst{i}", [128, 8, 512], BF) for i in range(2)]
        tmst = [T(), T()]
        sg = [sb(f"sg{i}", [128, 512], F32) for i in range(4)]
        tsg = [T() for _ in range(4)]
        mm_ = [sb(f"mm{i}", [128, 512], F32) for i in range(4)]
        tmm = [T() for _ in range(4)]
        kb.dma("sp", hb[0][:], hT_d[:, :, 0:512], reads=[t_hd], writes=[thb[0]])
        for blk in range(8):
            sl = slice(blk * 512, (blk + 1) * 512)
            hbb, th = hb[blk % 2], thb[blk % 2]
            if blk + 1 < 8:
                kb.dma("sp", hb[(blk + 1) % 2][:], hT_d[:, :, (blk + 1) * 512:(blk + 2) * 512], reads=[t_hd], writes=[thb[(blk + 1) % 2]])
            for j in range(8):
                bg1, bg2, by1, by2 = nb(), nb(), nb(), nb()
                for k in range(8):
                    mm(ps[bg1][:], wgt[:, k, j * 128:(j + 1) * 128], hbb[:, k, :], start=(k == 0), stop=(k == 7), reads=[t_wg[j], th], writes=[tps[bg1]])
                for k in range(8):
                    mm(ps[bg2][:], wgt[:, k, 1024 + j * 128:1024 + (j + 1) * 128], hbb[:, k, :], start=(k == 0), stop=(k == 7), reads=[t_wg[8 + j], th], writes=[tps[bg2]])
                for k in range(4):
                    mm(ps[by1][:], whu[:, k, j * 128:(j + 1) * 128], zTs[:, k, sl], start=(k == 0), stop=(k == 3), reads=[t_wh[j], t_zb[blk]], writes=[tps[by1]])
                for k in range(4):
                    mm(ps[by2][:], wau[:, k, j * 128:(j + 1) * 128], aTs[:, k, sl], start=(k == 0), stop=(k == 3), reads=[t_wa[j], t_ab[blk]], writes=[tps[by2]])
                i0 = (j % 2) * 2
                act(sg[i0][:], ps[bg1][:], AF.Sigmoid, reads=[tps[bg1]], writes=[tsg[i0]])
                act(sg[i0 + 1][:], ps[bg2][:], AF.Sigmoid, reads=[tps[bg2]], writes=[tsg[i0 + 1]])
                tt("dve", mm_[i0][:], ps[by1][:], sg[i0][:], ALU.mult, reads=[tps[by1], tsg[i0]], writes=[tmm[i0]])
                tt("dve", mm_[i0 + 1][:], ps[by2][:], sg[i0 + 1][:], ALU.mult, reads=[tps[by2], tsg[i0 + 1]], writes=[tmm[i0 + 1]])
                tt("pool", mst[blk % 2][:, j, :], mm_[i0][:], mm_[i0 + 1][:], ALU.add, reads=[tmm[i0], tmm[i0 + 1]], writes=[tmst[blk % 2]])
            kb.dma("sp", mT_d[:, :, sl], mst[blk % 2][:], reads=[tmst[blk % 2]], writes=[t_md])
        if "p4" in dbg:
            d = dbg_tensor("mT", [128, 8, L], BF)
            kb.dma("sp", d, mT_d, reads=[t_md])
        kb.barrier()
    if stop_after == "p4":
        kb.finish()
        return nc, dbg_out

    with ExitStack() as ph:
        def sb(name, shape, dt):
            return ph.enter_context(nc.sbuf_tensor(f"s{next_id()}_" + name, list(shape), dt))

        _bk = [0]

        def nb():
            _bk[0] = (_bk[0] + 1) % 8
            return _bk[0]

        wout = sb("wout", [128, 8, D], BF)
        wfg = sb("wfg", [128, 8, DFF], BF)
        wfu = sb("wfu", [128, 8, DFF], BF)
        wfd = sb("wfd", [128, NFF, D], BF)
        t_wo, t_wg, t_wu, t_wd = T("wo"), T("wg"), T("wu"), T("wd")
        kb.dma("pool", wout[:], din["w_out"].rearrange("(k p) n -> p k n", p=128), writes=[t_wo])
        t_wgs = [T(f"wfg{g}") for g in range(6)]
        t_wus = [T(f"wfu{g}") for g in range(6)]
        wfg_v = din["w_fg"].rearrange("(k p) n -> p k n", p=128)
        wfu_v = din["w_fu"].rearrange("(k p) n -> p k n", p=128)
        for g in range(6):
            c0 = g * 512
            w_ = min(512, DFF - c0)
            kb.dma("pool", wfg[:, :, c0:c0 + w_], wfg_v[:, :, c0:c0 + w_], writes=[t_wgs[g]])
            kb.dma("pool", wfu[:, :, c0:c0 + w_], wfu_v[:, :, c0:c0 + w_], writes=[t_wus[g]])
        for j in range(0, NFF, 2):
            kb.dma("pool", wfd[:, j:j + 2, :], din["w_fd"][j * 128:(j + 2) * 128, :].rearrange("(k p) n -> p k n", p=128), writes=[t_wd])
        xts = [sb(f"fx{i}", [128, D], F32) for i in range(3)]
        txt = [T(), T(), T()]
        mts = [sb(f"fm{i}", [128, 8, 128], BF) for i in range(2)]
        tmt = [T(), T()]
        tmp = sb("ftmp", [128, D], F32)
        t_tmp = T()
        junk = sb("fjunk", [128, D], BF)
        t_junk = T()
        xs = sb("fxs", [128, D], BF)
        t_xs = T()
        hfT = [sb(f"hfT{i}", [128, 8, 128], BF) for i in range(2)]
        t_hf = [T(), T()]
        aT = sb("aT", [128, NFF, 128], BF)
        t_aT = T()
        sgt = [sb(f"sgt{i}", [128, 512], F32) for i in range(2)]
        tsgt = [T(), T()]
        atm = sb("atm", [128, DFF], BF)
        t_atm = T()
        fin = sb("ffin", [128, 16], F32)
        t_fin = [T(), T(), T()]
        t_out = T("out")

        def norm_resid(banks, xt, tx, grow, c0, tf):
            for n in range(2):
                act(junk[:, n * 512:(n + 1) * 512], ps[banks[n]][:], AF.Square, reads=[tps[banks[n]]], writes=[t_junk, tf],
                    accum_out=fin[:, c0 + n:c0 + n + 1])
            tt("dve", fin[:, c0 + 2:c0 + 3], fin[:, c0:c0 + 1], fin[:, c0 + 1:c0 + 2], ALU.add, reads=[tf], writes=[tf])
            rstd_of(fin[:, c0 + 2:c0 + 3], fin[:, c0 + 3:c0 + 4], 1.0 / D, tf)
            for n in range(2):
                hs = slice(n * 512, (n + 1) * 512)
                stt(tmp[:, hs], ps[banks[n]][:], fin[:, c0 + 3:c0 + 4], grow[:, hs], ALU.mult, ALU.mult,
                    reads=[tps[banks[n]], tf, t_rows], writes=[t_tmp])
            tt("pool", xt[:], xt[:], tmp[:], ALU.add, reads=[t_tmp, tx], writes=[tx])

        def s1a(i):
            tsl = slice(i * 128, (i + 1) * 128)
            xt, tx = xts[i % 3], txt[i % 3]
            mt, tm = mts[i % 2], tmt[i % 2]
            kb.dma("sp", xt[:], din["x"][tsl, :], writes=[tx])
            kb.dma("sp", mt[:], mT_d[:, :, tsl], reads=[t_md], writes=[tm])
            for n in range(2):
                for k in range(8):
                    mm(ps[n][:], mt[:, k, :], wout[:, k, n * 512:(n + 1) * 512], start=(k == 0), stop=(k == 7),
                       reads=[tm, t_wo], writes=[tps[n]])

        def s1b(i):
            xt, tx = xts[i % 3], txt[i % 3]
            norm_resid([0, 1], xt, tx, g1row, 0, t_fin[0])
            act(junk[:], xt[:], AF.Square, reads=[tx], writes=[t_junk, t_fin[1]], accum_out=fin[:, 4:5])
            rstd_of(fin[:, 4:5], fin[:, 5:6], 1.0 / D, t_fin[1])
            act(xs[:], xt[:], AF.Copy, reads=[tx, t_fin[1]], writes=[t_xs], scale=fin[:, 5:6])

        def s1c(i):
            for k in range(8):
                tr(psb[4][:, k * 128:(k + 1) * 128], xs[:, k * 128:(k + 1) * 128], ident_bf[:], reads=[t_xs, t_const], writes=[tps[4]])
            for k in range(8):
                ts("dve", hfT[i % 2][:, k, :], psb[4][:, k * 128:(k + 1) * 128], A2(k), B2(k), ALU.mult, ALU.add,
                   reads=[tps[4], t_cols], writes=[t_hf[i % 2]])

        def s2a(i):
            h_, th_ = hfT[i % 2], t_hf[i % 2]
            pairs = [(5, 6), (7, 4)]
            for g in range(6):
                c0 = g * 512
                w_ = min(512, DFF - c0)
                bg, bu = pairs[g % 2]
                for k in range(8):
                    mm(ps[bg][:, 0:w_], h_[:, k, :], wfg[:, k, c0:c0 + w_], start=(k == 0), stop=(k == 7), reads=[t_wgs[g], th_], writes=[tps[bg]])
                for k in range(8):
                    mm(ps[bu][:, 0:w_], h_[:, k, :], wfu[:, k, c0:c0 + w_], start=(k == 0), stop=(k == 7), reads=[t_wus[g], th_], writes=[tps[bu]])
                act(sgt[g % 2][:, 0:w_], ps[bg][:, 0:w_], AF.Silu, reads=[tps[bg]], writes=[tsgt[g % 2]])
                tt("dve", atm[:, c0:c0 + w_], ps[bu][:, 0:w_], sgt[g % 2][:, 0:w_], ALU.mult, reads=[tps[bu], tsgt[g % 2]], writes=[t_atm])
            j = 0
            bi = 0
            while j < NFF:
                n = min(8, NFF - j)
                b = (7, 4, 5)[bi % 3]
                bi += 1
                for jj in range(n):
                    tr(psb[b][:, jj * 128:(jj + 1) * 128], atm[:, (j + jj) * 128:(j + jj + 1) * 128], ident_bf[:], reads=[t_atm, t_const], writes=[tps[b]])
                cp("dve" if bi % 2 else "act", aT[:, j:j + n, :], psb[b][:, 0:n * 128].rearrange("p (j t) -> p j t", t=128), reads=[tps[b]], writes=[t_aT])
                j += n

        def s2b(i):
            tsl = slice(i * 128, (i + 1) * 128)
            xt, tx = xts[i % 3], txt[i % 3]
            for n in range(2):
                for j in range(NFF):
                    mm(ps[2 + n][:], aT[:, j, :], wfd[:, j, n * 512:(n + 1) * 512], start=(j == 0), stop=(j == NFF - 1),
                       reads=[t_aT, t_wd], writes=[tps[2 + n]])
            norm_resid([2, 3], xt, tx, g2row, 8, t_fin[2])
            kb.dma("sp", out[tsl, :], xt[:], reads=[tx], writes=[t_out])

        s1a(0)
        s1b(0)
        s1c(0)
        s1a(1)
        s1b(1)
        for i in range(32):
            s2a(i)
            if i + 1 < 32:
                s1c(i + 1)
            if i + 2 < 32:
                s1a(i + 2)
                s1b(i + 2)
            s2b(i)
        kb.barrier()
    kb.finish()
    return nc, dbg_out


_NC = None


def kernel(**inputs):
    global _NC
    if _NC is None:
        _NC = build()[0]
    in_maps = [layout_inputs(inputs, b) for b in range(8)]
    res = run_bass_kernel_spmd(_NC, in_maps, core_ids=list(range(8)))
    return np.stack([np.asarray(r["out"], dtype=np.float32) for r in res.results], 0)
```

```python
import math
from contextlib import ExitStack
import numpy as np
import ml_dtypes
import concourse.bass as bass
import concourse.mybir as mybir
from concourse.bass_utils import run_bass_kernel_spmd

F32 = mybir.dt.float32
BF = mybir.dt.bfloat16
AF = mybir.ActivationFunctionType
ALU = mybir.AluOpType
AX = mybir.AxisListType

L = 4096
D = 1024
CT = 256
LK = L + CT
DH = 512
DFF = 2816
NFF = DFF // 128
OFF_Q, OFF_K, OFF_V, OFF_G = 1536, 2048, 2560, 3072
EPS = 1e-6
LAM_INIT = 0.8 - 0.6 * math.exp(0.0)
PI = math.pi


class Sem:
    def __init__(self, h):
        self.h = h
        self.count = 0


class T:
    __slots__ = ("name", "w", "r")

    def __init__(self, name=""):
        self.name = name
        self.w = None
        self.r = []


class Eng:
    def __init__(self, name, sem):
        self.name = name
        self.sem = sem
        self.ops = []
        self.seen = {}


class KB:
    def __init__(self, nc, nsem_dma=14):
        self.nc = nc
        self.engs = {}
        for n in ("pe", "act", "dve", "pool", "sp"):
            self.engs[n] = Eng(n, Sem(nc.alloc_semaphore("s_" + n)))
        self.dsems = {q: [Sem(nc.alloc_semaphore(f"d_{q}{i}")) for i in range(nsem_dma)] for q in ("sp", "pool")}
        self.drr = {"sp": 0, "pool": 0}

    def _waits(self, eng, reads, writes, extra=()):
        deps = {}

        def add(d):
            if d is None:
                return
            s, v = d
            if deps.get(s, 0) < v:
                deps[s] = v

        for t in reads:
            add(t.w)
        for t in writes:
            add(t.w)
            for d in t.r:
                add(d)
        for d in extra:
            add(d)
        out = []
        for s, v in deps.items():
            if s is eng.sem and eng.name == "pe":
                continue
            if eng.seen.get(s, 0) >= v:
                continue
            eng.seen[s] = v
            out.append((s, v))
        return out

    def _mark(self, tok, reads, writes):
        for t in reads:
            t.r = [d for d in t.r if d[0] is not tok[0]]
            t.r.append(tok)
        for t in writes:
            t.w = tok
            t.r = []

    def op(self, engname, fn, reads=(), writes=()):
        eng = self.engs[engname]
        waits = self._waits(eng, reads, writes)
        eng.sem.count += 1
        tok = (eng.sem, eng.sem.count)
        eng.ops.append((waits, fn, (eng.sem, 1)))
        self._mark(tok, reads, writes)
        return tok

    def dma(self, q, out_ap, in_ap, reads=(), writes=(), **kw):
        eng = self.engs[q]
        sems = self.dsems[q]
        s = sems[self.drr[q] % len(sems)]
        self.drr[q] += 1
        waits = self._waits(eng, reads, writes, extra=[(s, s.count)] if s.count else [])
        s.count += 16
        tok = (s, s.count)
        eng.ops.append((waits, lambda e: e.dma_start(out=out_ap, in_=in_ap, **kw), (s, 16)))
        self._mark(tok, reads, writes)
        return tok

    def collective(self, kind, ins, outs, reads=(), writes=()):
        eng = self.engs["pool"]
        if not hasattr(self, "ccsem"):
            self.ccsem = Sem(self.nc.alloc_semaphore("s_cc"))
        s = self.ccsem
        waits = self._waits(eng, reads, writes)
        s.count += 1
        tok = (s, s.count)
        eng.ops.append((waits, lambda e: e.collective_compute(kind, ALU.bypass, replica_groups=[list(range(8))], ins=ins, outs=outs), (s, 1)))
        self._mark(tok, reads, writes)
        return tok

    def barrier(self, include_cc=False):
        allsems = [e.sem for e in self.engs.values()] + [s for q in self.dsems.values() for s in q]
        if include_cc and hasattr(self, "ccsem"):
            allsems.append(self.ccsem)
        for eng in self.engs.values():
            waits = []
            for s in allsems:
                if s is eng.sem or s.count == 0:
                    continue
                if eng.seen.get(s, 0) >= s.count:
                    continue
                eng.seen[s] = s.count
                waits.append((s, s.count))
            if waits:
                eng.ops.append((waits, None, None))

    def finish(self):
        nc = self.nc
        self.barrier(include_cc=True)
        with nc.Block() as block:
            def emit(e, en):
                for waits, fn, inc in en.ops:
                    for (ws, wv) in waits:
                        e.wait_ge(ws.h, wv)
                    if fn is not None:
                        ins = fn(e)
                        ins.then_inc(inc[0].h, inc[1])

            @block.tensor
            def _(e):
                emit(e, self.engs["pe"])

            @block.scalar
            def _(e):
                emit(e, self.engs["act"])

            @block.vector
            def _(e):
                emit(e, self.engs["dve"])

            @block.gpsimd
            def _(e):
                emit(e, self.engs["pool"])

            @block.sync
            def _(e):
                emit(e, self.engs["sp"])


def _bf(a):
    return np.ascontiguousarray(a.astype(np.float32)).astype(ml_dtypes.bfloat16)


_CONST = None


def host_consts():
    global _CONST
    if _CONST is not None:
        return _CONST
    c = {}
    c["ident_bf"] = _bf(np.eye(128))
    c["ident_f"] = np.eye(128, dtype=np.float32)
    t = np.arange(L)
    row = (t // 64).astype(np.float32)
    col = (t % 64).astype(np.float32)
    inv = (10000.0 ** (-np.arange(16, dtype=np.float32) / 16)).astype(np.float32)
    cos64 = np.zeros((64, L), np.float32)
    sin64 = np.zeros((64, L), np.float32)
    for half, pos in ((0, row), (1, col)):
        ang = pos[None, :] * inv[:, None]
        base = half * 32
        cos64[base:base + 16] = np.cos(ang)
        cos64[base + 16:base + 32] = np.cos(ang)
        sin64[base:base + 16] = -np.sin(ang)
        sin64[base + 16:base + 32] = np.sin(ang)
    c["rope_cos"] = np.concatenate([cos64, cos64], 0)
    c["rope_sin"] = np.concatenate([sin64, sin64], 0)
    f32 = np.float32
    bands = 16
    tt = np.linspace(0.0, 1.0, L, dtype=f32)[:, None]
    w = (f32(2.0 * math.pi / L) * np.arange(L, dtype=f32))[:, None]
    fr = np.linspace(1e-4, bands - 1, bands, dtype=f32)[None, :]
    z = np.concatenate([tt, np.cos(fr * w), -np.sin(fr * w)], axis=-1).astype(f32)
    c["zT"] = np.ascontiguousarray(z.T)
    deltas = np.abs(np.linspace(math.log(1e-2) / 1.5, math.log(1e-2) / 0.3, DH, dtype=f32))
    nd = (-deltas).reshape(4, 128).T
    c["negdelta"] = np.ascontiguousarray(nd.astype(f32))
    offs = (np.arange(8) * 512 / (L - 1)).astype(f32)
    c["ndoff"] = np.ascontiguousarray((nd[:, :, None] * offs[None, None, :]).astype(f32))
    c["tv0"] = np.ascontiguousarray(np.broadcast_to((np.arange(512) / (L - 1)).astype(f32)[None, :], (128, 512)))
    n1 = np.arange(32)[:, None]
    k1 = np.arange(33)[None, :]
    a = 2 * np.pi * n1 * k1 / 64.0
    c["d1f"] = _bf(np.concatenate([np.cos(a), -np.sin(a)], 1))
    c["d1fp"] = np.ascontiguousarray(np.concatenate([c["d1f"], np.zeros((96, 66), ml_dtypes.bfloat16)], 0))
    c["d1b"] = _bf(np.concatenate([np.cos(a), np.sin(a)], 1))
    n2 = np.arange(128)[:, None, None]
    k1g = np.arange(33)[None, :, None]
    k2 = np.arange(128)[None, None, :]
    ang = 2 * np.pi * n2 * (k1g + 64 * k2) / 8192.0
    gr, gi = np.cos(ang), -np.sin(ang)
    c["gtab"] = _bf(np.stack([gr, gi, -gi], 2))
    k2e = np.arange(128)[:, None]
    n2e = np.arange(128)[None, :]
    ae = 2 * np.pi * k2e * n2e / 128.0
    c["etab"] = _bf(np.stack([np.cos(ae).reshape(128, 4, 32), np.sin(ae).reshape(128, 4, 32)], 2))
    k1t = np.arange(33)[:, None, None]
    n2t = np.arange(128)[None, :, None]
    n1t = np.arange(32)[None, None, :]
    at = 2 * np.pi * k1t * (128 * n1t + n2t) / 8192.0
    wgt = np.full((33, 1, 1), 2.0)
    wgt[0] = 1.0
    wgt[32] = 1.0
    tr = wgt * np.cos(at) / 8192.0
    ti = wgt * np.sin(at) / 8192.0
    t0 = np.concatenate([tr, -ti], 0)
    t1 = np.concatenate([-ti, -tr], 0)
    c["ttab"] = _bf(np.stack([t0, t1], 1))
    _CONST = c
    return c


CONST_SHAPES = {
    "ident_bf": ([128, 128], BF), "ident_f": ([128, 128], F32),
    "rope_cos": ([128, L], F32), "rope_sin": ([128, L], F32),
    "zT": ([33, L], F32), "negdelta": ([128, 4], F32), "ndoff": ([128, 4, 8], F32), "tv0": ([128, 512], F32),
    "d1f": ([32, 66], BF), "d1fp": ([128, 66], BF), "d1b": ([32, 66], BF), "gtab": ([128, 33, 3, 128], BF),
    "etab": ([128, 4, 2, 32], BF), "ttab": ([66, 2, 128, 32], BF),
}

IN_SHAPES = {
    "x": [L, D], "ctx": [CT, D], "cc": [128, 8, 2], "w_ada": [D, 6 * D], "b_adaT": [128, 48], "gcols": [128, 4, 8],
    "w_in": [D, 5120], "w_qk_sw": [D, 1024], "cw": [128, 12, 3], "cb": [128, 12],
    "fw1": [33, 64], "fb1": [64, 1], "fw2": [64, 64], "fb2": [64, 1], "fw3": [64, 2048], "fb3": [1, 2048],
    "ffreq": [64, 1], "hyb": [128, 2, 4], "lamv": [1, 256], "subg": [1, 128],
    "w_hy_up": [DH, D], "w_att_up": [DH, D], "w_out": [D, D], "w_fg": [D, DFF], "w_fu": [D, DFF], "w_fd": [DFF, D],
}


def layout_inputs(inp, b):
    f = lambda a: np.ascontiguousarray(np.asarray(a, dtype=np.float32))
    m = {}
    m["x"] = f(inp["x"][b])
    m["ctx"] = f(inp["ctx"][b])
    cc = np.stack([np.asarray(inp["c"][b]), np.asarray(inp["c_ctx"])], -1)
    m["cc"] = f(cc.reshape(8, 128, 2).transpose(1, 0, 2))
    m["w_ada"] = f(inp["w_ada"][0])
    m["b_adaT"] = f(np.asarray(inp["b_ada"][0]).reshape(48, 128).T)
    g = np.stack([np.asarray(inp[k][0]) for k in ("g_mix_pre", "g_mix_post", "g_ffn_pre", "g_ffn_post")], 0)
    m["gcols"] = f(g.reshape(4, 8, 128).transpose(2, 0, 1))
    w_in = np.asarray(inp["w_in"][0])
    m["w_in"] = f(w_in)
    perm = np.arange(1024).reshape(16, 2, 2, 16)[:, :, ::-1, :].reshape(-1)
    m["w_qk_sw"] = f(w_in[:, OFF_Q:OFF_V][:, perm])
    m["cw"] = f(np.asarray(inp["hy_conv_w"][0]).reshape(3, 12, 128).transpose(2, 1, 0))
    m["cb"] = f(np.asarray(inp["hy_conv_b"][0]).reshape(12, 128).T)
    m["fw1"] = f(inp["hy_f_w1"][0])
    m["fb1"] = f(np.asarray(inp["hy_f_b1"][0]).reshape(64, 1))
    m["fw2"] = f(inp["hy_f_w2"][0])
    m["fb2"] = f(np.asarray(inp["hy_f_b2"][0]).reshape(64, 1))
    m["fw3"] = f(inp["hy_f_w3"][0])
    m["fb3"] = f(np.asarray(inp["hy_f_b3"][0]).reshape(1, 2048))
    m["ffreq"] = f(np.asarray(inp["hy_f_freq"][0]).reshape(64, 1))
    m["hyb"] = f(np.asarray(inp["hy_bias"][0]).reshape(2, 4, 128).transpose(2, 0, 1))
    m["lamv"] = f(np.concatenate([np.asarray(inp[k][0]) for k in ("lambda_q1", "lambda_q2", "lambda_k1", "lambda_k2")]).reshape(1, 256))
    m["subg"] = f(np.asarray(inp["att_subln_g"][0]).reshape(1, 128))
    m["w_hy_up"] = f(inp["w_hy_up"][0])
    m["w_att_up"] = f(inp["w_att_up"][0])
    m["w_out"] = f(inp["w_out"][0])
    m["w_fg"] = f(inp["w_ffn_gate"][0])
    m["w_fu"] = f(inp["w_ffn_up"][0])
    m["w_fd"] = f(inp["w_ffn_down"][0])
    m.update(host_consts())
    return m


def build(dbg=(), stop_after=None, skip=()):
    nc = bass.Bass("TRN2", target_bir_lowering=False)
    kb = KB(nc)
    din = {}
    for k, shp in IN_SHAPES.items():
        din[k] = nc.dram_tensor(k, list(shp), F32, kind="ExternalInput").ap()
    for k, (shp, dt) in CONST_SHAPES.items():
        din[k] = nc.dram_tensor(k, list(shp), dt, kind="ExternalInput").ap()
    out = nc.dram_tensor("out", [L, D], F32, kind="ExternalOutput").ap()
    hT_d = nc.dram_tensor("hT_d", [128, 8, L], BF, kind="Internal").ap()
    mT_d = nc.dram_tensor("mT_d", [128, 8, L], BF, kind="Internal").ap()
    zT_d = nc.dram_tensor("zT_d", [128, 4, L], BF, kind="Internal").ap()
    aT_d = nc.dram_tensor("aT_d", [128, 4, L], BF, kind="Internal").ap()
    sig_d = nc.dram_tensor("sig_d", [5, 128, L], BF, kind="Internal").ap()
    dbg_out = {}

    def dbg_tensor(name, shape, dt=F32):
        dbg_out[name] = nc.dram_tensor("dbg_" + name, list(shape), dt, kind="ExternalOutput").ap()
        return dbg_out[name]

    psall = nc.alloc_psum_tensor("psall", [128, 4096], F32)
    psall_b = psall.bitcast(BF)
    ps = [psall[:, i * 512:(i + 1) * 512] for i in range(8)]
    psb = [psall_b[:, i * 1024:(i + 1) * 1024] for i in range(8)]
    tps = [T(f"ps{i}") for i in range(8)]

    def mm(out_ap, lhsT, rhs, start, stop, reads, writes, tile_position=None):
        if tile_position is None:
            kb.op("pe", lambda e: e.matmul(out_ap, lhsT, rhs, start=start, stop=stop), reads=reads, writes=writes)
        else:
            kb.op("pe", lambda e: e.matmul(out_ap, lhsT, rhs, start=start, stop=stop, tile_position=tile_position), reads=reads, writes=writes)

    def tr(out_ap, in_ap, ident, reads, writes):
        kb.op("pe", lambda e: e.transpose(out_ap, in_ap, ident), reads=reads, writes=writes)

    def act(out_ap, in_ap, func, reads, writes, **kw):
        kb.op("act", lambda e: e.activation(out=out_ap, in_=in_ap, func=func, **kw), reads=reads, writes=writes)

    def ts(eng, out_ap, in0, s1, s2, op0, op1, reads, writes, **kw):
        if s2 is None:
            kb.op(eng, lambda e: e.tensor_scalar(out_ap, in0, s1, None, op0, **kw), reads=reads, writes=writes)
        else:
            kb.op(eng, lambda e: e.tensor_scalar(out_ap, in0, s1, s2, op0, op1, **kw), reads=reads, writes=writes)

    def tt(eng, out_ap, in0, in1, op, reads, writes):
        kb.op(eng, lambda e: e.tensor_tensor(out_ap, in0, in1, op), reads=reads, writes=writes)

    def stt(out_ap, in0, scalar, in1, op0, op1, reads, writes):
        kb.op("dve", lambda e: e.scalar_tensor_tensor(out_ap, in0, scalar, in1, op0, op1), reads=reads, writes=writes)

    def cp(eng, out_ap, in_ap, reads, writes):
        if eng == "act":
            kb.op("act", lambda e: e.copy(out_ap, in_ap), reads=reads, writes=writes)
        else:
            kb.op(eng, lambda e: e.tensor_copy(out_ap, in_ap), reads=reads, writes=writes)

    def recip(out_ap, in_ap, reads, writes):
        kb.op("dve", lambda e: e.reciprocal(out_ap, in_ap), reads=reads, writes=writes)

    def rsum(out_ap, in_ap, reads, writes):
        kb.op("dve", lambda e: e.reduce_sum(out_ap, in_ap, AX.X), reads=reads, writes=writes)

    def memset(eng, ap, val, writes):
        kb.op(eng, lambda e: e.memset(ap, val), writes=writes)

    _ids = [0]

    def next_id():
        _ids[0] += 1
        return _ids[0]

    P = ExitStack()

    def sbp(name, shape, dt):
        return P.enter_context(nc.sbuf_tensor("sp_" + name, list(shape), dt))

    ident_bf = sbp("ident_bf", [128, 128], BF)
    ident_f = sbp("ident_f", [128, 128], F32)
    ones_f = sbp("ones_f", [128, 128], F32)
    mhalf = sbp("mhalf", [128, 32], F32)
    cols = sbp("cols", [128, 8, 8], F32)
    g1row = sbp("g1row", [128, D], F32)
    g2row = sbp("g2row", [128, D], F32)
    neglam = sbp("neglam", [128, 1], F32)
    sgrow = sbp("sgrow", [128, 128], F32)
    hcT = sbp("hcT", [128, 8, CT], BF)
    t_const = T("const")
    t_cols = T("cols")
    t_rows = T("rows")
    t_hcT = T("hcT")
    kb.dma("sp", ident_bf[:], din["ident_bf"], writes=[t_const])
    kb.dma("sp", ident_f[:], din["ident_f"], writes=[t_const])
    memset("dve", ones_f[:], 1.0, [t_const])
    memset("dve", mhalf[:], -0.5, [t_const])
    A1 = lambda k: cols[:, 0, k:k + 1]
    B1 = lambda k: cols[:, 1, k:k + 1]
    A2 = lambda k: cols[:, 2, k:k + 1]
    B2 = lambda k: cols[:, 3, k:k + 1]
    A1c = lambda k: cols[:, 4, k:k + 1]
    B1c = lambda k: cols[:, 5, k:k + 1]

    def rstd_of(ssq_ap, out_ap, inv_n, tl):
        n = ssq_ap.shape[-1] if len(ssq_ap.shape) > 1 else 1
        ts("dve", out_ap, ssq_ap, inv_n, EPS, ALU.mult, ALU.add, reads=[tl], writes=[tl])
        tt("pool", out_ap, out_ap, mhalf[:, 0:n], ALU.pow, reads=[tl, t_const], writes=[tl])

    with ExitStack() as ph:
        def sb(name, shape, dt):
            return ph.enter_context(nc.sbuf_tensor(f"s{next_id()}_" + name, list(shape), dt))

        ccs = sb("ccs", [128, 16], F32)
        scs = sb("scs", [128, 16], F32)
        bada = sb("bada", [128, 48], F32)
        gc = sb("gc", [128, 4, 8], F32)
        adaT = sb("adaT", [128, 48, 2], F32)
        wa = [sb(f"wa{i}", [128, 6 * D], F32) for i in range(2)]
        twa = [T("wa0"), T("wa1")]
        t_s = T("p0small")
        kb.dma("sp", ccs[:], din["cc"].rearrange("p k c -> p (k c)"), writes=[t_s])
        kb.dma("sp", bada[:], din["b_adaT"], writes=[t_s])
        kb.dma("sp", gc[:], din["gcols"], writes=[t_s])
        act(scs[:], ccs[:], AF.Silu, reads=[t_s], writes=[t_s])
        def ada_chunk(k):
            kb.dma("sp", wa[k % 2][:], din["w_ada"][k * 128:(k + 1) * 128, :], writes=[twa[k % 2]])
            for f in range(48):
                mm(ps[0][:, 2 * f:2 * f + 2], wa[k % 2][:, f * 128:(f + 1) * 128], scs[:, 2 * k:2 * k + 2],
                   start=(k == 0 and f == 0), stop=(k == 7 and f == 47), reads=[twa[k % 2], t_s], writes=[tps[0]])
        lamb = sb("lamb", [128, 256], F32)
        lp = sb("lp", [128, 2, 64], F32)
        le = sb("le", [128, 2], F32)
        kb.dma("sp", lamb[:], bass.AP(din["lamv"].tensor, 0, [[0, 128], [1, 256]]), writes=[t_s])
        kb.dma("sp", sgrow[:], bass.AP(din["subg"].tensor, 0, [[0, 128], [1, 128]]), writes=[t_rows])
        tt("dve", lp[:].rearrange("p a b -> p (a b)"), lamb[:, 0:128], lamb[:, 128:256], ALU.mult, reads=[t_s], writes=[t_s])
        rsum(le[:], lp[:], reads=[t_s], writes=[t_s])
        act(le[:], le[:], AF.Exp, reads=[t_s], writes=[t_s])
        tt("dve", neglam[:], le[:, 1:2], le[:, 0:1], ALU.subtract, reads=[t_s], writes=[t_cols])
        ts("dve", neglam[:], neglam[:], -LAM_INIT, None, ALU.add, None, reads=[t_cols], writes=[t_cols])
        ts("dve", sgrow[:], sgrow[:], 1.0 - LAM_INIT, None, ALU.mult, None, reads=[t_rows], writes=[t_rows])
        xts = [sb(f"xt{i}", [128, D], F32) for i in range(4)]
        txt = [T() for _ in range(4)]
        junk = sb("junk", [128, D], BF)
        t_junk = T()
        xsa = sb("xsa", [128, 34, D], BF)
        txs = [T() for _ in range(34)]
        ssq = sb("ssq", [128, 34], F32)
        t_ssq = [T() for _ in range(34)]
        hst = [sb(f"hst{i}", [128, 8, 512], BF) for i in range(2)]
        thst = [T(), T()]
        t_hd = T("hT_d")
        for i in range(34):
            lat = i < 32
            src = din["x"][i * 128:(i + 1) * 128, :] if lat else din["ctx"][(i - 32) * 128:(i - 31) * 128, :]
            xt, tx = xts[i % 4], txt[i % 4]
            if i % 4 == 0 and i // 4 < 8:
                ada_chunk(i // 4)
            kb.dma("sp", xt[:], src, writes=[tx])
            act(junk[:], xt[:], AF.Square, reads=[tx], writes=[t_junk, t_ssq[i]], accum_out=ssq[:, i:i + 1])
            rstd_of(ssq[:, i:i + 1], ssq[:, i:i + 1], 1.0 / D, t_ssq[i])
            if i >= 2:
                j = i - 2
                act(xsa[:, j, :], xts[j % 4][:], AF.Copy, reads=[txt[j % 4], t_ssq[j]], writes=[txs[j]], scale=ssq[:, j:j + 1])
        for j in (32, 33):
            act(xsa[:, j, :], xts[j % 4][:], AF.Copy, reads=[txt[j % 4], t_ssq[j]], writes=[txs[j]], scale=ssq[:, j:j + 1])
        for c in range(2):
            tt("dve", adaT[:, :, c], ps[0][:, c:96:2], bada[:], ALU.add, reads=[tps[0], t_s], writes=[t_s])
        for (dst, sc_f, g_i, c) in ((0, 8, 0, 0), (2, 32, 2, 0), (4, 8, 0, 1)):
            stt(cols[:, dst, :], adaT[:, sc_f:sc_f + 8, c], 1.0, gc[:, g_i, :], ALU.add, ALU.mult, reads=[t_s], writes=[t_cols])
        for (dst, sh_f, c) in ((1, 0, 0), (3, 24, 0), (5, 0, 1)):
            cp("dve", cols[:, dst, :], adaT[:, sh_f:sh_f + 8, c], reads=[t_s], writes=[t_cols])
        tt("dve", cols[:, 6, :], adaT[:, 16:24, 0], gc[:, 1, :], ALU.mult, reads=[t_s], writes=[t_cols])
        tt("dve", cols[:, 7, :], adaT[:, 40:48, 0], gc[:, 3, :], ALU.mult, reads=[t_s], writes=[t_cols])
        diag = sb("diag", [128, 4, 128], F32)
        t_diag = T("diag")
        for gi, rowt in ((6, g1row), (7, g2row)):
            for half in range(2):
                for j in range(4):
                    ts("dve", diag[:, j, :], ident_f[:], cols[:, gi, half * 4 + j:half * 4 + j + 1], None, ALU.mult, None,
                       reads=[t_const, t_cols], writes=[t_diag])
                for j in range(4):
                    mm(ps[1][:, j * 128:(j + 1) * 128], ones_f[:], diag[:, j, :], start=True, stop=True,
                       reads=[t_diag, t_const], writes=[tps[1]])
                cp("dve", rowt[:, half * 512:(half + 1) * 512], ps[1][:], reads=[tps[1]], writes=[t_rows])
        if "p0" in dbg:
            d = dbg_tensor("cols", [128, 64])
            kb.dma("sp", d, cols[:].rearrange("p a b -> p (a b)"), reads=[t_cols])
            d = dbg_tensor("g1row", [128, D])
            kb.dma("sp", d, g1row[:], reads=[t_rows])
            d = dbg_tensor("neglam", [128, 1])
            kb.dma("sp", d, neglam[:], reads=[t_cols])

        for i in range(34):
            lat = i < 32
            bk = 2 + i % 4
            for k in range(8):
                tr(psb[bk][:, k * 128:(k + 1) * 128], xsa[:, i, k * 128:(k + 1) * 128], ident_bf[:],
                   reads=[txs[i], t_const], writes=[tps[bk]])
            for k in range(8):
                if lat:
                    h = hst[(i // 4) % 2]
                    ts("dve", h[:, k, (i % 4) * 128:(i % 4 + 1) * 128], psb[bk][:, k * 128:(k + 1) * 128], A1(k), B1(k),
                       ALU.mult, ALU.add, reads=[tps[bk], t_cols], writes=[thst[(i // 4) % 2]])
                else:
                    ts("dve", hcT[:, k, (i - 32) * 128:(i - 31) * 128], psb[bk][:, k * 128:(k + 1) * 128], A1c(k), B1c(k),
                       ALU.mult, ALU.add, reads=[tps[bk], t_cols], writes=[t_hcT])
            if lat and i % 4 == 3:
                blk = i // 4
                kb.dma("sp", hT_d[:, :, blk * 512:(blk + 1) * 512], hst[blk % 2][:], reads=[thst[blk % 2]], writes=[t_hd])
        if "p1" in dbg:
            d = dbg_tensor("hT", [128, 8, L], BF)
            kb.dma("sp", d, hT_d, reads=[t_hd])
            d = dbg_tensor("hcT", [128, 8, CT], BF)
            kb.dma("sp", d, hcT[:], reads=[t_hcT])
        kb.barrier()
    if stop_after == "p1":
        kb.finish()
        return nc, dbg_out


    hall_d = nc.dram_tensor("hall_d", [1024, 8448], BF, kind="Internal").ap()
    t_hall = [T(f"hall{i}") for i in range(8)]
    with ExitStack() as ph:
        def sb(name, shape, dt):
            return ph.enter_context(nc.sbuf_tensor(f"s{next_id()}_" + name, list(shape), dt))

        _bk = [0]

        def nb():
            _bk[0] = (_bk[0] + 1) % 8
            return _bk[0]

        t_hc = T("hfconst")
        gtab = sb("gtab", [128, 33, 3, 128], BF)
        d1f = sb("d1fp", [128, 66], BF)
        tv0 = sb("tv0", [128, 512], F32)
        negd = sb("negd", [128, 4], F32)
        ndoff = sb("ndoff", [128, 4, 8], F32)
        hybs = sb("hybs", [128, 2, 4], F32)
        for dst, nm in ((gtab, "gtab"), (d1f, "d1fp"), (tv0, "tv0"), (negd, "negdelta"), (ndoff, "ndoff"), (hybs, "hyb")):
            kb.dma("sp", dst[:], din[nm], writes=[t_hc])
        hdn2 = sb("hdn2", [65, L], BF)
        fw3a = sb("fw3a", [65, 2048], BF)
        t_h2 = T("hdn2")
        t_fw3 = T("fw3")
        kb.dma("pool", fw3a[0:64, :], din["fw3"], writes=[t_fw3])
        kb.dma("pool", fw3a[64:65, :], din["fb3"], writes=[t_fw3])
        memset("pool", hdn2[64:65, :], 1.0, [t_h2])
        zTs = sb("zTs", [33, L], F32)
        fw1s = sb("fw1s", [33, 64], F32)
        fw2s = sb("fw2s", [64, 64], F32)
        fcol = sb("fcol", [64, 5], F32)
        t_f = T("fmlp")
        kb.dma("sp", zTs[:], din["zT"], writes=[t_f])
        kb.dma("sp", fw1s[:], din["fw1"], writes=[t_f])
        kb.dma("sp", fw2s[:], din["fw2"], writes=[t_f])
        kb.dma("sp", fcol[:, 0:1], din["ffreq"], writes=[t_f])
        kb.dma("sp", fcol[:, 1:2], din["fb1"], writes=[t_f])
        kb.dma("sp", fcol[:, 2:3], din["fb2"], writes=[t_f])
        tt("dve", fcol[:, 3:4], fcol[:, 0:1], fcol[:, 1:2], ALU.mult, reads=[t_f], writes=[t_f])
        tt("dve", fcol[:, 4:5], fcol[:, 0:1], fcol[:, 2:3], ALU.mult, reads=[t_f], writes=[t_f])
        with ExitStack() as phm:
            def sbm(name, shape, dt):
                return phm.enter_context(nc.sbuf_tensor(f"s{next_id()}_" + name, list(shape), dt))

            halfpi = sbm("halfpi", [64, 1], F32)
            memset("dve", halfpi[:], PI / 2, [t_f])
            arg = [sbm(f"arg{i}", [64, 512], F32) for i in range(8)]
            s4 = [sbm(f"s4{i}", [64, 512], F32) for i in range(8)]
            c4 = [sbm(f"c4{i}", [64, 512], F32) for i in range(8)]
            hd1 = [sbm(f"hd1{i}", [64, 512], F32) for i in range(8)]
            t_m = [T() for _ in range(8)]

            def sin_layer_bf(ps_of, bias_col, out_of, t_out_of):
                for i_ in range(8):
                    ts("dve", arg[i_][:], ps[ps_of(i_)][0:64, :], fcol[:, 0:1], fcol[:, bias_col:bias_col + 1], ALU.mult, ALU.add,
                       reads=[tps[ps_of(i_)], t_f], writes=[t_m[i_]])
                for i_ in range(8):
                    act(s4[i_][:], arg[i_][:], AF.Sin, reads=[t_m[i_]], writes=[t_m[i_]], scale=0.25)
                    act(c4[i_][:], arg[i_][:], AF.Sin, reads=[t_m[i_], t_f], writes=[t_m[i_]], scale=0.25, bias=halfpi[:])
                for i_ in range(8):
                    tt("pool", arg[i_][:], s4[i_][:], s4[i_][:], ALU.mult, reads=[t_m[i_]], writes=[t_m[i_]])
                    tt("pool", s4[i_][:], s4[i_][:], c4[i_][:], ALU.mult, reads=[t_m[i_]], writes=[t_m[i_]])
                for i_ in range(8):
                    ts("dve", arg[i_][:], arg[i_][:], -2.0, 1.0, ALU.mult, ALU.add, reads=[t_m[i_]], writes=[t_m[i_]])
                    stt(out_of(i_), s4[i_][:], 4.0, arg[i_][:], ALU.mult, ALU.mult, reads=[t_m[i_]], writes=[t_m[i_], t_out_of(i_)])

            for blk in range(8):
                mm(ps[blk][0:64, :], fw1s[:], zTs[:, blk * 512:(blk + 1) * 512], start=True, stop=True, reads=[t_f], writes=[tps[blk]])
            sin_layer_bf(lambda i_: i_, 3, lambda i_: hd1[i_][:], lambda i_: t_m[i_])
            for blk in range(8):
                mm(ps[blk][0:64, :], fw2s[:], hd1[blk][:], start=True, stop=True, reads=[t_f, t_m[blk]], writes=[tps[blk]])
            sin_layer_bf(lambda i_: i_, 4, lambda i_: hdn2[0:64, i_ * 512:(i_ + 1) * 512], lambda i_: t_h2)
            kb.barrier()
        dec = [sb(f"dec{i}", [128, L], BF) for i in range(2)]
        t_dec = [T(), T()]
        Kf = [[sb(f"Kf{i}_{d_}", [128, L], BF) for d_ in range(2)] for i in range(2)]
        t_Kf = [[T(), T()], [T(), T()]]
        Ut = [sb(f"Ut{i}", [128, 32, 128], BF) for i in range(2)]
        tUt = [T(), T()]
        for i in range(2):
            memset("pool", Ut[i][32:64, :, :], 0.0, [tUt[i]])
            memset("pool", Ut[i][64:128, :, :], 0.0, [tUt[i]])
        A = sb("A", [128, 66, 128], BF)
        tA = [T() for _ in range(33)]
        H = [sb(f"H{i}", [128, 33, 2, 128], BF) for i in range(2)]
        tH = [[T() for _ in range(33)] for _ in range(2)]
        t_sdf = [T() for _ in range(4)]
        cnt = [0]

        def stage_k(r):
            cc, o = r // 2, r % 2
            pi_ = r % 2
            if o == 0:
                for b8 in range(8):
                    act(dec[cc % 2][:, b8 * 512:(b8 + 1) * 512], tv0[:], AF.Exp, reads=[t_hc], writes=[t_dec[cc % 2]],
                        scale=negd[:, cc:cc + 1], bias=ndoff[:, cc, b8:b8 + 1])
            for d_ in range(2):
                col0 = (o * 2 + d_) * 512 + cc * 128
                kf, tk = Kf[pi_][d_], t_Kf[pi_][d_]
                for blk in range(8):
                    sl = slice(blk * 512, (blk + 1) * 512)
                    b = nb()
                    mm(ps[b][:], fw3a[:, col0:col0 + 128], hdn2[:, sl], start=True, stop=True, reads=[t_fw3, t_h2], writes=[tps[b]])
                    tt("dve", kf[:, sl], ps[b][:], dec[cc % 2][:, sl], ALU.mult, reads=[tps[b], t_dec[cc % 2]], writes=[tk])
                if d_ == 0:
                    tt("dve", kf[:, 0:1], kf[:, 0:1], hybs[:, o, cc:cc + 1], ALU.add, reads=[tk, t_hc], writes=[tk])
                else:
                    memset("dve", kf[:, 0:1], 0.0, [tk])
                kb.dma("pool", sig_d[pi_ * 2 + d_], kf[:], reads=[tk], writes=[t_sdf[pi_ * 2 + d_]])

        def stage_f(r):
            pi_ = r % 2
            Hh, tHh = H[pi_], tH[pi_]
            for d_ in range(2):
                slot = pi_ * 2 + d_
                for g in range(4):
                    u, tu = Ut[g % 2], tUt[g % 2]
                    kb.dma("sp", u[0:32, :, :], sig_d[slot][g * 32:(g + 1) * 32, :].rearrange("c (a b) -> a c b", a=32), reads=[t_sdf[slot]], writes=[tu])
                    j = 0
                    while j < 32:
                        n = min(7, 32 - j)
                        b = nb()
                        for jj in range(n):
                            mm(ps[b][:, jj * 66:(jj + 1) * 66], u[:, j + jj, :], d1f[:], start=True, stop=True, reads=[tu, t_hc], writes=[tps[b]])
                        c0 = g * 32 + j
                        cp("act" if cnt[0] % 2 == 0 else "dve", A[:, :, c0:c0 + n], ps[b][:, 0:n * 66].rearrange("p (c k) -> p k c", k=66),
                           reads=[tps[b]], writes=tA)
                        cnt[0] += 1
                        j += n
                k1 = 0
                while k1 < 33:
                    n = 2 if k1 + 1 < 33 else 1
                    b = nb()
                    for u_ in range(n):
                        kk = k1 + u_
                        o_ = u_ * 256
                        ar, ai = A[:, kk, :], A[:, 33 + kk, :]
                        mm(ps[b][:, o_:o_ + 128], gtab[:, kk, 0, :], ar, start=True, stop=False, reads=[t_hc, tA[kk]], writes=[tps[b]])
                        mm(ps[b][:, o_:o_ + 128], gtab[:, kk, 2, :], ai, start=False, stop=True, reads=[t_hc, tA[kk]], writes=[tps[b]])
                        mm(ps[b][:, o_ + 128:o_ + 256], gtab[:, kk, 1, :], ar, start=True, stop=False, reads=[t_hc, tA[kk]], writes=[tps[b]])
                        mm(ps[b][:, o_ + 128:o_ + 256], gtab[:, kk, 0, :], ai, start=False, stop=True, reads=[t_hc, tA[kk]], writes=[tps[b]])
                    xv = ps[b][:, 0:n * 256].rearrange("p (k r c) -> p k r c", k=n, r=2)
                    wh = [tHh[k1 + u_] for u_ in range(n)]
                    if d_ == 0:
                        cp("act", Hh[:, k1:k1 + n, :, :], xv, reads=[tps[b]], writes=wh)
                    else:
                        tt("dve", Hh[:, k1:k1 + n, 0, :], Hh[:, k1:k1 + n, 0, :], xv[:, :, 0, :], ALU.add, reads=[tps[b]] + wh, writes=wh)
                        tt("dve", Hh[:, k1:k1 + n, 1, :], Hh[:, k1:k1 + n, 1, :], xv[:, :, 1, :], ALU.subtract, reads=[tps[b]] + wh, writes=wh)
                    k1 += n
            kb.dma("sp", hall_d[r * 128:(r + 1) * 128, :], Hh[:].rearrange("p a b c -> p (a b c)"), reads=tHh, writes=[t_hall[r]])

        stage_k(0)
        for r in range(8):
            if r + 1 < 8:
                stage_k(r + 1)
            stage_f(r)
        kb.barrier()
    if stop_after == "hf":
        kb.finish()
        return nc, dbg_out

    t_hd_r = T("hT_d_r")
    w_in_v = din["w_in"].rearrange("(k p) n -> p k n", p=128)
    w_sw_v = din["w_qk_sw"].rearrange("(k p) n -> p k n", p=128)

    def load_w(dst, src_view, c0, tl, ncols=128):
        kb.dma("pool", dst, src_view[:, :, c0:c0 + ncols], writes=[tl])

    with ExitStack() as ph:
        def sb(name, shape, dt):
            return ph.enter_context(nc.sbuf_tensor(f"s{next_id()}_" + name, list(shape), dt))

        NH = 0 if "p3" in skip else 4
        rcos = sb("rcos", [128, L], F32)
        rsin = sb("rsin", [128, L], F32)
        t_ropes = [T(f"rope{i}") for i in range(8)]
        for i in range(8):
            kb.dma("sp", rcos[:, i * 512:(i + 1) * 512], din["rope_cos"][:, i * 512:(i + 1) * 512], writes=[t_ropes[i]])
            kb.dma("sp", rsin[:, i * 512:(i + 1) * 512], din["rope_sin"][:, i * 512:(i + 1) * 512], writes=[t_ropes[i]])
        QT = [sb(f"QT{i}", [128, L], BF) for i in range(2)]
        KT = [[sb(f"KT{i}_{m_}", [128, LK], BF) for m_ in range(2)] for i in range(2)]
        V = [sb(f"V{i}", [128, 34, 129], BF) for i in range(2)]
        t_Q, t_K, t_V = [T(), T()], [T(), T()], [T(), T()]
        for i in range(2):
            memset("pool", V[i][:, :, 128:129], 1.0, [t_V[i]])
        wts = [[sb(f"aw{i}_{j}", [128, 8, 128], BF) for j in range(5)] for i in range(2)]
        twts = [[T() for _ in range(5)] for _ in range(2)]
        hb = [sb(f"hb{i}", [128, 8, 512], BF) for i in range(2)]
        thb = [T(), T()]
        rt = [sb(f"rt{i}", [128, 512], F32) for i in range(4)]
        trt = [T() for _ in range(4)]
        E = [sb(f"E{i}", [128, 1024], BF) for i in range(4)]
        tE = [T() for _ in range(4)]
        attst = [sb(f"attst{i}", [128, 512], BF) for i in range(2)]
        t_attst = [T(), T()]
        fin = sb("fin", [128, 16], F32)
        accs = sb("accs", [128, 1161], F32)
        oa = sb("oa", [128, 4, 128], F32)
        ob = sb("ob", [128, 4, 128], F32)
        on = sb("on", [128, 4, 128], BF)
        t_fin, t_on, t_accs, t_oa, t_ob = T(), T(), T(), T(), T()
        t_ad = T("aT_d")
        hbcnt = [0]

        def prologue(h, banks):
            bi = h % 2
            w_, tw_ = wts[bi], twts[bi]
            brr = [0]

            def nbk():
                brr[0] += 1
                return banks[brr[0] % len(banks)]

            load_w(w_[0][:], w_in_v, OFF_Q + h * 128, tw_[0])
            load_w(w_[1][:], w_sw_v, h * 128, tw_[1])
            load_w(w_[2][:], w_in_v, OFF_K + h * 128, tw_[2])
            load_w(w_[3][:], w_sw_v, 512 + h * 128, tw_[3])
            load_w(w_[4][:], w_in_v, OFF_V + h * 128, tw_[4])
            b = nbk()
            for k in range(8):
                mm(ps[b][:, 0:CT], w_[2][:, k, :], hcT[:, k, :], start=(k == 0), stop=(k == 7), reads=[tw_[2], t_hcT], writes=[tps[b]])
            cp("dve", KT[bi][0][0:64, L:LK], ps[b][0:64, 0:CT], reads=[tps[b]], writes=[t_K[bi]])
            cp("dve", KT[bi][1][64:128, L:LK], ps[b][64:128, 0:CT], reads=[tps[b]], writes=[t_K[bi]])
            yield
            b = nbk()
            for s_ in range(2):
                for k in range(8):
                    mm(ps[b][:, s_ * 128:(s_ + 1) * 128], hcT[:, k, s_ * 128:(s_ + 1) * 128], w_[4][:, k, :], start=(k == 0), stop=(k == 7),
                       reads=[tw_[4], t_hcT], writes=[tps[b]])
            cp("dve", V[bi][:, 32:34, 0:128], ps[b][:, 0:256].rearrange("p (s v) -> p s v", s=2), reads=[tps[b]], writes=[t_V[bi]])
            yield
            for blk in range(8):
                hi = hbcnt[0] % 2
                hbcnt[0] += 1
                hbb, th = hb[hi], thb[hi]
                kb.dma("sp", hbb[:], hT_d[:, :, blk * 512:(blk + 1) * 512], reads=[t_hd], writes=[th])
                sl = slice(blk * 512, (blk + 1) * 512)
                for qk in range(2):
                    for gg in range(2):
                        g = 2 * qk + gg
                        b = nbk()
                        for k in range(8):
                            mm(ps[b][:], w_[g][:, k, :], hbb[:, k, :], start=(k == 0), stop=(k == 7), reads=[tw_[g], th], writes=[tps[b]])
                        tt("dve", rt[g][:], ps[b][:], (rcos if gg == 0 else rsin)[:, sl], ALU.mult, reads=[tps[b], t_ropes[blk]], writes=[trt[g]])
                        yield
                    r0, r1 = rt[2 * qk], rt[2 * qk + 1]
                    if qk == 0:
                        tt("pool", QT[bi][:, sl], r0[:], r1[:], ALU.add, reads=[trt[0], trt[1]], writes=[t_Q[bi]])
                    else:
                        tt("pool", KT[bi][0][0:64, sl], r0[0:64, :], r1[0:64, :], ALU.add, reads=[trt[2], trt[3]], writes=[t_K[bi]])
                        tt("pool", KT[bi][1][64:128, sl], r0[64:128, :], r1[64:128, :], ALU.add, reads=[trt[2], trt[3]], writes=[t_K[bi]])
                b = nbk()
                for s_ in range(4):
                    for k in range(8):
                        mm(ps[b][:, s_ * 128:(s_ + 1) * 128], hbb[:, k, s_ * 128:(s_ + 1) * 128], w_[4][:, k, :], start=(k == 0), stop=(k == 7),
                           reads=[tw_[4], th], writes=[tps[b]])
                cp("dve", V[bi][:, blk * 4:blk * 4 + 4, 0:128], ps[b][:].rearrange("p (s v) -> p s v", s=4), reads=[tps[b]], writes=[t_V[bi]])
                yield

        def bc_last(ap2d, n):
            return bass.AP(ap2d.tensor, ap2d.offset, [list(ap2d.ap[0]), list(ap2d.ap[1]), [0, n]])

        items = [(qb, kc) for qb in range(8) for kc in range(34)]

        def head_loop(h, gen):
            bi = h % 2
            Qh, Kh, Vh = QT[bi], KT[bi], V[bi]

            def emit_S(idx):
                qb, kc = items[idx]
                b0 = (idx % 2) * 2
                for m_ in range(2):
                    pr = slice(64 * m_, 64 * m_ + 64)
                    mm(ps[b0 + m_][:], Kh[m_][pr, kc * 128:(kc + 1) * 128], Qh[pr, qb * 512:(qb + 1) * 512], start=True, stop=True,
                       reads=[t_K[bi], t_Q[bi]], writes=[tps[b0 + m_]])
                ei = idx % 4
                act(E[ei][:], psall[:, b0 * 512:(b0 + 2) * 512], AF.Exp, reads=[tps[b0], tps[b0 + 1]], writes=[tE[ei]], scale=0.125)

            def emit_AV(idx):
                qb, kc = items[idx]
                ei = idx % 4
                for m_ in range(2):
                    for s_ in range(4):
                        slot = m_ * 4 + s_
                        bk, c0 = 4 + slot // 3, (slot % 3) * 129
                        mm(ps[bk][:, c0:c0 + 129], E[ei][:, m_ * 512 + s_ * 128:m_ * 512 + (s_ + 1) * 128], Vh[:, kc, :],
                           start=(kc == 0 and slot % 3 == 0), stop=(kc == 33 and (slot % 3 == 2 or slot == 7)),
                           reads=[tE[ei], t_V[bi]], writes=[tps[bk]])
                if kc == 33:
                    finalize(qb)
                    pend.append(qb)
                if kc == 10 and pend:
                    finalize_tr(pend.pop())

            def finalize(qb):
                qsl = slice(qb * 512, (qb + 1) * 512)
                cp("dve", accs[:, 0:387], ps[4][:, 0:387], reads=[tps[4]], writes=[t_accs])
                cp("dve", accs[:, 387:774], ps[5][:, 0:387], reads=[tps[5]], writes=[t_accs])
                cp("dve", accs[:, 774:1032], ps[6][:, 0:258], reads=[tps[6]], writes=[t_accs])
                av = accs[:, 0:1032].rearrange("p (s c) -> p s c", c=129)
                recip(fin[:, 0:8], av[:, :, 128], reads=[t_accs], writes=[t_fin])
                ts("dve", fin[:, 4:8], fin[:, 4:8], neglam[:], None, ALU.mult, None, reads=[t_fin, t_cols], writes=[t_fin])
                tt("dve", oa[:], av[:, 0:4, 0:128], bc_last(fin[:, 0:4], 128), ALU.mult, reads=[t_accs, t_fin], writes=[t_oa])
                tt("dve", ob[:], av[:, 4:8, 0:128], bc_last(fin[:, 4:8], 128), ALU.mult, reads=[t_accs, t_fin], writes=[t_ob])
                tt("pool", ob[:], ob[:], oa[:], ALU.add, reads=[t_oa, t_ob], writes=[t_ob])
                tt("pool", oa[:], ob[:], ob[:], ALU.mult, reads=[t_ob], writes=[t_oa])
                rsum(fin[:, 8:12], oa[:], reads=[t_oa], writes=[t_fin])
                rstd_of(fin[:, 8:12], fin[:, 12:16], 1.0 / 128, t_fin)
                tt("dve", ob[:], ob[:], bc_last(fin[:, 12:16], 128), ALU.mult, reads=[t_ob, t_fin], writes=[t_ob])
                sg_b = bass.AP(sgrow[:].tensor, sgrow[:].offset, [list(sgrow[:].ap[0]), [0, 4], [1, 128]])
                tt("dve", on[:], ob[:], sg_b, ALU.mult, reads=[t_ob, t_rows], writes=[t_on])

            def finalize_tr(qb):
                qsl = slice(qb * 512, (qb + 1) * 512)
                for s_ in range(4):
                    tr(psb[7][:, s_ * 128:(s_ + 1) * 128], on[:, s_, :], ident_bf[:], reads=[t_on, t_const], writes=[tps[7]])
                cp("dve", attst[qb % 2][:], psb[7][:, 0:512], reads=[tps[7]], writes=[t_attst[qb % 2]])
                kb.dma("sp", aT_d[:, h, qsl], attst[qb % 2][:], reads=[t_attst[qb % 2]], writes=[t_ad])

            pend = []
            emit_S(0)
            emit_S(1)
            for idx in range(len(items)):
                if idx + 2 < len(items):
                    emit_S(idx + 2)
                emit_AV(idx)
                if gen is not None and idx % 6 == 3 and items[idx][1] not in (32, 33, 0):
                    next(gen, None)
            while pend:
                finalize_tr(pend.pop())
            if gen is not None:
                for _ in gen:
                    pass

        if NH:
            for _ in prologue(0, [0, 1, 2, 3, 4, 5, 6, 7]):
                pass
        for h in range(NH):
            gen = prologue(h + 1, [7]) if h + 1 < NH else None
            head_loop(h, gen)
        if "p3" in dbg:
            d = dbg_tensor("attT", [128, 4, L], BF)
            kb.dma("sp", d, aT_d, reads=[t_ad])
        kb.barrier()
    if stop_after == "p3":
        kb.finish()
        return nc, dbg_out

    with ExitStack() as ph:
        def sb(name, shape, dt):
            return ph.enter_context(nc.sbuf_tensor(f"s{next_id()}_" + name, list(shape), dt))

        _bk = [0]

        def nb():
            _bk[0] = (_bk[0] + 1) % 8
            return _bk[0]

        t_hc = T("hyconst")
        gtab = sb("gtab", [128, 33, 3, 128], BF)
        ttab = sb("ttab", [66, 2, 128, 32], BF)
        etab = sb("etab", [128, 4, 2, 32], BF)
        d1f = sb("d1fp", [128, 66], BF)
        cwt = sb("cwt", [128, 12, 3], F32)
        cbt = sb("cbt", [128, 12], F32)
        for dst, nm in ((gtab, "gtab"), (ttab, "ttab"), (etab, "etab"), (d1f, "d1fp"), (cwt, "cw"), (cbt, "cb")):
            kb.dma("sp", dst[:], din[nm], writes=[t_hc])
        Z = [sb(f"Z{i}", [128, L], BF) for i in range(3)]
        tZ = [T(f"Z{i}") for i in range(3)]
        H = sb("H", [128, 33, 2, 128], BF)
        tH = [T() for _ in range(33)]
        t_sd = [T(f"sig{i}") for i in range(5)]
        t_zd = T("zT_d")
        Ut = [sb(f"Ut{i}", [128, 32, 128], BF) for i in range(2)]
        tUt = [T(), T()]
        for i in range(2):
            memset("dve", Ut[i][32:64, :, :], 0.0, [tUt[i]])
            memset("dve", Ut[i][64:128, :, :], 0.0, [tUt[i]])
        A = sb("A", [128, 66, 128], BF)
        tA = [T() for _ in range(33)]
        f1cnt = [0]

        def f1_pre(sig, t_sig, slot):
            kb.dma("pool", sig_d[slot], sig, reads=[t_sig], writes=[t_sd[slot]])
            for g in range(2):
                kb.dma("pool", Ut[g][0:32, :, :], sig_d[slot][g * 32:(g + 1) * 32, :].rearrange("c (a b) -> a c b", a=32),
                       reads=[t_sd[slot]], writes=[tUt[g]])

        def f1_part(sig, t_sig, slot, pre=False, act_only=False):
            if not pre:
                f1_pre(sig, t_sig, slot)
            for g in range(4):
                u, tu = Ut[g % 2], tUt[g % 2]
                if g >= 2:
                    kb.dma("pool", u[0:32, :, :], sig_d[slot][g * 32:(g + 1) * 32, :].rearrange("c (a b) -> a c b", a=32),
                           reads=[t_sd[slot]], writes=[tu])
                j = 0
                while j < 32:
                    n = min(7, 32 - j)
                    b = nb()
                    for jj in range(n):
                        mm(ps[b][:, jj * 66:(jj + 1) * 66], u[:, j + jj, :], d1f[:], start=True, stop=True,
                           reads=[tu, t_hc], writes=[tps[b]])
                    c0 = g * 32 + j
                    eng = "act" if (act_only or f1cnt[0] % 2 == 0) else "dve"
                    f1cnt[0] += 1
                    cp(eng, A[:, :, c0:c0 + n], ps[b][:, 0:n * 66].rearrange("p (c k) -> p k c", k=66), reads=[tps[b]], writes=tA)
                    j += n

        for cc in range(4):
            with ExitStack() as pa:
                def sba(name, shape, dt):
                    return pa.enter_context(nc.sbuf_tensor(f"s{next_id()}_" + name, list(shape), dt))

                wts = [sba(f"hw{i}", [128, 8, 128], BF) for i in range(3)]
                twts = [T() for _ in range(3)]
                hb = [sba(f"hhb{i}", [128, 8, 512], BF) for i in range(2)]
                thb = [T(), T()]
                Ur = [sba(f"Ur{i}", [128, L + 2], BF) for i in range(3)]
                tUr = [T() for _ in range(3)]
                ctmp = sba("ctmp", [128, L], F32)
                t_ct = T()
                for s in range(3):
                    load_w(wts[s][:], w_in_v, s * 512 + cc * 128, twts[s])
                    memset("pool", Ur[s][:, 0:1], 0.0, [tUr[s]])
                    memset("pool", Ur[s][:, L + 1:L + 2], 0.0, [tUr[s]])
                def proj_pass(sigs, off):
                    for blk in range(8):
                        hbb, th = hb[(blk + off) % 2], thb[(blk + off) % 2]
                        kb.dma("sp", hbb[:], hT_d[:, :, blk * 512:(blk + 1) * 512], reads=[t_hd], writes=[th])
                        for s in sigs:
                            b = nb()
                            for k in range(8):
                                mm(ps[b][:], wts[s][:, k, :], hbb[:, k, :], start=(k == 0), stop=(k == 7), reads=[twts[s], th], writes=[tps[b]])
                            cp("act", Ur[s][:, 1 + blk * 512:1 + (blk + 1) * 512], ps[b][:], reads=[tps[b]], writes=[tUr[s]])

                def sconv(s):
                    slot = s * 4 + cc
                    act(ctmp[:], Ur[s][:, 1:L + 1], AF.Identity, reads=[tUr[s], t_hc], writes=[t_ct],
                        scale=cwt[:, slot, 1:2], bias=cbt[:, slot:slot + 1])
                    stt(ctmp[:], Ur[s][:, 0:L], cwt[:, slot, 0:1], ctmp[:], ALU.mult, ALU.add, reads=[tUr[s], t_hc, t_ct], writes=[t_ct])
                    stt(Z[s][:], Ur[s][:, 2:L + 2], cwt[:, slot, 2:3], ctmp[:], ALU.mult, ALU.add, reads=[tUr[s], t_hc, t_ct], writes=[tZ[s]])

                proj_pass([0, 1, 2], 0)
                sconv(0)
                f1_pre(Z[0][:], tZ[0], 4)
                sconv(1)
                f1_part(Z[0][:], tZ[0], 4, pre=True, act_only=True)
                sconv(2)
                if "p2a" in dbg and cc == 0:
                    for s in range(3):
                        d = dbg_tensor(f"Z{s}", [128, L], BF)
                        kb.dma("sp", d, Z[s][:], reads=[tZ[s]])
                kb.barrier()
            with ExitStack() as pb:
                def sbb(name, shape, dt):
                    return pb.enter_context(nc.sbuf_tensor(f"s{next_id()}_" + name, list(shape), dt))

                Y = sbb("Y", [128, 128, 66], BF)
                tY = [T() for _ in range(33)]
                Pqs = [sbb(f"Pq{i}", [66, 64, 128], BF) for i in range(2)]
                t_Pqs = [T(), T()]
                pw = [sbb(f"pw{i}", [128, 2, 2, 128], F32) for i in range(4)]
                tpw = [T() for _ in range(4)]

                def f2_part(consumer):
                    k1 = 0
                    while k1 < 33:
                        n = 2 if k1 + 1 < 33 else 1
                        b = nb()
                        for u_ in range(n):
                            kk = k1 + u_
                            o_ = u_ * 256
                            ar, ai = A[:, kk, :], A[:, 33 + kk, :]
                            mm(ps[b][:, o_:o_ + 128], gtab[:, kk, 0, :], ar, start=True, stop=False, reads=[t_hc, tA[kk]], writes=[tps[b]])
                            mm(ps[b][:, o_:o_ + 128], gtab[:, kk, 2, :], ai, start=False, stop=True, reads=[t_hc, tA[kk]], writes=[tps[b]])
                            mm(ps[b][:, o_ + 128:o_ + 256], gtab[:, kk, 1, :], ar, start=True, stop=False, reads=[t_hc, tA[kk]], writes=[tps[b]])
                            mm(ps[b][:, o_ + 128:o_ + 256], gtab[:, kk, 0, :], ai, start=False, stop=True, reads=[t_hc, tA[kk]], writes=[tps[b]])
                        consumer(k1, n, b)
                        k1 += n

                def conv(o, sig, t_sig, xm, t_xm, zout, t_zout):
                    r_ = 2 * cc + o
                    kb.dma("sp", H[:].rearrange("p a b c -> p (a b c)"), hall_d[r_ * 128:(r_ + 1) * 128, :], reads=[t_hall[r_]], writes=tH)
                    if "p2h" in dbg and cc == 0:
                        d = dbg_tensor(f"H{o}", [128, 33 * 2 * 128], BF)
                        kb.dma("sp", d, H[:].rearrange("p a b c -> p (a b c)"), reads=tH)

                    def cons_d(k1, n, b):
                        i0 = ((k1 // 2) % 2) * 2
                        p1, p2 = pw[i0], pw[i0 + 1]
                        xv = ps[b][:, 0:n * 256].rearrange("p (k r c) -> p k r c", k=n, r=2)
                        hb_ = H[:, k1, 0, :]
                        pst = list(hb_.ap[0])
                        hr = bass.AP(hb_.tensor, hb_.offset, [pst, [256, n], [0, 2], [1, 128]])
                        hi = bass.AP(hb_.tensor, hb_.offset + 128, [pst, [256, n], [0, 2], [1, 128]])
                        rd = [tps[b]] + [tH[k1 + u_] for u_ in range(n)]
                        tt("dve", p1[:, 0:n, :, :], xv, hr, ALU.mult, reads=rd, writes=[tpw[i0]])
                        tt("dve", p2[:, 0:n, :, :], xv, hi, ALU.mult, reads=rd, writes=[tpw[i0 + 1]])

                        def ck(t_, r_):
                            v = t_[:, 0:n, r_, :]
                            return bass.AP(v.tensor, v.offset, [list(v.ap[0]), [1, 128], [256, n]])

                        wy = [tY[k1 + u_] for u_ in range(n)]
                        tt("dve", Y[:, :, k1:k1 + n], ck(p1, 0), ck(p2, 1), ALU.subtract, reads=[tpw[i0], tpw[i0 + 1]], writes=wy)
                        tt("pool", Y[:, :, 33 + k1:33 + k1 + n], ck(p2, 0), ck(p1, 1), ALU.add, reads=[tpw[i0], tpw[i0 + 1]], writes=wy)

                    if o == 1:
                        f1_part(sig, t_sig, 4)
                    f2_part(cons_d)
                    cnt = 0
                    zv = zout.rearrange("c (a b) -> c b a", a=32)
                    xv_ = xm.rearrange("c (a b) -> c b a", a=32)
                    def i1_stage(q):
                        Pq, t_Pq = Pqs[q % 2], t_Pqs[q % 2]
                        for c0 in range(0, 128, 8):
                            b = nb()
                            for jj in range(8):
                                mm(ps[b][0:66, jj * 64:(jj + 1) * 64], Y[:, c0 + jj, :], etab[:, q, :, :].rearrange("p e n -> p (e n)"),
                                   start=True, stop=True, reads=tY + [t_hc], writes=[tps[b]])
                            eng = "dve" if (c0 // 8) % 4 == 3 else "act"
                            cp(eng, Pq[:, :, c0:c0 + 8], ps[b][0:66, :].rearrange("p (c k) -> p k c", k=64), reads=[tps[b]], writes=[t_Pq])

                    def i2_stage(q):
                        Pq, t_Pq = Pqs[q % 2], t_Pqs[q % 2]
                        for hh in range(2):
                            b = nb()
                            for j in range(16):
                                n2l = hh * 16 + j
                                n2 = q * 32 + n2l
                                for e_ in range(2):
                                    mm(ps[b][:, j * 32:(j + 1) * 32], Pq[:, e_ * 32 + n2l, :], ttab[:, e_, n2, :], start=(e_ == 0), stop=(e_ == 1),
                                       reads=[t_Pq, t_hc], writes=[tps[b]])
                            n20 = q * 32 + hh * 16
                            tt("dve", zv[:, n20:n20 + 16, :], ps[b][:].rearrange("p (b a) -> p b a", a=32), xv_[:, n20:n20 + 16, :], ALU.mult,
                               reads=[tps[b], t_xm], writes=[t_zout])

                    i1_stage(0)
                    for q in range(4):
                        if q + 1 < 4:
                            i1_stage(q + 1)
                        i2_stage(q)

                conv(0, Z[0][:], tZ[0], Z[1][:], tZ[1], Z[0][:], tZ[0])
                if "p2c" in dbg and cc == 0:
                    d = dbg_tensor("z1", [128, L], BF)
                    kb.dma("sp", d, Z[0][:], reads=[tZ[0]])
                conv(1, Z[0][:], tZ[0], Z[2][:], tZ[2], Z[1][:], tZ[1])
                kb.dma("sp", zT_d[:, cc, :], Z[1][:], reads=[tZ[1]], writes=[t_zd])
                kb.barrier()
            if stop_after == "p2c0":
                break
        if "p2" in dbg:
            d = dbg_tensor("zT", [128, 4, L], BF)
            kb.dma("sp", d, zT_d, reads=[t_zd])
        kb.barrier()
    if stop_after in ("p2", "p2c0"):
        kb.finish()
        return nc, dbg_out

    t_md = T("mT_d")
    with ExitStack() as ph:
        def sb(name, shape, dt):
            return ph.enter_context(nc.sbuf_tensor(f"s{next_id()}_" + name, list(shape), dt))

        _bk = [0]

        def nb():
            _bk[0] = (_bk[0] + 1) % 8
            return _bk[0]

        zTs = sb("zTs", [128, 4, L], BF)
        aTs = sb("aTs", [128, 4, L], BF)
        wgt = sb("wgt", [128, 8, 2048], BF)
        whu = sb("whu", [128, 4, D], BF)
        wau = sb("wau", [128, 4, D], BF)
        t_wg = [T(f"wg{j}") for j in range(16)]
        t_wh = [T(f"wh{j}") for j in range(8)]
        t_wa = [T(f"wa{j}") for j in range(8)]
        t_zb = [T(f"zb{i}") for i in range(8)]
        t_ab = [T(f"ab{i}") for i in range(8)]
        whv = din["w_hy_up"].rearrange("(k p) n -> p k n", p=128)
        wav = din["w_att_up"].rearrange("(k p) n -> p k n", p=128)
        for j in range(8):
            for half in range(2):
                jj = half * 8 + j
                load_w(wgt[:, :, jj * 128:(jj + 1) * 128], w_in_v, OFF_G + jj * 128, t_wg[jj])
            kb.dma("pool", whu[:, :, j * 128:(j + 1) * 128], whv[:, :, j * 128:(j + 1) * 128], writes=[t_wh[j]])
            kb.dma("pool", wau[:, :, j * 128:(j + 1) * 128], wav[:, :, j * 128:(j + 1) * 128], writes=[t_wa[j]])
        for i in range(8):
            kb.dma("sp", zTs[:, :, i * 512:(i + 1) * 512], zT_d[:, :, i * 512:(i + 1) * 512], writes=[t_zb[i]])
            kb.dma("sp", aTs[:, :, i * 512:(i + 1) * 512], aT_d[:, :, i * 512:(i + 1) * 512], writes=[t_ab[i]])
        hb = [sb(f"mhb{i}", [128, 8, 512], BF) for i in range(2)]
        thb = [T(), T()]
        mst = [sb(f"mst{i}", [128, 8, 512], BF) for i in range(2)]
        tmst = [T(), T()]
        sg = [sb(f"sg{i}", [128, 512], F32) for i in range(4)]
        tsg = [T() for _ in range(4)]
        mm_ = [sb(f"mm{i}", [128, 512], F32) for i in range(4)]
        tmm = [T() for _ in range(4)]
        kb.dma("sp", hb[0][:], hT_d[:, :, 0:512], reads=[t_hd], writes=[thb[0]])
        for blk in range(8):
            sl = slice(blk * 512, (blk + 1) * 512)
            hbb, th = hb[blk % 2], thb[blk % 2]
            if blk + 1 < 8:
                kb.dma("sp", hb[(blk + 1) % 2][:], hT_d[:, :, (blk + 1) * 512:(blk + 2) * 512], reads=[t_hd], writes=[thb[(blk + 1) % 2]])
            for j in range(8):
                bg1, bg2, by1, by2 = nb(), nb(), nb(), nb()
                for k in range(8):
                    mm(ps[bg1][:], wgt[:, k, j * 128:(j + 1) * 128], hbb[:, k, :], start=(k == 0), stop=(k == 7), reads=[t_wg[j], th], writes=[tps[bg1]])
                for k in range(8):
                    mm(ps[bg2][:], wgt[:, k, 1024 + j * 128:1024 + (j + 1) * 128], hbb[:, k, :], start=(k == 0), stop=(k == 7), reads=[t_wg[8 + j], th], writes=[tps[bg2]])
                for k in range(4):
                    mm(ps[by1][:], whu[:, k, j * 128:(j + 1) * 128], zTs[:, k, sl], start=(k == 0), stop=(k == 3), reads=[t_wh[j], t_zb[blk]], writes=[tps[by1]])
                for k in range(4):
                    mm(ps[by2][:], wau[:, k, j * 128:(j + 1) * 128], aTs[:, k, sl], start=(k == 0), stop=(k == 3), reads=[t_wa[j], t_ab[blk]], writes=[tps[by2]])
                i0 = (j % 2) * 2
                act(sg[i0][:], ps[bg1][:], AF.Sigmoid, reads=[tps[bg1]], writes=[tsg[i0]])
                act(sg[i0 + 1][:], ps[bg2][:], AF.Sigmoid, reads=[tps[bg2]], writes=[tsg[i0 + 1]])
                tt("dve", mm_[i0][:], ps[by1][:], sg[i0][:], ALU.mult, reads=[tps[by1], tsg[i0]], writes=[tmm[i0]])
                tt("dve", mm_[i0 + 1][:], ps[by2][:], sg[i0 + 1][:], ALU.mult, reads=[tps[by2], tsg[i0 + 1]], writes=[tmm[i0 + 1]])
                tt("pool", mst[blk % 2][:, j, :], mm_[i0][:], mm_[i0 + 1][:], ALU.add, reads=[tmm[i0], tmm[i0 + 1]], writes=[tmst[blk % 2]])
            kb.dma("sp", mT_d[:, :, sl], mst[blk % 2][:], reads=[tmst[blk % 2]], writes=[t_md])
        if "p4" in dbg:
            d = dbg_tensor("mT", [128, 8, L], BF)
            kb.dma("sp", d, mT_d, reads=[t_md])
        kb.barrier()
    if stop_after == "p4":
        kb.finish()
        return nc, dbg_out

    with ExitStack() as ph:
        def sb(name, shape, dt):
            return ph.enter_context(nc.sbuf_tensor(f"s{next_id()}_" + name, list(shape), dt))

        _bk = [0]

        def nb():
            _bk[0] = (_bk[0] + 1) % 8
            return _bk[0]

        wout = sb("wout", [128, 8, D], BF)
        wfg = sb("wfg", [128, 8, DFF], BF)
        wfu = sb("wfu", [128, 8, DFF], BF)
        wfd = sb("wfd", [128, NFF, D], BF)
        t_wo, t_wg, t_wu, t_wd = T("wo"), T("wg"), T("wu"), T("wd")
        kb.dma("pool", wout[:], din["w_out"].rearrange("(k p) n -> p k n", p=128), writes=[t_wo])
        t_wgs = [T(f"wfg{g}") for g in range(6)]
        t_wus = [T(f"wfu{g}") for g in range(6)]
        wfg_v = din["w_fg"].rearrange("(k p) n -> p k n", p=128)
        wfu_v = din["w_fu"].rearrange("(k p) n -> p k n", p=128)
        for g in range(6):
            c0 = g * 512
            w_ = min(512, DFF - c0)
            kb.dma("pool", wfg[:, :, c0:c0 + w_], wfg_v[:, :, c0:c0 + w_], writes=[t_wgs[g]])
            kb.dma("pool", wfu[:, :, c0:c0 + w_], wfu_v[:, :, c0:c0 + w_], writes=[t_wus[g]])
        for j in range(0, NFF, 2):
            kb.dma("pool", wfd[:, j:j + 2, :], din["w_fd"][j * 128:(j + 2) * 128, :].rearrange("(k p) n -> p k n", p=128), writes=[t_wd])
        xts = [sb(f"fx{i}", [128, D], F32) for i in range(3)]
        txt = [T(), T(), T()]
        mts = [sb(f"fm{i}", [128, 8, 128], BF) for i in range(2)]
        tmt = [T(), T()]
        tmp = sb("ftmp", [128, D], F32)
        t_tmp = T()
        junk = sb("fjunk", [128, D], BF)
        t_junk = T()
        xs = sb("fxs", [128, D], BF)
        t_xs = T()
        hfT = [sb(f"hfT{i}", [128, 8, 128], BF) for i in range(2)]
        t_hf = [T(), T()]
        aT = sb("aT", [128, NFF, 128], BF)
        t_aT = T()
        sgt = [sb(f"sgt{i}", [128, 512], F32) for i in range(2)]
        tsgt = [T(), T()]
        atm = sb("atm", [128, DFF], BF)
        t_atm = T()
        fin = sb("ffin", [128, 16], F32)
        t_fin = [T(), T(), T()]
        t_out = T("out")

        def norm_resid(banks, xt, tx, grow, c0, tf):
            for n in range(2):
                act(junk[:, n * 512:(n + 1) * 512], ps[banks[n]][:], AF.Square, reads=[tps[banks[n]]], writes=[t_junk, tf],
                    accum_out=fin[:, c0 + n:c0 + n + 1])
            tt("dve", fin[:, c0 + 2:c0 + 3], fin[:, c0:c0 + 1], fin[:, c0 + 1:c0 + 2], ALU.add, reads=[tf], writes=[tf])
            rstd_of(fin[:, c0 + 2:c0 + 3], fin[:, c0 + 3:c0 + 4], 1.0 / D, tf)
            for n in range(2):
                hs = slice(n * 512, (n + 1) * 512)
                stt(tmp[:, hs], ps[banks[n]][:], fin[:, c0 + 3:c0 + 4], grow[:, hs], ALU.mult, ALU.mult,
                    reads=[tps[banks[n]], tf, t_rows], writes=[t_tmp])
            tt("pool", xt[:], xt[:], tmp[:], ALU.add, reads=[t_tmp, tx], writes=[tx])

        def s1a(i):
            tsl = slice(i * 128, (i + 1) * 128)
            xt, tx = xts[i % 3], txt[i % 3]
            mt, tm = mts[i % 2], tmt[i % 2]
            kb.dma("sp", xt[:], din["x"][tsl, :], writes=[tx])
            kb.dma("sp", mt[:], mT_d[:, :, tsl], reads=[t_md], writes=[tm])
            for n in range(2):
                for k in range(8):
                    mm(ps[n][:], mt[:, k, :], wout[:, k, n * 512:(n + 1) * 512], start=(k == 0), stop=(k == 7),
                       reads=[tm, t_wo], writes=[tps[n]])

        def s1b(i):
            xt, tx = xts[i % 3], txt[i % 3]
            norm_resid([0, 1], xt, tx, g1row, 0, t_fin[0])
            act(junk[:], xt[:], AF.Square, reads=[tx], writes=[t_junk, t_fin[1]], accum_out=fin[:, 4:5])
            rstd_of(fin[:, 4:5], fin[:, 5:6], 1.0 / D, t_fin[1])
            act(xs[:], xt[:], AF.Copy, reads=[tx, t_fin[1]], writes=[t_xs], scale=fin[:, 5:6])

        def s1c(i):
            for k in range(8):
                tr(psb[4][:, k * 128:(k + 1) * 128], xs[:, k * 128:(k + 1) * 128], ident_bf[:], reads=[t_xs, t_const], writes=[tps[4]])
            for k in range(8):
                ts("dve", hfT[i % 2][:, k, :], psb[4][:, k * 128:(k + 1) * 128], A2(k), B2(k), ALU.mult, ALU.add,
                   reads=[tps[4], t_cols], writes=[t_hf[i % 2]])

        def s2a(i):
            h_, th_ = hfT[i % 2], t_hf[i % 2]
            pairs = [(5, 6), (7, 4)]
            for g in range(6):
                c0 = g * 512
                w_ = min(512, DFF - c0)
                bg, bu = pairs[g % 2]
                for k in range(8):
                    mm(ps[bg][:, 0:w_], h_[:, k, :], wfg[:, k, c0:c0 + w_], start=(k == 0), stop=(k == 7), reads=[t_wgs[g], th_], writes=[tps[bg]])
                for k in range(8):
                    mm(ps[bu][:, 0:w_], h_[:, k, :], wfu[:, k, c0:c0 + w_], start=(k == 0), stop=(k == 7), reads=[t_wus[g], th_], writes=[tps[bu]])
                act(sgt[g % 2][:, 0:w_], ps[bg][:, 0:w_], AF.Silu, reads=[tps[bg]], writes=[tsgt[g % 2]])
                tt("dve", atm[:, c0:c0 + w_], ps[bu][:, 0:w_], sgt[g % 2][:, 0:w_], ALU.mult, reads=[tps[bu], tsgt[g % 2]], writes=[t_atm])
            j = 0
            bi = 0
            while j < NFF:
                n = min(8, NFF - j)
                b = (7, 4, 5)[bi % 3]
                bi += 1
                for jj in range(n):
                    tr(psb[b][:, jj * 128:(jj + 1) * 128], atm[:, (j + jj) * 128:(j + jj + 1) * 128], ident_bf[:], reads=[t_atm, t_const], writes=[tps[b]])
                cp("dve" if bi % 2 else "act", aT[:, j:j + n, :], psb[b][:, 0:n * 128].rearrange("p (j t) -> p j t", t=128), reads=[tps[b]], writes=[t_aT])
                j += n

        def s2b(i):
            tsl = slice(i * 128, (i + 1) * 128)
            xt, tx = xts[i % 3], txt[i % 3]
            for n in range(2):
                for j in range(NFF):
                    mm(ps[2 + n][:], aT[:, j, :], wfd[:, j, n * 512:(n + 1) * 512], start=(j == 0), stop=(j == NFF - 1),
                       reads=[t_aT, t_wd], writes=[tps[2 + n]])
            norm_resid([2, 3], xt, tx, g2row, 8, t_fin[2])
            kb.dma("sp", out[tsl, :], xt[:], reads=[tx], writes=[t_out])

        s1a(0)
        s1b(0)
        s1c(0)
        s1a(1)
        s1b(1)
        for i in range(32):
            s2a(i)
            if i + 1 < 32:
                s1c(i + 1)
            if i + 2 < 32:
                s1a(i + 2)
                s1b(i + 2)
            s2b(i)
        kb.barrier()
    kb.finish()
    return nc, dbg_out


_NC = None


def kernel(**inputs):
    global _NC
    if _NC is None:
        _NC = build()[0]
    in_maps = [layout_inputs(inputs, b) for b in range(8)]
    res = run_bass_kernel_spmd(_NC, in_maps, core_ids=list(range(8)))
    return np.stack([np.asarray(r["out"], dtype=np.float32) for r in res.results], 0)
```

```python
import math
from contextlib import ExitStack
import numpy as np
import ml_dtypes
import concourse.bass as bass
import concourse.mybir as mybir
from concourse.bass_utils import run_bass_kernel_spmd

F32 = mybir.dt.float32
BF = mybir.dt.bfloat16
AF = mybir.ActivationFunctionType
ALU = mybir.AluOpType
AX = mybir.AxisListType

L = 4096
D = 1024
CT = 256
LK = L + CT
DH = 512
DFF = 2816
NFF = DFF // 128
OFF_Q, OFF_K, OFF_V, OFF_G = 1536, 2048, 2560, 3072
EPS = 1e-6
LAM_INIT = 0.8 - 0.6 * math.exp(0.0)
PI = math.pi


class Sem:
    def __init__(self, h):
        self.h = h
        self.count = 0


class T:
    __slots__ = ("name", "w", "r")

    def __init__(self, name=""):
        self.name = name
        self.w = None
        self.r = []


class Eng:
    def __init__(self, name, sem):
        self.name = name
        self.sem = sem
        self.ops = []
        self.seen = {}


class KB:
    def __init__(self, nc, nsem_dma=14):
        self.nc = nc
        self.engs = {}
        for n in ("pe", "act", "dve", "pool", "sp"):
            self.engs[n] = Eng(n, Sem(nc.alloc_semaphore("s_" + n)))
        self.dsems = {q: [Sem(nc.alloc_semaphore(f"d_{q}{i}")) for i in range(nsem_dma)] for q in ("sp", "pool")}
        self.drr = {"sp": 0, "pool": 0}

    def _waits(self, eng, reads, writes, extra=()):
        deps = {}

        def add(d):
            if d is None:
                return
            s, v = d
            if deps.get(s, 0) < v:
                deps[s] = v

        for t in reads:
            add(t.w)
        for t in writes:
            add(t.w)
            for d in t.r:
                add(d)
        for d in extra:
            add(d)
        out = []
        for s, v in deps.items():
            if s is eng.sem and eng.name == "pe":
                continue
            if eng.seen.get(s, 0) >= v:
                continue
            eng.seen[s] = v
            out.append((s, v))
        return out

    def _mark(self, tok, reads, writes):
        for t in reads:
            t.r = [d for d in t.r if d[0] is not tok[0]]
            t.r.append(tok)
        for t in writes:
            t.w = tok
            t.r = []

    def op(self, engname, fn, reads=(), writes=()):
        eng = self.engs[engname]
        waits = self._waits(eng, reads, writes)
        eng.sem.count += 1
        tok = (eng.sem, eng.sem.count)
        eng.ops.append((waits, fn, (eng.sem, 1)))
        self._mark(tok, reads, writes)
        return tok

    def dma(self, q, out_ap, in_ap, reads=(), writes=(), **kw):
        eng = self.engs[q]
        sems = self.dsems[q]
        s = sems[self.drr[q] % len(sems)]
        self.drr[q] += 1
        waits = self._waits(eng, reads, writes, extra=[(s, s.count)] if s.count else [])
        s.count += 16
        tok = (s, s.count)
        eng.ops.append((waits, lambda e: e.dma_start(out=out_ap, in_=in_ap, **kw), (s, 16)))
        self._mark(tok, reads, writes)
        return tok

    def collective(self, kind, ins, outs, reads=(), writes=()):
        eng = self.engs["pool"]
        if not hasattr(self, "ccsem"):
            self.ccsem = Sem(self.nc.alloc_semaphore("s_cc"))
        s = self.ccsem
        waits = self._waits(eng, reads, writes)
        s.count += 1
        tok = (s, s.count)
        eng.ops.append((waits, lambda e: e.collective_compute(kind, ALU.bypass, replica_groups=[list(range(8))], ins=ins, outs=outs), (s, 1)))
        self._mark(tok, reads, writes)
        return tok

    def barrier(self, include_cc=False):
        allsems = [e.sem for e in self.engs.values()] + [s for q in self.dsems.values() for s in q]
        if include_cc and hasattr(self, "ccsem"):
            allsems.append(self.ccsem)
        for eng in self.engs.values():
            waits = []
            for s in allsems:
                if s is eng.sem or s.count == 0:
                    continue
                if eng.seen.get(s, 0) >= s.count:
                    continue
                eng.seen[s] = s.count
                waits.append((s, s.count))
            if waits:
                eng.ops.append((waits, None, None))

    def finish(self):
        nc = self.nc
        self.barrier(include_cc=True)
        with nc.Block() as block:
            def emit(e, en):
                for waits, fn, inc in en.ops:
                    for (ws, wv) in waits:
                        e.wait_ge(ws.h, wv)
                    if fn is not None:
                        ins = fn(e)
                        ins.then_inc(inc[0].h, inc[1])

            @block.tensor
            def _(e):
                emit(e, self.engs["pe"])

            @block.scalar
            def _(e):
                emit(e, self.engs["act"])

            @block.vector
            def _(e):
                emit(e, self.engs["dve"])

            @block.gpsimd
            def _(e):
                emit(e, self.engs["pool"])

            @block.sync
            def _(e):
                emit(e, self.engs["sp"])


def _bf(a):
    return np.ascontiguousarray(a.astype(np.float32)).astype(ml_dtypes.bfloat16)


_CONST = None


def host_consts():
    global _CONST
    if _CONST is not None:
        return _CONST
    c = {}
    c["ident_bf"] = _bf(np.eye(128))
    c["ident_f"] = np.eye(128, dtype=np.float32)
    t = np.arange(L)
    row = (t // 64).astype(np.float32)
    col = (t % 64).astype(np.float32)
    inv = (10000.0 ** (-np.arange(16, dtype=np.float32) / 16)).astype(np.float32)
    cos64 = np.zeros((64, L), np.float32)
    sin64 = np.zeros((64, L), np.float32)
    for half, pos in ((0, row), (1, col)):
        ang = pos[None, :] * inv[:, None]
        base = half * 32
        cos64[base:base + 16] = np.cos(ang)
        cos64[base + 16:base + 32] = np.cos(ang)
        sin64[base:base + 16] = -np.sin(ang)
        sin64[base + 16:base + 32] = np.sin(ang)
    c["rope_cos"] = np.concatenate([cos64, cos64], 0)
    c["rope_sin"] = np.concatenate([sin64, sin64], 0)
    f32 = np.float32
    bands = 16
    tt = np.linspace(0.0, 1.0, L, dtype=f32)[:, None]
    w = (f32(2.0 * math.pi / L) * np.arange(L, dtype=f32))[:, None]
    fr = np.linspace(1e-4, bands - 1, bands, dtype=f32)[None, :]
    z = np.concatenate([tt, np.cos(fr * w), -np.sin(fr * w)], axis=-1).astype(f32)
    c["zT"] = np.ascontiguousarray(z.T)
    deltas = np.abs(np.linspace(math.log(1e-2) / 1.5, math.log(1e-2) / 0.3, DH, dtype=f32))
    nd = (-deltas).reshape(4, 128).T
    c["negdelta"] = np.ascontiguousarray(nd.astype(f32))
    offs = (np.arange(8) * 512 / (L - 1)).astype(f32)
    c["ndoff"] = np.ascontiguousarray((nd[:, :, None] * offs[None, None, :]).astype(f32))
    c["tv0"] = np.ascontiguousarray(np.broadcast_to((np.arange(512) / (L - 1)).astype(f32)[None, :], (128, 512)))
    n1 = np.arange(32)[:, None]
    k1 = np.arange(33)[None, :]
    a = 2 * np.pi * n1 * k1 / 64.0
    c["d1f"] = _bf(np.concatenate([np.cos(a), -np.sin(a)], 1))
    c["d1fp"] = np.ascontiguousarray(np.concatenate([c["d1f"], np.zeros((96, 66), ml_dtypes.bfloat16)], 0))
    c["d1b"] = _bf(np.concatenate([np.cos(a), np.sin(a)], 1))
    n2 = np.arange(128)[:, None, None]
    k1g = np.arange(33)[None, :, None]
    k2 = np.arange(128)[None, None, :]
    ang = 2 * np.pi * n2 * (k1g + 64 * k2) / 8192.0
    gr, gi = np.cos(ang), -np.sin(ang)
    c["gtab"] = _bf(np.stack([gr, gi, -gi], 2))
    k2e = np.arange(128)[:, None]
    n2e = np.arange(128)[None, :]
    ae = 2 * np.pi * k2e * n2e / 128.0
    c["etab"] = _bf(np.stack([np.cos(ae).reshape(128, 4, 32), np.sin(ae).reshape(128, 4, 32)], 2))
    k1t = np.arange(33)[:, None, None]
    n2t = np.arange(128)[None, :, None]
    n1t = np.arange(32)[None, None, :]
    at = 2 * np.pi * k1t * (128 * n1t + n2t) / 8192.0
    wgt = np.full((33, 1, 1), 2.0)
    wgt[0] = 1.0
    wgt[32] = 1.0
    tr = wgt * np.cos(at) / 8192.0
    ti = wgt * np.sin(at) / 8192.0
    t0 = np.concatenate([tr, -ti], 0)
    t1 = np.concatenate([-ti, -tr], 0)
    c["ttab"] = _bf(np.stack([t0, t1], 1))
    _CONST = c
    return c


CONST_SHAPES = {
    "ident_bf": ([128, 128], BF), "ident_f": ([128, 128], F32),
    "rope_cos": ([128, L], F32), "rope_sin": ([128, L], F32),
    "zT": ([33, L], F32), "negdelta": ([128, 4], F32), "ndoff": ([128, 4, 8], F32), "tv0": ([128, 512], F32),
    "d1f": ([32, 66], BF), "d1fp": ([128, 66], BF), "d1b": ([32, 66], BF), "gtab": ([128, 33, 3, 128], BF),
    "etab": ([128, 4, 2, 32], BF), "ttab": ([66, 2, 128, 32], BF),
}

IN_SHAPES = {
    "x": [L, D], "ctx": [CT, D], "cc": [128, 8, 2], "w_ada": [D, 6 * D], "b_adaT": [128, 48], "gcols": [128, 4, 8],
    "w_in": [D, 5120], "w_qk_sw": [D, 1024], "cw": [128, 12, 3], "cb": [128, 12],
    "fw1": [33, 64], "fb1": [64, 1], "fw2": [64, 64], "fb2": [64, 1], "fw3": [64, 2048], "fb3": [1, 2048],
    "ffreq": [64, 1], "hyb": [128, 2, 4], "lamv": [1, 256], "subg": [1, 128],
    "w_hy_up": [DH, D], "w_att_up": [DH, D], "w_out": [D, D], "w_fg": [D, DFF], "w_fu": [D, DFF], "w_fd": [DFF, D],
}


def layout_inputs(inp, b):
    f = lambda a: np.ascontiguousarray(np.asarray(a, dtype=np.float32))
    m = {}
    m["x"] = f(inp["x"][b])
    m["ctx"] = f(inp["ctx"][b])
    cc = np.stack([np.asarray(inp["c"][b]), np.asarray(inp["c_ctx"])], -1)
    m["cc"] = f(cc.reshape(8, 128, 2).transpose(1, 0, 2))
    m["w_ada"] = f(inp["w_ada"][0])
    m["b_adaT"] = f(np.asarray(inp["b_ada"][0]).reshape(48, 128).T)
    g = np.stack([np.asarray(inp[k][0]) for k in ("g_mix_pre", "g_mix_post", "g_ffn_pre", "g_ffn_post")], 0)
    m["gcols"] = f(g.reshape(4, 8, 128).transpose(2, 0, 1))
    w_in = np.asarray(inp["w_in"][0])
    m["w_in"] = f(w_in)
    perm = np.arange(1024).reshape(16, 2, 2, 16)[:, :, ::-1, :].reshape(-1)
    m["w_qk_sw"] = f(w_in[:, OFF_Q:OFF_V][:, perm])
    m["cw"] = f(np.asarray(inp["hy_conv_w"][0]).reshape(3, 12, 128).transpose(2, 1, 0))
    m["cb"] = f(np.asarray(inp["hy_conv_b"][0]).reshape(12, 128).T)
    m["fw1"] = f(inp["hy_f_w1"][0])
    m["fb1"] = f(np.asarray(inp["hy_f_b1"][0]).reshape(64, 1))
    m["fw2"] = f(inp["hy_f_w2"][0])
    m["fb2"] = f(np.asarray(inp["hy_f_b2"][0]).reshape(64, 1))
    m["fw3"] = f(inp["hy_f_w3"][0])
    m["fb3"] = f(np.asarray(inp["hy_f_b3"][0]).reshape(1, 2048))
    m["ffreq"] = f(np.asarray(inp["hy_f_freq"][0]).reshape(64, 1))
    m["hyb"] = f(np.asarray(inp["hy_bias"][0]).reshape(2, 4, 128).transpose(2, 0, 1))
    m["lamv"] = f(np.concatenate([np.asarray(inp[k][0]) for k in ("lambda_q1", "lambda_q2", "lambda_k1", "lambda_k2")]).reshape(1, 256))
    m["subg"] = f(np.asarray(inp["att_subln_g"][0]).reshape(1, 128))
    m["w_hy_up"] = f(inp["w_hy_up"][0])
    m["w_att_up"] = f(inp["w_att_up"][0])
    m["w_out"] = f(inp["w_out"][0])
    m["w_fg"] = f(inp["w_ffn_gate"][0])
    m["w_fu"] = f(inp["w_ffn_up"][0])
    m["w_fd"] = f(inp["w_ffn_down"][0])
    m.update(host_consts())
    return m


def build(dbg=(), stop_after=None, skip=()):
    nc = bass.Bass("TRN2", target_bir_lowering=False)
    kb = KB(nc)
    din = {}
    for k, shp in IN_SHAPES.items():
        din[k] = nc.dram_tensor(k, list(shp), F32, kind="ExternalInput").ap()
    for k, (shp, dt) in CONST_SHAPES.items():
        din[k] = nc.dram_tensor(k, list(shp), dt, kind="ExternalInput").ap()
    out = nc.dram_tensor("out", [L, D], F32, kind="ExternalOutput").ap()
    hT_d = nc.dram_tensor("hT_d", [128, 8, L], BF, kind="Internal").ap()
    mT_d = nc.dram_tensor("mT_d", [128, 8, L], BF, kind="Internal").ap()
    zT_d = nc.dram_tensor("zT_d", [128, 4, L], BF, kind="Internal").ap()
    aT_d = nc.dram_tensor("aT_d", [128, 4, L], BF, kind="Internal").ap()
    sig_d = nc.dram_tensor("sig_d", [5, 128, L], BF, kind="Internal").ap()
    dbg_out = {}

    def dbg_tensor(name, shape, dt=F32):
        dbg_out[name] = nc.dram_tensor("dbg_" + name, list(shape), dt, kind="ExternalOutput").ap()
        return dbg_out[name]

    psall = nc.alloc_psum_tensor("psall", [128, 4096], F32)
    psall_b = psall.bitcast(BF)
    ps = [psall[:, i * 512:(i + 1) * 512] for i in range(8)]
    psb = [psall_b[:, i * 1024:(i + 1) * 1024] for i in range(8)]
    tps = [T(f"ps{i}") for i in range(8)]

    def mm(out_ap, lhsT, rhs, start, stop, reads, writes, tile_position=None):
        if tile_position is None:
            kb.op("pe", lambda e: e.matmul(out_ap, lhsT, rhs, start=start, stop=stop), reads=reads, writes=writes)
        else:
            kb.op("pe", lambda e: e.matmul(out_ap, lhsT, rhs, start=start, stop=stop, tile_position=tile_position), reads=reads, writes=writes)

    def tr(out_ap, in_ap, ident, reads, writes):
        kb.op("pe", lambda e: e.transpose(out_ap, in_ap, ident), reads=reads, writes=writes)

    def act(out_ap, in_ap, func, reads, writes, **kw):
        kb.op("act", lambda e: e.activation(out=out_ap, in_=in_ap, func=func, **kw), reads=reads, writes=writes)

    def ts(eng, out_ap, in0, s1, s2, op0, op1, reads, writes, **kw):
        if s2 is None:
            kb.op(eng, lambda e: e.tensor_scalar(out_ap, in0, s1, None, op0, **kw), reads=reads, writes=writes)
        else:
            kb.op(eng, lambda e: e.tensor_scalar(out_ap, in0, s1, s2, op0, op1, **kw), reads=reads, writes=writes)

    def tt(eng, out_ap, in0, in1, op, reads, writes):
        kb.op(eng, lambda e: e.tensor_tensor(out_ap, in0, in1, op), reads=reads, writes=writes)

    def stt(out_ap, in0, scalar, in1, op0, op1, reads, writes):
        kb.op("dve", lambda e: e.scalar_tensor_tensor(out_ap, in0, scalar, in1, op0, op1), reads=reads, writes=writes)

    def cp(eng, out_ap, in_ap, reads, writes):
        if eng == "act":
            kb.op("act", lambda e: e.copy(out_ap, in_ap), reads=reads, writes=writes)
        else:
            kb.op(eng, lambda e: e.tensor_copy(out_ap, in_ap), reads=reads, writes=writes)

    def recip(out_ap, in_ap, reads, writes):
        kb.op("dve", lambda e: e.reciprocal(out_ap, in_ap), reads=reads, writes=writes)

    def rsum(out_ap, in_ap, reads, writes):
        kb.op("dve", lambda e: e.reduce_sum(out_ap, in_ap, AX.X), reads=reads, writes=writes)

    def memset(eng, ap, val, writes):
        kb.op(eng, lambda e: e.memset(ap, val), writes=writes)

    _ids = [0]

    def next_id():
        _ids[0] += 1
        return _ids[0]

    P = ExitStack()

    def sbp(name, shape, dt):
        return P.enter_context(nc.sbuf_tensor("sp_" + name, list(shape), dt))

    ident_bf = sbp("ident_bf", [128, 128], BF)
    ident_f = sbp("ident_f", [128, 128], F32)
    ones_f = sbp("ones_f", [128, 128], F32)
    mhalf = sbp("mhalf", [128, 32], F32)
    cols = sbp("cols", [128, 8, 8], F32)
    g1row = sbp("g1row", [128, D], F32)
    g2row = sbp("g2row", [128, D], F32)
    neglam = sbp("neglam", [128, 1], F32)
    sgrow = sbp("sgrow", [128, 128], F32)
    hcT = sbp("hcT", [128, 8, CT], BF)
    t_const = T("const")
    t_cols = T("cols")
    t_rows = T("rows")
    t_hcT = T("hcT")
    kb.dma("sp", ident_bf[:], din["ident_bf"], writes=[t_const])
    kb.dma("sp", ident_f[:], din["ident_f"], writes=[t_const])
    memset("dve", ones_f[:], 1.0, [t_const])
    memset("dve", mhalf[:], -0.5, [t_const])
    A1 = lambda k: cols[:, 0, k:k + 1]
    B1 = lambda k: cols[:, 1, k:k + 1]
    A2 = lambda k: cols[:, 2, k:k + 1]
    B2 = lambda k: cols[:, 3, k:k + 1]
    A1c = lambda k: cols[:, 4, k:k + 1]
    B1c = lambda k: cols[:, 5, k:k + 1]

    def rstd_of(ssq_ap, out_ap, inv_n, tl):
        n = ssq_ap.shape[-1] if len(ssq_ap.shape) > 1 else 1
        ts("dve", out_ap, ssq_ap, inv_n, EPS, ALU.mult, ALU.add, reads=[tl], writes=[tl])
        tt("pool", out_ap, out_ap, mhalf[:, 0:n], ALU.pow, reads=[tl, t_const], writes=[tl])

    with ExitStack() as ph:
        def sb(name, shape, dt):
            return ph.enter_context(nc.sbuf_tensor(f"s{next_id()}_" + name, list(shape), dt))

        ccs = sb("ccs", [128, 16], F32)
        scs = sb("scs", [128, 16], F32)
        bada = sb("bada", [128, 48], F32)
        gc = sb("gc", [128, 4, 8], F32)
        adaT = sb("adaT", [128, 48, 2], F32)
        wa = [sb(f"wa{i}", [128, 6 * D], F32) for i in range(2)]
        twa = [T("wa0"), T("wa1")]
        t_s = T("p0small")
        kb.dma("sp", ccs[:], din["cc"].rearrange("p k c -> p (k c)"), writes=[t_s])
        kb.dma("sp", bada[:], din["b_adaT"], writes=[t_s])
        kb.dma("sp", gc[:], din["gcols"], writes=[t_s])
        act(scs[:], ccs[:], AF.Silu, reads=[t_s], writes=[t_s])
        def ada_chunk(k):
            kb.dma("sp", wa[k % 2][:], din["w_ada"][k * 128:(k + 1) * 128, :], writes=[twa[k % 2]])
            for f in range(48):
                mm(ps[0][:, 2 * f:2 * f + 2], wa[k % 2][:, f * 128:(f + 1) * 128], scs[:, 2 * k:2 * k + 2],
                   start=(k == 0 and f == 0), stop=(k == 7 and f == 47), reads=[twa[k % 2], t_s], writes=[tps[0]])
        lamb = sb("lamb", [128, 256], F32)
        lp = sb("lp", [128, 2, 64], F32)
        le = sb("le", [128, 2], F32)
        kb.dma("sp", lamb[:], bass.AP(din["lamv"].tensor, 0, [[0, 128], [1, 256]]), writes=[t_s])
        kb.dma("sp", sgrow[:], bass.AP(din["subg"].tensor, 0, [[0, 128], [1, 128]]), writes=[t_rows])
        tt("dve", lp[:].rearrange("p a b -> p (a b)"), lamb[:, 0:128], lamb[:, 128:256], ALU.mult, reads=[t_s], writes=[t_s])
        rsum(le[:], lp[:], reads=[t_s], writes=[t_s])
        act(le[:], le[:], AF.Exp, reads=[t_s], writes=[t_s])
        tt("dve", neglam[:], le[:, 1:2], le[:, 0:1], ALU.subtract, reads=[t_s], writes=[t_cols])
        ts("dve", neglam[:], neglam[:], -LAM_INIT, None, ALU.add, None, reads=[t_cols], writes=[t_cols])
        ts("dve", sgrow[:], sgrow[:], 1.0 - LAM_INIT, None, ALU.mult, None, reads=[t_rows], writes=[t_rows])
        xts = [sb(f"xt{i}", [128, D], F32) for i in range(4)]
        txt = [T() for _ in range(4)]
        junk = sb("junk", [128, D], BF)
        t_junk = T()
        xsa = sb("xsa", [128, 34, D], BF)
        txs = [T() for _ in range(34)]
        ssq = sb("ssq", [128, 34], F32)
        t_ssq = [T() for _ in range(34)]
        hst = [sb(f"hst{i}", [128, 8, 512], BF) for i in range(2)]
        thst = [T(), T()]
        t_hd = T("hT_d")
        for i in range(34):
            lat = i < 32
            src = din["x"][i * 128:(i + 1) * 128, :] if lat else din["ctx"][(i - 32) * 128:(i - 31) * 128, :]
            xt, tx = xts[i % 4], txt[i % 4]
            if i % 4 == 0 and i // 4 < 8:
                ada_chunk(i // 4)
            kb.dma("sp", xt[:], src, writes=[tx])
            act(junk[:], xt[:], AF.Square, reads=[tx], writes=[t_junk, t_ssq[i]], accum_out=ssq[:, i:i + 1])
            rstd_of(ssq[:, i:i + 1], ssq[:, i:i + 1], 1.0 / D, t_ssq[i])
            if i >= 2:
                j = i - 2
                act(xsa[:, j, :], xts[j % 4][:], AF.Copy, reads=[txt[j % 4], t_ssq[j]], writes=[txs[j]], scale=ssq[:, j:j + 1])
        for j in (32, 33):
            act(xsa[:, j, :], xts[j % 4][:], AF.Copy, reads=[txt[j % 4], t_ssq[j]], writes=[txs[j]], scale=ssq[:, j:j + 1])
        for c in range(2):
            tt("dve", adaT[:, :, c], ps[0][:, c:96:2], bada[:], ALU.add, reads=[tps[0], t_s], writes=[t_s])
        for (dst, sc_f, g_i, c) in ((0, 8, 0, 0), (2, 32, 2, 0), (4, 8, 0, 1)):
            stt(cols[:, dst, :], adaT[:, sc_f:sc_f + 8, c], 1.0, gc[:, g_i, :], ALU.add, ALU.mult, reads=[t_s], writes=[t_cols])
        for (dst, sh_f, c) in ((1, 0, 0), (3, 24, 0), (5, 0, 1)):
            cp("dve", cols[:, dst, :], adaT[:, sh_f:sh_f + 8, c], reads=[t_s], writes=[t_cols])
        tt("dve", cols[:, 6, :], adaT[:, 16:24, 0], gc[:, 1, :], ALU.mult, reads=[t_s], writes=[t_cols])
        tt("dve", cols[:, 7, :], adaT[:, 40:48, 0], gc[:, 3, :], ALU.mult, reads=[t_s], writes=[t_cols])
        diag = sb("diag", [128, 4, 128], F32)
        t_diag = T("diag")
        for gi, rowt in ((6, g1row), (7, g2row)):
            for half in range(2):
                for j in range(4):
                    ts("dve", diag[:, j, :], ident_f[:], cols[:, gi, half * 4 + j:half * 4 + j + 1], None, ALU.mult, None,
                       reads=[t_const, t_cols], writes=[t_diag])
                for j in range(4):
                    mm(ps[1][:, j * 128:(j + 1) * 128], ones_f[:], diag[:, j, :], start=True, stop=True,
                       reads=[t_diag, t_const], writes=[tps[1]])
                cp("dve", rowt[:, half * 512:(half + 1) * 512], ps[1][:], reads=[tps[1]], writes=[t_rows])
        if "p0" in dbg:
            d = dbg_tensor("cols", [128, 64])
            kb.dma("sp", d, cols[:].rearrange("p a b -> p (a b)"), reads=[t_cols])
            d = dbg_tensor("g1row", [128, D])
            kb.dma("sp", d, g1row[:], reads=[t_rows])
            d = dbg_tensor("neglam", [128, 1])
            kb.dma("sp", d, neglam[:], reads=[t_cols])

        for i in range(34):
            lat = i < 32
            bk = 2 + i % 4
            for k in range(8):
                tr(psb[bk][:, k * 128:(k + 1) * 128], xsa[:, i, k * 128:(k + 1) * 128], ident_bf[:],
                   reads=[txs[i], t_const], writes=[tps[bk]])
            for k in range(8):
                if lat:
                    h = hst[(i // 4) % 2]
                    ts("dve", h[:, k, (i % 4) * 128:(i % 4 + 1) * 128], psb[bk][:, k * 128:(k + 1) * 128], A1(k), B1(k),
                       ALU.mult, ALU.add, reads=[tps[bk], t_cols], writes=[thst[(i // 4) % 2]])
                else:
                    ts("dve", hcT[:, k, (i - 32) * 128:(i - 31) * 128], psb[bk][:, k * 128:(k + 1) * 128], A1c(k), B1c(k),
                       ALU.mult, ALU.add, reads=[tps[bk], t_cols], writes=[t_hcT])
            if lat and i % 4 == 3:
                blk = i // 4
                kb.dma("sp", hT_d[:, :, blk * 512:(blk + 1) * 512], hst[blk % 2][:], reads=[thst[blk % 2]], writes=[t_hd])
        if "p1" in dbg:
            d = dbg_tensor("hT", [128, 8, L], BF)
            kb.dma("sp", d, hT_d, reads=[t_hd])
            d = dbg_tensor("hcT", [128, 8, CT], BF)
            kb.dma("sp", d, hcT[:], reads=[t_hcT])
        kb.barrier()
    if stop_after == "p1":
        kb.finish()
        return nc, dbg_out


    hall_d = nc.dram_tensor("hall_d", [1024, 8448], BF, kind="Internal").ap()
    t_hall = [T(f"hall{i}") for i in range(8)]
    with ExitStack() as ph:
        def sb(name, shape, dt):
            return ph.enter_context(nc.sbuf_tensor(f"s{next_id()}_" + name, list(shape), dt))

        _bk = [0]

        def nb():
            _bk[0] = (_bk[0] + 1) % 8
            return _bk[0]

        t_hc = T("hfconst")
        gtab = sb("gtab", [128, 33, 3, 128], BF)
        d1f = sb("d1fp", [128, 66], BF)
        tv0 = sb("tv0", [128, 512], F32)
        negd = sb("negd", [128, 4], F32)
        ndoff = sb("ndoff", [128, 4, 8], F32)
        hybs = sb("hybs", [128, 2, 4], F32)
        for dst, nm in ((gtab, "gtab"), (d1f, "d1fp"), (tv0, "tv0"), (negd, "negdelta"), (ndoff, "ndoff"), (hybs, "hyb")):
            kb.dma("sp", dst[:], din[nm], writes=[t_hc])
        hdn2 = sb("hdn2", [65, L], BF)
        fw3a = sb("fw3a", [65, 2048], BF)
        t_h2 = T("hdn2")
        t_fw3 = T("fw3")
        kb.dma("pool", fw3a[0:64, :], din["fw3"], writes=[t_fw3])
        kb.dma("pool", fw3a[64:65, :], din["fb3"], writes=[t_fw3])
        memset("pool", hdn2[64:65, :], 1.0, [t_h2])
        zTs = sb("zTs", [33, L], F32)
        fw1s = sb("fw1s", [33, 64], F32)
        fw2s = sb("fw2s", [64, 64], F32)
        fcol = sb("fcol", [64, 5], F32)
        t_f = T("fmlp")
        kb.dma("sp", zTs[:], din["zT"], writes=[t_f])
        kb.dma("sp", fw1s[:], din["fw1"], writes=[t_f])
        kb.dma("sp", fw2s[:], din["fw2"], writes=[t_f])
        kb.dma("sp", fcol[:, 0:1], din["ffreq"], writes=[t_f])
        kb.dma("sp", fcol[:, 1:2], din["fb1"], writes=[t_f])
        kb.dma("sp", fcol[:, 2:3], din["fb2"], writes=[t_f])
        tt("dve", fcol[:, 3:4], fcol[:, 0:1], fcol[:, 1:2], ALU.mult, reads=[t_f], writes=[t_f])
        tt("dve", fcol[:, 4:5], fcol[:, 0:1], fcol[:, 2:3], ALU.mult, reads=[t_f], writes=[t_f])
        with ExitStack() as phm:
            def sbm(name, shape, dt):
                return phm.enter_context(nc.sbuf_tensor(f"s{next_id()}_" + name, list(shape), dt))

            halfpi = sbm("halfpi", [64, 1], F32)
            memset("dve", halfpi[:], PI / 2, [t_f])
            arg = [sbm(f"arg{i}", [64, 512], F32) for i in range(8)]
            s4 = [sbm(f"s4{i}", [64, 512], F32) for i in range(8)]
            c4 = [sbm(f"c4{i}", [64, 512], F32) for i in range(8)]
            hd1 = [sbm(f"hd1{i}", [64, 512], F32) for i in range(8)]
            t_m = [T() for _ in range(8)]

            def sin_layer_bf(ps_of, bias_col, out_of, t_out_of):
                for i_ in range(8):
                    ts("dve", arg[i_][:], ps[ps_of(i_)][0:64, :], fcol[:, 0:1], fcol[:, bias_col:bias_col + 1], ALU.mult, ALU.add,
                       reads=[tps[ps_of(i_)], t_f], writes=[t_m[i_]])
                for i_ in range(8):
                    act(s4[i_][:], arg[i_][:], AF.Sin, reads=[t_m[i_]], writes=[t_m[i_]], scale=0.25)
                    act(c4[i_][:], arg[i_][:], AF.Sin, reads=[t_m[i_], t_f], writes=[t_m[i_]], scale=0.25, bias=halfpi[:])
                for i_ in range(8):
                    tt("pool", arg[i_][:], s4[i_][:], s4[i_][:], ALU.mult, reads=[t_m[i_]], writes=[t_m[i_]])
                    tt("pool", s4[i_][:], s4[i_][:], c4[i_][:], ALU.mult, reads=[t_m[i_]], writes=[t_m[i_]])
                for i_ in range(8):
                    ts("dve", arg[i_][:], arg[i_][:], -2.0, 1.0, ALU.mult, ALU.add, reads=[t_m[i_]], writes=[t_m[i_]])
                    stt(out_of(i_), s4[i_][:], 4.0, arg[i_][:], ALU.mult, ALU.mult, reads=[t_m[i_]], writes=[t_m[i_], t_out_of(i_)])

            for blk in range(8):
                mm(ps[blk][0:64, :], fw1s[:], zTs[:, blk * 512:(blk + 1) * 512], start=True, stop=True, reads=[t_f], writes=[tps[blk]])
            sin_layer_bf(lambda i_: i_, 3, lambda i_: hd1[i_][:], lambda i_: t_m[i_])
            for blk in range(8):
                mm(ps[blk][0:64, :], fw2s[:], hd1[blk][:], start=True, stop=True, reads=[t_f, t_m[blk]], writes=[tps[blk]])
            sin_layer_bf(lambda i_: i_, 4, lambda i_: hdn2[0:64, i_ * 512:(i_ + 1) * 512], lambda i_: t_h2)
            kb.barrier()
        dec = [sb(f"dec{i}", [128, L], BF) for i in range(2)]
        t_dec = [T(), T()]
        Kf = [[sb(f"Kf{i}_{d_}", [128, L], BF) for d_ in range(2)] for i in range(2)]
        t_Kf = [[T(), T()], [T(), T()]]
        Ut = [sb(f"Ut{i}", [128, 32, 128], BF) for i in range(2)]
        tUt = [T(), T()]
        for i in range(2):
            memset("pool", Ut[i][32:64, :, :], 0.0, [tUt[i]])
            memset("pool", Ut[i][64:128, :, :], 0.0, [tUt[i]])
        A = sb("A", [128, 66, 128], BF)
        tA = [T() for _ in range(33)]
        H = [sb(f"H{i}", [128, 33, 2, 128], BF) for i in range(2)]
        tH = [[T() for _ in range(33)] for _ in range(2)]
        t_sdf = [T() for _ in range(4)]
        cnt = [0]

        def stage_k(r):
            cc, o = r // 2, r % 2
            pi_ = r % 2
            if o == 0:
                for b8 in range(8):
                    act(dec[cc % 2][:, b8 * 512:(b8 + 1) * 512], tv0[:], AF.Exp, reads=[t_hc], writes=[t_dec[cc % 2]],
                        scale=negd[:, cc:cc + 1], bias=ndoff[:, cc, b8:b8 + 1])
            for d_ in range(2):
                col0 = (o * 2 + d_) * 512 + cc * 128
                kf, tk = Kf[pi_][d_], t_Kf[pi_][d_]
                for blk in range(8):
                    sl = slice(blk * 512, (blk + 1) * 512)
                    b = nb()
                    mm(ps[b][:], fw3a[:, col0:col0 + 128], hdn2[:, sl], start=True, stop=True, reads=[t_fw3, t_h2], writes=[tps[b]])
                    tt("dve", kf[:, sl], ps[b][:], dec[cc % 2][:, sl], ALU.mult, reads=[tps[b], t_dec[cc % 2]], writes=[tk])
                if d_ == 0:
                    tt("dve", kf[:, 0:1], kf[:, 0:1], hybs[:, o, cc:cc + 1], ALU.add, reads=[tk, t_hc], writes=[tk])
                else:
                    memset("dve", kf[:, 0:1], 0.0, [tk])
                kb.dma("pool", sig_d[pi_ * 2 + d_], kf[:], reads=[tk], writes=[t_sdf[pi_ * 2 + d_]])

        def stage_f(r):
            pi_ = r % 2
            Hh, tHh = H[pi_], tH[pi_]
            for d_ in range(2):
                slot = pi_ * 2 + d_
                for g in range(4):
                    u, tu = Ut[g % 2], tUt[g % 2]
                    kb.dma("sp", u[0:32, :, :], sig_d[slot][g * 32:(g + 1) * 32, :].rearrange("c (a b) -> a c b", a=32), reads=[t_sdf[slot]], writes=[tu])
                    j = 0
                    while j < 32:
                        n = min(7, 32 - j)
                        b = nb()
                        for jj in range(n):
                            mm(ps[b][:, jj * 66:(jj + 1) * 66], u[:, j + jj, :], d1f[:], start=True, stop=True, reads=[tu, t_hc], writes=[tps[b]])
                        c0 = g * 32 + j
                        cp("act" if cnt[0] % 2 == 0 else "dve", A[:, :, c0:c0 + n], ps[b][:, 0:n * 66].rearrange("p (c k) -> p k c", k=66),
                           reads=[tps[b]], writes=tA)
                        cnt[0] += 1
                        j += n
                k1 = 0
                while k1 < 33:
                    n = 2 if k1 + 1 < 33 else 1
                    b = nb()
                    for u_ in range(n):
                        kk = k1 + u_
                        o_ = u_ * 256
                        ar, ai = A[:, kk, :], A[:, 33 + kk, :]
                        mm(ps[b][:, o_:o_ + 128], gtab[:, kk, 0, :], ar, start=True, stop=False, reads=[t_hc, tA[kk]], writes=[tps[b]])
                        mm(ps[b][:, o_:o_ + 128], gtab[:, kk, 2, :], ai, start=False, stop=True, reads=[t_hc, tA[kk]], writes=[tps[b]])
                        mm(ps[b][:, o_ + 128:o_ + 256], gtab[:, kk, 1, :], ar, start=True, stop=False, reads=[t_hc, tA[kk]], writes=[tps[b]])
                        mm(ps[b][:, o_ + 128:o_ + 256], gtab[:, kk, 0, :], ai, start=False, stop=True, reads=[t_hc, tA[kk]], writes=[tps[b]])
                    xv = ps[b][:, 0:n * 256].rearrange("p (k r c) -> p k r c", k=n, r=2)
                    wh = [tHh[k1 + u_] for u_ in range(n)]
                    if d_ == 0:
                        cp("act", Hh[:, k1:k1 + n, :, :], xv, reads=[tps[b]], writes=wh)
                    else:
                        tt("dve", Hh[:, k1:k1 + n, 0, :], Hh[:, k1:k1 + n, 0, :], xv[:, :, 0, :], ALU.add, reads=[tps[b]] + wh, writes=wh)
                        tt("dve", Hh[:, k1:k1 + n, 1, :], Hh[:, k1:k1 + n, 1, :], xv[:, :, 1, :], ALU.subtract, reads=[tps[b]] + wh, writes=wh)
                    k1 += n
            kb.dma("sp", hall_d[r * 128:(r + 1) * 128, :], Hh[:].rearrange("p a b c -> p (a b c)"), reads=tHh, writes=[t_hall[r]])

        stage_k(0)
        for r in range(8):
            if r + 1 < 8:
                stage_k(r + 1)
            stage_f(r)
        kb.barrier()
    if stop_after == "hf":
        kb.finish()
        return nc, dbg_out

    t_hd_r = T("hT_d_r")
    w_in_v = din["w_in"].rearrange("(k p) n -> p k n", p=128)
    w_sw_v = din["w_qk_sw"].rearrange("(k p) n -> p k n", p=128)

    def load_w(dst, src_view, c0, tl, ncols=128):
        kb.dma("pool", dst, src_view[:, :, c0:c0 + ncols], writes=[tl])

    with ExitStack() as ph:
        def sb(name, shape, dt):
            return ph.enter_context(nc.sbuf_tensor(f"s{next_id()}_" + name, list(shape), dt))

        NH = 0 if "p3" in skip else 4
        rcos = sb("rcos", [128, L], F32)
        rsin = sb("rsin", [128, L], F32)
        t_ropes = [T(f"rope{i}") for i in range(8)]
        for i in range(8):
            kb.dma("sp", rcos[:, i * 512:(i + 1) * 512], din["rope_cos"][:, i * 512:(i + 1) * 512], writes=[t_ropes[i]])
            kb.dma("sp", rsin[:, i * 512:(i + 1) * 512], din["rope_sin"][:, i * 512:(i + 1) * 512], writes=[t_ropes[i]])
        QT = [sb(f"QT{i}", [128, L], BF) for i in range(2)]
        KT = [[sb(f"KT{i}_{m_}", [128, LK], BF) for m_ in range(2)] for i in range(2)]
        V = [sb(f"V{i}", [128, 34, 129], BF) for i in range(2)]
        t_Q, t_K, t_V = [T(), T()], [T(), T()], [T(), T()]
        for i in range(2):
            memset("pool", KT[i][0][64:128, :], 0.0, [t_K[i]])
            memset("pool", KT[i][1][0:64, :], 0.0, [t_K[i]])
            memset("pool", V[i][:, :, 128:129], 1.0, [t_V[i]])
        wts = [[sb(f"aw{i}_{j}", [128, 8, 128], BF) for j in range(5)] for i in range(2)]
        twts = [[T() for _ in range(5)] for _ in range(2)]
        hb = [sb(f"hb{i}", [128, 8, 512], BF) for i in range(2)]
        thb = [T(), T()]
        rt = [sb(f"rt{i}", [128, 512], F32) for i in range(4)]
        trt = [T() for _ in range(4)]
        E = [sb(f"E{i}", [128, 1024], BF) for i in range(4)]
        tE = [T() for _ in range(4)]
        attst = [sb(f"attst{i}", [128, 512], BF) for i in range(2)]
        t_attst = [T(), T()]
        fin = sb("fin", [128, 16], F32)
        accs = sb("accs", [128, 1161], F32)
        oa = sb("oa", [128, 4, 128], F32)
        ob = sb("ob", [128, 4, 128], F32)
        on = sb("on", [128, 4, 128], BF)
        t_fin, t_on, t_accs, t_oa, t_ob = T(), T(), T(), T(), T()
        t_ad = T("aT_d")
        hbcnt = [0]

        def prologue(h, banks):
            bi = h % 2
            w_, tw_ = wts[bi], twts[bi]
            brr = [0]

            def nbk():
                brr[0] += 1
                return banks[brr[0] % len(banks)]

            load_w(w_[0][:], w_in_v, OFF_Q + h * 128, tw_[0])
            load_w(w_[1][:], w_sw_v, h * 128, tw_[1])
            load_w(w_[2][:], w_in_v, OFF_K + h * 128, tw_[2])
            load_w(w_[3][:], w_sw_v, 512 + h * 128, tw_[3])
            load_w(w_[4][:], w_in_v, OFF_V + h * 128, tw_[4])
            b = nbk()
            for k in range(8):
                mm(ps[b][:, 0:CT], w_[2][:, k, :], hcT[:, k, :], start=(k == 0), stop=(k == 7), reads=[tw_[2], t_hcT], writes=[tps[b]])
            cp("dve", KT[bi][0][0:64, L:LK], ps[b][0:64, 0:CT], reads=[tps[b]], writes=[t_K[bi]])
            cp("dve", KT[bi][1][64:128, L:LK], ps[b][64:128, 0:CT], reads=[tps[b]], writes=[t_K[bi]])
            yield
            b = nbk()
            for s_ in range(2):
                for k in range(8):
                    mm(ps[b][:, s_ * 128:(s_ + 1) * 128], hcT[:, k, s_ * 128:(s_ + 1) * 128], w_[4][:, k, :], start=(k == 0), stop=(k == 7),
                       reads=[tw_[4], t_hcT], writes=[tps[b]])
            cp("dve", V[bi][:, 32:34, 0:128], ps[b][:, 0:256].rearrange("p (s v) -> p s v", s=2), reads=[tps[b]], writes=[t_V[bi]])
            yield
            for blk in range(8):
                hi = hbcnt[0] % 2
                hbcnt[0] += 1
                hbb, th = hb[hi], thb[hi]
                kb.dma("sp", hbb[:], hT_d[:, :, blk * 512:(blk + 1) * 512], reads=[t_hd], writes=[th])
                sl = slice(blk * 512, (blk + 1) * 512)
                for qk in range(2):
                    for gg in range(2):
                        g = 2 * qk + gg
                        b = nbk()
                        for k in range(8):
                            mm(ps[b][:], w_[g][:, k, :], hbb[:, k, :], start=(k == 0), stop=(k == 7), reads=[tw_[g], th], writes=[tps[b]])
                        tt("dve", rt[g][:], ps[b][:], (rcos if gg == 0 else rsin)[:, sl], ALU.mult, reads=[tps[b], t_ropes[blk]], writes=[trt[g]])
                        yield
                    r0, r1 = rt[2 * qk], rt[2 * qk + 1]
                    if qk == 0:
                        tt("pool", QT[bi][:, sl], r0[:], r1[:], ALU.add, reads=[trt[0], trt[1]], writes=[t_Q[bi]])
                    else:
                        tt("pool", KT[bi][0][0:64, sl], r0[0:64, :], r1[0:64, :], ALU.add, reads=[trt[2], trt[3]], writes=[t_K[bi]])
                        tt("pool", KT[bi][1][64:128, sl], r0[64:128, :], r1[64:128, :], ALU.add, reads=[trt[2], trt[3]], writes=[t_K[bi]])
                b = nbk()
                for s_ in range(4):
                    for k in range(8):
                        mm(ps[b][:, s_ * 128:(s_ + 1) * 128], hbb[:, k, s_ * 128:(s_ + 1) * 128], w_[4][:, k, :], start=(k == 0), stop=(k == 7),
                           reads=[tw_[4], th], writes=[tps[b]])
                cp("dve", V[bi][:, blk * 4:blk * 4 + 4, 0:128], ps[b][:].rearrange("p (s v) -> p s v", s=4), reads=[tps[b]], writes=[t_V[bi]])
                yield

        def bc_last(ap2d, n):
            return bass.AP(ap2d.tensor, ap2d.offset, [list(ap2d.ap[0]), list(ap2d.ap[1]), [0, n]])

        items = [(qb, kc) for qb in range(8) for kc in range(34)]

        def head_loop(h, gen):
            bi = h % 2
            Qh, Kh, Vh = QT[bi], KT[bi], V[bi]

            def emit_S(idx):
                qb, kc = items[idx]
                b0 = (idx % 2) * 2
                for m_ in range(2):
                    mm(ps[b0 + m_][:], Kh[m_][:, kc * 128:(kc + 1) * 128], Qh[:, qb * 512:(qb + 1) * 512], start=True, stop=True,
                       reads=[t_K[bi], t_Q[bi]], writes=[tps[b0 + m_]])
                ei = idx % 4
                act(E[ei][:], psall[:, b0 * 512:(b0 + 2) * 512], AF.Exp, reads=[tps[b0], tps[b0 + 1]], writes=[tE[ei]], scale=0.125)

            def emit_AV(idx):
                qb, kc = items[idx]
                ei = idx % 4
                for m_ in range(2):
                    for s_ in range(4):
                        slot = m_ * 4 + s_
                        bk, c0 = 4 + slot // 3, (slot % 3) * 129
                        mm(ps[bk][:, c0:c0 + 129], E[ei][:, m_ * 512 + s_ * 128:m_ * 512 + (s_ + 1) * 128], Vh[:, kc, :],
                           start=(kc == 0 and slot % 3 == 0), stop=(kc == 33 and (slot % 3 == 2 or slot == 7)),
                           reads=[tE[ei], t_V[bi]], writes=[tps[bk]])
                if kc == 33:
                    finalize(qb)
                    pend.append(qb)
                if kc == 10 and pend:
                    finalize_tr(pend.pop())

            def finalize(qb):
                qsl = slice(qb * 512, (qb + 1) * 512)
                cp("dve", accs[:, 0:387], ps[4][:, 0:387], reads=[tps[4]], writes=[t_accs])
                cp("dve", accs[:, 387:774], ps[5][:, 0:387], reads=[tps[5]], writes=[t_accs])
                cp("dve", accs[:, 774:1032], ps[6][:, 0:258], reads=[tps[6]], writes=[t_accs])
                av = accs[:, 0:1032].rearrange("p (s c) -> p s c", c=129)
                recip(fin[:, 0:8], av[:, :, 128], reads=[t_accs], writes=[t_fin])
                ts("dve", fin[:, 4:8], fin[:, 4:8], neglam[:], None, ALU.mult, None, reads=[t_fin, t_cols], writes=[t_fin])
                tt("dve", oa[:], av[:, 0:4, 0:128], bc_last(fin[:, 0:4], 128), ALU.mult, reads=[t_accs, t_fin], writes=[t_oa])
                tt("dve", ob[:], av[:, 4:8, 0:128], bc_last(fin[:, 4:8], 128), ALU.mult, reads=[t_accs, t_fin], writes=[t_ob])
                tt("pool", ob[:], ob[:], oa[:], ALU.add, reads=[t_oa, t_ob], writes=[t_ob])
                tt("pool", oa[:], ob[:], ob[:], ALU.mult, reads=[t_ob], writes=[t_oa])
                rsum(fin[:, 8:12], oa[:], reads=[t_oa], writes=[t_fin])
                rstd_of(fin[:, 8:12], fin[:, 12:16], 1.0 / 128, t_fin)
                tt("dve", ob[:], ob[:], bc_last(fin[:, 12:16], 128), ALU.mult, reads=[t_ob, t_fin], writes=[t_ob])
                sg_b = bass.AP(sgrow[:].tensor, sgrow[:].offset, [list(sgrow[:].ap[0]), [0, 4], [1, 128]])
                tt("dve", on[:], ob[:], sg_b, ALU.mult, reads=[t_ob, t_rows], writes=[t_on])

            def finalize_tr(qb):
                qsl = slice(qb * 512, (qb + 1) * 512)
                for s_ in range(4):
                    tr(psb[7][:, s_ * 128:(s_ + 1) * 128], on[:, s_, :], ident_bf[:], reads=[t_on, t_const], writes=[tps[7]])
                cp("dve", attst[qb % 2][:], psb[7][:, 0:512], reads=[tps[7]], writes=[t_attst[qb % 2]])
                kb.dma("sp", aT_d[:, h, qsl], attst[qb % 2][:], reads=[t_attst[qb % 2]], writes=[t_ad])

            pend = []
            emit_S(0)
            emit_S(1)
            for idx in range(len(items)):
                if idx + 2 < len(items):
                    emit_S(idx + 2)
                emit_AV(idx)
                if gen is not None and idx % 6 == 3 and items[idx][1] not in (32, 33, 0):
                    next(gen, None)
            while pend:
                finalize_tr(pend.pop())
            if gen is not None:
                for _ in gen:
                    pass

        if NH:
            for _ in prologue(0, [0, 1, 2, 3, 4, 5, 6, 7]):
                pass
        for h in range(NH):
            gen = prologue(h + 1, [7]) if h + 1 < NH else None
            head_loop(h, gen)
        if "p3" in dbg:
            d = dbg_tensor("attT", [128, 4, L], BF)
            kb.dma("sp", d, aT_d, reads=[t_ad])
        kb.barrier()
    if stop_after == "p3":
        kb.finish()
        return nc, dbg_out

    with ExitStack() as ph:
        def sb(name, shape, dt):
            return ph.enter_context(nc.sbuf_tensor(f"s{next_id()}_" + name, list(shape), dt))

        _bk = [0]

        def nb():
            _bk[0] = (_bk[0] + 1) % 8
            return _bk[0]

        t_hc = T("hyconst")
        gtab = sb("gtab", [128, 33, 3, 128], BF)
        ttab = sb("ttab", [66, 2, 128, 32], BF)
        etab = sb("etab", [128, 4, 2, 32], BF)
        d1f = sb("d1fp", [128, 66], BF)
        cwt = sb("cwt", [128, 12, 3], F32)
        cbt = sb("cbt", [128, 12], F32)
        for dst, nm in ((gtab, "gtab"), (ttab, "ttab"), (etab, "etab"), (d1f, "d1fp"), (cwt, "cw"), (cbt, "cb")):
            kb.dma("sp", dst[:], din[nm], writes=[t_hc])
        Z = [sb(f"Z{i}", [128, L], BF) for i in range(3)]
        tZ = [T(f"Z{i}") for i in range(3)]
        H = sb("H", [128, 33, 2, 128], BF)
        tH = [T() for _ in range(33)]
        t_sd = [T(f"sig{i}") for i in range(5)]
        t_zd = T("zT_d")
        Ut = [sb(f"Ut{i}", [128, 32, 128], BF) for i in range(2)]
        tUt = [T(), T()]
        for i in range(2):
            memset("dve", Ut[i][32:64, :, :], 0.0, [tUt[i]])
            memset("dve", Ut[i][64:128, :, :], 0.0, [tUt[i]])
        A = sb("A", [128, 66, 128], BF)
        tA = [T() for _ in range(33)]
        f1cnt = [0]

        def f1_pre(sig, t_sig, slot):
            kb.dma("pool", sig_d[slot], sig, reads=[t_sig], writes=[t_sd[slot]])
            for g in range(2):
                kb.dma("pool", Ut[g][0:32, :, :], sig_d[slot][g * 32:(g + 1) * 32, :].rearrange("c (a b) -> a c b", a=32),
                       reads=[t_sd[slot]], writes=[tUt[g]])

        def f1_part(sig, t_sig, slot, pre=False, act_only=False):
            if not pre:
                f1_pre(sig, t_sig, slot)
            for g in range(4):
                u, tu = Ut[g % 2], tUt[g % 2]
                if g >= 2:
                    kb.dma("pool", u[0:32, :, :], sig_d[slot][g * 32:(g + 1) * 32, :].rearrange("c (a b) -> a c b", a=32),
                           reads=[t_sd[slot]], writes=[tu])
                j = 0
                while j < 32:
                    n = min(7, 32 - j)
                    b = nb()
                    for jj in range(n):
                        mm(ps[b][:, jj * 66:(jj + 1) * 66], u[:, j + jj, :], d1f[:], start=True, stop=True,
                           reads=[tu, t_hc], writes=[tps[b]])
                    c0 = g * 32 + j
                    eng = "act" if (act_only or f1cnt[0] % 2 == 0) else "dve"
                    f1cnt[0] += 1
                    cp(eng, A[:, :, c0:c0 + n], ps[b][:, 0:n * 66].rearrange("p (c k) -> p k c", k=66), reads=[tps[b]], writes=tA)
                    j += n

        for cc in range(4):
            with ExitStack() as pa:
                def sba(name, shape, dt):
                    return pa.enter_context(nc.sbuf_tensor(f"s{next_id()}_" + name, list(shape), dt))

                wts = [sba(f"hw{i}", [128, 8, 128], BF) for i in range(3)]
                twts = [T() for _ in range(3)]
                hb = [sba(f"hhb{i}", [128, 8, 512], BF) for i in range(2)]
                thb = [T(), T()]
                Ur = [sba(f"Ur{i}", [128, L + 2], BF) for i in range(3)]
                tUr = [T() for _ in range(3)]
                ctmp = sba("ctmp", [128, L], F32)
                t_ct = T()
                for s in range(3):
                    load_w(wts[s][:], w_in_v, s * 512 + cc * 128, twts[s])
                    memset("pool", Ur[s][:, 0:1], 0.0, [tUr[s]])
                    memset("pool", Ur[s][:, L + 1:L + 2], 0.0, [tUr[s]])
                def proj_pass(sigs, off):
                    for blk in range(8):
                        hbb, th = hb[(blk + off) % 2], thb[(blk + off) % 2]
                        kb.dma("sp", hbb[:], hT_d[:, :, blk * 512:(blk + 1) * 512], reads=[t_hd], writes=[th])
                        for s in sigs:
                            b = nb()
                            for k in range(8):
                                mm(ps[b][:], wts[s][:, k, :], hbb[:, k, :], start=(k == 0), stop=(k == 7), reads=[twts[s], th], writes=[tps[b]])
                            cp("act", Ur[s][:, 1 + blk * 512:1 + (blk + 1) * 512], ps[b][:], reads=[tps[b]], writes=[tUr[s]])

                def sconv(s):
                    slot = s * 4 + cc
                    act(ctmp[:], Ur[s][:, 1:L + 1], AF.Identity, reads=[tUr[s], t_hc], writes=[t_ct],
                        scale=cwt[:, slot, 1:2], bias=cbt[:, slot:slot + 1])
                    stt(ctmp[:], Ur[s][:, 0:L], cwt[:, slot, 0:1], ctmp[:], ALU.mult, ALU.add, reads=[tUr[s], t_hc, t_ct], writes=[t_ct])
                    stt(Z[s][:], Ur[s][:, 2:L + 2], cwt[:, slot, 2:3], ctmp[:], ALU.mult, ALU.add, reads=[tUr[s], t_hc, t_ct], writes=[tZ[s]])

                proj_pass([0, 1, 2], 0)
                sconv(0)
                f1_pre(Z[0][:], tZ[0], 4)
                sconv(1)
                f1_part(Z[0][:], tZ[0], 4, pre=True, act_only=True)
                sconv(2)
                if "p2a" in dbg and cc == 0:
                    for s in range(3):
                        d = dbg_tensor(f"Z{s}", [128, L], BF)
                        kb.dma("sp", d, Z[s][:], reads=[tZ[s]])
                kb.barrier()
            with ExitStack() as pb:
                def sbb(name, shape, dt):
                    return pb.enter_context(nc.sbuf_tensor(f"s{next_id()}_" + name, list(shape), dt))

                Y = sbb("Y", [128, 128, 66], BF)
                tY = [T() for _ in range(33)]
                Pqs = [sbb(f"Pq{i}", [66, 64, 128], BF) for i in range(2)]
                t_Pqs = [T(), T()]
                pw = [sbb(f"pw{i}", [128, 2, 2, 128], F32) for i in range(4)]
                tpw = [T() for _ in range(4)]

                def f2_part(consumer):
                    k1 = 0
                    while k1 < 33:
                        n = 2 if k1 + 1 < 33 else 1
                        b = nb()
                        for u_ in range(n):
                            kk = k1 + u_
                            o_ = u_ * 256
                            ar, ai = A[:, kk, :], A[:, 33 + kk, :]
                            mm(ps[b][:, o_:o_ + 128], gtab[:, kk, 0, :], ar, start=True, stop=False, reads=[t_hc, tA[kk]], writes=[tps[b]])
                            mm(ps[b][:, o_:o_ + 128], gtab[:, kk, 2, :], ai, start=False, stop=True, reads=[t_hc, tA[kk]], writes=[tps[b]])
                            mm(ps[b][:, o_ + 128:o_ + 256], gtab[:, kk, 1, :], ar, start=True, stop=False, reads=[t_hc, tA[kk]], writes=[tps[b]])
                            mm(ps[b][:, o_ + 128:o_ + 256], gtab[:, kk, 0, :], ai, start=False, stop=True, reads=[t_hc, tA[kk]], writes=[tps[b]])
                        consumer(k1, n, b)
                        k1 += n

                def conv(o, sig, t_sig, xm, t_xm, zout, t_zout):
                    r_ = 2 * cc + o
                    kb.dma("sp", H[:].rearrange("p a b c -> p (a b c)"), hall_d[r_ * 128:(r_ + 1) * 128, :], reads=[t_hall[r_]], writes=tH)
                    if "p2h" in dbg and cc == 0:
                        d = dbg_tensor(f"H{o}", [128, 33 * 2 * 128], BF)
                        kb.dma("sp", d, H[:].rearrange("p a b c -> p (a b c)"), reads=tH)

                    def cons_d(k1, n, b):
                        i0 = ((k1 // 2) % 2) * 2
                        p1, p2 = pw[i0], pw[i0 + 1]
                        xv = ps[b][:, 0:n * 256].rearrange("p (k r c) -> p k r c", k=n, r=2)
                        hb_ = H[:, k1, 0, :]
                        pst = list(hb_.ap[0])
                        hr = bass.AP(hb_.tensor, hb_.offset, [pst, [256, n], [0, 2], [1, 128]])
                        hi = bass.AP(hb_.tensor, hb_.offset + 128, [pst, [256, n], [0, 2], [1, 128]])
                        rd = [tps[b]] + [tH[k1 + u_] for u_ in range(n)]
                        tt("dve", p1[:, 0:n, :, :], xv, hr, ALU.mult, reads=rd, writes=[tpw[i0]])
                        tt("dve", p2[:, 0:n, :, :], xv, hi, ALU.mult, reads=rd, writes=[tpw[i0 + 1]])

                        def ck(t_, r_):
                            v = t_[:, 0:n, r_, :]
                            return bass.AP(v.tensor, v.offset, [list(v.ap[0]), [1, 128], [256, n]])

                        wy = [tY[k1 + u_] for u_ in range(n)]
                        tt("dve", Y[:, :, k1:k1 + n], ck(p1, 0), ck(p2, 1), ALU.subtract, reads=[tpw[i0], tpw[i0 + 1]], writes=wy)
                        tt("pool", Y[:, :, 33 + k1:33 + k1 + n], ck(p2, 0), ck(p1, 1), ALU.add, reads=[tpw[i0], tpw[i0 + 1]], writes=wy)

                    if o == 1:
                        f1_part(sig, t_sig, 4)
                    f2_part(cons_d)
                    cnt = 0
                    zv = zout.rearrange("c (a b) -> c b a", a=32)
                    xv_ = xm.rearrange("c (a b) -> c b a", a=32)
                    def i1_stage(q):
                        Pq, t_Pq = Pqs[q % 2], t_Pqs[q % 2]
                        for c0 in range(0, 128, 8):
                            b = nb()
                            for jj in range(8):
                                mm(ps[b][0:66, jj * 64:(jj + 1) * 64], Y[:, c0 + jj, :], etab[:, q, :, :].rearrange("p e n -> p (e n)"),
                                   start=True, stop=True, reads=tY + [t_hc], writes=[tps[b]])
                            eng = "dve" if (c0 // 8) % 4 == 3 else "act"
                            cp(eng, Pq[:, :, c0:c0 + 8], ps[b][0:66, :].rearrange("p (c k) -> p k c", k=64), reads=[tps[b]], writes=[t_Pq])

                    def i2_stage(q):
                        Pq, t_Pq = Pqs[q % 2], t_Pqs[q % 2]
                        for hh in range(2):
                            b = nb()
                            for j in range(16):
                                n2l = hh * 16 + j
                                n2 = q * 32 + n2l
                                for e_ in range(2):
                                    mm(ps[b][:, j * 32:(j + 1) * 32], Pq[:, e_ * 32 + n2l, :], ttab[:, e_, n2, :], start=(e_ == 0), stop=(e_ == 1),
                                       reads=[t_Pq, t_hc], writes=[tps[b]])
                            n20 = q * 32 + hh * 16
                            tt("dve", zv[:, n20:n20 + 16, :], ps[b][:].rearrange("p (b a) -> p b a", a=32), xv_[:, n20:n20 + 16, :], ALU.mult,
                               reads=[tps[b], t_xm], writes=[t_zout])

                    i1_stage(0)
                    for q in range(4):
                        if q + 1 < 4:
                            i1_stage(q + 1)
                        i2_stage(q)

                conv(0, Z[0][:], tZ[0], Z[1][:], tZ[1], Z[0][:], tZ[0])
                if "p2c" in dbg and cc == 0:
                    d = dbg_tensor("z1", [128, L], BF)
                    kb.dma("sp", d, Z[0][:], reads=[tZ[0]])
                conv(1, Z[0][:], tZ[0], Z[2][:], tZ[2], Z[1][:], tZ[1])
                kb.dma("sp", zT_d[:, cc, :], Z[1][:], reads=[tZ[1]], writes=[t_zd])
                kb.barrier()
            if stop_after == "p2c0":
                break
        if "p2" in dbg:
            d = dbg_tensor("zT", [128, 4, L], BF)
            kb.dma("sp", d, zT_d, reads=[t_zd])
        kb.barrier()
    if stop_after in ("p2", "p2c0"):
        kb.finish()
        return nc, dbg_out

    t_md = T("mT_d")
    with ExitStack() as ph:
        def sb(name, shape, dt):
            return ph.enter_context(nc.sbuf_tensor(f"s{next_id()}_" + name, list(shape), dt))

        _bk = [0]

        def nb():
            _bk[0] = (_bk[0] + 1) % 8
            return _bk[0]

        zTs = sb("zTs", [128, 4, L], BF)
        aTs = sb("aTs", [128, 4, L], BF)
        wgt = sb("wgt", [128, 8, 2048], BF)
        whu = sb("whu", [128, 4, D], BF)
        wau = sb("wau", [128, 4, D], BF)
        t_wg = [T(f"wg{j}") for j in range(16)]
        t_wh = [T(f"wh{j}") for j in range(8)]
        t_wa = [T(f"wa{j}") for j in range(8)]
        t_zb = [T(f"zb{i}") for i in range(8)]
        t_ab = [T(f"ab{i}") for i in range(8)]
        whv = din["w_hy_up"].rearrange("(k p) n -> p k n", p=128)
        wav = din["w_att_up"].rearrange("(k p) n -> p k n", p=128)
        for j in range(8):
            for half in range(2):
                jj = half * 8 + j
                load_w(wgt[:, :, jj * 128:(jj + 1) * 128], w_in_v, OFF_G + jj * 128, t_wg[jj])
            kb.dma("pool", whu[:, :, j * 128:(j + 1) * 128], whv[:, :, j * 128:(j + 1) * 128], writes=[t_wh[j]])
            kb.dma("pool", wau[:, :, j * 128:(j + 1) * 128], wav[:, :, j * 128:(j + 1) * 128], writes=[t_wa[j]])
        for i in range(8):
            kb.dma("sp", zTs[:, :, i * 512:(i + 1) * 512], zT_d[:, :, i * 512:(i + 1) * 512], writes=[t_zb[i]])
            kb.dma("sp", aTs[:, :, i * 512:(i + 1) * 512], aT_d[:, :, i * 512:(i + 1) * 512], writes=[t_ab[i]])
        hb = [sb(f"mhb{i}", [128, 8, 512], BF) for i in range(2)]
        thb = [T(), T()]
        mst = [sb(f"mst{i}", [128, 8, 512], BF) for i in range(2)]
        tmst = [T(), T()]
        sg = [sb(f"sg{i}", [128, 512], F32) for i in range(4)]
        tsg = [T() for _ in range(4)]
        mm_ = [sb(f"mm{i}", [128, 512], F32) for i in range(4)]
        tmm = [T() for _ in range(4)]
        kb.dma("sp", hb[0][:], hT_d[:, :, 0:512], reads=[t_hd], writes=[thb[0]])
        for blk in range(8):
            sl = slice(blk * 512, (blk + 1) * 512)
            hbb, th = hb[blk % 2], thb[blk % 2]
            if blk + 1 < 8:
                kb.dma("sp", hb[(blk + 1) % 2][:], hT_d[:, :, (blk + 1) * 512:(blk + 2) * 512], reads=[t_hd], writes=[thb[(blk + 1) % 2]])
            for j in range(8):
                bg1, bg2, by1, by2 = nb(), nb(), nb(), nb()
                for k in range(8):
                    mm(ps[bg1][:], wgt[:, k, j * 128:(j + 1) * 128], hbb[:, k, :], start=(k == 0), stop=(k == 7), reads=[t_wg[j], th], writes=[tps[bg1]])
                for k in range(8):
                    mm(ps[bg2][:], wgt[:, k, 1024 + j * 128:1024 + (j + 1) * 128], hbb[:, k, :], start=(k == 0), stop=(k == 7), reads=[t_wg[8 + j], th], writes=[tps[bg2]])
                for k in range(4):
                    mm(ps[by1][:], whu[:, k, j * 128:(j + 1) * 128], zTs[:, k, sl], start=(k == 0), stop=(k == 3), reads=[t_wh[j], t_zb[blk]], writes=[tps[by1]])
                for k in range(4):
                    mm(ps[by2][:], wau[:, k, j * 128:(j + 1) * 128], aTs[:, k, sl], start=(k == 0), stop=(k == 3), reads=[t_wa[j], t_ab[blk]], writes=[tps[by2]])
                i0 = (j % 2) * 2
                act(sg[i0][:], ps[bg1][:], AF.Sigmoid, reads=[tps[bg1]], writes=[tsg[i0]])
                act(sg[i0 + 1][:], ps[bg2][:], AF.Sigmoid, reads=[tps[bg2]], writes=[tsg[i0 + 1]])
                tt("dve", mm_[i0][:], ps[by1][:], sg[i0][:], ALU.mult, reads=[tps[by1], tsg[i0]], writes=[tmm[i0]])
                tt("dve", mm_[i0 + 1][:], ps[by2][:], sg[i0 + 1][:], ALU.mult, reads=[tps[by2], tsg[i0 + 1]], writes=[tmm[i0 + 1]])
                tt("pool", mst[blk % 2][:, j, :], mm_[i0][:], mm_[i0 + 1][:], ALU.add, reads=[tmm[i0], tmm[i0 + 1]], writes=[tmst[blk % 2]])
            kb.dma("sp", mT_d[:, :, sl], mst[blk % 2][:], reads=[tmst[blk % 2]], writes=[t_md])
        if "p4" in dbg:
            d = dbg_tensor("mT", [128, 8, L], BF)
            kb.dma("sp", d, mT_d, reads=[t_md])
        kb.barrier()
    if stop_after == "p4":
        kb.finish()
        return nc, dbg_out

    with ExitStack() as ph:
        def sb(name, shape, dt):
            return ph.enter_context(nc.sbuf_tensor(f"s{next_id()}_" + name, list(shape), dt))

        _bk = [0]

        def nb():
            _bk[0] = (_bk[0] + 1) % 8
            return _bk[0]

        wout = sb("wout", [128, 8, D], BF)
        wfg = sb("wfg", [128, 8, DFF], BF)
        wfu = sb("wfu", [128, 8, DFF], BF)
        wfd = sb("wfd", [128, NFF, D], BF)
        t_wo, t_wg, t_wu, t_wd = T("wo"), T("wg"), T("wu"), T("wd")
        kb.dma("pool", wout[:], din["w_out"].rearrange("(k p) n -> p k n", p=128), writes=[t_wo])
        t_wgs = [T(f"wfg{g}") for g in range(6)]
        t_wus = [T(f"wfu{g}") for g in range(6)]
        wfg_v = din["w_fg"].rearrange("(k p) n -> p k n", p=128)
        wfu_v = din["w_fu"].rearrange("(k p) n -> p k n", p=128)
        for g in range(6):
            c0 = g * 512
            w_ = min(512, DFF - c0)
            kb.dma("pool", wfg[:, :, c0:c0 + w_], wfg_v[:, :, c0:c0 + w_], writes=[t_wgs[g]])
            kb.dma("pool", wfu[:, :, c0:c0 + w_], wfu_v[:, :, c0:c0 + w_], writes=[t_wus[g]])
        for j in range(0, NFF, 2):
            kb.dma("pool", wfd[:, j:j + 2, :], din["w_fd"][j * 128:(j + 2) * 128, :].rearrange("(k p) n -> p k n", p=128), writes=[t_wd])
        xts = [sb(f"fx{i}", [128, D], F32) for i in range(3)]
        txt = [T(), T(), T()]
        mts = [sb(f"fm{i}", [128, 8, 128], BF) for i in range(2)]
        tmt = [T(), T()]
        tmp = sb("ftmp", [128, D], F32)
        t_tmp = T()
        junk = sb("fjunk", [128, D], BF)
        t_junk = T()
        xs = sb("fxs", [128, D], BF)
        t_xs = T()
        hfT = [sb(f"hfT{i}", [128, 8, 128], BF) for i in range(2)]
        t_hf = [T(), T()]
        aT = sb("aT", [128, NFF, 128], BF)
        t_aT = T()
        sgt = [sb(f"sgt{i}", [128, 512], F32) for i in range(2)]
        tsgt = [T(), T()]
        atm = sb("atm", [128, DFF], BF)
        t_atm = T()
        fin = sb("ffin", [128, 16], F32)
        t_fin = [T(), T(), T()]
        t_out = T("out")

        def norm_resid(banks, xt, tx, grow, c0, tf):
            for n in range(2):
                act(junk[:, n * 512:(n + 1) * 512], ps[banks[n]][:], AF.Square, reads=[tps[banks[n]]], writes=[t_junk, tf],
                    accum_out=fin[:, c0 + n:c0 + n + 1])
            tt("dve", fin[:, c0 + 2:c0 + 3], fin[:, c0:c0 + 1], fin[:, c0 + 1:c0 + 2], ALU.add, reads=[tf], writes=[tf])
            rstd_of(fin[:, c0 + 2:c0 + 3], fin[:, c0 + 3:c0 + 4], 1.0 / D, tf)
            for n in range(2):
                hs = slice(n * 512, (n + 1) * 512)
                stt(tmp[:, hs], ps[banks[n]][:], fin[:, c0 + 3:c0 + 4], grow[:, hs], ALU.mult, ALU.mult,
                    reads=[tps[banks[n]], tf, t_rows], writes=[t_tmp])
            tt("pool", xt[:], xt[:], tmp[:], ALU.add, reads=[t_tmp, tx], writes=[tx])

        def s1a(i):
            tsl = slice(i * 128, (i + 1) * 128)
            xt, tx = xts[i % 3], txt[i % 3]
            mt, tm = mts[i % 2], tmt[i % 2]
            kb.dma("sp", xt[:], din["x"][tsl, :], writes=[tx])
            kb.dma("sp", mt[:], mT_d[:, :, tsl], reads=[t_md], writes=[tm])
            for n in range(2):
                for k in range(8):
                    mm(ps[n][:], mt[:, k, :], wout[:, k, n * 512:(n + 1) * 512], start=(k == 0), stop=(k == 7),
                       reads=[tm, t_wo], writes=[tps[n]])

        def s1b(i):
            xt, tx = xts[i % 3], txt[i % 3]
            norm_resid([0, 1], xt, tx, g1row, 0, t_fin[0])
            act(junk[:], xt[:], AF.Square, reads=[tx], writes=[t_junk, t_fin[1]], accum_out=fin[:, 4:5])
            rstd_of(fin[:, 4:5], fin[:, 5:6], 1.0 / D, t_fin[1])
            act(xs[:], xt[:], AF.Copy, reads=[tx, t_fin[1]], writes=[t_xs], scale=fin[:, 5:6])

        def s1c(i):
            for k in range(8):
                tr(psb[4][:, k * 128:(k + 1) * 128], xs[:, k * 128:(k + 1) * 128], ident_bf[:], reads=[t_xs, t_const], writes=[tps[4]])
            for k in range(8):
                ts("dve", hfT[i % 2][:, k, :], psb[4][:, k * 128:(k + 1) * 128], A2(k), B2(k), ALU.mult, ALU.add,
                   reads=[tps[4], t_cols], writes=[t_hf[i % 2]])

        def s2a(i):
            h_, th_ = hfT[i % 2], t_hf[i % 2]
            pairs = [(5, 6), (7, 4)]
            for g in range(6):
                c0 = g * 512
                w_ = min(512, DFF - c0)
                bg, bu = pairs[g % 2]
                for k in range(8):
                    mm(ps[bg][:, 0:w_], h_[:, k, :], wfg[:, k, c0:c0 + w_], start=(k == 0), stop=(k == 7), reads=[t_wgs[g], th_], writes=[tps[bg]])
                for k in range(8):
                    mm(ps[bu][:, 0:w_], h_[:, k, :], wfu[:, k, c0:c0 + w_], start=(k == 0), stop=(k == 7), reads=[t_wus[g], th_], writes=[tps[bu]])
                act(sgt[g % 2][:, 0:w_], ps[bg][:, 0:w_], AF.Silu, reads=[tps[bg]], writes=[tsgt[g % 2]])
                tt("dve", atm[:, c0:c0 + w_], ps[bu][:, 0:w_], sgt[g % 2][:, 0:w_], ALU.mult, reads=[tps[bu], tsgt[g % 2]], writes=[t_atm])
            j = 0
            bi = 0
            while j < NFF:
                n = min(8, NFF - j)
                b = (7, 4, 5)[bi % 3]
                bi += 1
                for jj in range(n):
                    tr(psb[b][:, jj * 128:(jj + 1) * 128], atm[:, (j + jj) * 128:(j + jj + 1) * 128], ident_bf[:], reads=[t_atm, t_const], writes=[tps[b]])
                cp("dve" if bi % 2 else "act", aT[:, j:j + n, :], psb[b][:, 0:n * 128].rearrange("p (j t) -> p j t", t=128), reads=[tps[b]], writes=[t_aT])
                j += n

        def s2b(i):
            tsl = slice(i * 128, (i + 1) * 128)
            xt, tx = xts[i % 3], txt[i % 3]
            for n in range(2):
                for j in range(NFF):
                    mm(ps[2 + n][:], aT[:, j, :], wfd[:, j, n * 512:(n + 1) * 512], start=(j == 0), stop=(j == NFF - 1),
                       reads=[t_aT, t_wd], writes=[tps[2 + n]])
            norm_resid([2, 3], xt, tx, g2row, 8, t_fin[2])
            kb.dma("sp", out[tsl, :], xt[:], reads=[tx], writes=[t_out])

        s1a(0)
        s1b(0)
        s1c(0)
        s1a(1)
        s1b(1)
        for i in range(32):
            s2a(i)
            if i + 1 < 32:
                s1c(i + 1)
            if i + 2 < 32:
                s1a(i + 2)
                s1b(i + 2)
            s2b(i)
        kb.barrier()
    kb.finish()
    return nc, dbg_out


_NC = None


def kernel(**inputs):
    global _NC
    if _NC is None:
        _NC = build()[0]
    in_maps = [layout_inputs(inputs, b) for b in range(8)]
    res = run_bass_kernel_spmd(_NC, in_maps, core_ids=list(range(8)))
    return np.stack([np.asarray(r["out"], dtype=np.float32) for r in res.results], 0)
```

```python
import math
from contextlib import ExitStack
import numpy as np
import ml_dtypes
import concourse.bass as bass
import concourse.mybir as mybir
from concourse.bass_utils import run_bass_kernel_spmd

F32 = mybir.dt.float32
BF = mybir.dt.bfloat16
AF = mybir.ActivationFunctionType
ALU = mybir.AluOpType
AX = mybir.AxisListType

L = 4096
D = 1024
CT = 256
LK = L + CT
DH = 512
DFF = 2816
NFF = DFF // 128
OFF_Q, OFF_K, OFF_V, OFF_G = 1536, 2048, 2560, 3072
EPS = 1e-6
LAM_INIT = 0.8 - 0.6 * math.exp(0.0)
PI = math.pi


class Sem:
    def __init__(self, h):
        self.h = h
        self.count = 0


class T:
    __slots__ = ("name", "w", "r")

    def __init__(self, name=""):
        self.name = name
        self.w = None
        self.r = []


class Eng:
    def __init__(self, name, sem):
        self.name = name
        self.sem = sem
        self.ops = []
        self.seen = {}


class KB:
    def __init__(self, nc, nsem_dma=14):
        self.nc = nc
        self.engs = {}
        for n in ("pe", "act", "dve", "pool", "sp"):
            self.engs[n] = Eng(n, Sem(nc.alloc_semaphore("s_" + n)))
        self.dsems = {q: [Sem(nc.alloc_semaphore(f"d_{q}{i}")) for i in range(nsem_dma)] for q in ("sp", "pool")}
        self.drr = {"sp": 0, "pool": 0}

    def _waits(self, eng, reads, writes, extra=()):
        deps = {}

        def add(d):
            if d is None:
                return
            s, v = d
            if deps.get(s, 0) < v:
                deps[s] = v

        for t in reads:
            add(t.w)
        for t in writes:
            add(t.w)
            for d in t.r:
                add(d)
        for d in extra:
            add(d)
        out = []
        for s, v in deps.items():
            if s is eng.sem and eng.name == "pe":
                continue
            if eng.seen.get(s, 0) >= v:
                continue
            eng.seen[s] = v
            out.append((s, v))
        return out

    def _mark(self, tok, reads, writes):
        for t in reads:
            t.r = [d for d in t.r if d[0] is not tok[0]]
            t.r.append(tok)
        for t in writes:
            t.w = tok
            t.r = []

    def op(self, engname, fn, reads=(), writes=()):
        eng = self.engs[engname]
        waits = self._waits(eng, reads, writes)
        eng.sem.count += 1
        tok = (eng.sem, eng.sem.count)
        eng.ops.append((waits, fn, (eng.sem, 1)))
        self._mark(tok, reads, writes)
        return tok

    def dma(self, q, out_ap, in_ap, reads=(), writes=(), **kw):
        eng = self.engs[q]
        sems = self.dsems[q]
        s = sems[self.drr[q] % len(sems)]
        self.drr[q] += 1
        waits = self._waits(eng, reads, writes, extra=[(s, s.count)] if s.count else [])
        s.count += 16
        tok = (s, s.count)
        eng.ops.append((waits, lambda e: e.dma_start(out=out_ap, in_=in_ap, **kw), (s, 16)))
        self._mark(tok, reads, writes)
        return tok

    def collective(self, kind, ins, outs, reads=(), writes=()):
        eng = self.engs["pool"]
        if not hasattr(self, "ccsem"):
            self.ccsem = Sem(self.nc.alloc_semaphore("s_cc"))
        s = self.ccsem
        waits = self._waits(eng, reads, writes)
        s.count += 1
        tok = (s, s.count)
        eng.ops.append((waits, lambda e: e.collective_compute(kind, ALU.bypass, replica_groups=[list(range(8))], ins=ins, outs=outs), (s, 1)))
        self._mark(tok, reads, writes)
        return tok

    def barrier(self, include_cc=False):
        allsems = [e.sem for e in self.engs.values()] + [s for q in self.dsems.values() for s in q]
        if include_cc and hasattr(self, "ccsem"):
            allsems.append(self.ccsem)
        for eng in self.engs.values():
            waits = []
            for s in allsems:
                if s is eng.sem or s.count == 0:
                    continue
                if eng.seen.get(s, 0) >= s.count:
                    continue
                eng.seen[s] = s.count
                waits.append((s, s.count))
            if waits:
                eng.ops.append((waits, None, None))

    def finish(self):
        nc = self.nc
        self.barrier(include_cc=True)
        with nc.Block() as block:
            def emit(e, en):
                for waits, fn, inc in en.ops:
                    for (ws, wv) in waits:
                        e.wait_ge(ws.h, wv)
                    if fn is not None:
                        ins = fn(e)
                        ins.then_inc(inc[0].h, inc[1])

            @block.tensor
            def _(e):
                emit(e, self.engs["pe"])

            @block.scalar
            def _(e):
                emit(e, self.engs["act"])

            @block.vector
            def _(e):
                emit(e, self.engs["dve"])

            @block.gpsimd
            def _(e):
                emit(e, self.engs["pool"])

            @block.sync
            def _(e):
                emit(e, self.engs["sp"])


def _bf(a):
    return np.ascontiguousarray(a.astype(np.float32)).astype(ml_dtypes.bfloat16)


_CONST = None


def host_consts():
    global _CONST
    if _CONST is not None:
        return _CONST
    c = {}
    c["ident_bf"] = _bf(np.eye(128))
    c["ident_f"] = np.eye(128, dtype=np.float32)
    t = np.arange(L)
    row = (t // 64).astype(np.float32)
    col = (t % 64).astype(np.float32)
    inv = (10000.0 ** (-np.arange(16, dtype=np.float32) / 16)).astype(np.float32)
    cos64 = np.zeros((64, L), np.float32)
    sin64 = np.zeros((64, L), np.float32)
    for half, pos in ((0, row), (1, col)):
        ang = pos[None, :] * inv[:, None]
        base = half * 32
        cos64[base:base + 16] = np.cos(ang)
        cos64[base + 16:base + 32] = np.cos(ang)
        sin64[base:base + 16] = -np.sin(ang)
        sin64[base + 16:base + 32] = np.sin(ang)
    c["rope_cos"] = np.concatenate([cos64, cos64], 0)
    c["rope_sin"] = np.concatenate([sin64, sin64], 0)
    f32 = np.float32
    bands = 16
    tt = np.linspace(0.0, 1.0, L, dtype=f32)[:, None]
    w = (f32(2.0 * math.pi / L) * np.arange(L, dtype=f32))[:, None]
    fr = np.linspace(1e-4, bands - 1, bands, dtype=f32)[None, :]
    z = np.concatenate([tt, np.cos(fr * w), -np.sin(fr * w)], axis=-1).astype(f32)
    c["zT"] = np.ascontiguousarray(z.T)
    deltas = np.abs(np.linspace(math.log(1e-2) / 1.5, math.log(1e-2) / 0.3, DH, dtype=f32))
    nd = (-deltas).reshape(4, 128).T
    c["negdelta"] = np.ascontiguousarray(nd.astype(f32))
    offs = (np.arange(8) * 512 / (L - 1)).astype(f32)
    c["ndoff"] = np.ascontiguousarray((nd[:, :, None] * offs[None, None, :]).astype(f32))
    c["tv0"] = np.ascontiguousarray(np.broadcast_to((np.arange(512) / (L - 1)).astype(f32)[None, :], (128, 512)))
    n1 = np.arange(32)[:, None]
    k1 = np.arange(33)[None, :]
    a = 2 * np.pi * n1 * k1 / 64.0
    c["d1f"] = _bf(np.concatenate([np.cos(a), -np.sin(a)], 1))
    c["d1fp"] = np.ascontiguousarray(np.concatenate([c["d1f"], np.zeros((96, 66), ml_dtypes.bfloat16)], 0))
    c["d1b"] = _bf(np.concatenate([np.cos(a), np.sin(a)], 1))
    n2 = np.arange(128)[:, None, None]
    k1g = np.arange(33)[None, :, None]
    k2 = np.arange(128)[None, None, :]
    ang = 2 * np.pi * n2 * (k1g + 64 * k2) / 8192.0
    gr, gi = np.cos(ang), -np.sin(ang)
    c["gtab"] = _bf(np.stack([gr, gi, -gi], 2))
    k2e = np.arange(128)[:, None]
    n2e = np.arange(128)[None, :]
    ae = 2 * np.pi * k2e * n2e / 128.0
    c["etab"] = _bf(np.stack([np.cos(ae).reshape(128, 4, 32), np.sin(ae).reshape(128, 4, 32)], 2))
    k1t = np.arange(33)[:, None, None]
    n2t = np.arange(128)[None, :, None]
    n1t = np.arange(32)[None, None, :]
    at = 2 * np.pi * k1t * (128 * n1t + n2t) / 8192.0
    wgt = np.full((33, 1, 1), 2.0)
    wgt[0] = 1.0
    wgt[32] = 1.0
    tr = wgt * np.cos(at) / 8192.0
    ti = wgt * np.sin(at) / 8192.0
    t0 = np.concatenate([tr, -ti], 0)
    t1 = np.concatenate([-ti, -tr], 0)
    c["ttab"] = _bf(np.stack([t0, t1], 1))
    _CONST = c
    return c


CONST_SHAPES = {
    "ident_bf": ([128, 128], BF), "ident_f": ([128, 128], F32),
    "rope_cos": ([128, L], F32), "rope_sin": ([128, L], F32),
    "zT": ([33, L], F32), "negdelta": ([128, 4], F32), "ndoff": ([128, 4, 8], F32), "tv0": ([128, 512], F32),
    "d1f": ([32, 66], BF), "d1fp": ([128, 66], BF), "d1b": ([32, 66], BF), "gtab": ([128, 33, 3, 128], BF),
    "etab": ([128, 4, 2, 32], BF), "ttab": ([66, 2, 128, 32], BF),
}

IN_SHAPES = {
    "x": [L, D], "ctx": [CT, D], "cc": [128, 8, 2], "w_ada": [D, 6 * D], "b_adaT": [128, 48], "gcols": [128, 4, 8],
    "w_in": [D, 5120], "w_qk_sw": [D, 1024], "cw": [128, 12, 3], "cb": [128, 12],
    "fw1": [33, 64], "fb1": [64, 1], "fw2": [64, 64], "fb2": [64, 1], "fw3": [64, 2048], "fb3": [1, 2048],
    "ffreq": [64, 1], "hyb": [128, 2, 4], "lamv": [1, 256], "subg": [1, 128],
    "w_hy_up": [DH, D], "w_att_up": [DH, D], "w_out": [D, D], "w_fg": [D, DFF], "w_fu": [D, DFF], "w_fd": [DFF, D],
}


def layout_inputs(inp, b):
    f = lambda a: np.ascontiguousarray(np.asarray(a, dtype=np.float32))
    m = {}
    m["x"] = f(inp["x"][b])
    m["ctx"] = f(inp["ctx"][b])
    cc = np.stack([np.asarray(inp["c"][b]), np.asarray(inp["c_ctx"])], -1)
    m["cc"] = f(cc.reshape(8, 128, 2).transpose(1, 0, 2))
    m["w_ada"] = f(inp["w_ada"][0])
    m["b_adaT"] = f(np.asarray(inp["b_ada"][0]).reshape(48, 128).T)
    g = np.stack([np.asarray(inp[k][0]) for k in ("g_mix_pre", "g_mix_post", "g_ffn_pre", "g_ffn_post")], 0)
    m["gcols"] = f(g.reshape(4, 8, 128).transpose(2, 0, 1))
    w_in = np.asarray(inp["w_in"][0])
    m["w_in"] = f(w_in)
    perm = np.arange(1024).reshape(16, 2, 2, 16)[:, :, ::-1, :].reshape(-1)
    m["w_qk_sw"] = f(w_in[:, OFF_Q:OFF_V][:, perm])
    m["cw"] = f(np.asarray(inp["hy_conv_w"][0]).reshape(3, 12, 128).transpose(2, 1, 0))
    m["cb"] = f(np.asarray(inp["hy_conv_b"][0]).reshape(12, 128).T)
    m["fw1"] = f(inp["hy_f_w1"][0])
    m["fb1"] = f(np.asarray(inp["hy_f_b1"][0]).reshape(64, 1))
    m["fw2"] = f(inp["hy_f_w2"][0])
    m["fb2"] = f(np.asarray(inp["hy_f_b2"][0]).reshape(64, 1))
    m["fw3"] = f(inp["hy_f_w3"][0])
    m["fb3"] = f(np.asarray(inp["hy_f_b3"][0]).reshape(1, 2048))
    m["ffreq"] = f(np.asarray(inp["hy_f_freq"][0]).reshape(64, 1))
    m["hyb"] = f(np.asarray(inp["hy_bias"][0]).reshape(2, 4, 128).transpose(2, 0, 1))
    m["lamv"] = f(np.concatenate([np.asarray(inp[k][0]) for k in ("lambda_q1", "lambda_q2", "lambda_k1", "lambda_k2")]).reshape(1, 256))
    m["subg"] = f(np.asarray(inp["att_subln_g"][0]).reshape(1, 128))
    m["w_hy_up"] = f(inp["w_hy_up"][0])
    m["w_att_up"] = f(inp["w_att_up"][0])
    m["w_out"] = f(inp["w_out"][0])
    m["w_fg"] = f(inp["w_ffn_gate"][0])
    m["w_fu"] = f(inp["w_ffn_up"][0])
    m["w_fd"] = f(inp["w_ffn_down"][0])
    m.update(host_consts())
    return m


def build(dbg=(), stop_after=None, skip=()):
    nc = bass.Bass("TRN2", target_bir_lowering=False)
    kb = KB(nc)
    din = {}
    for k, shp in IN_SHAPES.items():
        din[k] = nc.dram_tensor(k, list(shp), F32, kind="ExternalInput").ap()
    for k, (shp, dt) in CONST_SHAPES.items():
        din[k] = nc.dram_tensor(k, list(shp), dt, kind="ExternalInput").ap()
    out = nc.dram_tensor("out", [L, D], F32, kind="ExternalOutput").ap()
    hT_d = nc.dram_tensor("hT_d", [128, 8, L], BF, kind="Internal").ap()
    mT_d = nc.dram_tensor("mT_d", [128, 8, L], BF, kind="Internal").ap()
    zT_d = nc.dram_tensor("zT_d", [128, 4, L], BF, kind="Internal").ap()
    aT_d = nc.dram_tensor("aT_d", [128, 4, L], BF, kind="Internal").ap()
    sig_d = nc.dram_tensor("sig_d", [5, 128, L], BF, kind="Internal").ap()
    dbg_out = {}

    def dbg_tensor(name, shape, dt=F32):
        dbg_out[name] = nc.dram_tensor("dbg_" + name, list(shape), dt, kind="ExternalOutput").ap()
        return dbg_out[name]

    psall = nc.alloc_psum_tensor("psall", [128, 4096], F32)
    psall_b = psall.bitcast(BF)
    ps = [psall[:, i * 512:(i + 1) * 512] for i in range(8)]
    psb = [psall_b[:, i * 1024:(i + 1) * 1024] for i in range(8)]
    tps = [T(f"ps{i}") for i in range(8)]

    def mm(out_ap, lhsT, rhs, start, stop, reads, writes, tile_position=None):
        if tile_position is None:
            kb.op("pe", lambda e: e.matmul(out_ap, lhsT, rhs, start=start, stop=stop), reads=reads, writes=writes)
        else:
            kb.op("pe", lambda e: e.matmul(out_ap, lhsT, rhs, start=start, stop=stop, tile_position=tile_position), reads=reads, writes=writes)

    def tr(out_ap, in_ap, ident, reads, writes):
        kb.op("pe", lambda e: e.transpose(out_ap, in_ap, ident), reads=reads, writes=writes)

    def act(out_ap, in_ap, func, reads, writes, **kw):
        kb.op("act", lambda e: e.activation(out=out_ap, in_=in_ap, func=func, **kw), reads=reads, writes=writes)

    def ts(eng, out_ap, in0, s1, s2, op0, op1, reads, writes, **kw):
        if s2 is None:
            kb.op(eng, lambda e: e.tensor_scalar(out_ap, in0, s1, None, op0, **kw), reads=reads, writes=writes)
        else:
            kb.op(eng, lambda e: e.tensor_scalar(out_ap, in0, s1, s2, op0, op1, **kw), reads=reads, writes=writes)

    def tt(eng, out_ap, in0, in1, op, reads, writes):
        kb.op(eng, lambda e: e.tensor_tensor(out_ap, in0, in1, op), reads=reads, writes=writes)

    def stt(out_ap, in0, scalar, in1, op0, op1, reads, writes):
        kb.op("dve", lambda e: e.scalar_tensor_tensor(out_ap, in0, scalar, in1, op0, op1), reads=reads, writes=writes)

    def cp(eng, out_ap, in_ap, reads, writes):
        if eng == "act":
            kb.op("act", lambda e: e.copy(out_ap, in_ap), reads=reads, writes=writes)
        else:
            kb.op(eng, lambda e: e.tensor_copy(out_ap, in_ap), reads=reads, writes=writes)

    def recip(out_ap, in_ap, reads, writes):
        kb.op("dve", lambda e: e.reciprocal(out_ap, in_ap), reads=reads, writes=writes)

    def rsum(out_ap, in_ap, reads, writes):
        kb.op("dve", lambda e: e.reduce_sum(out_ap, in_ap, AX.X), reads=reads, writes=writes)

    def memset(eng, ap, val, writes):
        kb.op(eng, lambda e: e.memset(ap, val), writes=writes)

    _ids = [0]

    def next_id():
        _ids[0] += 1
        return _ids[0]

    P = ExitStack()

    def sbp(name, shape, dt):
        return P.enter_context(nc.sbuf_tensor("sp_" + name, list(shape), dt))

    ident_bf = sbp("ident_bf", [128, 128], BF)
    ident_f = sbp("ident_f", [128, 128], F32)
    ones_f = sbp("ones_f", [128, 128], F32)
    mhalf = sbp("mhalf", [128, 32], F32)
    cols = sbp("cols", [128, 8, 8], F32)
    g1row = sbp("g1row", [128, D], F32)
    g2row = sbp("g2row", [128, D], F32)
    neglam = sbp("neglam", [128, 1], F32)
    sgrow = sbp("sgrow", [128, 128], F32)
    hcT = sbp("hcT", [128, 8, CT], BF)
    t_const = T("const")
    t_cols = T("cols")
    t_rows = T("rows")
    t_hcT = T("hcT")
    kb.dma("sp", ident_bf[:], din["ident_bf"], writes=[t_const])
    kb.dma("sp", ident_f[:], din["ident_f"], writes=[t_const])
    memset("dve", ones_f[:], 1.0, [t_const])
    memset("dve", mhalf[:], -0.5, [t_const])
    A1 = lambda k: cols[:, 0, k:k + 1]
    B1 = lambda k: cols[:, 1, k:k + 1]
    A2 = lambda k: cols[:, 2, k:k + 1]
    B2 = lambda k: cols[:, 3, k:k + 1]
    A1c = lambda k: cols[:, 4, k:k + 1]
    B1c = lambda k: cols[:, 5, k:k + 1]

    def rstd_of(ssq_ap, out_ap, inv_n, tl):
        n = ssq_ap.shape[-1] if len(ssq_ap.shape) > 1 else 1
        ts("dve", out_ap, ssq_ap, inv_n, EPS, ALU.mult, ALU.add, reads=[tl], writes=[tl])
        tt("pool", out_ap, out_ap, mhalf[:, 0:n], ALU.pow, reads=[tl, t_const], writes=[tl])

    with ExitStack() as ph:
        def sb(name, shape, dt):
            return ph.enter_context(nc.sbuf_tensor(f"s{next_id()}_" + name, list(shape), dt))

        ccs = sb("ccs", [128, 16], F32)
        scs = sb("scs", [128, 16], F32)
        bada = sb("bada", [128, 48], F32)
        gc = sb("gc", [128, 4, 8], F32)
        adaT = sb("adaT", [128, 48, 2], F32)
        wa = [sb(f"wa{i}", [128, 6 * D], F32) for i in range(2)]
        twa = [T("wa0"), T("wa1")]
        t_s = T("p0small")
        kb.dma("sp", ccs[:], din["cc"].rearrange("p k c -> p (k c)"), writes=[t_s])
        kb.dma("sp", bada[:], din["b_adaT"], writes=[t_s])
        kb.dma("sp", gc[:], din["gcols"], writes=[t_s])
        act(scs[:], ccs[:], AF.Silu, reads=[t_s], writes=[t_s])
        def ada_chunk(k):
            kb.dma("sp", wa[k % 2][:], din["w_ada"][k * 128:(k + 1) * 128, :], writes=[twa[k % 2]])
            for f in range(48):
                mm(ps[0][:, 2 * f:2 * f + 2], wa[k % 2][:, f * 128:(f + 1) * 128], scs[:, 2 * k:2 * k + 2],
                   start=(k == 0 and f == 0), stop=(k == 7 and f == 47), reads=[twa[k % 2], t_s], writes=[tps[0]])
        lamb = sb("lamb", [128, 256], F32)
        lp = sb("lp", [128, 2, 64], F32)
        le = sb("le", [128, 2], F32)
        kb.dma("sp", lamb[:], bass.AP(din["lamv"].tensor, 0, [[0, 128], [1, 256]]), writes=[t_s])
        kb.dma("sp", sgrow[:], bass.AP(din["subg"].tensor, 0, [[0, 128], [1, 128]]), writes=[t_rows])
        tt("dve", lp[:].rearrange("p a b -> p (a b)"), lamb[:, 0:128], lamb[:, 128:256], ALU.mult, reads=[t_s], writes=[t_s])
        rsum(le[:], lp[:], reads=[t_s], writes=[t_s])
        act(le[:], le[:], AF.Exp, reads=[t_s], writes=[t_s])
        tt("dve", neglam[:], le[:, 1:2], le[:, 0:1], ALU.subtract, reads=[t_s], writes=[t_cols])
        ts("dve", neglam[:], neglam[:], -LAM_INIT, None, ALU.add, None, reads=[t_cols], writes=[t_cols])
        ts("dve", sgrow[:], sgrow[:], 1.0 - LAM_INIT, None, ALU.mult, None, reads=[t_rows], writes=[t_rows])
        xts = [sb(f"xt{i}", [128, D], F32) for i in range(4)]
        txt = [T() for _ in range(4)]
        junk = sb("junk", [128, D], BF)
        t_junk = T()
        xsa = sb("xsa", [128, 34, D], BF)
        txs = [T() for _ in range(34)]
        ssq = sb("ssq", [128, 34], F32)
        t_ssq = [T() for _ in range(34)]
        hst = [sb(f"hst{i}", [128, 8, 512], BF) for i in range(2)]
        thst = [T(), T()]
        t_hd = T("hT_d")
        for i in range(34):
            lat = i < 32
            src = din["x"][i * 128:(i + 1) * 128, :] if lat else din["ctx"][(i - 32) * 128:(i - 31) * 128, :]
            xt, tx = xts[i % 4], txt[i % 4]
            if i % 4 == 0 and i // 4 < 8:
                ada_chunk(i // 4)
            kb.dma("sp", xt[:], src, writes=[tx])
            act(junk[:], xt[:], AF.Square, reads=[tx], writes=[t_junk, t_ssq[i]], accum_out=ssq[:, i:i + 1])
            rstd_of(ssq[:, i:i + 1], ssq[:, i:i + 1], 1.0 / D, t_ssq[i])
            if i >= 2:
                j = i - 2
                act(xsa[:, j, :], xts[j % 4][:], AF.Copy, reads=[txt[j % 4], t_ssq[j]], writes=[txs[j]], scale=ssq[:, j:j + 1])
        for j in (32, 33):
            act(xsa[:, j, :], xts[j % 4][:], AF.Copy, reads=[txt[j % 4], t_ssq[j]], writes=[txs[j]], scale=ssq[:, j:j + 1])
        for c in range(2):
            tt("dve", adaT[:, :, c], ps[0][:, c:96:2], bada[:], ALU.add, reads=[tps[0], t_s], writes=[t_s])
        for (dst, sc_f, g_i, c) in ((0, 8, 0, 0), (2, 32, 2, 0), (4, 8, 0, 1)):
            stt(cols[:, dst, :], adaT[:, sc_f:sc_f + 8, c], 1.0, gc[:, g_i, :], ALU.add, ALU.mult, reads=[t_s], writes=[t_cols])
        for (dst, sh_f, c) in ((1, 0, 0), (3, 24, 0), (5, 0, 1)):
            cp("dve", cols[:, dst, :], adaT[:, sh_f:sh_f + 8, c], reads=[t_s], writes=[t_cols])
        tt("dve", cols[:, 6, :], adaT[:, 16:24, 0], gc[:, 1, :], ALU.mult, reads=[t_s], writes=[t_cols])
        tt("dve", cols[:, 7, :], adaT[:, 40:48, 0], gc[:, 3, :], ALU.mult, reads=[t_s], writes=[t_cols])
        diag = sb("diag", [128, 4, 128], F32)
        t_diag = T("diag")
        for gi, rowt in ((6, g1row), (7, g2row)):
            for half in range(2):
                for j in range(4):
                    ts("dve", diag[:, j, :], ident_f[:], cols[:, gi, half * 4 + j:half * 4 + j + 1], None, ALU.mult, None,
                       reads=[t_const, t_cols], writes=[t_diag])
                for j in range(4):
                    mm(ps[1][:, j * 128:(j + 1) * 128], ones_f[:], diag[:, j, :], start=True, stop=True,
                       reads=[t_diag, t_const], writes=[tps[1]])
                cp("dve", rowt[:, half * 512:(half + 1) * 512], ps[1][:], reads=[tps[1]], writes=[t_rows])
        if "p0" in dbg:
            d = dbg_tensor("cols", [128, 64])
            kb.dma("sp", d, cols[:].rearrange("p a b -> p (a b)"), reads=[t_cols])
            d = dbg_tensor("g1row", [128, D])
            kb.dma("sp", d, g1row[:], reads=[t_rows])
            d = dbg_tensor("neglam", [128, 1])
            kb.dma("sp", d, neglam[:], reads=[t_cols])

        for i in range(34):
            lat = i < 32
            bk = 2 + i % 4
            for k in range(8):
                tr(psb[bk][:, k * 128:(k + 1) * 128], xsa[:, i, k * 128:(k + 1) * 128], ident_bf[:],
                   reads=[txs[i], t_const], writes=[tps[bk]])
            for k in range(8):
                if lat:
                    h = hst[(i // 4) % 2]
                    ts("dve", h[:, k, (i % 4) * 128:(i % 4 + 1) * 128], psb[bk][:, k * 128:(k + 1) * 128], A1(k), B1(k),
                       ALU.mult, ALU.add, reads=[tps[bk], t_cols], writes=[thst[(i // 4) % 2]])
                else:
                    ts("dve", hcT[:, k, (i - 32) * 128:(i - 31) * 128], psb[bk][:, k * 128:(k + 1) * 128], A1c(k), B1c(k),
                       ALU.mult, ALU.add, reads=[tps[bk], t_cols], writes=[t_hcT])
            if lat and i % 4 == 3:
                blk = i // 4
                kb.dma("sp", hT_d[:, :, blk * 512:(blk + 1) * 512], hst[blk % 2][:], reads=[thst[blk % 2]], writes=[t_hd])
        if "p1" in dbg:
            d = dbg_tensor("hT", [128, 8, L], BF)
            kb.dma("sp", d, hT_d, reads=[t_hd])
            d = dbg_tensor("hcT", [128, 8, CT], BF)
            kb.dma("sp", d, hcT[:], reads=[t_hcT])
        kb.barrier()
    if stop_after == "p1":
        kb.finish()
        return nc, dbg_out


    hall_d = nc.dram_tensor("hall_d", [1024, 8448], BF, kind="Internal").ap()
    t_hall = [T(f"hall{i}") for i in range(8)]
    with ExitStack() as ph:
        def sb(name, shape, dt):
            return ph.enter_context(nc.sbuf_tensor(f"s{next_id()}_" + name, list(shape), dt))

        _bk = [0]

        def nb():
            _bk[0] = (_bk[0] + 1) % 8
            return _bk[0]

        t_hc = T("hfconst")
        gtab = sb("gtab", [128, 33, 3, 128], BF)
        d1f = sb("d1fp", [128, 66], BF)
        tv0 = sb("tv0", [128, 512], F32)
        negd = sb("negd", [128, 4], F32)
        ndoff = sb("ndoff", [128, 4, 8], F32)
        hybs = sb("hybs", [128, 2, 4], F32)
        hdn2 = sb("hdn2", [65, L], BF)
        fw3a = sb("fw3a", [65, 2048], BF)
        t_h2 = T("hdn2")
        t_fw3 = T("fw3")
        kb.dma("pool", fw3a[0:64, :], din["fw3"], writes=[t_fw3])
        kb.dma("pool", fw3a[64:65, :], din["fb3"], writes=[t_fw3])
        memset("pool", hdn2[64:65, :], 1.0, [t_h2])
        zTs = sb("zTs", [33, L], F32)
        fw1s = sb("fw1s", [33, 64], F32)
        fw2s = sb("fw2s", [64, 64], F32)
        fcol = sb("fcol", [64, 5], F32)
        t_f = T("fmlp")
        kb.dma("sp", zTs[:], din["zT"], writes=[t_f])
        kb.dma("sp", fw1s[:], din["fw1"], writes=[t_f])
        kb.dma("sp", fw2s[:], din["fw2"], writes=[t_f])
        kb.dma("sp", fcol[:, 0:1], din["ffreq"], writes=[t_f])
        kb.dma("sp", fcol[:, 1:2], din["fb1"], writes=[t_f])
        kb.dma("sp", fcol[:, 2:3], din["fb2"], writes=[t_f])
        for dst, nm in ((tv0, "tv0"), (negd, "negdelta"), (ndoff, "ndoff"), (hybs, "hyb"), (d1f, "d1fp"), (gtab, "gtab")):
            kb.dma("sp", dst[:], din[nm], writes=[t_hc])
        tt("dve", fcol[:, 3:4], fcol[:, 0:1], fcol[:, 1:2], ALU.mult, reads=[t_f], writes=[t_f])
        tt("dve", fcol[:, 4:5], fcol[:, 0:1], fcol[:, 2:3], ALU.mult, reads=[t_f], writes=[t_f])
        with ExitStack() as phm:
            def sbm(name, shape, dt):
                return phm.enter_context(nc.sbuf_tensor(f"s{next_id()}_" + name, list(shape), dt))

            halfpi = sbm("halfpi", [64, 1], F32)
            memset("dve", halfpi[:], PI / 2, [t_f])
            arg = [sbm(f"arg{i}", [64, 512], F32) for i in range(8)]
            s4 = [sbm(f"s4{i}", [64, 512], F32) for i in range(8)]
            c4 = [sbm(f"c4{i}", [64, 512], F32) for i in range(8)]
            hd1 = [sbm(f"hd1{i}", [64, 512], F32) for i in range(8)]
            t_m = [T() for _ in range(8)]

            def sin_layer_bf(ps_of, bias_col, out_of, t_out_of):
                for i_ in range(8):
                    ts("dve", arg[i_][:], ps[ps_of(i_)][0:64, :], fcol[:, 0:1], fcol[:, bias_col:bias_col + 1], ALU.mult, ALU.add,
                       reads=[tps[ps_of(i_)], t_f], writes=[t_m[i_]])
                for i_ in range(8):
                    act(s4[i_][:], arg[i_][:], AF.Sin, reads=[t_m[i_]], writes=[t_m[i_]], scale=0.25)
                    act(c4[i_][:], arg[i_][:], AF.Sin, reads=[t_m[i_], t_f], writes=[t_m[i_]], scale=0.25, bias=halfpi[:])
                for i_ in range(8):
                    tt("pool", arg[i_][:], s4[i_][:], s4[i_][:], ALU.mult, reads=[t_m[i_]], writes=[t_m[i_]])
                    tt("pool", s4[i_][:], s4[i_][:], c4[i_][:], ALU.mult, reads=[t_m[i_]], writes=[t_m[i_]])
                for i_ in range(8):
                    ts("dve", arg[i_][:], arg[i_][:], -2.0, 1.0, ALU.mult, ALU.add, reads=[t_m[i_]], writes=[t_m[i_]])
                    stt(out_of(i_), s4[i_][:], 4.0, arg[i_][:], ALU.mult, ALU.mult, reads=[t_m[i_]], writes=[t_m[i_], t_out_of(i_)])

            for blk in range(8):
                mm(ps[blk][0:64, :], fw1s[:], zTs[:, blk * 512:(blk + 1) * 512], start=True, stop=True, reads=[t_f], writes=[tps[blk]])
            sin_layer_bf(lambda i_: i_, 3, lambda i_: hd1[i_][:], lambda i_: t_m[i_])
            for blk in range(8):
                mm(ps[blk][0:64, :], fw2s[:], hd1[blk][:], start=True, stop=True, reads=[t_f, t_m[blk]], writes=[tps[blk]])
            sin_layer_bf(lambda i_: i_, 4, lambda i_: hdn2[0:64, i_ * 512:(i_ + 1) * 512], lambda i_: t_h2)
            kb.barrier()
        dec = [sb(f"dec{i}", [128, L], BF) for i in range(2)]
        t_dec = [T(), T()]
        Kf = [[sb(f"Kf{i}_{d_}", [128, L], BF) for d_ in range(2)] for i in range(2)]
        t_Kf = [[T(), T()], [T(), T()]]
        Ut = [sb(f"Ut{i}", [128, 32, 128], BF) for i in range(2)]
        tUt = [T(), T()]
        for i in range(2):
            memset("pool", Ut[i][32:64, :, :], 0.0, [tUt[i]])
            memset("pool", Ut[i][64:128, :, :], 0.0, [tUt[i]])
        A = sb("A", [128, 66, 128], BF)
        tA = [T() for _ in range(33)]
        H = [sb(f"H{i}", [128, 33, 2, 128], BF) for i in range(2)]
        tH = [[T() for _ in range(33)] for _ in range(2)]
        t_sdf = [T() for _ in range(4)]
        cnt = [0]

        def stage_k(r):
            cc, o = r // 2, r % 2
            pi_ = r % 2
            if o == 0:
                for b8 in range(8):
                    act(dec[cc % 2][:, b8 * 512:(b8 + 1) * 512], tv0[:], AF.Exp, reads=[t_hc], writes=[t_dec[cc % 2]],
                        scale=negd[:, cc:cc + 1], bias=ndoff[:, cc, b8:b8 + 1])
            for d_ in range(2):
                col0 = (o * 2 + d_) * 512 + cc * 128
                kf, tk = Kf[pi_][d_], t_Kf[pi_][d_]
                for blk in range(8):
                    sl = slice(blk * 512, (blk + 1) * 512)
                    b = nb()
                    mm(ps[b][:], fw3a[:, col0:col0 + 128], hdn2[:, sl], start=True, stop=True, reads=[t_fw3, t_h2], writes=[tps[b]])
                    tt("dve", kf[:, sl], ps[b][:], dec[cc % 2][:, sl], ALU.mult, reads=[tps[b], t_dec[cc % 2]], writes=[tk])
                if d_ == 0:
                    tt("dve", kf[:, 0:1], kf[:, 0:1], hybs[:, o, cc:cc + 1], ALU.add, reads=[tk, t_hc], writes=[tk])
                else:
                    memset("dve", kf[:, 0:1], 0.0, [tk])
                kb.dma("pool", sig_d[pi_ * 2 + d_], kf[:], reads=[tk], writes=[t_sdf[pi_ * 2 + d_]])

        def stage_f(r):
            pi_ = r % 2
            Hh, tHh = H[pi_], tH[pi_]
            for d_ in range(2):
                slot = pi_ * 2 + d_
                for g in range(4):
                    u, tu = Ut[g % 2], tUt[g % 2]
                    kb.dma("sp", u[0:32, :, :], sig_d[slot][g * 32:(g + 1) * 32, :].rearrange("c (a b) -> a c b", a=32), reads=[t_sdf[slot]], writes=[tu])
                    j = 0
                    while j < 32:
                        n = min(7, 32 - j)
                        b = nb()
                        for jj in range(n):
                            mm(ps[b][:, jj * 66:(jj + 1) * 66], u[:, j + jj, :], d1f[:], start=True, stop=True, reads=[tu, t_hc], writes=[tps[b]])
                        c0 = g * 32 + j
                        cp("act" if cnt[0] % 2 == 0 else "dve", A[:, :, c0:c0 + n], ps[b][:, 0:n * 66].rearrange("p (c k) -> p k c", k=66),
                           reads=[tps[b]], writes=tA)
                        cnt[0] += 1
                        j += n
                k1 = 0
                while k1 < 33:
                    n = 2 if k1 + 1 < 33 else 1
                    b = nb()
                    for u_ in range(n):
                        kk = k1 + u_
                        o_ = u_ * 256
                        ar, ai = A[:, kk, :], A[:, 33 + kk, :]
                        mm(ps[b][:, o_:o_ + 128], gtab[:, kk, 0, :], ar, start=True, stop=False, reads=[t_hc, tA[kk]], writes=[tps[b]])
                        mm(ps[b][:, o_:o_ + 128], gtab[:, kk, 2, :], ai, start=False, stop=True, reads=[t_hc, tA[kk]], writes=[tps[b]])
                        mm(ps[b][:, o_ + 128:o_ + 256], gtab[:, kk, 1, :], ar, start=True, stop=False, reads=[t_hc, tA[kk]], writes=[tps[b]])
                        mm(ps[b][:, o_ + 128:o_ + 256], gtab[:, kk, 0, :], ai, start=False, stop=True, reads=[t_hc, tA[kk]], writes=[tps[b]])
                    xv = ps[b][:, 0:n * 256].rearrange("p (k r c) -> p k r c", k=n, r=2)
                    wh = [tHh[k1 + u_] for u_ in range(n)]
                    if d_ == 0:
                        cp("act", Hh[:, k1:k1 + n, :, :], xv, reads=[tps[b]], writes=wh)
                    else:
                        tt("dve", Hh[:, k1:k1 + n, 0, :], Hh[:, k1:k1 + n, 0, :], xv[:, :, 0, :], ALU.add, reads=[tps[b]] + wh, writes=wh)
                        tt("dve", Hh[:, k1:k1 + n, 1, :], Hh[:, k1:k1 + n, 1, :], xv[:, :, 1, :], ALU.subtract, reads=[tps[b]] + wh, writes=wh)
                    k1 += n
            kb.dma("sp", hall_d[r * 128:(r + 1) * 128, :], Hh[:].rearrange("p a b c -> p (a b c)"), reads=tHh, writes=[t_hall[r]])

        stage_k(0)
        for r in range(8):
            if r + 1 < 8:
                stage_k(r + 1)
            stage_f(r)
        kb.barrier()
    if stop_after == "hf":
        kb.finish()
        return nc, dbg_out

    t_hd_r = T("hT_d_r")
    w_in_v = din["w_in"].rearrange("(k p) n -> p k n", p=128)
    w_sw_v = din["w_qk_sw"].rearrange("(k p) n -> p k n", p=128)

    def load_w(dst, src_view, c0, tl, ncols=128):
        kb.dma("pool", dst, src_view[:, :, c0:c0 + ncols], writes=[tl])

    with ExitStack() as ph:
        def sb(name, shape, dt):
            return ph.enter_context(nc.sbuf_tensor(f"s{next_id()}_" + name, list(shape), dt))

        NH = 0 if "p3" in skip else 4
        rcos = sb("rcos", [128, L], F32)
        rsin = sb("rsin", [128, L], F32)
        t_ropes = [T(f"rope{i}") for i in range(8)]

        def load_rope(i):
            kb.dma("sp", rcos[:, i * 512:(i + 1) * 512], din["rope_cos"][:, i * 512:(i + 1) * 512], writes=[t_ropes[i]])
            kb.dma("sp", rsin[:, i * 512:(i + 1) * 512], din["rope_sin"][:, i * 512:(i + 1) * 512], writes=[t_ropes[i]])
        QT = [sb(f"QT{i}", [128, L], BF) for i in range(2)]
        KT = [[sb(f"KT{i}_{m_}", [128, LK], BF) for m_ in range(2)] for i in range(2)]
        V = [sb(f"V{i}", [128, 34, 129], BF) for i in range(2)]
        t_Q, t_K, t_V = [T(), T()], [T(), T()], [T(), T()]
        for i in range(2):
            memset("pool", KT[i][0][64:128, :], 0.0, [t_K[i]])
            memset("pool", KT[i][1][0:64, :], 0.0, [t_K[i]])
            memset("pool", V[i][:, :, 128:129], 1.0, [t_V[i]])
        wts = [[sb(f"aw{i}_{j}", [128, 8, 128], BF) for j in range(5)] for i in range(2)]
        twts = [[T() for _ in range(5)] for _ in range(2)]
        hb = [sb(f"hb{i}", [128, 8, 512], BF) for i in range(2)]
        thb = [T(), T()]
        rt = [sb(f"rt{i}", [128, 512], F32) for i in range(4)]
        trt = [T() for _ in range(4)]
        E = [sb(f"E{i}", [128, 1024], BF) for i in range(4)]
        tE = [T() for _ in range(4)]
        attst = [sb(f"attst{i}", [128, 512], BF) for i in range(2)]
        t_attst = [T(), T()]
        fin = sb("fin", [128, 16], F32)
        accs = sb("accs", [128, 1161], F32)
        oa = sb("oa", [128, 4, 128], F32)
        ob = sb("ob", [128, 4, 128], F32)
        on = sb("on", [128, 4, 128], BF)
        t_fin, t_on, t_accs, t_oa, t_ob = T(), T(), T(), T(), T()
        t_ad = T("aT_d")
        hbcnt = [0]

        def prologue(h, banks):
            bi = h % 2
            w_, tw_ = wts[bi], twts[bi]
            brr = [0]

            def nbk():
                brr[0] += 1
                return banks[brr[0] % len(banks)]

            load_w(w_[0][:], w_in_v, OFF_Q + h * 128, tw_[0])
            load_w(w_[1][:], w_sw_v, h * 128, tw_[1])
            load_w(w_[2][:], w_in_v, OFF_K + h * 128, tw_[2])
            load_w(w_[3][:], w_sw_v, 512 + h * 128, tw_[3])
            load_w(w_[4][:], w_in_v, OFF_V + h * 128, tw_[4])
            b = nbk()
            for k in range(8):
                mm(ps[b][:, 0:CT], w_[2][:, k, :], hcT[:, k, :], start=(k == 0), stop=(k == 7), reads=[tw_[2], t_hcT], writes=[tps[b]])
            cp("dve", KT[bi][0][0:64, L:LK], ps[b][0:64, 0:CT], reads=[tps[b]], writes=[t_K[bi]])
            cp("dve", KT[bi][1][64:128, L:LK], ps[b][64:128, 0:CT], reads=[tps[b]], writes=[t_K[bi]])
            yield
            b = nbk()
            for s_ in range(2):
                for k in range(8):
                    mm(ps[b][:, s_ * 128:(s_ + 1) * 128], hcT[:, k, s_ * 128:(s_ + 1) * 128], w_[4][:, k, :], start=(k == 0), stop=(k == 7),
                       reads=[tw_[4], t_hcT], writes=[tps[b]])
            cp("dve", V[bi][:, 32:34, 0:128], ps[b][:, 0:256].rearrange("p (s v) -> p s v", s=2), reads=[tps[b]], writes=[t_V[bi]])
            yield
            for blk in range(8):
                hi = hbcnt[0] % 2
                hbcnt[0] += 1
                hbb, th = hb[hi], thb[hi]
                kb.dma("sp", hbb[:], hT_d[:, :, blk * 512:(blk + 1) * 512], reads=[t_hd], writes=[th])
                if h == 0:
                    load_rope(blk)
                sl = slice(blk * 512, (blk + 1) * 512)
                for qk in range(2):
                    for gg in range(2):
                        g = 2 * qk + gg
                        b = nbk()
                        for k in range(8):
                            mm(ps[b][:], w_[g][:, k, :], hbb[:, k, :], start=(k == 0), stop=(k == 7), reads=[tw_[g], th], writes=[tps[b]])
                        tt("dve", rt[g][:], ps[b][:], (rcos if gg == 0 else rsin)[:, sl], ALU.mult, reads=[tps[b], t_ropes[blk]], writes=[trt[g]])
                        yield
                    r0, r1 = rt[2 * qk], rt[2 * qk + 1]
                    if qk == 0:
                        tt("pool", QT[bi][:, sl], r0[:], r1[:], ALU.add, reads=[trt[0], trt[1]], writes=[t_Q[bi]])
                    else:
                        tt("pool", KT[bi][0][0:64, sl], r0[0:64, :], r1[0:64, :], ALU.add, reads=[trt[2], trt[3]], writes=[t_K[bi]])
                        tt("pool", KT[bi][1][64:128, sl], r0[64:128, :], r1[64:128, :], ALU.add, reads=[trt[2], trt[3]], writes=[t_K[bi]])
                b = nbk()
                for s_ in range(4):
                    for k in range(8):
                        mm(ps[b][:, s_ * 128:(s_ + 1) * 128], hbb[:, k, s_ * 128:(s_ + 1) * 128], w_[4][:, k, :], start=(k == 0), stop=(k == 7),
                           reads=[tw_[4], th], writes=[tps[b]])
                cp("dve", V[bi][:, blk * 4:blk * 4 + 4, 0:128], ps[b][:].rearrange("p (s v) -> p s v", s=4), reads=[tps[b]], writes=[t_V[bi]])
                yield

        def bc_last(ap2d, n):
            return bass.AP(ap2d.tensor, ap2d.offset, [list(ap2d.ap[0]), list(ap2d.ap[1]), [0, n]])

        items = [(qb, kc) for qb in range(8) for kc in range(34)]

        def head_loop(h, gen):
            bi = h % 2
            Qh, Kh, Vh = QT[bi], KT[bi], V[bi]

            def emit_S(idx):
                qb, kc = items[idx]
                b0 = (idx % 2) * 2
                for m_ in range(2):
                    mm(ps[b0 + m_][:], Kh[m_][:, kc * 128:(kc + 1) * 128], Qh[:, qb * 512:(qb + 1) * 512], start=True, stop=True,
                       reads=[t_K[bi], t_Q[bi]], writes=[tps[b0 + m_]])
                ei = idx % 4
                act(E[ei][:], psall[:, b0 * 512:(b0 + 2) * 512], AF.Exp, reads=[tps[b0], tps[b0 + 1]], writes=[tE[ei]], scale=0.125)

            def emit_AV(idx):
                qb, kc = items[idx]
                ei = idx % 4
                for m_ in range(2):
                    for s_ in range(4):
                        slot = m_ * 4 + s_
                        bk, c0 = 4 + slot // 3, (slot % 3) * 129
                        mm(ps[bk][:, c0:c0 + 129], E[ei][:, m_ * 512 + s_ * 128:m_ * 512 + (s_ + 1) * 128], Vh[:, kc, :],
                           start=(kc == 0 and slot % 3 == 0), stop=(kc == 33 and (slot % 3 == 2 or slot == 7)),
                           reads=[tE[ei], t_V[bi]], writes=[tps[bk]])
                if kc == 33:
                    finalize(qb)
                    pend.append(qb)
                if kc == 10 and pend:
                    finalize_tr(pend.pop())

            def finalize(qb):
                qsl = slice(qb * 512, (qb + 1) * 512)
                cp("dve", accs[:, 0:387], ps[4][:, 0:387], reads=[tps[4]], writes=[t_accs])
                cp("dve", accs[:, 387:774], ps[5][:, 0:387], reads=[tps[5]], writes=[t_accs])
                cp("dve", accs[:, 774:1032], ps[6][:, 0:258], reads=[tps[6]], writes=[t_accs])
                av = accs[:, 0:1032].rearrange("p (s c) -> p s c", c=129)
                recip(fin[:, 0:8], av[:, :, 128], reads=[t_accs], writes=[t_fin])
                ts("dve", fin[:, 4:8], fin[:, 4:8], neglam[:], None, ALU.mult, None, reads=[t_fin, t_cols], writes=[t_fin])
                tt("dve", oa[:], av[:, 0:4, 0:128], bc_last(fin[:, 0:4], 128), ALU.mult, reads=[t_accs, t_fin], writes=[t_oa])
                tt("dve", ob[:], av[:, 4:8, 0:128], bc_last(fin[:, 4:8], 128), ALU.mult, reads=[t_accs, t_fin], writes=[t_ob])
                tt("pool", ob[:], ob[:], oa[:], ALU.add, reads=[t_oa, t_ob], writes=[t_ob])
                tt("pool", oa[:], ob[:], ob[:], ALU.mult, reads=[t_ob], writes=[t_oa])
                rsum(fin[:, 8:12], oa[:], reads=[t_oa], writes=[t_fin])
                rstd_of(fin[:, 8:12], fin[:, 12:16], 1.0 / 128, t_fin)
                tt("dve", ob[:], ob[:], bc_last(fin[:, 12:16], 128), ALU.mult, reads=[t_ob, t_fin], writes=[t_ob])
                sg_b = bass.AP(sgrow[:].tensor, sgrow[:].offset, [list(sgrow[:].ap[0]), [0, 4], [1, 128]])
                tt("dve", on[:], ob[:], sg_b, ALU.mult, reads=[t_ob, t_rows], writes=[t_on])

            def finalize_tr(qb):
                qsl = slice(qb * 512, (qb + 1) * 512)
                for s_ in range(4):
                    tr(psb[7][:, s_ * 128:(s_ + 1) * 128], on[:, s_, :], ident_bf[:], reads=[t_on, t_const], writes=[tps[7]])
                cp("dve", attst[qb % 2][:], psb[7][:, 0:512], reads=[tps[7]], writes=[t_attst[qb % 2]])
                kb.dma("sp", aT_d[:, h, qsl], attst[qb % 2][:], reads=[t_attst[qb % 2]], writes=[t_ad])

            pend = []
            emit_S(0)
            emit_S(1)
            for idx in range(len(items)):
                if idx + 2 < len(items):
                    emit_S(idx + 2)
                emit_AV(idx)
                if gen is not None and idx % 6 == 3 and items[idx][1] not in (32, 33, 0):
                    next(gen, None)
            while pend:
                finalize_tr(pend.pop())
            if gen is not None:
                for _ in gen:
                    pass

        if NH:
            for _ in prologue(0, [0, 1, 2, 3, 4, 5, 6, 7]):
                pass
        for h in range(NH):
            gen = prologue(h + 1, [7]) if h + 1 < NH else None
            head_loop(h, gen)
        if "p3" in dbg:
            d = dbg_tensor("attT", [128, 4, L], BF)
            kb.dma("sp", d, aT_d, reads=[t_ad])
        kb.barrier()
    if stop_after == "p3":
        kb.finish()
        return nc, dbg_out

    with ExitStack() as ph:
        def sb(name, shape, dt):
            return ph.enter_context(nc.sbuf_tensor(f"s{next_id()}_" + name, list(shape), dt))

        _bk = [0]

        def nb():
            _bk[0] = (_bk[0] + 1) % 8
            return _bk[0]

        t_hc = T("hyconst")
        gtab = sb("gtab", [128, 33, 3, 128], BF)
        ttab = sb("ttab", [66, 2, 128, 32], BF)
        etab = sb("etab", [128, 4, 2, 32], BF)
        d1f = sb("d1fp", [128, 66], BF)
        cwt = sb("cwt", [128, 12, 3], F32)
        cbt = sb("cbt", [128, 12], F32)
        t_hc2 = T("hyconst2")
        for dst, nm in ((cwt, "cw"), (cbt, "cb"), (d1f, "d1fp")):
            kb.dma("sp", dst[:], din[nm], writes=[t_hc])
        Z = [sb(f"Z{i}", [128, L], BF) for i in range(3)]
        tZ = [T(f"Z{i}") for i in range(3)]
        H = sb("H", [128, 33, 2, 128], BF)
        tH = [T() for _ in range(33)]
        t_sd = [T(f"sig{i}") for i in range(5)]
        t_zd = T("zT_d")
        Ut = [sb(f"Ut{i}", [128, 32, 128], BF) for i in range(2)]
        tUt = [T(), T()]
        for i in range(2):
            memset("dve", Ut[i][32:64, :, :], 0.0, [tUt[i]])
            memset("dve", Ut[i][64:128, :, :], 0.0, [tUt[i]])
        A = sb("A", [128, 66, 128], BF)
        tA = [T() for _ in range(33)]
        f1cnt = [0]

        def f1_pre(sig, t_sig, slot):
            kb.dma("pool", sig_d[slot], sig, reads=[t_sig], writes=[t_sd[slot]])
            for g in range(2):
                kb.dma("pool", Ut[g][0:32, :, :], sig_d[slot][g * 32:(g + 1) * 32, :].rearrange("c (a b) -> a c b", a=32),
                       reads=[t_sd[slot]], writes=[tUt[g]])

        def f1_part(sig, t_sig, slot, pre=False, act_only=False):
            if not pre:
                f1_pre(sig, t_sig, slot)
            for g in range(4):
                u, tu = Ut[g % 2], tUt[g % 2]
                if g >= 2:
                    kb.dma("pool", u[0:32, :, :], sig_d[slot][g * 32:(g + 1) * 32, :].rearrange("c (a b) -> a c b", a=32),
                           reads=[t_sd[slot]], writes=[tu])
                j = 0
                while j < 32:
                    n = min(7, 32 - j)
                    b = nb()
                    for jj in range(n):
                        mm(ps[b][:, jj * 66:(jj + 1) * 66], u[:, j + jj, :], d1f[:], start=True, stop=True,
                           reads=[tu, t_hc], writes=[tps[b]])
                    c0 = g * 32 + j
                    eng = "act" if (act_only or f1cnt[0] % 2 == 0) else "dve"
                    f1cnt[0] += 1
                    cp(eng, A[:, :, c0:c0 + n], ps[b][:, 0:n * 66].rearrange("p (c k) -> p k c", k=66), reads=[tps[b]], writes=tA)
                    j += n

        for cc in range(4):
            with ExitStack() as pa:
                def sba(name, shape, dt):
                    return pa.enter_context(nc.sbuf_tensor(f"s{next_id()}_" + name, list(shape), dt))

                wts = [sba(f"hw{i}", [128, 8, 128], BF) for i in range(3)]
                twts = [T() for _ in range(3)]
                hb = [sba(f"hhb{i}", [128, 8, 512], BF) for i in range(2)]
                thb = [T(), T()]
                Ur = [sba(f"Ur{i}", [128, L + 2], BF) for i in range(3)]
                tUr = [T() for _ in range(3)]
                ctmp = sba("ctmp", [128, L], F32)
                t_ct = T()
                for s in range(3):
                    load_w(wts[s][:], w_in_v, s * 512 + cc * 128, twts[s])
                    memset("pool", Ur[s][:, 0:1], 0.0, [tUr[s]])
                    memset("pool", Ur[s][:, L + 1:L + 2], 0.0, [tUr[s]])
                def proj_pass(sigs, off):
                    for blk in range(8):
                        hbb, th = hb[(blk + off) % 2], thb[(blk + off) % 2]
                        kb.dma("sp", hbb[:], hT_d[:, :, blk * 512:(blk + 1) * 512], reads=[t_hd], writes=[th])
                        for s in sigs:
                            b = nb()
                            for k in range(8):
                                mm(ps[b][:], wts[s][:, k, :], hbb[:, k, :], start=(k == 0), stop=(k == 7), reads=[twts[s], th], writes=[tps[b]])
                            cp("act", Ur[s][:, 1 + blk * 512:1 + (blk + 1) * 512], ps[b][:], reads=[tps[b]], writes=[tUr[s]])

                def sconv(s):
                    slot = s * 4 + cc
                    act(ctmp[:], Ur[s][:, 1:L + 1], AF.Identity, reads=[tUr[s], t_hc], writes=[t_ct],
                        scale=cwt[:, slot, 1:2], bias=cbt[:, slot:slot + 1])
                    stt(ctmp[:], Ur[s][:, 0:L], cwt[:, slot, 0:1], ctmp[:], ALU.mult, ALU.add, reads=[tUr[s], t_hc, t_ct], writes=[t_ct])
                    stt(Z[s][:], Ur[s][:, 2:L + 2], cwt[:, slot, 2:3], ctmp[:], ALU.mult, ALU.add, reads=[tUr[s], t_hc, t_ct], writes=[tZ[s]])

                proj_pass([0, 1, 2], 0)
                if cc == 0:
                    for dst, nm in ((gtab, "gtab"), (etab, "etab"), (ttab, "ttab")):
                        kb.dma("sp", dst[:], din[nm], writes=[t_hc2])
                sconv(0)
                f1_pre(Z[0][:], tZ[0], 4)
                sconv(1)
                f1_part(Z[0][:], tZ[0], 4, pre=True, act_only=True)
                sconv(2)
                if "p2a" in dbg and cc == 0:
                    for s in range(3):
                        d = dbg_tensor(f"Z{s}", [128, L], BF)
                        kb.dma("sp", d, Z[s][:], reads=[tZ[s]])
                kb.barrier()
            with ExitStack() as pb:
                def sbb(name, shape, dt):
                    return pb.enter_context(nc.sbuf_tensor(f"s{next_id()}_" + name, list(shape), dt))

                Y = sbb("Y", [128, 128, 66], BF)
                tY = [T() for _ in range(33)]
                Pqs = [sbb(f"Pq{i}", [66, 64, 128], BF) for i in range(2)]
                t_Pqs = [T(), T()]
                pw = [sbb(f"pw{i}", [128, 2, 2, 128], F32) for i in range(4)]
                tpw = [T() for _ in range(4)]

                def f2_part(consumer):
                    k1 = 0
                    while k1 < 33:
                        n = 2 if k1 + 1 < 33 else 1
                        b = nb()
                        for u_ in range(n):
                            kk = k1 + u_
                            o_ = u_ * 256
                            ar, ai = A[:, kk, :], A[:, 33 + kk, :]
                            mm(ps[b][:, o_:o_ + 128], gtab[:, kk, 0, :], ar, start=True, stop=False, reads=[t_hc2, tA[kk]], writes=[tps[b]])
                            mm(ps[b][:, o_:o_ + 128], gtab[:, kk, 2, :], ai, start=False, stop=True, reads=[t_hc2, tA[kk]], writes=[tps[b]])
                            mm(ps[b][:, o_ + 128:o_ + 256], gtab[:, kk, 1, :], ar, start=True, stop=False, reads=[t_hc2, tA[kk]], writes=[tps[b]])
                            mm(ps[b][:, o_ + 128:o_ + 256], gtab[:, kk, 0, :], ai, start=False, stop=True, reads=[t_hc2, tA[kk]], writes=[tps[b]])
                        consumer(k1, n, b)
                        k1 += n

                def conv(o, sig, t_sig, xm, t_xm, zout, t_zout):
                    r_ = 2 * cc + o
                    kb.dma("sp", H[:].rearrange("p a b c -> p (a b c)"), hall_d[r_ * 128:(r_ + 1) * 128, :], reads=[t_hall[r_]], writes=tH)
                    if "p2h" in dbg and cc == 0:
                        d = dbg_tensor(f"H{o}", [128, 33 * 2 * 128], BF)
                        kb.dma("sp", d, H[:].rearrange("p a b c -> p (a b c)"), reads=tH)

                    def cons_d(k1, n, b):
                        i0 = ((k1 // 2) % 2) * 2
                        p1, p2 = pw[i0], pw[i0 + 1]
                        xv = ps[b][:, 0:n * 256].rearrange("p (k r c) -> p k r c", k=n, r=2)
                        hb_ = H[:, k1, 0, :]
                        pst = list(hb_.ap[0])
                        hr = bass.AP(hb_.tensor, hb_.offset, [pst, [256, n], [0, 2], [1, 128]])
                        hi = bass.AP(hb_.tensor, hb_.offset + 128, [pst, [256, n], [0, 2], [1, 128]])
                        rd = [tps[b]] + [tH[k1 + u_] for u_ in range(n)]
                        tt("dve", p1[:, 0:n, :, :], xv, hr, ALU.mult, reads=rd, writes=[tpw[i0]])
                        tt("dve", p2[:, 0:n, :, :], xv, hi, ALU.mult, reads=rd, writes=[tpw[i0 + 1]])

                        def ck(t_, r_):
                            v = t_[:, 0:n, r_, :]
                            return bass.AP(v.tensor, v.offset, [list(v.ap[0]), [1, 128], [256, n]])

                        wy = [tY[k1 + u_] for u_ in range(n)]
                        tt("dve", Y[:, :, k1:k1 + n], ck(p1, 0), ck(p2, 1), ALU.subtract, reads=[tpw[i0], tpw[i0 + 1]], writes=wy)
                        tt("pool", Y[:, :, 33 + k1:33 + k1 + n], ck(p2, 0), ck(p1, 1), ALU.add, reads=[tpw[i0], tpw[i0 + 1]], writes=wy)

                    if o == 1:
                        f1_part(sig, t_sig, 4)
                    f2_part(cons_d)
                    cnt = 0
                    zv = zout.rearrange("c (a b) -> c b a", a=32)
                    xv_ = xm.rearrange("c (a b) -> c b a", a=32)
                    def i1_stage(q):
                        Pq, t_Pq = Pqs[q % 2], t_Pqs[q % 2]
                        for c0 in range(0, 128, 8):
                            b = nb()
                            for jj in range(8):
                                mm(ps[b][0:66, jj * 64:(jj + 1) * 64], Y[:, c0 + jj, :], etab[:, q, :, :].rearrange("p e n -> p (e n)"),
                                   start=True, stop=True, reads=tY + [t_hc2], writes=[tps[b]])
                            eng = "dve" if (c0 // 8) % 4 == 3 else "act"
                            cp(eng, Pq[:, :, c0:c0 + 8], ps[b][0:66, :].rearrange("p (c k) -> p k c", k=64), reads=[tps[b]], writes=[t_Pq])

                    def i2_stage(q):
                        Pq, t_Pq = Pqs[q % 2], t_Pqs[q % 2]
                        for hh in range(2):
                            b = nb()
                            for j in range(16):
                                n2l = hh * 16 + j
                                n2 = q * 32 + n2l
                                for e_ in range(2):
                                    mm(ps[b][:, j * 32:(j + 1) * 32], Pq[:, e_ * 32 + n2l, :], ttab[:, e_, n2, :], start=(e_ == 0), stop=(e_ == 1),
                                       reads=[t_Pq, t_hc2], writes=[tps[b]])
                            n20 = q * 32 + hh * 16
                            tt("dve", zv[:, n20:n20 + 16, :], ps[b][:].rearrange("p (b a) -> p b a", a=32), xv_[:, n20:n20 + 16, :], ALU.mult,
                               reads=[tps[b], t_xm], writes=[t_zout])

                    i1_stage(0)
                    for q in range(4):
                        if q + 1 < 4:
                            i1_stage(q + 1)
                        i2_stage(q)

                conv(0, Z[0][:], tZ[0], Z[1][:], tZ[1], Z[0][:], tZ[0])
                if "p2c" in dbg and cc == 0:
                    d = dbg_tensor("z1", [128, L], BF)
                    kb.dma("sp", d, Z[0][:], reads=[tZ[0]])
                conv(1, Z[0][:], tZ[0], Z[2][:], tZ[2], Z[1][:], tZ[1])
                kb.dma("sp", zT_d[:, cc, :], Z[1][:], reads=[tZ[1]], writes=[t_zd])
                kb.barrier()
            if stop_after == "p2c0":
                break
        if "p2" in dbg:
            d = dbg_tensor("zT", [128, 4, L], BF)
            kb.dma("sp", d, zT_d, reads=[t_zd])
        kb.barrier()
    if stop_after in ("p2", "p2c0"):
        kb.finish()
        return nc, dbg_out

    t_md = T("mT_d")
    P45 = ExitStack()
    wout = P45.enter_context(nc.sbuf_tensor("sq_wout", [128, 8, D], BF))
    wfg = P45.enter_context(nc.sbuf_tensor("sq_wfg", [128, 8, DFF], BF))
    t_wo = T("wo")
    t_wgs = [T(f"wfg{g}") for g in range(6)]
    with ExitStack() as ph:
        def sb(name, shape, dt):
            return ph.enter_context(nc.sbuf_tensor(f"s{next_id()}_" + name, list(shape), dt))

        _bk = [0]

        def nb():
            _bk[0] = (_bk[0] + 1) % 8
            return _bk[0]

        zTs = [sb(f"zTs{i}", [128, 4, 512], BF) for i in range(2)]
        aTs = [sb(f"aTs{i}", [128, 4, 512], BF) for i in range(2)]
        wgt = sb("wgt", [128, 8, 2048], BF)
        whu = sb("whu", [128, 4, D], BF)
        wau = sb("wau", [128, 4, D], BF)
        t_wg = [T(f"wg{j}") for j in range(16)]
        t_wh = [T(f"wh{j}") for j in range(8)]
        t_wa = [T(f"wa{j}") for j in range(8)]
        t_zb = [T(f"zb{i}") for i in range(2)]
        t_ab = [T(f"ab{i}") for i in range(2)]
        whv = din["w_hy_up"].rearrange("(k p) n -> p k n", p=128)
        wav = din["w_att_up"].rearrange("(k p) n -> p k n", p=128)
        for j in range(8):
            for half in range(2):
                jj = half * 8 + j
                load_w(wgt[:, :, jj * 128:(jj + 1) * 128], w_in_v, OFF_G + jj * 128, t_wg[jj])
            kb.dma("pool", whu[:, :, j * 128:(j + 1) * 128], whv[:, :, j * 128:(j + 1) * 128], writes=[t_wh[j]])
            kb.dma("pool", wau[:, :, j * 128:(j + 1) * 128], wav[:, :, j * 128:(j + 1) * 128], writes=[t_wa[j]])
        def load_za(i):
            kb.dma("sp", zTs[i % 2][:], zT_d[:, :, i * 512:(i + 1) * 512], writes=[t_zb[i % 2]])
            kb.dma("sp", aTs[i % 2][:], aT_d[:, :, i * 512:(i + 1) * 512], writes=[t_ab[i % 2]])

        load_za(0)
        kb.dma("pool", wout[:], din["w_out"].rearrange("(k p) n -> p k n", p=128), writes=[t_wo])
        wfg_v = din["w_fg"].rearrange("(k p) n -> p k n", p=128)
        for g in range(6):
            c0 = g * 512
            w_ = min(512, DFF - c0)
            kb.dma("pool", wfg[:, :, c0:c0 + w_], wfg_v[:, :, c0:c0 + w_], writes=[t_wgs[g]])
        hb = [sb(f"mhb{i}", [128, 8, 512], BF) for i in range(2)]
        thb = [T(), T()]
        mst = [sb(f"mst{i}", [128, 8, 512], BF) for i in range(2)]
        tmst = [T(), T()]
        sg = [sb(f"sg{i}", [128, 512], F32) for i in range(4)]
        tsg = [T() for _ in range(4)]
        mm_ = [sb(f"mm{i}", [128, 512], F32) for i in range(4)]
        tmm = [T() for _ in range(4)]
        kb.dma("sp", hb[0][:], hT_d[:, :, 0:512], reads=[t_hd], writes=[thb[0]])
        for blk in range(8):
            sl = slice(blk * 512, (blk + 1) * 512)
            hbb, th = hb[blk % 2], thb[blk % 2]
            if blk + 1 < 8:
                kb.dma("sp", hb[(blk + 1) % 2][:], hT_d[:, :, (blk + 1) * 512:(blk + 2) * 512], reads=[t_hd], writes=[thb[(blk + 1) % 2]])
                load_za(blk + 1)
            for j in range(8):
                bg1, bg2, by1, by2 = nb(), nb(), nb(), nb()
                for k in range(8):
                    mm(ps[bg1][:], wgt[:, k, j * 128:(j + 1) * 128], hbb[:, k, :], start=(k == 0), stop=(k == 7), reads=[t_wg[j], th], writes=[tps[bg1]])
                for k in range(8):
                    mm(ps[bg2][:], wgt[:, k, 1024 + j * 128:1024 + (j + 1) * 128], hbb[:, k, :], start=(k == 0), stop=(k == 7), reads=[t_wg[8 + j], th], writes=[tps[bg2]])
                for k in range(4):
                    mm(ps[by1][:], whu[:, k, j * 128:(j + 1) * 128], zTs[blk % 2][:, k, :], start=(k == 0), stop=(k == 3), reads=[t_wh[j], t_zb[blk % 2]], writes=[tps[by1]])
                for k in range(4):
                    mm(ps[by2][:], wau[:, k, j * 128:(j + 1) * 128], aTs[blk % 2][:, k, :], start=(k == 0), stop=(k == 3), reads=[t_wa[j], t_ab[blk % 2]], writes=[tps[by2]])
                i0 = (j % 2) * 2
                act(sg[i0][:], ps[bg1][:], AF.Sigmoid, reads=[tps[bg1]], writes=[tsg[i0]])
                act(sg[i0 + 1][:], ps[bg2][:], AF.Sigmoid, reads=[tps[bg2]], writes=[tsg[i0 + 1]])
                tt("dve", mm_[i0][:], ps[by1][:], sg[i0][:], ALU.mult, reads=[tps[by1], tsg[i0]], writes=[tmm[i0]])
                tt("dve", mm_[i0 + 1][:], ps[by2][:], sg[i0 + 1][:], ALU.mult, reads=[tps[by2], tsg[i0 + 1]], writes=[tmm[i0 + 1]])
                tt("pool", mst[blk % 2][:, j, :], mm_[i0][:], mm_[i0 + 1][:], ALU.add, reads=[tmm[i0], tmm[i0 + 1]], writes=[tmst[blk % 2]])
            kb.dma("sp", mT_d[:, :, sl], mst[blk % 2][:], reads=[tmst[blk % 2]], writes=[t_md])
        if "p4" in dbg:
            d = dbg_tensor("mT", [128, 8, L], BF)
            kb.dma("sp", d, mT_d, reads=[t_md])
        kb.barrier()
    if stop_after == "p4":
        kb.finish()
        return nc, dbg_out

    with ExitStack() as ph:
        def sb(name, shape, dt):
            return ph.enter_context(nc.sbuf_tensor(f"s{next_id()}_" + name, list(shape), dt))

        _bk = [0]

        def nb():
            _bk[0] = (_bk[0] + 1) % 8
            return _bk[0]

        wfu = sb("wfu", [128, 8, DFF], BF)
        wfd = sb("wfd", [128, NFF, D], BF)
        t_wu, t_wd = T("wu"), T("wd")
        t_wus = [T(f"wfu{g}") for g in range(6)]
        wfu_v = din["w_fu"].rearrange("(k p) n -> p k n", p=128)
        for g in range(6):
            c0 = g * 512
            w_ = min(512, DFF - c0)
            kb.dma("pool", wfu[:, :, c0:c0 + w_], wfu_v[:, :, c0:c0 + w_], writes=[t_wus[g]])
        for j in range(0, NFF, 2):
            kb.dma("pool", wfd[:, j:j + 2, :], din["w_fd"][j * 128:(j + 2) * 128, :].rearrange("(k p) n -> p k n", p=128), writes=[t_wd])
        xts = [sb(f"fx{i}", [128, D], F32) for i in range(3)]
        txt = [T(), T(), T()]
        mts = [sb(f"fm{i}", [128, 8, 128], BF) for i in range(2)]
        tmt = [T(), T()]
        tmp = sb("ftmp", [128, D], F32)
        t_tmp = T()
        junk = sb("fjunk", [128, D], BF)
        t_junk = T()
        xs = sb("fxs", [128, D], BF)
        t_xs = T()
        hfT = [sb(f"hfT{i}", [128, 8, 128], BF) for i in range(2)]
        t_hf = [T(), T()]
        aT = sb("aT", [128, NFF, 128], BF)
        t_aT = T()
        sgt = [sb(f"sgt{i}", [128, 512], F32) for i in range(2)]
        tsgt = [T(), T()]
        atm = sb("atm", [128, DFF], BF)
        t_atm = T()
        fin = sb("ffin", [128, 16], F32)
        t_fin = [T(), T(), T()]
        t_out = T("out")

        def norm_resid(banks, xt, tx, grow, c0, tf):
            for n in range(2):
                act(junk[:, n * 512:(n + 1) * 512], ps[banks[n]][:], AF.Square, reads=[tps[banks[n]]], writes=[t_junk, tf],
                    accum_out=fin[:, c0 + n:c0 + n + 1])
            tt("dve", fin[:, c0 + 2:c0 + 3], fin[:, c0:c0 + 1], fin[:, c0 + 1:c0 + 2], ALU.add, reads=[tf], writes=[tf])
            rstd_of(fin[:, c0 + 2:c0 + 3], fin[:, c0 + 3:c0 + 4], 1.0 / D, tf)
            for n in range(2):
                hs = slice(n * 512, (n + 1) * 512)
                stt(tmp[:, hs], ps[banks[n]][:], fin[:, c0 + 3:c0 + 4], grow[:, hs], ALU.mult, ALU.mult,
                    reads=[tps[banks[n]], tf, t_rows], writes=[t_tmp])
            tt("pool", xt[:], xt[:], tmp[:], ALU.add, reads=[t_tmp, tx], writes=[tx])

        def s1a(i):
            tsl = slice(i * 128, (i + 1) * 128)
            xt, tx = xts[i % 3], txt[i % 3]
            mt, tm = mts[i % 2], tmt[i % 2]
            kb.dma("sp", xt[:], din["x"][tsl, :], writes=[tx])
            kb.dma("sp", mt[:], mT_d[:, :, tsl], reads=[t_md], writes=[tm])
            for n in range(2):
                for k in range(8):
                    mm(ps[n][:], mt[:, k, :], wout[:, k, n * 512:(n + 1) * 512], start=(k == 0), stop=(k == 7),
                       reads=[tm, t_wo], writes=[tps[n]])

        def s1b(i):
            xt, tx = xts[i % 3], txt[i % 3]
            norm_resid([0, 1], xt, tx, g1row, 0, t_fin[0])
            act(junk[:], xt[:], AF.Square, reads=[tx], writes=[t_junk, t_fin[1]], accum_out=fin[:, 4:5])
            rstd_of(fin[:, 4:5], fin[:, 5:6], 1.0 / D, t_fin[1])
            act(xs[:], xt[:], AF.Copy, reads=[tx, t_fin[1]], writes=[t_xs], scale=fin[:, 5:6])

        def s1c(i):
            for k in range(8):
                tr(psb[4][:, k * 128:(k + 1) * 128], xs[:, k * 128:(k + 1) * 128], ident_bf[:], reads=[t_xs, t_const], writes=[tps[4]])
            for k in range(8):
                ts("dve", hfT[i % 2][:, k, :], psb[4][:, k * 128:(k + 1) * 128], A2(k), B2(k), ALU.mult, ALU.add,
                   reads=[tps[4], t_cols], writes=[t_hf[i % 2]])

        def s2a(i):
            h_, th_ = hfT[i % 2], t_hf[i % 2]
            pairs = [(5, 6), (7, 4)]
            for g in range(6):
                c0 = g * 512
                w_ = min(512, DFF - c0)
                bg, bu = pairs[g % 2]
                for k in range(8):
                    mm(ps[bg][:, 0:w_], h_[:, k, :], wfg[:, k, c0:c0 + w_], start=(k == 0), stop=(k == 7), reads=[t_wgs[g], th_], writes=[tps[bg]])
                for k in range(8):
                    mm(ps[bu][:, 0:w_], h_[:, k, :], wfu[:, k, c0:c0 + w_], start=(k == 0), stop=(k == 7), reads=[t_wus[g], th_], writes=[tps[bu]])
                act(sgt[g % 2][:, 0:w_], ps[bg][:, 0:w_], AF.Silu, reads=[tps[bg]], writes=[tsgt[g % 2]])
                tt("dve", atm[:, c0:c0 + w_], ps[bu][:, 0:w_], sgt[g % 2][:, 0:w_], ALU.mult, reads=[tps[bu], tsgt[g % 2]], writes=[t_atm])
            j = 0
            bi = 0
            while j < NFF:
                n = min(8, NFF - j)
                b = (7, 4, 5)[bi % 3]
                bi += 1
                for jj in range(n):
                    tr(psb[b][:, jj * 128:(jj + 1) * 128], atm[:, (j + jj) * 128:(j + jj + 1) * 128], ident_bf[:], reads=[t_atm, t_const], writes=[tps[b]])
                cp("dve" if bi % 2 else "act", aT[:, j:j + n, :], psb[b][:, 0:n * 128].rearrange("p (j t) -> p j t", t=128), reads=[tps[b]], writes=[t_aT])
                j += n

        def s2b(i):
            tsl = slice(i * 128, (i + 1) * 128)
            xt, tx = xts[i % 3], txt[i % 3]
            for n in range(2):
                for j in range(NFF):
                    mm(ps[2 + n][:], aT[:, j, :], wfd[:, j, n * 512:(n + 1) * 512], start=(j == 0), stop=(j == NFF - 1),
                       reads=[t_aT, t_wd], writes=[tps[2 + n]])
            norm_resid([2, 3], xt, tx, g2row, 8, t_fin[2])
            kb.dma("sp", out[tsl, :], xt[:], reads=[tx], writes=[t_out])

        s1a(0)
        s1b(0)
        s1c(0)
        s1a(1)
        s1b(1)
        for i in range(32):
            s2a(i)
            if i + 1 < 32:
                s1c(i + 1)
            if i + 2 < 32:
                s1a(i + 2)
                s1b(i + 2)
            s2b(i)
        kb.barrier()
    kb.finish()
    return nc, dbg_out


_NC = None


def kernel(**inputs):
    global _NC
    if _NC is None:
        _NC = build()[0]
    in_maps = [layout_inputs(inputs, b) for b in range(8)]
    res = run_bass_kernel_spmd(_NC, in_maps, core_ids=list(range(8)))
    return np.stack([np.asarray(r["out"], dtype=np.float32) for r in res.results], 0)
```
